# Optimizing a Trainium2 kernel written in Bass

```python
import math
import jax, jax.numpy as jnp
from jax import lax
import numpy as np

D_MODEL = 1024
BATCH = 2
SEQ = 8192
DEPTH = 2

LA_HEADS = 4
LA_DK = 128
LA_DV = 128
LA_QK = LA_HEADS * LA_DK
LA_V = LA_HEADS * LA_DV
CONV_K = 4
CHUNK = 64
DIFF_HEADS = 4
DIFF_DH = 64
DIFF_DV = 2 * DIFF_DH
DIFF_QK = DIFF_HEADS * 2 * DIFF_DH
DIFF_V = DIFF_HEADS * DIFF_DV
Q_BLOCK = 128
NUM_BUCKETS = 32
MAX_DISTANCE = 128
D_MIX = LA_V + DIFF_V
SPLIT_SIZES = [3 * LA_QK if False else 2 * LA_QK + LA_V, LA_V, LA_HEADS, LA_HEADS, DIFF_QK, DIFF_QK, DIFF_V]
SPLIT_IDX = [int(s) for s in np.cumsum(SPLIT_SIZES)[:-1]]
IN_DIM = int(sum(SPLIT_SIZES))
LA_CONV_DIM = 2 * LA_QK + LA_V
D_FF = -(-(8 * D_MODEL // 3) // 256) * 256
NORM_EPS = 1e-6

kernel_name = "hybrid_gdn_diffattn_parallel_heads"


def rmsnorm(x, w, eps=NORM_EPS):
    xf = x.astype(jnp.float32)
    y = xf * lax.rsqrt(jnp.mean(xf * xf, axis=-1, keepdims=True) + eps) * w.astype(jnp.float32)
    return y.astype(x.dtype)


def l2norm(x, eps=1e-6):
    return x * lax.rsqrt(jnp.sum(x * x, axis=-1, keepdims=True) + eps)


def causal_conv(x, w):
    T = x.shape[1]
    K = w.shape[0]
    xp = jnp.pad(x, ((0, 0), (K - 1, 0), (0, 0)))
    out = xp[:, 0:T] * w[0]
    for i in range(1, K):
        out = out + xp[:, i:i + T] * w[i]
    return out


def gated_delta_rule(q, k, v, g, beta):
    f32 = jnp.float32
    B, T, H, Dk = q.shape
    Dv = v.shape[-1]
    n = T // CHUNK
    q = l2norm(q.astype(f32)) * (Dk ** -0.5)
    k = l2norm(k.astype(f32))
    v = v.astype(f32)

    def chunk(t):
        return t.reshape(B, n, CHUNK, H, -1).transpose(1, 0, 3, 2, 4)

    q, k, v = chunk(q), chunk(k), chunk(v)
    beta = beta.astype(f32).reshape(B, n, CHUNK, H).transpose(1, 0, 3, 2)
    g = jnp.cumsum(g.astype(f32).reshape(B, n, CHUNK, H).transpose(1, 0, 3, 2), axis=-1)

    causal = jnp.tril(jnp.ones((CHUNK, CHUNK), dtype=bool))
    strict = jnp.tril(jnp.ones((CHUNK, CHUNK), dtype=bool), k=-1)
    gdiff = g[..., :, None] - g[..., None, :]
    decay = jnp.where(causal, jnp.exp(jnp.where(causal, gdiff, 0.0)), 0.0)

    k_beta = k * beta[..., None]
    L = jnp.where(strict, jnp.einsum('nbhcd,nbhsd->nbhcs', k_beta, k) * decay, 0.0)
    A = L + jnp.eye(CHUNK, dtype=f32)
    rhs = jnp.concatenate([v * beta[..., None], k_beta * jnp.exp(g)[..., None]], axis=-1)
    sol = lax.linalg.triangular_solve(A, rhs, left_side=True, lower=True, unit_diagonal=True)
    u, w = sol[..., :Dv], sol[..., Dv:]
    intra = jnp.where(causal, jnp.einsum('nbhcd,nbhsd->nbhcs', q, k) * decay, 0.0)

    def step(S, inp):
        q_i, k_i, u_i, w_i, g_i, a_i = inp
        v_new = u_i - jnp.einsum('bhcd,bhde->bhce', w_i, S)
        o = (jnp.einsum('bhcd,bhde->bhce', q_i * jnp.exp(g_i)[..., None], S)
             + jnp.einsum('bhcs,bhse->bhce', a_i, v_new))
        g_last = g_i[..., -1]
        S = (S * jnp.exp(g_last)[..., None, None]
             + jnp.einsum('bhcd,bhce->bhde', k_i * jnp.exp(g_last[..., None] - g_i)[..., None], v_new))
        return S, o

    S0 = jnp.zeros((B, H, Dk, Dv), dtype=f32)
    _, o = lax.scan(step, S0, (q, k, u, w, g, intra))
    return o.transpose(1, 0, 3, 2, 4).reshape(B, T, H, Dv)


def t5_bucket(rel):
    n = jnp.maximum(rel, 0)
    max_exact = NUM_BUCKETS // 2
    nf = jnp.maximum(n, 1).astype(jnp.float32)
    large = max_exact + (jnp.log(nf / max_exact) / math.log(MAX_DISTANCE / max_exact)
                         * (NUM_BUCKETS - max_exact)).astype(jnp.int32)
    large = jnp.minimum(large, NUM_BUCKETS - 1)
    return jnp.where(n < max_exact, n, large)


def diff_attention(q, k, v, lam, rel_bias):
    B, T, H, _, dh = q.shape
    nb = T // Q_BLOCK
    q = q * (dh ** -0.5)
    qb = q.reshape(B, nb, Q_BLOCK, H, 2, dh).transpose(1, 0, 2, 3, 4, 5)
    kpos = jnp.arange(T)

    def block(args):
        q_blk, i = args
        qpos = i * Q_BLOCK + jnp.arange(Q_BLOCK)
        rel = qpos[:, None] - kpos[None, :]
        bias = rel_bias[t5_bucket(rel)].astype(jnp.float32)
        s = jnp.einsum('bqhmd,bkhmd->bhmqk', q_blk, k).astype(jnp.float32)
        s = s + bias.transpose(2, 0, 1)[None, :, None]
        s = jnp.where(rel >= 0, s, -jnp.inf)
        p = jax.nn.softmax(s, axis=-1)
        p = p[:, :, 0] - lam * p[:, :, 1]
        return jnp.einsum('bhqk,bkhe->bqhe', p.astype(v.dtype), v)

    o = lax.map(block, (qb, jnp.arange(nb)))
    return o.transpose(1, 0, 2, 3, 4).reshape(B, T, H, 2 * dh)


def setup_inputs(seed: int = 0) -> dict:
    key = jax.random.key(seed)
    ks = jax.random.split(key, 20)
    f32 = jnp.float32
    nrm = lambda k, s, sc: jax.random.normal(k, s, f32) * sc
    x = jax.random.normal(ks[0], (BATCH, SEQ, D_MODEL), f32)
    attn_norm_w = 1.0 + nrm(ks[1], (DEPTH, D_MODEL), 0.1)
    w_in = nrm(ks[2], (DEPTH, D_MODEL, IN_DIM), D_MODEL ** -0.5)
    conv_w = nrm(ks[3], (DEPTH, CONV_K, LA_CONV_DIM), CONV_K ** -0.5)
    a_log = jnp.log(jax.random.uniform(ks[4], (DEPTH, LA_HEADS), f32, 1.0, 16.0))
    dt = jnp.exp(jax.random.uniform(ks[5], (DEPTH, LA_HEADS), f32, math.log(1e-3), math.log(1e-1)))
    dt_bias = jnp.log(jnp.expm1(dt))
    la_norm_w = 1.0 + nrm(ks[6], (DEPTH, LA_DV), 0.1)
    lambda_q1 = nrm(ks[7], (DEPTH, DIFF_DH), 0.1)
    lambda_k1 = nrm(ks[8], (DEPTH, DIFF_DH), 0.1)
    lambda_q2 = nrm(ks[9], (DEPTH, DIFF_DH), 0.1)
    lambda_k2 = nrm(ks[10], (DEPTH, DIFF_DH), 0.1)
    diff_norm_w = 1.0 + nrm(ks[11], (DEPTH, DIFF_DV), 0.1)
    rel_bias = nrm(ks[12], (NUM_BUCKETS, DIFF_HEADS), 0.5)
    w_out = nrm(ks[13], (DEPTH, D_MIX, D_MODEL), D_MIX ** -0.5)
    ffn_norm_w = 1.0 + nrm(ks[14], (DEPTH, D_MODEL), 0.1)
    w_gate_up = nrm(ks[15], (DEPTH, D_MODEL, 2 * D_FF), D_MODEL ** -0.5)
    w_down = nrm(ks[16], (DEPTH, D_FF, D_MODEL), D_FF ** -0.5)
    final_norm_w = 1.0 + nrm(ks[17], (D_MODEL,), 0.1)
    return {"x": x, "attn_norm_w": attn_norm_w, "w_in": w_in, "conv_w": conv_w,
            "a_log": a_log, "dt_bias": dt_bias, "la_norm_w": la_norm_w,
            "lambda_q1": lambda_q1, "lambda_k1": lambda_k1, "lambda_q2": lambda_q2,
            "lambda_k2": lambda_k2, "diff_norm_w": diff_norm_w, "rel_bias": rel_bias,
            "w_out": w_out, "ffn_norm_w": ffn_norm_w, "w_gate_up": w_gate_up,
            "w_down": w_down, "final_norm_w": final_norm_w}


def reference(x, attn_norm_w, w_in, conv_w, a_log, dt_bias, la_norm_w, lambda_q1, lambda_k1,
              lambda_q2, lambda_k2, diff_norm_w, rel_bias, w_out, ffn_norm_w, w_gate_up,
              w_down, final_norm_w):
    f32 = jnp.float32
    B, T, _ = x.shape
    for l in range(DEPTH):
        h = rmsnorm(x, attn_norm_w[l])
        proj = h @ w_in[l]
        qkv_la, z_la, b_la, a_la, q_d, k_d, v_d = jnp.split(proj, SPLIT_IDX, axis=-1)

        qkv_la = jax.nn.silu(causal_conv(qkv_la, conv_w[l]))
        q_la, k_la, v_la = jnp.split(qkv_la, [LA_QK, 2 * LA_QK], axis=-1)
        q_la = q_la.reshape(B, T, LA_HEADS, LA_DK)
        k_la = k_la.reshape(B, T, LA_HEADS, LA_DK)
        v_la = v_la.reshape(B, T, LA_HEADS, LA_DV)
        beta = jax.nn.sigmoid(b_la.astype(f32))
        g = -jnp.exp(a_log[l].astype(f32)) * jax.nn.softplus(a_la.astype(f32) + dt_bias[l].astype(f32))
        o_la = gated_delta_rule(q_la, k_la, v_la, g, beta).astype(x.dtype)
        o_la = rmsnorm(o_la, la_norm_w[l]) * jax.nn.silu(z_la.reshape(B, T, LA_HEADS, LA_DV))
        o_la = o_la.reshape(B, T, LA_V)

        lam_init = 0.8 - 0.6 * math.exp(-0.3 * l)
        lam = (jnp.exp(jnp.sum(lambda_q1[l].astype(f32) * lambda_k1[l].astype(f32)))
               - jnp.exp(jnp.sum(lambda_q2[l].astype(f32) * lambda_k2[l].astype(f32))) + lam_init)
        o_d = diff_attention(q_d.reshape(B, T, DIFF_HEADS, 2, DIFF_DH),
                             k_d.reshape(B, T, DIFF_HEADS, 2, DIFF_DH),
                             v_d.reshape(B, T, DIFF_HEADS, DIFF_DV), lam, rel_bias)
        o_d = rmsnorm(o_d, diff_norm_w[l], eps=1e-5) * (1.0 - lam_init)
        o_d = o_d.reshape(B, T, DIFF_V)

        x = x + jnp.concatenate([o_la, o_d], axis=-1) @ w_out[l]

        h = rmsnorm(x, ffn_norm_w[l])
        gate, up = jnp.split(h @ w_gate_up[l], 2, axis=-1)
        x = x + (jax.nn.silu(gate) * up) @ w_down[l]
    return rmsnorm(x, final_norm_w)
```

```python
import math
from contextlib import ExitStack

import numpy as np
import ml_dtypes
import concourse.bass as bass
import concourse.mybir as mybir
from concourse.bass_utils import run_bass_kernel_spmd

F32 = mybir.dt.float32
BF16 = mybir.dt.bfloat16
AF = mybir.ActivationFunctionType
ALU = mybir.AluOpType
AX = mybir.AxisListType

D_MODEL = 1024
SEQ = 8192
BATCH = 2
DEPTH = 2
NH = 4
D_FF = 2816
IN_DIM = 3592
NORM_EPS = 1e-6
NCORES = 8

ENGS = ("pe", "act", "dve", "pool", "sp")


class Buf:
    __slots__ = ("name", "w", "r", "dsem", "dcnt", "excl")

    def __init__(self, name, excl=False):
        self.name = name
        self.excl = excl
        self.w = None
        self.r = []
        self.dsem = None
        self.dcnt = 0


class Op:
    __slots__ = ("eng", "fn", "deps", "dma", "sem", "val", "needed", "inc")

    def __init__(self, eng, fn, dma=False):
        self.eng = eng
        self.fn = fn
        self.deps = []
        self.dma = dma
        self.sem = None
        self.val = 0
        self.needed = False
        self.inc = 16


class Sched:
    def __init__(self, nc, es):
        self.nc = nc
        self.es = es
        self.ops = {e: [] for e in ENGS}
        self.esem = {e: es.enter_context(nc.semaphore("s_" + e)) for e in ENGS}
        self.nbuf = 0
        self.cnt = {e: 0 for e in ENGS}
        self.phase_dmas = []
        self.nsem = 0

    def buf(self, name=None, excl=False):
        self.nbuf += 1
        return Buf(name or ("b%d" % self.nbuf), excl)

    def bufs(self, n, name="b", excl=False):
        return [self.buf("%s%d" % (name, i), excl) for i in range(n)]

    def _link(self, o, R, W):
        deps = []
        for b in R:
            if b.w is not None:
                d = b.w
                if d.dma or o.dma or d.eng != o.eng or o.eng != "pe":
                    deps.append(d)
            if b.excl:
                for d in b.r:
                    if d.eng != o.eng:
                        deps.append(d)
        for b in W:
            if b.w is not None:
                d = b.w
                if d.dma or o.dma or d.eng != o.eng or o.eng != "pe":
                    deps.append(d)
            for d in b.r:
                if d.dma or o.dma or d.eng != o.eng or o.eng != "pe":
                    deps.append(d)
        o.deps = deps
        for b in R:
            if b in W:
                continue
            if b.excl:
                b.r = []
            elif not o.dma:
                b.r = [x for x in b.r if x.dma or x.eng != o.eng]
            b.r.append(o)
        for b in W:
            b.w = o
            b.r = []

    def op(self, eng, fn, R=(), W=()):
        o = Op(eng, fn)
        self._link(o, R, W)
        self.ops[eng].append(o)
        return o

    def dma(self, q, out, in_, sb, R=(), W=()):
        return self.dma_fn(q, lambda e, out=out, in_=in_: e.dma_start(out=out, in_=in_), sb, R, W)

    def dma_fn(self, q, fn, sb, R=(), W=(), inc=16):
        if sb.dsem is None:
            self.nsem += 1
            sb.dsem = self.es.enter_context(self.nc.semaphore("d%d_%s" % (self.nsem, sb.name)))
        o = Op(q, fn, dma=True)
        o.inc = inc
        sb.dcnt += (inc if inc else 1)
        o.sem = sb.dsem
        o.val = sb.dcnt
        self._link(o, R, W)
        self.ops[q].append(o)
        self.phase_dmas.append(o)
        return o

    def phase_barrier(self):
        lasts = []
        for e in ENGS:
            real = [o for o in self.ops[e] if o.fn is not None and not o.dma]
            if real:
                lasts.append(real[-1])
        deps = lasts + list(self.phase_dmas)
        for e in ENGS:
            o = Op(e, None)
            o.deps = [d for d in deps if d.dma or d.eng != e]
            self.ops[e].append(o)
        self.phase_dmas = []

    def barrier_wait(self, eng, R):
        o = Op(eng, None)
        self._link(o, (), R)
        self.ops[eng].append(o)
        return o

    def finalize(self):
        for e in ENGS:
            for o in self.ops[e]:
                for d in o.deps:
                    d.needed = True
        for e in ENGS:
            c = self.cnt[e]
            for o in self.ops[e]:
                if not o.dma and o.needed and o.fn is not None:
                    c += 1
                    o.sem = self.esem[e]
                    o.val = c
            self.cnt[e] = c
        ops = self.ops
        self.ops = {e: [] for e in ENGS}

        def run(eng, lst):
            seen = {}
            for o in lst:
                waits = {}
                for d in o.deps:
                    k = id(d.sem)
                    if k not in waits or waits[k][1] < d.val:
                        waits[k] = (d.sem, d.val)
                for k, (sem, val) in waits.items():
                    if seen.get(k, 0) < val:
                        eng.wait_ge(sem, val)
                        seen[k] = val
                if o.fn is None:
                    continue
                ins = o.fn(eng)
                if o.dma:
                    if o.inc:
                        ins.then_inc(o.sem, o.inc)
                    else:
                        ins.then_inc(o.sem)
                elif o.needed:
                    ins.then_inc(o.sem, 1)

        with self.nc.Block() as block:
            @block.tensor
            def _(e):
                run(e, ops["pe"])

            @block.scalar
            def _(e):
                run(e, ops["act"])

            @block.vector
            def _(e):
                run(e, ops["dve"])

            @block.gpsimd
            def _(e):
                run(e, ops["pool"])

            @block.sync
            def _(e):
                run(e, ops["sp"])


class KB:
    def __init__(self):
        self.nc = bass.Bass("TRN2", target_bir_lowering=False)
        self.es = ExitStack()
        self.S = Sched(self.nc, self.es)
        self.n = 0
        self.pes = None
        self.phase = 0

    def begin_phase(self):
        self.phase += 1
        self.pes = ExitStack()

    def end_phase(self):
        self.S.phase_barrier()
        self.S.finalize()
        self.pes.close()
        self.pes = None
        for b in getattr(self, "persist", []):
            b.w = None
            b.r = []

    def sb(self, shape, dt, name=None):
        self.n += 1
        st = self.pes if self.pes is not None else self.es
        return st.enter_context(self.nc.sbuf_tensor("sb%d_" % self.phase + (name or ("t%d" % self.n)), list(shape), dt))

    def dint(self, name, shape, dt):
        return self.nc.dram_tensor(name, list(shape), dt).ap()

    def ps(self, shape, dt, name=None):
        self.n += 1
        return self.es.enter_context(self.nc.psum_tensor("ps_" + (name or ("p%d" % self.n)), list(shape), dt))

    def din(self, name, shape, dt):
        return self.nc.dram_tensor(name, list(shape), dt, kind="ExternalInput").ap()

    def dout(self, name, shape, dt):
        return self.nc.dram_tensor(name, list(shape), dt, kind="ExternalOutput").ap()

    def done(self):
        self.S.finalize()
        self.es.close()
        return self.nc


def build_ffn(NT, final_norm, ext=None):
    kb = ext["kb"] if ext else KB()
    nc, S = kb.nc, kb.S
    NTL = NT // 128
    KD = D_MODEL // 128
    JF = D_FF // 128

    if ext:
        x_d, wout_d, wgu_d, wdn_d, fnw_d, idb_d, y_d = (
            ext[k] for k in ("x", "w_out", "w_gu", "w_down", "ffn_norm_w", "ident_bf", "y"))
        if final_norm:
            fin_d = ext["final_w_bc"]
        o_d = None
    else:
        x_d = kb.din("x", [NT, D_MODEL], F32)
        o_d = kb.din("o", [NT, D_MODEL], BF16)
        wout_d = kb.din("w_out", [D_MODEL, D_MODEL], F32)
        wgu_d = kb.din("w_gu", [D_MODEL, 2 * D_FF], F32)
        wdn_d = kb.din("w_down", [D_FF, D_MODEL], F32)
        fnw_d = kb.din("ffn_norm_w", [128, KD], F32)
        idb_d = kb.din("ident_bf", [128, 128], BF16)
        if final_norm:
            fin_d = kb.din("final_w_bc", [128, D_MODEL], F32)
        y_d = kb.dout("y", [NT, D_MODEL], F32)

    wout = kb.sb([128, KD, D_MODEL], BF16, "wout")
    wgu = kb.sb([128, KD, 2 * D_FF], BF16, "wgu")
    wdn = kb.sb([128, JF, D_MODEL], BF16, "wdn")
    fnw = kb.sb([128, KD], F32, "fnw")
    idb = kb.sb([128, 128], BF16, "idb")
    b_wout, b_wgu, b_wdn, b_fnw, b_idb = S.bufs(5, "wres")
    if final_norm:
        finw = kb.sb([128, D_MODEL], F32, "finw")
        b_finw = S.buf("finw")
        S.dma("sp", finw[:], fin_d, b_finw, W=[b_finw])
    S.dma("sp", fnw[:], fnw_d, b_fnw, W=[b_fnw])
    S.dma("sp", idb[:], idb_d, b_idb, W=[b_idb])

    STG = 1408
    NSTG = 3
    stg = [kb.sb([128, STG], F32, "stg%d" % i) for i in range(NSTG)]
    b_stg = S.bufs(NSTG, "stg")
    cnt = [0]
    cast_engs = ("dve", "pool")

    def load_cast(dst_ap, src_ap, n, wbuf, scale_ap=None):
        i = cnt[0] % NSTG
        q = "sp" if (cnt[0] % 2 == 0) else "act"
        S.dma(q, stg[i][:, 0:n], src_ap, b_stg[i], W=[b_stg[i]])
        ce = cast_engs[cnt[0] % 2]
        if scale_ap is None:
            S.op(ce, lambda e, d=dst_ap, s=stg[i][:, 0:n]: e.tensor_copy(out=d, in_=s),
                 R=[b_stg[i]], W=[wbuf])
        else:
            S.op(ce, lambda e, d=dst_ap, s=stg[i][:, 0:n], sc=scale_ap:
                 e.tensor_scalar(out=d, in0=s, scalar1=sc, scalar2=None, op0=ALU.mult),
                 R=[b_stg[i], b_fnw], W=[wbuf])
        cnt[0] += 1

    wout_v = wout_d.rearrange("(ko p) n -> p ko n", p=128)
    for ko in range(KD):
        load_cast(wout[:, ko, :], wout_v[:, ko, :], D_MODEL, b_wout)
    wgu_v = wgu_d.rearrange("(ko p) n -> p ko n", p=128)
    for ko in range(KD):
        for c in range(4):
            load_cast(wgu[:, ko, c * STG:(c + 1) * STG], wgu_v[:, ko, c * STG:(c + 1) * STG], STG,
                      b_wgu, scale_ap=fnw[:, ko:ko + 1])
    wdn_v = wdn_d.rearrange("(j p) n -> p j n", p=128)
    for j in range(JF):
        load_cast(wdn[:, j, :], wdn_v[:, j, :], D_MODEL, b_wdn)

    xin = [kb.sb([128, D_MODEL], F32, "xin%d" % i) for i in range(2)]
    oin = [kb.sb([128, D_MODEL], BF16, "oin%d" % i) for i in range(2)]
    b_xin = S.bufs(2, "xin")
    b_oin = S.bufs(2, "oin")
    tbuf = kb.sb([128, KD, 128], BF16, "tbuf")
    b_tbuf = S.buf("tbuf")
    hn = kb.sb([128, D_MODEL], BF16, "hn")
    b_hn = S.buf("hn")
    junk = kb.sb([128, D_MODEL], BF16, "junk")
    b_junk = S.buf("junk")
    aT = kb.sb([128, JF, 128], BF16, "aT")
    b_aT = S.buf("aT")
    sg = [kb.sb([128, 128], F32, "sg%d" % i) for i in range(2)]
    b_sg = S.bufs(2, "sg")
    stat = kb.sb([128, 8], F32, "stat")
    b_stat = S.buf("stat")
    epsc = kb.sb([128, 1], F32, "epsc")
    b_epsc = S.buf("epsc")
    S.op("dve", lambda e: e.memset(epsc[:], NORM_EPS), W=[b_epsc])

    if ext:
        pbank, b_pb = ext["pb"], ext["b_pb"]
        ptb_t = pbank[0][:].bitcast(BF16)
        idx_sb = kb.sb([128, 4 * NTL], mybir.dt.int32, "idx")
        b_idx = S.buf("idx")
        S.dma("sp", idx_sb[:], ext["idx"], b_idx, W=[b_idx])
        TT_ = ext["T"]
    else:
        pbank = [None] + [kb.ps([128, 512], F32, "pb%d" % i) for i in range(1, 8)]
        b_pb = S.bufs(8, "pb", excl=True)
        ptb_t = kb.ps([128, D_MODEL], BF16, "ptb")

    x_v = x_d.rearrange("(t p) d -> t p d", p=128)
    o_v = o_d.rearrange("(t p) d -> t p d", p=128) if o_d is not None else None
    y_v = y_d.rearrange("(t p) d -> t p d", p=128)

    def rms_scale(src, b_src, col):
        S.op("act", lambda e: e.activation(out=junk[:], in_=src, func=AF.Square,
                                           accum_out=stat[:, col:col + 1]),
             R=[b_src], W=[b_junk, b_stat])
        S.op("act", lambda e: e.activation(out=stat[:, col:col + 1], in_=stat[:, col:col + 1], func=AF.Sqrt,
                                           scale=1.0 / D_MODEL, bias=epsc[:, 0:1]),
             R=[b_stat, b_epsc], W=[b_stat])
        S.op("dve", lambda e: e.reciprocal(out=stat[:, col:col + 1], in_=stat[:, col:col + 1]),
             R=[b_stat], W=[b_stat])

    hT2 = [kb.sb([128, KD, 128], BF16, "hT2_%d" % i) for i in range(2)]
    b_hT2 = S.bufs(2, "hT2")
    aT2 = [aT, kb.sb([128, JF, 128], BF16, "aT_1")]
    b_aT2 = [b_aT, S.buf("aT_1")]
    ptb = ptb_t

    def stageA(t):
        i = t % 2
        xt, ot = xin[i], oin[i]
        S.dma("sp", xt[:], x_v[t], b_xin[i], W=[b_xin[i]])
        if ext:
            for r_ in range(4):
                S.dma_fn("pool", lambda e, ot=ot, r_=r_, t=t: e.indirect_dma_start(
                    out=ot[:, r_ * 256:(r_ + 1) * 256], out_offset=None,
                    in_=ext["og_all"],
                    in_offset=bass.IndirectOffsetOnAxis(ap=idx_sb[:, r_ * NTL + t:r_ * NTL + t + 1], axis=0)),
                    b_oin[i], R=[b_idx] + list(ext["b_og_all"]), W=[b_oin[i]])
        else:
            S.dma("pool", ot[:], o_v[t], b_oin[i], W=[b_oin[i]])

        def tr_group(e, src=ot):
            ins = None
            for k in range(KD):
                ins = e.transpose(out=ptb[:, k * 128:(k + 1) * 128], in_=src[:, k * 128:(k + 1) * 128],
                                  identity=idb[:])
            return ins
        S.op("pe", tr_group, R=[b_oin[i], b_idb], W=[b_pb[0]])
        S.op("act", lambda e: e.copy(out=tbuf[:].rearrange("p k t -> p (k t)"), in_=ptb[:, 0:KD * 128]),
             R=[b_pb[0]], W=[b_tbuf])
        for nchunk in range(2):
            bk = 1 + nchunk

            def mm_out(e, bk=bk, nchunk=nchunk):
                ins = None
                for k in range(KD):
                    ins = e.matmul(out=pbank[bk][:], lhsT=tbuf[:, k, :],
                                   rhs=wout[:, k, nchunk * 512:(nchunk + 1) * 512],
                                   start=(k == 0), stop=(k == KD - 1))
                return ins
            S.op("pe", mm_out, R=[b_tbuf, b_wout], W=[b_pb[bk]])
            S.op("dve", lambda e, bk=bk, nchunk=nchunk, xt=xt:
                 e.tensor_tensor(out=xt[:, nchunk * 512:(nchunk + 1) * 512],
                                 in0=pbank[bk][:], in1=xt[:, nchunk * 512:(nchunk + 1) * 512], op=ALU.add),
                 R=[b_pb[bk], b_xin[i]], W=[b_xin[i]])
        rms_scale(xt[:], b_xin[i], 0)
        S.op("act", lambda e, xt=xt: e.activation(out=hn[:], in_=xt[:], func=AF.Copy, scale=stat[:, 0:1]),
             R=[b_xin[i], b_stat], W=[b_hn])

        def tr_group2(e):
            ins = None
            for k in range(KD):
                ins = e.transpose(out=ptb[:, k * 128:(k + 1) * 128], in_=hn[:, k * 128:(k + 1) * 128],
                                  identity=idb[:])
            return ins
        S.op("pe", tr_group2, R=[b_hn, b_idb], W=[b_pb[0]])
        S.op("act", lambda e, i=i: e.copy(out=hT2[i][:].rearrange("p k t -> p (k t)"), in_=ptb[:, 0:KD * 128]),
             R=[b_pb[0]], W=[b_hT2[i]])

    def stageB(t):
        i = t % 2
        for j in range(JF):
            bk = 3 + (j % 2)

            def mm_gu(e, bk=bk, j=j, i=i):
                ins = None
                for half in range(2):
                    for k in range(KD):
                        c0 = half * D_FF + j * 128
                        ins = e.matmul(out=pbank[bk][:, half * 128:(half + 1) * 128],
                                       lhsT=wgu[:, k, c0:c0 + 128], rhs=hT2[i][:, k, :],
                                       start=(k == 0), stop=(k == KD - 1))
                return ins
            S.op("pe", mm_gu, R=[b_hT2[i], b_wgu], W=[b_pb[bk]])
            s_ = j % 2
            S.op("act", lambda e, bk=bk, s_=s_: e.activation(out=sg[s_][:], in_=pbank[bk][:, 0:128], func=AF.Silu),
                 R=[b_pb[bk]], W=[b_sg[s_]])
            S.op("dve", lambda e, bk=bk, s_=s_, j=j, i=i: e.tensor_tensor(out=aT2[i][:, j, :],
                                                                          in0=pbank[bk][:, 128:256],
                                                                          in1=sg[s_][:], op=ALU.mult),
                 R=[b_pb[bk], b_sg[s_]], W=[b_aT2[i]])

    def stageC(t):
        i = t % 2
        xt = xin[i]
        for nchunk in range(2):
            bk = 5 + nchunk

            def mm_dn(e, bk=bk, nchunk=nchunk, i=i):
                ins = None
                for j in range(JF):
                    ins = e.matmul(out=pbank[bk][:], lhsT=aT2[i][:, j, :],
                                   rhs=wdn[:, j, nchunk * 512:(nchunk + 1) * 512],
                                   start=(j == 0), stop=(j == JF - 1))
                return ins
            S.op("pe", mm_dn, R=[b_aT2[i], b_wdn], W=[b_pb[bk]])
            S.op("dve", lambda e, bk=bk, nchunk=nchunk, xt=xt:
                 e.tensor_tensor(out=xt[:, nchunk * 512:(nchunk + 1) * 512],
                                 in0=pbank[bk][:], in1=xt[:, nchunk * 512:(nchunk + 1) * 512], op=ALU.add),
                 R=[b_pb[bk], b_xin[i]], W=[b_xin[i]])
        if final_norm:
            rms_scale(xt[:], b_xin[i], 1)
            S.op("dve", lambda e, xt=xt: e.scalar_tensor_tensor(out=xt[:], in0=xt[:], scalar=stat[:, 1:2],
                                                                in1=finw[:], op0=ALU.mult, op1=ALU.mult),
                 R=[b_xin[i], b_stat, b_finw], W=[b_xin[i]])
        S.dma("sp", y_v[t], xt[:], b_xin[i], R=[b_xin[i]])

    stageA(0)
    for t in range(NTL):
        if t + 1 < NTL:
            stageA(t + 1)
        stageB(t)
        stageC(t)

    if ext:
        return None
    S.barrier_wait("sp", b_xin)
    return kb.done()


def _ident_bf():
    return np.eye(128, dtype=np.float32).astype(ml_dtypes.bfloat16)


def run_ffn(x_sl, o_sl, w_out, ffn_norm_w, w_gu, w_down, final_w, nc_cache={}):
    NT = x_sl[0].shape[0]
    key = (NT, final_w is not None)
    if key not in nc_cache:
        nc_cache[key] = build_ffn(NT, final_w is not None)
    nc = nc_cache[key]
    fnw = np.ascontiguousarray(ffn_norm_w.reshape(D_MODEL // 128, 128).T)
    in_maps = []
    for c in range(len(x_sl)):
        m = {"x": np.ascontiguousarray(x_sl[c]), "o": np.ascontiguousarray(o_sl[c]),
             "w_out": w_out, "w_gu": w_gu, "w_down": w_down, "ffn_norm_w": fnw,
             "ident_bf": _ident_bf()}
        if final_w is not None:
            m["final_w_bc"] = np.ascontiguousarray(np.broadcast_to(final_w[None, :], (128, D_MODEL)))
        in_maps.append(m)
    res = run_bass_kernel_spmd(nc, in_maps, core_ids=list(range(len(x_sl))))
    return [r["y"] for r in res.results]


NW = 898


def build_mixer(T, lam_init, dbg=99, ext=None):
    SKIP = ''
    kb = ext["kb"] if ext else KB()
    nc, S = kb.nc, kb.S
    NCH = T // 512
    NTL = T // 128
    KD = D_MODEL // 128

    if ext:
        x_d = ext["x"]
        wh_d, anw_d, cw_d, sc_d, lamv_d, lnw_d, dnw_d, bt_d, idb_d, cf_d = (
            ext[k] for k in ("wh", "anw", "cw", "sc", "lamv", "lnw", "dnw", "btoep", "ident_bf", "cf"))
        ola_d = ext["og"][:, 0:128]
        od_d = ext["og"][:, 128:256]
    else:
        x_d = kb.din("x", [T, D_MODEL], F32)
        wh_d = kb.din("wh", [D_MODEL, NW], F32)
        anw_d = kb.din("anw", [128, KD], F32)
        cw_d = kb.din("cw", [128, 12], F32)
        sc_d = kb.din("sc", [128, 4], F32)
        lamv_d = kb.din("lamv", [128, 4 * 64], F32)
        lnw_d = kb.din("lnw", [128, 128], F32)
        dnw_d = kb.din("dnw", [128, 128], F32)
        bt_d = kb.din("btoep", [128, 512], F32)
        idb_d = kb.din("ident_bf", [128, 128], BF16)
        cf_d = kb.din("cf", [128, 7 * 128], F32)
        ola_d = kb.dout("o_la", [T, 128], BF16)
        od_d = kb.dout("o_d", [T, 128], BF16)

    def T_(shape, dt, name):
        return kb.sb(shape, dt, name), S.buf(name)

    anw, b_anw = T_([128, KD], F32, "anw")
    cw, b_cw = T_([128, 12], F32, "cw")
    sc, b_sc = T_([128, 4], F32, "sc")
    lamv, b_lamv = T_([128, 256], F32, "lamv")
    lnw, b_lnw = T_([128, 128], F32, "lnw")
    dnw, b_dnw = T_([128, 128], F32, "dnw")
    bt, b_bt = T_([128, 512], F32, "bt")
    idb, b_idb = T_([128, 128], BF16, "idb")
    cf, b_cf = T_([128, 7 * 128], F32, "cf")
    for (t_, d_, b_) in ((anw, anw_d, b_anw), (cw, cw_d, b_cw), (sc, sc_d, b_sc), (lamv, lamv_d, b_lamv),
                         (lnw, lnw_d, b_lnw), (dnw, dnw_d, b_dnw), (bt, bt_d, b_bt), (idb, idb_d, b_idb),
                         (cf, cf_d, b_cf)):
        S.dma("sp", t_[:], d_, b_, W=[b_])
    IDF = cf[:, 0:128]
    ONES = cf[:, 128:256]
    MASKL = cf[:, 256:384]
    MASKU = cf[:, 384:512]
    TRI = cf[:, 512:640]
    SEL63 = cf[:, 640:768]
    SEL127 = cf[:, 768:896]

    cst_, b_cst = T_([128, 8], F32, "cst")
    S.op("dve", lambda e: e.memset(cst_[:, 0:1], 1.0), W=[b_cst])
    S.op("dve", lambda e: e.memset(cst_[:, 1:2], 1e-6), W=[b_cst])
    S.op("dve", lambda e: e.memset(cst_[:, 2:3], 1e-5), W=[b_cst])
    S.op("dve", lambda e: e.memset(cst_[:, 5:6], 0.0), W=[b_cst])
    C_ONE, C_EPS6, C_EPS5, C_NA, C_NLAM, C_ZERO = (cst_[:, i:i + 1] for i in range(6))
    S.op("act", lambda e: e.activation(out=cst_[:, 3:4], in_=sc[:, 0:1], func=AF.Exp), R=[b_sc], W=[b_cst])
    S.op("dve", lambda e: e.tensor_scalar(out=cst_[:, 3:4], in0=cst_[:, 3:4], scalar1=-1.0, scalar2=None,
                                          op0=ALU.mult), R=[b_cst], W=[b_cst])
    lt, b_lt = T_([128, 128], F32, "lamtmp")
    ls, b_ls = T_([128, 4], F32, "lamsum")
    S.op("dve", lambda e: e.tensor_tensor(out=lt[:, 0:64], in0=lamv[:, 0:64], in1=lamv[:, 64:128], op=ALU.mult),
         R=[b_lamv], W=[b_lt])
    S.op("dve", lambda e: e.tensor_tensor(out=lt[:, 64:128], in0=lamv[:, 128:192], in1=lamv[:, 192:256],
                                          op=ALU.mult), R=[b_lamv, b_lt], W=[b_lt])
    if 'r' not in SKIP:
        S.op("dve", lambda e: e.reduce_sum(out=ls[:, 0:1], in_=lt[:, 0:64], axis=AX.X), R=[b_lt], W=[b_ls])
        S.op("dve", lambda e: e.reduce_sum(out=ls[:, 1:2], in_=lt[:, 64:128], axis=AX.X), R=[b_lt, b_ls], W=[b_ls])
    S.op("act", lambda e: e.activation(out=ls[:, 2:4], in_=ls[:, 0:2], func=AF.Exp), R=[b_ls], W=[b_ls])
    S.op("dve", lambda e: e.scalar_tensor_tensor(out=cst_[:, 4:5], in0=ls[:, 3:4], scalar=float(-lam_init),
                                                 in1=ls[:, 2:3], op0=ALU.add, op1=ALU.subtract),
         R=[b_ls, b_cst], W=[b_cst])
    S.op("dve", lambda e: e.tensor_scalar(out=dnw[:], in0=dnw[:], scalar1=float(1.0 - lam_init), scalar2=None,
                                          op0=ALU.mult), R=[b_dnw], W=[b_dnw])

    Wb, b_Wb = T_([128, KD, 1024], BF16, "Wb")
    wst = [kb.sb([128, NW], F32, "wst%d" % i) for i in range(2)]
    b_wst = S.bufs(2, "wst")
    wh_v = wh_d.rearrange("(ko p) n -> p ko n", p=128)
    for ko in range(KD):
        i = ko % 2
        S.dma("sp", wst[i][:], wh_v[:, ko, :], b_wst[i], W=[b_wst[i]])
        S.op("dve" if i == 0 else "pool",
             lambda e, i=i, ko=ko: e.tensor_scalar(out=Wb[:, ko, 0:NW], in0=wst[i][:], scalar1=anw[:, ko:ko + 1],
                                                   scalar2=None, op0=ALU.mult),
             R=[b_wst[i], b_anw], W=[b_Wb])

    KdT, b_KdT = T_([128, T], BF16, "KdT")
    Vaug, b_Vaug = T_([128, NTL, 144], BF16, "Vaug")
    if 'v' not in SKIP:
        S.op("pool", lambda e: e.memset(Vaug[:, :, 128:129], 1.0), W=[b_Vaug])

    xt = [kb.sb([128, D_MODEL], F32, "xt%d" % i) for i in range(2)]
    b_xt = S.bufs(2, "xt")
    xn, b_xn = T_([128, D_MODEL], BF16, "xn")
    junk, b_junk = T_([128, D_MODEL], BF16, "junk")
    junkf, b_junkf = T_([128, 128], F32, "junkf")
    stat, b_stat = T_([128, 4], F32, "stat")
    hT, b_hT = T_([128, KD, 512], BF16, "hT")
    cstg = [kb.sb([128, 515], F32, "cstg%d" % g) for g in range(3)]
    b_cstg = S.bufs(3, "cstg")
    cacc = [kb.sb([128, 512], F32, "cacc%d" % g) for g in range(3)]
    b_cacc = S.bufs(3, "cacc")
    sil = [kb.sb([128, 512], F32, "sil%d" % g) for g in range(2)]
    b_sil = S.bufs(2, "sil")
    sq, b_sq = T_([128, 512], F32, "sq")
    rs, b_rs = T_([128, 512], F32, "rs")
    qnT, b_qnT = T_([128, 512], BF16, "qnT")
    knT, b_knT = T_([128, 512], BF16, "knT")
    vsT, b_vsT = T_([128, 512], BF16, "vsT")
    QdT, b_QdT = T_([128, 512], BF16, "QdT")
    qgT, b_qgT = T_([128, 512], BF16, "qgT")
    kvt, b_kvt = T_([128, 8, 128], BF16, "kvt")
    zs, b_zs = T_([128, 4, 128], BF16, "zs")
    ba, b_ba = T_([128, 4, 2], F32, "ba")
    for g in range(3):
        S.op("dve", lambda e, g=g: e.memset(cstg[g][:, 0:3], 0.0), W=[b_cstg[g]])
    pt = {}
    for nm in ("beta", "eb", "g", "gc", "egc", "bg", "glt", "ekd", "tmpa"):
        pt[nm] = T_([128, 4], F32, "pt_" + nm)
    egl, b_egl = T_([128, 8], F32, "egl")
    NS = 4
    gset = []
    for s_ in range(NS):
        d = {}
        for nm, shp, dt in (("dg", [128, 128], F32), ("Eb", [128, 128], F32), ("Ds", [128, 128], F32),
                            ("EA", [128, 128], F32), ("t1", [128, 128], F32), ("t2", [128, 128], F32),
                            ("MPa", [128, 256], F32), ("MPb", [128, 256], F32),
                            ("MTa", [128, 128], F32), ("MTb", [128, 128], F32),
                            ("TT", [128, 128], BF16), ("vb", [128, 128], BF16), ("kbg", [128, 128], BF16)):
            d[nm] = T_(shp, dt, "%s_%d" % (nm, s_))
        gset.append(d)
    u_sb, b_u = T_([128, 4, 128], F32, "u_sb")
    wT_sb, b_wT = T_([128, 512], BF16, "wT_sb")
    apT, b_apT = T_([128, 4, 128], BF16, "apT")
    kd_sb, b_kd = T_([128, 4, 128], BF16, "kd_sb")
    S_f, b_Sf = T_([128, 128], F32, "S_f")
    Sb, b_Sb = T_([128, 128], BF16, "Sb")
    vn, b_vn = T_([128, 128], BF16, "vn")
    o_sb = [kb.sb([128, 128], F32, "o_sb%d" % i) for i in range(2)]
    b_osb = S.bufs(2, "o_sb")
    on_, b_on = T_([128, 128], F32, "on")
    ola_st = [kb.sb([128, 4, 128], BF16, "ola_st%d" % i) for i in range(2)]
    b_olast = S.bufs(2, "ola_st")
    S.op("dve", lambda e: e.memset(S_f[:], 0.0), W=[b_Sf])
    S.op("dve", lambda e: e.memset(Sb[:], 0.0), W=[b_Sb])
    pT = [[kb.sb([128, 256], BF16, "pT%d%d" % (m, p)) for p in range(2)] for m in range(2)]
    b_pT = [[S.buf("pT%d%d" % (m, p)) for p in range(2)] for m in range(2)]
    s2 = [kb.sb([128, 256], F32, "s2_%d" % m) for m in range(2)]
    b_s2 = S.bufs(2, "s2")
    rden, b_rden = T_([128, 4], F32, "rden")
    O1, b_O1 = T_([128, 128], F32, "O1")
    odf, b_odf = T_([128, 128], F32, "odf")
    od_st = [kb.sb([128, 2, 128], BF16, "od_st%d" % i) for i in range(2)]
    b_odst = S.bufs(2, "od_st")

    if ext:
        pb, b_pb = ext["pb"], ext["b_pb"]
    else:
        pb = [kb.ps([128, 512], F32, "pb%d" % i) for i in range(8)]
        b_pb = S.bufs(8, "pb", excl=True)
    pb0_bf = pb[0][:].bitcast(BF16)

    x_v = x_d.rearrange("(t p) d -> t p d", p=128)
    ola_v = ola_d.rearrange("(c t p) e -> c p t e", p=128, t=4)
    od_v = od_d.rearrange("(c t p) e -> c p t e", p=128, t=2)

    def rstd_from_ss(col, n, epsc):
        S.op("act", lambda e: e.activation(out=stat[:, col:col + 1], in_=stat[:, col:col + 1], func=AF.Sqrt,
                                           scale=1.0 / n, bias=epsc), R=[b_stat, b_cst], W=[b_stat])
        S.op("dve", lambda e: e.reciprocal(out=stat[:, col:col + 1], in_=stat[:, col:col + 1]),
             R=[b_stat], W=[b_stat])

    for j in range(NCH):
        if dbg <= 0:
            continue
        for tt in range(4):
            t = 4 * j + tt
            i = t % 2
            S.dma("sp" if i == 0 else "pool", xt[i][:],
                  (ext["x_tile"](t) if (ext and ext.get("x_tile") is not None) else x_v[t]), b_xt[i], W=[b_xt[i]],
                  R=(list(ext["b_x"]) if (ext and ext.get("b_x") is not None) else []))
            S.op("act", lambda e, i=i: e.activation(out=junk[:], in_=xt[i][:], func=AF.Square,
                                                    accum_out=stat[:, 0:1]), R=[b_xt[i]], W=[b_junk, b_stat])
            rstd_from_ss(0, D_MODEL, C_EPS6)
            S.op("act", lambda e, i=i: e.activation(out=xn[:], in_=xt[i][:], func=AF.Copy, scale=stat[:, 0:1]),
                 R=[b_xt[i], b_stat], W=[b_xn])

            def trx(e):
                ins = None
                for k in range(KD):
                    ins = e.transpose(out=pb0_bf[:, k * 128:(k + 1) * 128], in_=xn[:, k * 128:(k + 1) * 128],
                                      identity=idb[:])
                return ins
            S.op("pe", trx, R=[b_xn, b_idb], W=[b_pb[0]])
            S.op("dve", lambda e, tt=tt: e.tensor_copy(out=hT[:, :, tt * 128:(tt + 1) * 128],
                                                        in_=pb0_bf[:, 0:1024].rearrange("p (k t) -> p k t", k=KD)),
                 R=[b_pb[0]], W=[b_hT])
        if dbg <= 1:
            continue
        for g in range(5 if 'f' not in SKIP else 0):
            bk = 1 + (g % 2)

            def mmf(e, g=g, bk=bk):
                ins = None
                for k in range(KD):
                    ins = e.matmul(out=pb[bk][:], lhsT=Wb[:, k, g * 128:(g + 1) * 128], rhs=hT[:, k, :],
                                   start=(k == 0), stop=(k == KD - 1))
                return ins
            S.op("pe", mmf, R=[b_Wb, b_hT], W=[b_pb[bk]])
            if g < 3:
                S.op("act", lambda e, g=g, bk=bk: e.copy(out=cstg[g][:, 3:515], in_=pb[bk][:]),
                     R=[b_pb[bk]], W=[b_cstg[g]])
            elif g == 3:
                S.op("act", lambda e, bk=bk: e.copy(out=QdT[:], in_=pb[bk][:]), R=[b_pb[bk]], W=[b_QdT])
            else:
                S.op("act", lambda e, bk=bk, j=j: e.copy(out=KdT[:, j * 512:(j + 1) * 512], in_=pb[bk][:]),
                     R=[b_pb[bk]], W=[b_KdT])
        for tt in range(4 if 't' not in SKIP else 0):
            t = 4 * j + tt

            def mmt(e, tt=tt):
                ins = None
                for k in range(KD):
                    NN = 256 if 'n' in SKIP else 258
                    ins = e.matmul(out=pb[3][:, 0:NN], lhsT=hT[:, k, tt * 128:(tt + 1) * 128],
                                   rhs=Wb[:, k, 640:640 + NN], start=(k == 0), stop=(k == KD - 1))
                return ins
            S.op("pe", mmt, R=[b_Wb, b_hT], W=[b_pb[3]])
            S.op("act", lambda e, tt=tt: e.activation(out=zs[:, tt, :], in_=pb[3][:, 0:128], func=AF.Silu),
                 R=[b_pb[3]], W=[b_zs])
            S.op("dve", lambda e, t=t: e.tensor_copy(out=Vaug[:, t, 0:128], in_=pb[3][:, 128:256]),
                 R=[b_pb[3]], W=[b_Vaug])
            S.op("dve", lambda e, tt=tt: e.tensor_copy(out=ba[:, tt, :], in_=pb[3][:, 256:258]),
                 R=[b_pb[3]], W=[b_ba])
        if dbg <= 2:
            continue
        for g in range(3):
            ce = "dve"
            S.op(ce, lambda e, g=g: e.tensor_scalar(out=cacc[g][:], in0=cstg[g][:, 3:515],
                                                    scalar1=cw[:, g * 4 + 3:g * 4 + 4], scalar2=None, op0=ALU.mult),
                 R=[b_cstg[g], b_cw], W=[b_cacc[g]])
            for tap in (2, 1, 0):
                S.op(ce, lambda e, g=g, tap=tap: e.scalar_tensor_tensor(
                    out=cacc[g][:], in0=cstg[g][:, tap:tap + 512], scalar=cw[:, g * 4 + tap:g * 4 + tap + 1],
                    in1=cacc[g][:], op0=ALU.mult, op1=ALU.add),
                    R=[b_cstg[g], b_cw, b_cacc[g]], W=[b_cacc[g]])
            S.op(ce, lambda e, g=g: e.tensor_copy(out=cstg[g][:, 0:3], in_=cstg[g][:, 512:515]),
                 R=[b_cstg[g]], W=[b_cstg[g]])
            if g < 2:
                S.op("act", lambda e, g=g: e.activation(out=sil[g][:], in_=cacc[g][:], func=AF.Silu),
                     R=[b_cacc[g]], W=[b_sil[g]])
            else:
                S.op("act", lambda e, g=g: e.activation(out=vsT[:], in_=cacc[g][:], func=AF.Silu),
                     R=[b_cacc[g]], W=[b_vsT])
        if dbg <= 3:
            continue
        for g in range(2):
            S.op("pool", lambda e, g=g: e.tensor_tensor(out=sq[:], in0=sil[g][:], in1=sil[g][:], op=ALU.mult),
                 R=[b_sil[g]], W=[b_sq])
            bk = 1 + g
            S.op("pe", lambda e, bk=bk: e.matmul(out=pb[bk][:], lhsT=ONES, rhs=sq[:], start=True, stop=True),
                 R=[b_sq, b_cf], W=[b_pb[bk]])
            S.op("act", lambda e, bk=bk: e.activation(out=rs[:], in_=pb[bk][:], func=AF.Sqrt, bias=C_EPS6),
                 R=[b_pb[bk], b_cst], W=[b_rs])
            S.op("dve", lambda e: e.reciprocal(out=rs[:], in_=rs[:]), R=[b_rs], W=[b_rs])
            if g == 0:
                S.op("dve", lambda e: e.scalar_tensor_tensor(out=qnT[:], in0=sil[0][:], scalar=float(128 ** -0.5),
                                                             in1=rs[:], op0=ALU.mult, op1=ALU.mult),
                     R=[b_sil[0], b_rs], W=[b_qnT])
            else:
                S.op("dve", lambda e: e.tensor_tensor(out=knT[:], in0=sil[1][:], in1=rs[:], op=ALU.mult),
                     R=[b_sil[1], b_rs], W=[b_knT])
        if dbg <= 4:
            continue
        def trkv(e):
            ins = None
            for tt in range(4):
                ins = e.transpose(out=pb0_bf[:, (2 * tt) * 128:(2 * tt + 1) * 128],
                                  in_=knT[:, tt * 128:(tt + 1) * 128], identity=idb[:])
                ins = e.transpose(out=pb0_bf[:, (2 * tt + 1) * 128:(2 * tt + 2) * 128],
                                  in_=vsT[:, tt * 128:(tt + 1) * 128], identity=idb[:])
            return ins
        S.op("pe", trkv, R=[b_knT, b_vsT, b_idb], W=[b_pb[0]])
        S.op("dve", lambda e: e.tensor_copy(out=kvt[:].rearrange("p a d -> p (a d)"), in_=pb0_bf[:, 0:1024]),
             R=[b_pb[0]], W=[b_kvt])
        if dbg <= 5:
            continue
        P = lambda nm: pt[nm][0]
        B = lambda nm: pt[nm][1]
        S.op("act", lambda e: e.activation(out=P("eb")[:], in_=ba[:, :, 0], func=AF.Exp, scale=-1.0),
             R=[b_ba], W=[B("eb")])
        S.op("dve", lambda e: e.tensor_scalar(out=P("eb")[:], in0=P("eb")[:], scalar1=1.0, scalar2=None,
                                              op0=ALU.add), R=[B("eb")], W=[B("eb")])
        S.op("dve", lambda e: e.reciprocal(out=P("beta")[:], in_=P("eb")[:]), R=[B("eb")], W=[B("beta")])
        S.op("act", lambda e: e.activation(out=P("tmpa")[:], in_=ba[:, :, 1], func=AF.Exp, bias=sc[:, 1:2]),
             R=[b_ba, b_sc], W=[B("tmpa")])
        S.op("act", lambda e: e.activation(out=P("tmpa")[:], in_=P("tmpa")[:], func=AF.Ln, bias=C_ONE),
             R=[B("tmpa"), b_cst], W=[B("tmpa")])
        S.op("dve", lambda e: e.tensor_scalar(out=P("g")[:], in0=P("tmpa")[:], scalar1=C_NA, scalar2=None,
                                              op0=ALU.mult), R=[B("tmpa"), b_cst], W=[B("g")])
        S.op("pe", lambda e: e.matmul(out=pb[3][:, 0:4], lhsT=TRI, rhs=P("g")[:], start=True, stop=True),
             R=[B("g"), b_cf], W=[b_pb[3]])
        S.op("act", lambda e: e.copy(out=P("gc")[:], in_=pb[3][:, 0:4]), R=[b_pb[3]], W=[B("gc")])

        def mmgl(e):
            e.matmul(out=pb[3][:, 0:4], lhsT=SEL63, rhs=P("gc")[:], start=True, stop=True)
            return e.matmul(out=pb[3][:, 4:8], lhsT=SEL127, rhs=P("gc")[:], start=True, stop=True)
        S.op("pe", mmgl, R=[B("gc"), b_cf], W=[b_pb[3]])
        eglv = egl[:].rearrange("p (t h) -> p t h", h=2)
        S.op("act", lambda e: e.activation(out=eglv[:, :, 0], in_=pb[3][:, 0:4], func=AF.Exp),
             R=[b_pb[3]], W=[b_egl])
        S.op("act", lambda e: e.activation(out=eglv[:, :, 1], in_=pb[3][:, 4:8], func=AF.Exp),
             R=[b_pb[3], b_egl], W=[b_egl])
        S.op("dve", lambda e: e.tensor_copy(out=P("glt")[0:64, :], in_=pb[3][0:64, 0:4]),
             R=[b_pb[3]], W=[B("glt")])
        S.op("dve", lambda e: e.tensor_copy(out=P("glt")[64:128, :], in_=pb[3][64:128, 4:8]),
             R=[b_pb[3], B("glt")], W=[B("glt")])
        S.op("dve", lambda e: e.tensor_tensor(out=P("ekd")[:], in0=P("glt")[:], in1=P("gc")[:], op=ALU.subtract),
             R=[B("glt"), B("gc")], W=[B("ekd")])
        S.op("act", lambda e: e.activation(out=P("ekd")[:], in_=P("ekd")[:], func=AF.Exp),
             R=[B("ekd")], W=[B("ekd")])
        S.op("act", lambda e: e.activation(out=P("egc")[:], in_=P("gc")[:], func=AF.Exp),
             R=[B("gc")], W=[B("egc")])
        S.op("dve", lambda e: e.tensor_tensor(out=P("bg")[:], in0=P("beta")[:], in1=P("egc")[:], op=ALU.mult),
             R=[B("beta"), B("egc")], W=[B("bg")])

        if dbg <= 6:
            continue
        def tile_gen(tt):
            gs = gset[tt % NS]
            bk = 4 + tt
            G = lambda nm, gs=gs: gs[nm][0]
            GB = lambda nm, gs=gs: gs[nm][1]
            csl = slice(tt * 128, (tt + 1) * 128)
            S.op("dve", lambda e, G=G, tt=tt: e.tensor_scalar(out=G("dg")[:], in0=IDF, scalar1=P("gc")[:, tt:tt + 1],
                                                                scalar2=None, op0=ALU.mult),
                 R=[b_cf, B("gc")], W=[GB("dg")])

            def mm_abb(e, G=G, bk=bk, csl=csl):
                e.matmul(out=pb[bk][:, 0:128], lhsT=ONES, rhs=G("dg")[:], start=True, stop=True)
                e.matmul(out=pb[bk][:, 128:256], lhsT=knT[:, csl], rhs=knT[:, csl], start=True, stop=True)
                return e.matmul(out=pb[bk][:, 256:384], lhsT=knT[:, csl], rhs=qnT[:, csl], start=True, stop=True)
            yield
            S.op("pe", mm_abb, R=[b_cf, GB("dg"), b_knT, b_qnT], W=[b_pb[bk]])
            S.op("dve", lambda e, G=G, bk=bk, tt=tt: e.tensor_scalar(
                out=G("Eb")[:], in0=pb[bk][:, 0:128], scalar1=P("gc")[:, tt:tt + 1], scalar2=None,
                op0=ALU.subtract), R=[b_pb[bk], B("gc")], W=[GB("Eb")])
            S.op("act", lambda e, G=G: e.activation(out=G("Eb")[:], in_=G("Eb")[:], func=AF.Abs),
                 R=[GB("Eb")], W=[GB("Eb")])
            S.op("act", lambda e, G=G: e.activation(out=G("Ds")[:], in_=G("Eb")[:], func=AF.Exp, scale=-1.0),
                 R=[GB("Eb")], W=[GB("Ds")])
            S.op("act", lambda e, G=G, bk=bk: e.activation(out=G("EA")[:], in_=pb[bk][:, 0:128], func=AF.Exp),
                 R=[b_pb[bk]], W=[GB("EA")])
            S.op("dve", lambda e, G=G, bk=bk, tt=tt: e.scalar_tensor_tensor(
                out=G("t1")[:], in0=pb[bk][:, 128:256], scalar=P("beta")[:, tt:tt + 1], in1=G("Ds")[:],
                op0=ALU.mult, op1=ALU.mult), R=[b_pb[bk], B("beta"), GB("Ds")], W=[GB("t1")])
            S.op("dve", lambda e, G=G: e.tensor_tensor(out=G("MTa")[:], in0=G("t1")[:], in1=MASKL, op=ALU.mult),
                 R=[GB("t1"), b_cf], W=[GB("MTa")])
            S.op("dve", lambda e, G=G, bk=bk: e.tensor_tensor(out=G("t2")[:], in0=pb[bk][:, 256:384], in1=G("Ds")[:],
                                                              op=ALU.mult), R=[b_pb[bk], GB("Ds")], W=[GB("t2")])
            S.op("pool", lambda e, G=G, tt=tt: e.tensor_tensor(out=apT[:, tt, :], in0=G("t2")[:], in1=MASKU,
                                                               op=ALU.mult), R=[GB("t2"), b_cf], W=[b_apT])
            S.op("pool", lambda e, G=G, csl=csl: e.tensor_tensor(out=qgT[:, csl], in0=qnT[:, csl], in1=G("EA")[:],
                                                                 op=ALU.mult), R=[b_qnT, GB("EA")], W=[b_qgT])
            yield
            S.op("pe", lambda e, G=G, bk=bk: e.transpose(out=pb[bk][:, 384:512], in_=G("MTa")[:], identity=IDF),
                 R=[GB("MTa"), b_cf], W=[b_pb[bk]])
            S.op("act", lambda e, G=G, bk=bk: e.copy(out=G("MPa")[:, 0:128], in_=pb[bk][:, 384:512]),
                 R=[b_pb[bk]], W=[GB("MPa")])
            S.op("dve", lambda e, G=G, bk=bk: e.tensor_tensor(out=G("MPb")[:, 128:256], in0=pb[bk][:, 384:512],
                                                              in1=IDF, op=ALU.add),
                 R=[b_pb[bk], b_cf], W=[GB("MPb")])

            def st0(e, G=G, bk=bk):
                e.matmul(out=pb[bk][:, 0:128], lhsT=G("MTa")[:], rhs=G("MPa")[:, 0:128], start=True, stop=True)
                return e.matmul(out=pb[bk][:, 128:256], lhsT=G("MPa")[:, 0:128], rhs=G("MTa")[:], start=True,
                                stop=True)
            yield
            S.op("pe", st0, R=[GB("MTa"), GB("MPa")], W=[b_pb[bk]])
            S.op("act", lambda e, G=G, bk=bk: e.copy(out=G("MPb")[:, 0:128], in_=pb[bk][:, 0:128]),
                 R=[b_pb[bk]], W=[GB("MPb")])
            S.op("dve", lambda e, G=G, bk=bk: e.tensor_copy(out=G("MTb")[:], in_=pb[bk][:, 128:256]),
                 R=[b_pb[bk]], W=[GB("MTb")])
            cur, nxt = ("MPb", "MTb"), ("MPa", "MTa")
            for stp in range(1, 5):
                def stj(e, G=G, bk=bk, cur=cur):
                    e.matmul(out=pb[bk][:, 0:256], lhsT=G(cur[1])[:], rhs=G(cur[0])[:, 0:256], start=True, stop=True)
                    return e.matmul(out=pb[bk][:, 256:384], lhsT=G(cur[0])[:, 0:128], rhs=G(cur[1])[:], start=True,
                                    stop=True)
                yield
                S.op("pe", stj, R=[GB(cur[0]), GB(cur[1])], W=[b_pb[bk]])
                S.op("act", lambda e, G=G, bk=bk, nxt=nxt: e.copy(out=G(nxt[0])[:, 0:128], in_=pb[bk][:, 0:128]),
                     R=[b_pb[bk]], W=[GB(nxt[0])])
                S.op("dve", lambda e, G=G, bk=bk, cur=cur, nxt=nxt: e.tensor_tensor(
                    out=G(nxt[0])[:, 128:256], in0=pb[bk][:, 128:256], in1=G(cur[0])[:, 128:256], op=ALU.add),
                    R=[b_pb[bk], GB(cur[0])], W=[GB(nxt[0])])
                S.op("act", lambda e, G=G, bk=bk, nxt=nxt: e.copy(out=G(nxt[1])[:], in_=pb[bk][:, 256:384]),
                     R=[b_pb[bk]], W=[GB(nxt[1])])
                cur, nxt = nxt, cur
            yield
            S.op("pe", lambda e, G=G, bk=bk, cur=cur: e.matmul(out=pb[bk][:, 0:128], lhsT=G(cur[1])[:],
                                                               rhs=G(cur[0])[:, 128:256], start=True, stop=True),
                 R=[GB(cur[0]), GB(cur[1])], W=[b_pb[bk]])
            S.op("dve", lambda e, G=G, bk=bk, cur=cur: e.tensor_tensor(out=G("TT")[:], in0=pb[bk][:, 0:128],
                                                                       in1=G(cur[0])[:, 128:256], op=ALU.add),
                 R=[b_pb[bk], GB(cur[0])], W=[GB("TT")])
            S.op("pool", lambda e, G=G, tt=tt: e.tensor_scalar(out=G("vb")[:], in0=kvt[:, 2 * tt + 1, :],
                                                               scalar1=P("beta")[:, tt:tt + 1], scalar2=None,
                                                               op0=ALU.mult), R=[b_kvt, B("beta")], W=[GB("vb")])
            S.op("pool", lambda e, G=G, tt=tt: e.tensor_scalar(out=G("kbg")[:], in0=kvt[:, 2 * tt, :],
                                                               scalar1=P("bg")[:, tt:tt + 1], scalar2=None,
                                                               op0=ALU.mult), R=[b_kvt, B("bg")], W=[GB("kbg")])
            S.op("pool", lambda e, tt=tt: e.tensor_scalar(out=kd_sb[:, tt, :], in0=kvt[:, 2 * tt, :],
                                                          scalar1=P("ekd")[:, tt:tt + 1], scalar2=None,
                                                          op0=ALU.mult), R=[b_kvt, B("ekd")], W=[b_kd])

            def mm_uw(e, G=G, bk=bk):
                e.matmul(out=pb[bk][:, 0:128], lhsT=G("TT")[:], rhs=G("vb")[:], start=True, stop=True)
                return e.matmul(out=pb[bk][:, 128:256], lhsT=G("kbg")[:], rhs=G("TT")[:], start=True, stop=True)
            yield
            S.op("pe", mm_uw, R=[GB("TT"), GB("vb"), GB("kbg")], W=[b_pb[bk]])
            S.op("act", lambda e, bk=bk, tt=tt: e.copy(out=u_sb[:, tt, :], in_=pb[bk][:, 0:128]),
                 R=[b_pb[bk]], W=[b_u])
            S.op("act", lambda e, bk=bk, csl=csl: e.copy(out=wT_sb[:, csl], in_=pb[bk][:, 128:256]),
                 R=[b_pb[bk]], W=[b_wT])

        gens = [tile_gen(tt) for tt in range(4)]
        while gens:
            for g_ in gens[:]:
                try:
                    next(g_)
                except StopIteration:
                    gens.remove(g_)
        if dbg <= 7:
            continue
        oi = j % 2
        for tt in range(4):
            csl = slice(tt * 128, (tt + 1) * 128)
            osb, b_o = o_sb[tt % 2], b_osb[tt % 2]
            for hh in range(2):
                r = slice(hh * 64, hh * 64 + 64)
                nl = 2 * tt + hh

                def mm1(e, csl=csl):
                    e.matmul(out=pb[6][:, 0:128], lhsT=wT_sb[:, csl], rhs=Sb[:], start=True, stop=True)
                    return e.matmul(out=pb[7][:, 0:128], lhsT=qgT[:, csl], rhs=Sb[:], start=True, stop=False)
                S.op("pe", mm1, R=[b_wT, b_qgT, b_Sb], W=[b_pb[6], b_pb[7]])
                S.op("dve", lambda e, r=r, tt=tt: e.tensor_tensor(out=vn[r, :], in0=u_sb[r, tt, :],
                                                                  in1=pb[6][r, 0:128], op=ALU.subtract),
                     R=[b_u, b_pb[6]], W=[b_vn])

                def mm2(e, r=r, tt=tt):
                    e.matmul(out=pb[7][:, 0:128], lhsT=apT[r, tt, :], rhs=vn[r, :], start=False, stop=True)
                    return e.matmul(out=pb[7][:, 128:256], lhsT=kd_sb[r, tt, :], rhs=vn[r, :], start=True, stop=True)
                S.op("pe", mm2, R=[b_apT, b_kd, b_vn], W=[b_pb[7]])
                S.op("act", lambda e, r=r, osb=osb: e.copy(out=osb[r, :], in_=pb[7][r, 0:128]),
                     R=[b_pb[7]], W=[b_o])
                S.op("dve", lambda e, nl=nl: e.scalar_tensor_tensor(out=S_f[:], in0=S_f[:], scalar=egl[:, nl:nl + 1],
                                                                    in1=pb[7][:, 128:256], op0=ALU.mult,
                                                                    op1=ALU.add),
                     R=[b_Sf, b_egl, b_pb[7]], W=[b_Sf])
                S.op("act", lambda e: e.copy(out=Sb[:], in_=S_f[:]), R=[b_Sf], W=[b_Sb])
            S.op("act", lambda e, osb=osb: e.activation(out=junkf[:], in_=osb[:], func=AF.Square,
                                                        accum_out=stat[:, 1:2]), R=[b_o], W=[b_junkf, b_stat])
            rstd_from_ss(1, 128, C_EPS6)
            S.op("dve", lambda e, osb=osb: e.scalar_tensor_tensor(out=on_[:], in0=osb[:], scalar=stat[:, 1:2],
                                                                  in1=lnw[:], op0=ALU.mult, op1=ALU.mult),
                 R=[b_o, b_stat, b_lnw], W=[b_on])
            S.op("dve", lambda e, tt=tt, oi=oi: e.tensor_tensor(out=ola_st[oi][:, tt, :], in0=on_[:], in1=zs[:, tt, :],
                                                                op=ALU.mult), R=[b_on, b_zs], W=[b_olast[oi]])
        S.dma("sp", ola_v[j], ola_st[oi][:], b_olast[oi], R=[b_olast[oi]])

        if dbg <= 8:
            continue
        for qq in range(2):
            qc = 2 * j + qq
            q0 = qc * 256
            qsl = slice(qq * 256, qq * 256 + 256)
            oi2 = qc % 2
            nkt = 2 * qc + 2
            def emit_qk(kt, qsl=qsl):
                k0 = kt * 128
                par = kt % 2
                for m in range(2):
                    bk = 4 + 2 * m + par
                    ms = slice(m * 64, m * 64 + 64)
                    S.op("pe", lambda e, bk=bk, ms=ms, k0=k0, qsl=qsl: e.matmul(
                        out=pb[bk][:, 0:256], lhsT=KdT[ms, k0:k0 + 128], rhs=QdT[ms, qsl], start=True, stop=True),
                        R=[b_KdT, b_QdT], W=[b_pb[bk]])

            def emit_exp(kt, q0=q0):
                k0 = kt * 128
                d = q0 - k0
                far = d >= 256
                par = kt % 2
                for m in range(2):
                    bk = 4 + 2 * m + par
                    if far:
                        S.op("act", lambda e, bk=bk, m=m, par=par: e.activation(
                            out=pT[m][par][:], in_=pb[bk][:, 0:256], func=AF.Exp, scale=0.125, bias=sc[:, 2:3]),
                            R=[b_pb[bk], b_sc], W=[b_pT[m][par]])
                    else:
                        S.op("dve", lambda e, bk=bk, m=m, d=d: e.scalar_tensor_tensor(
                            out=s2[m][:], in0=pb[bk][:, 0:256], scalar=0.125, in1=bt[:, d + 128:d + 128 + 256],
                            op0=ALU.mult, op1=ALU.add), R=[b_pb[bk], b_bt], W=[b_s2[m]])
                        S.op("act", lambda e, m=m, par=par: e.activation(out=pT[m][par][:], in_=s2[m][:],
                                                                         func=AF.Exp),
                             R=[b_s2[m]], W=[b_pT[m][par]])

            def emit_pv(kt, qc=qc):
                par = kt % 2
                for m in range(2):
                    for qb in range(2):
                        klast = 2 * qc + qb
                        if kt > klast:
                            continue
                        ab = qb * 2 + m
                        S.op("pe", lambda e, ab=ab, m=m, par=par, qb=qb, kt=kt, klast=klast: e.matmul(
                            out=pb[ab][:, 0:129], lhsT=pT[m][par][:, qb * 128:(qb + 1) * 128],
                            rhs=Vaug[:, kt, 0:129], start=(kt == 0), stop=(kt == klast)),
                            R=[b_pT[m][par], b_Vaug], W=[b_pb[ab]])

            emit_qk(0)
            for kt in range(nkt):
                if kt + 1 < nkt:
                    emit_qk(kt + 1)
                emit_exp(kt)
                emit_pv(kt)
            for qb in range(2):
                a1, a2 = qb * 2, qb * 2 + 1
                S.op("dve", lambda e, a1=a1: e.reciprocal(out=rden[:, 0:1], in_=pb[a1][:, 128:129]),
                     R=[b_pb[a1]], W=[b_rden])
                S.op("dve", lambda e, a2=a2: e.reciprocal(out=rden[:, 1:2], in_=pb[a2][:, 128:129]),
                     R=[b_pb[a2], b_rden], W=[b_rden])
                S.op("dve", lambda e: e.tensor_scalar(out=rden[:, 2:3], in0=rden[:, 1:2], scalar1=C_NLAM,
                                                      scalar2=None, op0=ALU.mult), R=[b_rden, b_cst], W=[b_rden])
                S.op("act", lambda e, a1=a1: e.activation(out=O1[:], in_=pb[a1][:, 0:128], func=AF.Copy,
                                                          scale=rden[:, 0:1]), R=[b_pb[a1], b_rden], W=[b_O1])
                S.op("dve", lambda e, a2=a2: e.scalar_tensor_tensor(out=odf[:], in0=pb[a2][:, 0:128],
                                                                    scalar=rden[:, 2:3], in1=O1[:], op0=ALU.mult,
                                                                    op1=ALU.add),
                     R=[b_pb[a2], b_rden, b_O1], W=[b_odf])
                S.op("act", lambda e: e.activation(out=junkf[:], in_=odf[:], func=AF.Square,
                                                   accum_out=stat[:, 2:3]), R=[b_odf], W=[b_junkf, b_stat])
                rstd_from_ss(2, 128, C_EPS5)
                S.op("dve", lambda e, qb=qb, oi2=oi2: e.scalar_tensor_tensor(
                    out=od_st[oi2][:, qb, :], in0=odf[:], scalar=stat[:, 2:3], in1=dnw[:], op0=ALU.mult,
                    op1=ALU.mult), R=[b_odf, b_stat, b_dnw], W=[b_odst[oi2]])
            S.dma("sp", od_v[qc], od_st[oi2][:], b_odst[oi2], R=[b_odst[oi2]])

    if ext:
        return None
    S.barrier_wait("sp", b_olast + b_odst)
    return kb.done()


def _t5_bucket_np(rel):
    n = np.maximum(rel, 0)
    nf = np.maximum(n, 1).astype(np.float32)
    large = 16 + (np.log(nf / np.float32(16)) / np.float32(math.log(128 / 16)) * np.float32(16)).astype(np.int32)
    large = np.minimum(large, 31)
    return np.where(n < 16, n, large)


def _mixer_consts():
    p = np.arange(128)
    same = (p[:, None] // 64) == (p[None, :] // 64)
    ident = np.eye(128, dtype=np.float32)
    ones = np.ones((128, 128), np.float32)
    maskl = np.where(same & (p[:, None] > p[None, :]), -1.0, 0.0).astype(np.float32)
    masku = np.where(same & (p[:, None] <= p[None, :]), 1.0, 0.0).astype(np.float32)
    tri = masku.copy()
    sel63 = np.zeros((128, 128), np.float32)
    sel63[63, :] = 1.0
    sel127 = np.zeros((128, 128), np.float32)
    sel127[127, :] = 1.0
    return np.ascontiguousarray(np.concatenate([ident, ones, maskl, masku, tri, sel63, sel127], axis=1))


def mixer_inputs(xb, l, h, P):
    w_in = P["w_in"][l]
    cols = np.concatenate([
        np.arange(h * 128, (h + 1) * 128),
        512 + np.arange(h * 128, (h + 1) * 128),
        1024 + np.arange(h * 128, (h + 1) * 128),
        2056 + np.arange(h * 128, (h + 1) * 128),
        2568 + np.arange(h * 128, (h + 1) * 128),
        1536 + np.arange(h * 128, (h + 1) * 128),
        3080 + np.arange(h * 128, (h + 1) * 128),
        np.array([2048 + h]),
        np.array([2052 + h]),
    ])
    wh = np.ascontiguousarray(w_in[:, cols])
    anw = np.ascontiguousarray(P["attn_norm_w"][l].reshape(8, 128).T)
    cwl = P["conv_w"][l]
    cw = np.concatenate([cwl[:, g * 512 + h * 128: g * 512 + (h + 1) * 128].T for g in range(3)], axis=1)
    sc = np.zeros((128, 4), np.float32)
    sc[:, 0] = P["a_log"][l, h]
    sc[:, 1] = P["dt_bias"][l, h]
    sc[:, 2] = P["rel_bias"][31, h]
    lamv = np.concatenate([P["lambda_q1"][l], P["lambda_k1"][l], P["lambda_q2"][l], P["lambda_k2"][l]])
    lamv = np.broadcast_to(lamv[None, :], (128, 256))
    lnw = np.broadcast_to(P["la_norm_w"][l][None, :], (128, 128))
    dnw = np.broadcast_to(P["diff_norm_w"][l][None, :], (128, 128))
    kl = np.arange(128)[:, None]
    jj = np.arange(512)[None, :]
    rel = jj - 128 - kl
    bt = np.where(rel >= 0, P["rel_bias"][_t5_bucket_np(rel), h], np.float32(-30000.0)).astype(np.float32)
    c = np.ascontiguousarray
    return {"x": c(xb), "wh": wh, "anw": anw, "cw": c(cw.astype(np.float32)), "sc": sc,
            "lamv": c(lamv.astype(np.float32)), "lnw": c(lnw.astype(np.float32)),
            "dnw": c(dnw.astype(np.float32)), "btoep": c(bt), "ident_bf": _ident_bf(), "cf": _mixer_consts()}


CC_GROUPS = [[0, 1, 2, 3], [4, 5, 6, 7]]
_MIX_KEYS = ("wh", "anw", "cw", "sc", "lamv", "lnw", "dnw")


def build_fused(T):
    kb = KB()
    nc, S = kb.nc, kb.S
    NT = T // NH
    NTL = NT // 128
    I32 = mybir.dt.int32
    shp = {"wh": [D_MODEL, NW], "anw": [128, 8], "cw": [128, 12], "sc": [128, 4], "lamv": [128, 256],
           "lnw": [128, 128], "dnw": [128, 128]}
    x_d = kb.din("x", [T, D_MODEL], F32)
    xs_d = kb.din("xs", [NT, D_MODEL], F32)
    idx_d = kb.din("idx", [128, NH * NTL], I32)
    bt_d = kb.din("btoep", [128, 512], F32)
    idb_d = kb.din("ident_bf", [128, 128], BF16)
    cf_d = kb.din("cf", [128, 7 * 128], F32)
    fin_d = kb.din("final_w_bc", [128, D_MODEL], F32)
    lay = []
    for l in range(DEPTH):
        d = {k: kb.din("%s%d" % (k, l), shp[k], F32) for k in _MIX_KEYS}
        d["w_out"] = kb.din("w_out%d" % l, [D_MODEL, D_MODEL], F32)
        d["w_gu"] = kb.din("w_gu%d" % l, [D_MODEL, 2 * D_FF], F32)
        d["w_down"] = kb.din("w_down%d" % l, [D_FF, D_MODEL], F32)
        d["ffn_norm_w"] = kb.din("fnw%d" % l, [128, 8], F32)
        lay.append(d)
    y_d = kb.dout("y", [NT, D_MODEL], F32)
    og_in = kb.dint("og_in", [T, 256], BF16)
    og_all = kb.dint("og_all", [NH * T, 256], BF16)
    xs_in = kb.dint("xs_in", [NT, D_MODEL], F32)
    x1_all = kb.dint("x1_all", [T, D_MODEL], F32)

    pb = [kb.ps([128, 512], F32, "pb%d" % i) for i in range(8)]
    b_pb = S.bufs(8, "pb", excl=True)
    ORC = min(T, 2048)
    NKO = T // ORC
    XRC = min(NT, 256)
    NKX = NT // XRC
    b_og_all = S.bufs(NKO, "og_all")
    b_x1all = S.bufs(NKX, "x1_all")
    kb.persist = b_pb + b_og_all + b_x1all

    def allgather(src, dst, b_dst, name):
        S.dma_fn("pool", lambda e: e.collective_compute("AllGather", ALU.bypass, replica_groups=CC_GROUPS,
                                                         ins=[src], outs=[dst]),
                 S.buf(name), W=[b_dst], inc=None)

    def x1_tile(t):
        tok = t * 128
        r, w = divmod(tok, NT)
        k, ww = divmod(w, XRC)
        base = k * (NH * XRC) + r * XRC + ww
        return x1_all[base:base + 128, :]

    for l in range(DEPTH):
        lam_init = 0.8 - 0.6 * math.exp(-0.3 * l)
        kb.begin_phase()
        if l > 0:
            for k in range(NKX):
                allgather(xs_in[k * XRC:(k + 1) * XRC, :], x1_all[k * NH * XRC:(k + 1) * NH * XRC, :], b_x1all[k],
                          "ccx%d_%d" % (l, k))
        ext = {"kb": kb, "pb": pb, "b_pb": b_pb, "x": x_d, "x_tile": (None if l == 0 else x1_tile),
               "b_x": (None if l == 0 else b_x1all), "og": og_in,
               "btoep": bt_d, "ident_bf": idb_d, "cf": cf_d}
        for k in _MIX_KEYS:
            ext[k] = lay[l][k]
        build_mixer(T, lam_init, ext=ext)
        kb.end_phase()
        kb.begin_phase()
        for k in range(NKO):
            allgather(og_in[k * ORC:(k + 1) * ORC, :], og_all[k * NH * ORC:(k + 1) * NH * ORC, :], b_og_all[k],
                      "cco%d_%d" % (l, k))
        last = (l == DEPTH - 1)
        ext = {"kb": kb, "pb": pb, "b_pb": b_pb, "x": (xs_d if l == 0 else xs_in), "y": (y_d if last else xs_in),
               "og_all": og_all, "b_og_all": b_og_all, "idx": idx_d, "T": T,
               "w_out": lay[l]["w_out"], "w_gu": lay[l]["w_gu"], "w_down": lay[l]["w_down"],
               "ffn_norm_w": lay[l]["ffn_norm_w"], "ident_bf": idb_d, "final_w_bc": fin_d}
        build_ffn(NT, last, ext=ext)
        kb.end_phase()
    kb.es.close()
    return nc


_FUSED_CACHE = {}


def _gather_idx(T, h):
    NT = T // NH
    NTL = NT // 128
    ORC = min(T, 2048)
    tok = h * NT + np.arange(NTL)[None, :] * 128 + np.arange(128)[:, None]
    k, w = tok // ORC, tok % ORC
    cols = [k * (NH * ORC) + r * ORC + w for r in range(NH)]
    return np.ascontiguousarray(np.concatenate(cols, axis=1).astype(np.int32))


def fused_inputs(P, T):
    NT = T // NH
    NTL = NT // 128
    perm = np.concatenate([np.concatenate([np.arange(r * 128, (r + 1) * 128),
                                           512 + np.arange(r * 128, (r + 1) * 128)]) for r in range(NH)])
    in_maps = []
    for c in range(NCORES):
        b, h = divmod(c, NH)
        xb = P["x"][b, :T]
        m = {"x": np.ascontiguousarray(xb), "xs": np.ascontiguousarray(xb[h * NT:(h + 1) * NT]),
             "idx": _gather_idx(T, h),
             "final_w_bc": np.ascontiguousarray(np.broadcast_to(P["final_norm_w"][None, :], (128, D_MODEL)))}
        for l in range(DEPTH):
            mi = mixer_inputs(xb, l, h, P)
            for k in _MIX_KEYS:
                m["%s%d" % (k, l)] = mi[k]
            if l == 0:
                m["btoep"], m["ident_bf"], m["cf"] = mi["btoep"], mi["ident_bf"], mi["cf"]
            m["w_out%d" % l] = np.ascontiguousarray(P["w_out"][l][perm])
            m["w_gu%d" % l] = P["w_gate_up"][l]
            m["w_down%d" % l] = P["w_down"][l]
            m["fnw%d" % l] = np.ascontiguousarray(P["ffn_norm_w"][l].reshape(8, 128).T)
        in_maps.append(m)
    return in_maps


def kernel_fused(P, T):
    if T not in _FUSED_CACHE:
        _FUSED_CACHE[T] = build_fused(T)
    nc = _FUSED_CACHE[T]
    NT = T // NH
    res = run_bass_kernel_spmd(nc, fused_inputs(P, T), core_ids=list(range(NCORES)))
    out = np.empty((BATCH, T, D_MODEL), np.float32)
    for c in range(NCORES):
        b, h = divmod(c, NH)
        out[b, h * NT:(h + 1) * NT] = np.asarray(res.results[c]["y"])
    return out


_MIX_CACHE = {}


def kernel(**inputs):
    P = {k: np.ascontiguousarray(np.asarray(v, dtype=np.float32)) for k, v in inputs.items()}
    return kernel_fused(P, P["x"].shape[1])


def kernel_unfused(**inputs):
    P = {k: np.ascontiguousarray(np.asarray(v, dtype=np.float32)) for k, v in inputs.items()}
    x = P["x"]
    B, T, D = x.shape
    NTOK = B * T
    per = NTOK // NCORES
    for l in range(DEPTH):
        lam_init = 0.8 - 0.6 * math.exp(-0.3 * l)
        key = (T, l)
        if key not in _MIX_CACHE:
            _MIX_CACHE[key] = build_mixer(T, lam_init)
        nc = _MIX_CACHE[key]
        in_maps = [mixer_inputs(x[c // NH], l, c % NH, P) for c in range(NCORES)]
        res = run_bass_kernel_spmd(nc, in_maps, core_ids=list(range(NCORES)))
        o = np.empty((B, T, D), dtype=ml_dtypes.bfloat16)
        for c in range(NCORES):
            b, h = divmod(c, NH)
            o[b, :, h * 128:(h + 1) * 128] = np.asarray(res.results[c]["o_la"])
            o[b, :, 512 + h * 128:512 + (h + 1) * 128] = np.asarray(res.results[c]["o_d"])
        xs = x.reshape(NTOK, D)
        os_ = o.reshape(NTOK, D)
        ys = run_ffn([xs[c * per:(c + 1) * per] for c in range(NCORES)],
                     [os_[c * per:(c + 1) * per] for c in range(NCORES)],
                     P["w_out"][l], P["ffn_norm_w"][l], P["w_gate_up"][l], P["w_down"][l],
                     P["final_norm_w"] if l == DEPTH - 1 else None)
        x = np.concatenate([np.asarray(y) for y in ys], axis=0).reshape(B, T, D)
    return np.ascontiguousarray(x.astype(np.float32))
```

```python
import math
from contextlib import ExitStack

import numpy as np
import ml_dtypes
import concourse.bass as bass
import concourse.mybir as mybir
from concourse.bass_utils import run_bass_kernel_spmd

F32 = mybir.dt.float32
BF16 = mybir.dt.bfloat16
AF = mybir.ActivationFunctionType
ALU = mybir.AluOpType
AX = mybir.AxisListType

D_MODEL = 1024
SEQ = 8192
BATCH = 2
DEPTH = 2
NH = 4
D_FF = 2816
IN_DIM = 3592
NORM_EPS = 1e-6
NCORES = 8

ENGS = ("pe", "act", "dve", "pool", "sp")


class Buf:
    __slots__ = ("name", "w", "r", "dsem", "dcnt", "excl")

    def __init__(self, name, excl=False):
        self.name = name
        self.excl = excl
        self.w = None
        self.r = []
        self.dsem = None
        self.dcnt = 0


class Op:
    __slots__ = ("eng", "fn", "deps", "dma", "sem", "val", "needed", "inc")

    def __init__(self, eng, fn, dma=False):
        self.eng = eng
        self.fn = fn
        self.deps = []
        self.dma = dma
        self.sem = None
        self.val = 0
        self.needed = False
        self.inc = 16


class Sched:
    def __init__(self, nc, es):
        self.nc = nc
        self.es = es
        self.ops = {e: [] for e in ENGS}
        self.esem = {e: es.enter_context(nc.semaphore("s_" + e)) for e in ENGS}
        self.nbuf = 0
        self.cnt = {e: 0 for e in ENGS}
        self.phase_dmas = []
        self.nsem = 0
        self.defer = None

    def buf(self, name=None, excl=False):
        self.nbuf += 1
        return Buf(name or ("b%d" % self.nbuf), excl)

    def bufs(self, n, name="b", excl=False):
        return [self.buf("%s%d" % (name, i), excl) for i in range(n)]

    def _link(self, o, R, W):
        deps = []
        for b in R:
            if b.w is not None:
                d = b.w
                if d.dma or o.dma or d.eng != o.eng or o.eng != "pe":
                    deps.append(d)
            if b.excl:
                for d in b.r:
                    if d.eng != o.eng:
                        deps.append(d)
        for b in W:
            if b.w is not None:
                d = b.w
                if d.dma or o.dma or d.eng != o.eng or o.eng != "pe":
                    deps.append(d)
            for d in b.r:
                if d.dma or o.dma or d.eng != o.eng or o.eng != "pe":
                    deps.append(d)
        o.deps = deps
        for b in R:
            if b in W:
                continue
            if b.excl:
                b.r = []
            elif not o.dma:
                b.r = [x for x in b.r if x.dma or x.eng != o.eng]
            b.r.append(o)
        for b in W:
            b.w = o
            b.r = []

    def op(self, eng, fn, R=(), W=()):
        if self.defer is not None:
            self.defer.append(lambda: self.op_now(eng, fn, R, W))
            return None
        return self.op_now(eng, fn, R, W)

    def op_now(self, eng, fn, R=(), W=()):
        o = Op(eng, fn)
        self._link(o, R, W)
        self.ops[eng].append(o)
        return o

    def dma(self, q, out, in_, sb, R=(), W=()):
        return self.dma_fn(q, lambda e, out=out, in_=in_: e.dma_start(out=out, in_=in_), sb, R, W)

    def dma_fn(self, q, fn, sb, R=(), W=(), inc=16):
        if self.defer is not None:
            self.defer.append(lambda: self.dma_fn_now(q, fn, sb, R, W, inc))
            return None
        return self.dma_fn_now(q, fn, sb, R, W, inc)

    def dma_fn_now(self, q, fn, sb, R=(), W=(), inc=16):
        if sb.dsem is None:
            self.nsem += 1
            sb.dsem = self.es.enter_context(self.nc.semaphore("d%d_%s" % (self.nsem, sb.name)))
        o = Op(q, fn, dma=True)
        o.inc = inc
        sb.dcnt += (inc if inc else 1)
        o.sem = sb.dsem
        o.val = sb.dcnt
        self._link(o, R, W)
        self.ops[q].append(o)
        self.phase_dmas.append(o)
        return o

    def phase_barrier(self):
        lasts = []
        for e in ENGS:
            real = [o for o in self.ops[e] if o.fn is not None and not o.dma]
            if real:
                lasts.append(real[-1])
        deps = lasts + list(self.phase_dmas)
        for e in ENGS:
            o = Op(e, None)
            o.deps = [d for d in deps if d.dma or d.eng != e]
            self.ops[e].append(o)
        self.phase_dmas = []

    def barrier_wait(self, eng, R):
        o = Op(eng, None)
        self._link(o, (), R)
        self.ops[eng].append(o)
        return o

    def finalize(self):
        for e in ENGS:
            for o in self.ops[e]:
                for d in o.deps:
                    d.needed = True
        for e in ENGS:
            c = self.cnt[e]
            for o in self.ops[e]:
                if not o.dma and o.needed and o.fn is not None:
                    c += 1
                    o.sem = self.esem[e]
                    o.val = c
            self.cnt[e] = c
        ops = self.ops
        self.ops = {e: [] for e in ENGS}

        def run(eng, lst):
            seen = {}
            for o in lst:
                waits = {}
                for d in o.deps:
                    k = id(d.sem)
                    if k not in waits or waits[k][1] < d.val:
                        waits[k] = (d.sem, d.val)
                for k, (sem, val) in waits.items():
                    if seen.get(k, 0) < val:
                        eng.wait_ge(sem, val)
                        seen[k] = val
                if o.fn is None:
                    continue
                ins = o.fn(eng)
                if o.dma:
                    if o.inc:
                        ins.then_inc(o.sem, o.inc)
                    else:
                        ins.then_inc(o.sem)
                elif o.needed:
                    ins.then_inc(o.sem, 1)

        with self.nc.Block() as block:
            @block.tensor
            def _(e):
                run(e, ops["pe"])

            @block.scalar
            def _(e):
                run(e, ops["act"])

            @block.vector
            def _(e):
                run(e, ops["dve"])

            @block.gpsimd
            def _(e):
                run(e, ops["pool"])

            @block.sync
            def _(e):
                run(e, ops["sp"])


class KB:
    def __init__(self):
        self.nc = bass.Bass("TRN2", target_bir_lowering=False)
        self.es = ExitStack()
        self.S = Sched(self.nc, self.es)
        self.n = 0
        self.pes = None
        self.phase = 0

    def begin_phase(self):
        self.phase += 1
        self.pes = ExitStack()

    def end_phase(self):
        self.S.phase_barrier()
        self.S.finalize()
        self.pes.close()
        self.pes = None
        for b in getattr(self, "persist", []):
            b.w = None
            b.r = []

    def sb(self, shape, dt, name=None):
        self.n += 1
        st = self.pes if self.pes is not None else self.es
        return st.enter_context(self.nc.sbuf_tensor("sb%d_" % self.phase + (name or ("t%d" % self.n)), list(shape), dt))

    def dint(self, name, shape, dt):
        return self.nc.dram_tensor(name, list(shape), dt).ap()

    def ps(self, shape, dt, name=None):
        self.n += 1
        return self.es.enter_context(self.nc.psum_tensor("ps_" + (name or ("p%d" % self.n)), list(shape), dt))

    def din(self, name, shape, dt):
        return self.nc.dram_tensor(name, list(shape), dt, kind="ExternalInput").ap()

    def dout(self, name, shape, dt):
        return self.nc.dram_tensor(name, list(shape), dt, kind="ExternalOutput").ap()

    def done(self):
        self.S.finalize()
        self.es.close()
        return self.nc


def build_ffn(NT, final_norm, ext=None):
    kb = ext["kb"] if ext else KB()
    nc, S = kb.nc, kb.S
    NTL = NT // 128
    KD = D_MODEL // 128
    JF = D_FF // 128

    if ext:
        x_d, wout_d, wgu_d, wdn_d, fnw_d, idb_d, y_d = (
            ext[k] for k in ("x", "w_out", "w_gu", "w_down", "ffn_norm_w", "ident_bf", "y"))
        if final_norm:
            fin_d = ext["final_w_bc"]
        o_d = None
    else:
        x_d = kb.din("x", [NT, D_MODEL], F32)
        o_d = kb.din("o", [NT, D_MODEL], BF16)
        wout_d = kb.din("w_out", [D_MODEL, D_MODEL], F32)
        wgu_d = kb.din("w_gu", [D_MODEL, 2 * D_FF], F32)
        wdn_d = kb.din("w_down", [D_FF, D_MODEL], F32)
        fnw_d = kb.din("ffn_norm_w", [128, KD], F32)
        idb_d = kb.din("ident_bf", [128, 128], BF16)
        if final_norm:
            fin_d = kb.din("final_w_bc", [128, D_MODEL], F32)
        y_d = kb.dout("y", [NT, D_MODEL], F32)

    wout = kb.sb([128, KD, D_MODEL], BF16, "wout")
    wgu = kb.sb([128, KD, 2 * D_FF], BF16, "wgu")
    wdn = kb.sb([128, JF, D_MODEL], BF16, "wdn")
    fnw = kb.sb([128, KD], F32, "fnw")
    idb = kb.sb([128, 128], BF16, "idb")
    b_wout, b_wgu, b_wdn, b_fnw, b_idb = S.bufs(5, "wres")
    if final_norm:
        finw = kb.sb([128, D_MODEL], F32, "finw")
        b_finw = S.buf("finw")
        S.dma("sp", finw[:], fin_d, b_finw, W=[b_finw])
    S.dma("sp", fnw[:], fnw_d, b_fnw, W=[b_fnw])
    S.dma("sp", idb[:], idb_d, b_idb, W=[b_idb])

    STG = 1408
    NSTG = 3
    stg = [kb.sb([128, STG], F32, "stg%d" % i) for i in range(NSTG)]
    b_stg = S.bufs(NSTG, "stg")
    cnt = [0]
    cast_engs = ("dve", "pool")

    def load_cast(dst_ap, src_ap, n, wbuf, scale_ap=None):
        i = cnt[0] % NSTG
        q = "sp" if (cnt[0] % 2 == 0) else "act"
        S.dma(q, stg[i][:, 0:n], src_ap, b_stg[i], W=[b_stg[i]])
        ce = cast_engs[cnt[0] % 2]
        if scale_ap is None:
            S.op(ce, lambda e, d=dst_ap, s=stg[i][:, 0:n]: e.tensor_copy(out=d, in_=s),
                 R=[b_stg[i]], W=[wbuf])
        else:
            S.op(ce, lambda e, d=dst_ap, s=stg[i][:, 0:n], sc=scale_ap:
                 e.tensor_scalar(out=d, in0=s, scalar1=sc, scalar2=None, op0=ALU.mult),
                 R=[b_stg[i], b_fnw], W=[wbuf])
        cnt[0] += 1

    wout_v = wout_d.rearrange("(ko p) n -> p ko n", p=128)
    for ko in range(KD):
        load_cast(wout[:, ko, :], wout_v[:, ko, :], D_MODEL, b_wout)
    wgu_v = wgu_d.rearrange("(ko p) n -> p ko n", p=128)
    for ko in range(KD):
        for c in range(4):
            load_cast(wgu[:, ko, c * STG:(c + 1) * STG], wgu_v[:, ko, c * STG:(c + 1) * STG], STG,
                      b_wgu, scale_ap=fnw[:, ko:ko + 1])
    wdn_v = wdn_d.rearrange("(j p) n -> p j n", p=128)
    for j in range(JF):
        load_cast(wdn[:, j, :], wdn_v[:, j, :], D_MODEL, b_wdn)

    xin = [kb.sb([128, D_MODEL], F32, "xin%d" % i) for i in range(2)]
    oin = [kb.sb([128, D_MODEL], BF16, "oin%d" % i) for i in range(2)]
    b_xin = S.bufs(2, "xin")
    b_oin = S.bufs(2, "oin")
    tbuf = kb.sb([128, KD, 128], BF16, "tbuf")
    b_tbuf = S.buf("tbuf")
    hn = kb.sb([128, D_MODEL], BF16, "hn")
    b_hn = S.buf("hn")
    junk = kb.sb([128, D_MODEL], BF16, "junk")
    b_junk = S.buf("junk")
    aT = kb.sb([128, JF, 128], BF16, "aT")
    b_aT = S.buf("aT")
    sg = [kb.sb([128, 128], F32, "sg%d" % i) for i in range(2)]
    b_sg = S.bufs(2, "sg")
    stat = kb.sb([128, 8], F32, "stat")
    b_stat = S.buf("stat")
    epsc = kb.sb([128, 1], F32, "epsc")
    b_epsc = S.buf("epsc")
    S.op("dve", lambda e: e.memset(epsc[:], NORM_EPS), W=[b_epsc])

    if ext:
        pbank, b_pb = ext["pb"], ext["b_pb"]
        ptb_t = pbank[0][:].bitcast(BF16)
        idx_sb = kb.sb([128, 4 * NTL], mybir.dt.int32, "idx")
        b_idx = S.buf("idx")
        S.dma("sp", idx_sb[:], ext["idx"], b_idx, W=[b_idx])
        TT_ = ext["T"]
    else:
        pbank = [None] + [kb.ps([128, 512], F32, "pb%d" % i) for i in range(1, 8)]
        b_pb = S.bufs(8, "pb", excl=True)
        ptb_t = kb.ps([128, D_MODEL], BF16, "ptb")

    x_v = x_d.rearrange("(t p) d -> t p d", p=128)
    o_v = o_d.rearrange("(t p) d -> t p d", p=128) if o_d is not None else None
    y_v = y_d.rearrange("(t p) d -> t p d", p=128)

    def rms_scale(src, b_src, col):
        S.op("act", lambda e: e.activation(out=junk[:], in_=src, func=AF.Square,
                                           accum_out=stat[:, col:col + 1]),
             R=[b_src], W=[b_junk, b_stat])
        S.op("act", lambda e: e.activation(out=stat[:, col:col + 1], in_=stat[:, col:col + 1], func=AF.Sqrt,
                                           scale=1.0 / D_MODEL, bias=epsc[:, 0:1]),
             R=[b_stat, b_epsc], W=[b_stat])
        S.op("dve", lambda e: e.reciprocal(out=stat[:, col:col + 1], in_=stat[:, col:col + 1]),
             R=[b_stat], W=[b_stat])

    hT2 = [kb.sb([128, KD, 128], BF16, "hT2_%d" % i) for i in range(2)]
    b_hT2 = S.bufs(2, "hT2")
    aT2 = [aT, kb.sb([128, JF, 128], BF16, "aT_1")]
    b_aT2 = [b_aT, S.buf("aT_1")]
    ptb = ptb_t

    def stageA(t):
        i = t % 2
        xt, ot = xin[i], oin[i]
        S.dma("sp", xt[:], x_v[t], b_xin[i], W=[b_xin[i]])
        if ext:
            for r_ in range(4):
                S.dma_fn("pool", lambda e, ot=ot, r_=r_, t=t: e.indirect_dma_start(
                    out=ot[:, r_ * 256:(r_ + 1) * 256], out_offset=None,
                    in_=ext["og_all"],
                    in_offset=bass.IndirectOffsetOnAxis(ap=idx_sb[:, r_ * NTL + t:r_ * NTL + t + 1], axis=0)),
                    b_oin[i], R=[b_idx] + list(ext["b_og_all"]), W=[b_oin[i]])
        else:
            S.dma("pool", ot[:], o_v[t], b_oin[i], W=[b_oin[i]])

        def tr_group(e, src=ot):
            ins = None
            for k in range(KD):
                ins = e.transpose(out=ptb[:, k * 128:(k + 1) * 128], in_=src[:, k * 128:(k + 1) * 128],
                                  identity=idb[:])
            return ins
        S.op("pe", tr_group, R=[b_oin[i], b_idb], W=[b_pb[0]])
        S.op("act", lambda e: e.copy(out=tbuf[:].rearrange("p k t -> p (k t)"), in_=ptb[:, 0:KD * 128]),
             R=[b_pb[0]], W=[b_tbuf])
        for nchunk in range(2):
            bk = 1 + nchunk

            def mm_out(e, bk=bk, nchunk=nchunk):
                ins = None
                for k in range(KD):
                    ins = e.matmul(out=pbank[bk][:], lhsT=tbuf[:, k, :],
                                   rhs=wout[:, k, nchunk * 512:(nchunk + 1) * 512],
                                   start=(k == 0), stop=(k == KD - 1))
                return ins
            S.op("pe", mm_out, R=[b_tbuf, b_wout], W=[b_pb[bk]])
            S.op("dve", lambda e, bk=bk, nchunk=nchunk, xt=xt:
                 e.tensor_tensor(out=xt[:, nchunk * 512:(nchunk + 1) * 512],
                                 in0=pbank[bk][:], in1=xt[:, nchunk * 512:(nchunk + 1) * 512], op=ALU.add),
                 R=[b_pb[bk], b_xin[i]], W=[b_xin[i]])
        rms_scale(xt[:], b_xin[i], 0)
        S.op("act", lambda e, xt=xt: e.activation(out=hn[:], in_=xt[:], func=AF.Copy, scale=stat[:, 0:1]),
             R=[b_xin[i], b_stat], W=[b_hn])

        def tr_group2(e):
            ins = None
            for k in range(KD):
                ins = e.transpose(out=ptb[:, k * 128:(k + 1) * 128], in_=hn[:, k * 128:(k + 1) * 128],
                                  identity=idb[:])
            return ins
        S.op("pe", tr_group2, R=[b_hn, b_idb], W=[b_pb[0]])
        S.op("act", lambda e, i=i: e.copy(out=hT2[i][:].rearrange("p k t -> p (k t)"), in_=ptb[:, 0:KD * 128]),
             R=[b_pb[0]], W=[b_hT2[i]])

    def stageB(t):
        i = t % 2
        for j in range(JF):
            bk = 3 + (j % 2)

            def mm_gu(e, bk=bk, j=j, i=i):
                ins = None
                for half in range(2):
                    for k in range(KD):
                        c0 = half * D_FF + j * 128
                        ins = e.matmul(out=pbank[bk][:, half * 128:(half + 1) * 128],
                                       lhsT=wgu[:, k, c0:c0 + 128], rhs=hT2[i][:, k, :],
                                       start=(k == 0), stop=(k == KD - 1))
                return ins
            S.op("pe", mm_gu, R=[b_hT2[i], b_wgu], W=[b_pb[bk]])
            s_ = j % 2
            S.op("act", lambda e, bk=bk, s_=s_: e.activation(out=sg[s_][:], in_=pbank[bk][:, 0:128], func=AF.Silu),
                 R=[b_pb[bk]], W=[b_sg[s_]])
            S.op("dve", lambda e, bk=bk, s_=s_, j=j, i=i: e.tensor_tensor(out=aT2[i][:, j, :],
                                                                          in0=pbank[bk][:, 128:256],
                                                                          in1=sg[s_][:], op=ALU.mult),
                 R=[b_pb[bk], b_sg[s_]], W=[b_aT2[i]])

    def stageC(t):
        i = t % 2
        xt = xin[i]
        for nchunk in range(2):
            bk = 5 + nchunk

            def mm_dn(e, bk=bk, nchunk=nchunk, i=i):
                ins = None
                for j in range(JF):
                    ins = e.matmul(out=pbank[bk][:], lhsT=aT2[i][:, j, :],
                                   rhs=wdn[:, j, nchunk * 512:(nchunk + 1) * 512],
                                   start=(j == 0), stop=(j == JF - 1))
                return ins
            S.op("pe", mm_dn, R=[b_aT2[i], b_wdn], W=[b_pb[bk]])
            S.op("dve", lambda e, bk=bk, nchunk=nchunk, xt=xt:
                 e.tensor_tensor(out=xt[:, nchunk * 512:(nchunk + 1) * 512],
                                 in0=pbank[bk][:], in1=xt[:, nchunk * 512:(nchunk + 1) * 512], op=ALU.add),
                 R=[b_pb[bk], b_xin[i]], W=[b_xin[i]])
        if final_norm:
            rms_scale(xt[:], b_xin[i], 1)
            S.op("dve", lambda e, xt=xt: e.scalar_tensor_tensor(out=xt[:], in0=xt[:], scalar=stat[:, 1:2],
                                                                in1=finw[:], op0=ALU.mult, op1=ALU.mult),
                 R=[b_xin[i], b_stat, b_finw], W=[b_xin[i]])
        S.dma("sp", y_v[t], xt[:], b_xin[i], R=[b_xin[i]])

    stageA(0)
    for t in range(NTL):
        if t + 1 < NTL:
            stageA(t + 1)
        stageB(t)
        stageC(t)

    if ext:
        return None
    S.barrier_wait("sp", b_xin)
    return kb.done()


def _ident_bf():
    return np.eye(128, dtype=np.float32).astype(ml_dtypes.bfloat16)


def run_ffn(x_sl, o_sl, w_out, ffn_norm_w, w_gu, w_down, final_w, nc_cache={}):
    NT = x_sl[0].shape[0]
    key = (NT, final_w is not None)
    if key not in nc_cache:
        nc_cache[key] = build_ffn(NT, final_w is not None)
    nc = nc_cache[key]
    fnw = np.ascontiguousarray(ffn_norm_w.reshape(D_MODEL // 128, 128).T)
    in_maps = []
    for c in range(len(x_sl)):
        m = {"x": np.ascontiguousarray(x_sl[c]), "o": np.ascontiguousarray(o_sl[c]),
             "w_out": w_out, "w_gu": w_gu, "w_down": w_down, "ffn_norm_w": fnw,
             "ident_bf": _ident_bf()}
        if final_w is not None:
            m["final_w_bc"] = np.ascontiguousarray(np.broadcast_to(final_w[None, :], (128, D_MODEL)))
        in_maps.append(m)
    res = run_bass_kernel_spmd(nc, in_maps, core_ids=list(range(len(x_sl))))
    return [r["y"] for r in res.results]


NW = 898


NPULL = 3


def build_mixer(T, lam_init, dbg=99, ext=None):
    SKIP = ''
    kb = ext["kb"] if ext else KB()
    nc, S = kb.nc, kb.S
    NCH = T // 512
    NTL = T // 128
    KD = D_MODEL // 128

    if ext:
        x_d = ext["x"]
        wh_d, anw_d, cw_d, sc_d, lamv_d, lnw_d, dnw_d, bt_d, idb_d, cf_d = (
            ext[k] for k in ("wh", "anw", "cw", "sc", "lamv", "lnw", "dnw", "btoep", "ident_bf", "cf"))
        ola_d = ext["og"][:, 0:128]
        od_d = ext["og"][:, 128:256]
    else:
        x_d = kb.din("x", [T, D_MODEL], F32)
        wh_d = kb.din("wh", [D_MODEL, NW], F32)
        anw_d = kb.din("anw", [128, KD], F32)
        cw_d = kb.din("cw", [128, 12], F32)
        sc_d = kb.din("sc", [128, 4], F32)
        lamv_d = kb.din("lamv", [128, 4 * 64], F32)
        lnw_d = kb.din("lnw", [128, 128], F32)
        dnw_d = kb.din("dnw", [128, 128], F32)
        bt_d = kb.din("btoep", [128, 512], F32)
        idb_d = kb.din("ident_bf", [128, 128], BF16)
        cf_d = kb.din("cf", [128, 7 * 128], F32)
        ola_d = kb.dout("o_la", [T, 128], BF16)
        od_d = kb.dout("o_d", [T, 128], BF16)

    def T_(shape, dt, name):
        return kb.sb(shape, dt, name), S.buf(name)

    anw, b_anw = T_([128, KD], F32, "anw")
    cw, b_cw = T_([128, 12], F32, "cw")
    sc, b_sc = T_([128, 4], F32, "sc")
    lamv, b_lamv = T_([128, 256], F32, "lamv")
    lnw, b_lnw = T_([128, 128], F32, "lnw")
    dnw, b_dnw = T_([128, 128], F32, "dnw")
    bt, b_bt = T_([128, 512], F32, "bt")
    idb, b_idb = T_([128, 128], BF16, "idb")
    cf, b_cf = T_([128, 7 * 128], F32, "cf")
    for (t_, d_, b_) in ((anw, anw_d, b_anw), (cw, cw_d, b_cw), (sc, sc_d, b_sc), (lamv, lamv_d, b_lamv),
                         (lnw, lnw_d, b_lnw), (dnw, dnw_d, b_dnw), (bt, bt_d, b_bt), (idb, idb_d, b_idb),
                         (cf, cf_d, b_cf)):
        S.dma("sp", t_[:], d_, b_, W=[b_])
    IDF = cf[:, 0:128]
    ONES = cf[:, 128:256]
    MASKL = cf[:, 256:384]
    MASKU = cf[:, 384:512]
    TRI = cf[:, 512:640]
    SEL63 = cf[:, 640:768]
    SEL127 = cf[:, 768:896]

    cst_, b_cst = T_([128, 8], F32, "cst")
    S.op("dve", lambda e: e.memset(cst_[:, 0:1], 1.0), W=[b_cst])
    S.op("dve", lambda e: e.memset(cst_[:, 1:2], 1e-6), W=[b_cst])
    S.op("dve", lambda e: e.memset(cst_[:, 2:3], 1e-5), W=[b_cst])
    S.op("dve", lambda e: e.memset(cst_[:, 5:6], 0.0), W=[b_cst])
    C_ONE, C_EPS6, C_EPS5, C_NA, C_NLAM, C_ZERO = (cst_[:, i:i + 1] for i in range(6))
    S.op("act", lambda e: e.activation(out=cst_[:, 3:4], in_=sc[:, 0:1], func=AF.Exp), R=[b_sc], W=[b_cst])
    S.op("dve", lambda e: e.tensor_scalar(out=cst_[:, 3:4], in0=cst_[:, 3:4], scalar1=-1.0, scalar2=None,
                                          op0=ALU.mult), R=[b_cst], W=[b_cst])
    lt, b_lt = T_([128, 128], F32, "lamtmp")
    ls, b_ls = T_([128, 4], F32, "lamsum")
    S.op("dve", lambda e: e.tensor_tensor(out=lt[:, 0:64], in0=lamv[:, 0:64], in1=lamv[:, 64:128], op=ALU.mult),
         R=[b_lamv], W=[b_lt])
    S.op("dve", lambda e: e.tensor_tensor(out=lt[:, 64:128], in0=lamv[:, 128:192], in1=lamv[:, 192:256],
                                          op=ALU.mult), R=[b_lamv, b_lt], W=[b_lt])
    if 'r' not in SKIP:
        S.op("dve", lambda e: e.reduce_sum(out=ls[:, 0:1], in_=lt[:, 0:64], axis=AX.X), R=[b_lt], W=[b_ls])
        S.op("dve", lambda e: e.reduce_sum(out=ls[:, 1:2], in_=lt[:, 64:128], axis=AX.X), R=[b_lt, b_ls], W=[b_ls])
    S.op("act", lambda e: e.activation(out=ls[:, 2:4], in_=ls[:, 0:2], func=AF.Exp), R=[b_ls], W=[b_ls])
    S.op("dve", lambda e: e.scalar_tensor_tensor(out=cst_[:, 4:5], in0=ls[:, 3:4], scalar=float(-lam_init),
                                                 in1=ls[:, 2:3], op0=ALU.add, op1=ALU.subtract),
         R=[b_ls, b_cst], W=[b_cst])
    S.op("dve", lambda e: e.tensor_scalar(out=dnw[:], in0=dnw[:], scalar1=float(1.0 - lam_init), scalar2=None,
                                          op0=ALU.mult), R=[b_dnw], W=[b_dnw])

    Wb, b_Wb = T_([128, KD, 1024], BF16, "Wb")
    wst = [kb.sb([128, NW], F32, "wst%d" % i) for i in range(2)]
    b_wst = S.bufs(2, "wst")
    wh_v = wh_d.rearrange("(ko p) n -> p ko n", p=128)
    for ko in range(KD):
        i = ko % 2
        S.dma("sp", wst[i][:], wh_v[:, ko, :], b_wst[i], W=[b_wst[i]])
        S.op("dve" if i == 0 else "pool",
             lambda e, i=i, ko=ko: e.tensor_scalar(out=Wb[:, ko, 0:NW], in0=wst[i][:], scalar1=anw[:, ko:ko + 1],
                                                   scalar2=None, op0=ALU.mult),
             R=[b_wst[i], b_anw], W=[b_Wb])

    KdT, b_KdT = T_([128, T], BF16, "KdT")
    Vaug, b_Vaug = T_([128, NTL, 144], BF16, "Vaug")
    if 'v' not in SKIP:
        S.op("pool", lambda e: e.memset(Vaug[:, :, 128:129], 1.0), W=[b_Vaug])

    xt = [kb.sb([128, D_MODEL], F32, "xt%d" % i) for i in range(2)]
    b_xt = S.bufs(2, "xt")
    xn, b_xn = T_([128, D_MODEL], BF16, "xn")
    junk, b_junk = T_([128, D_MODEL], BF16, "junk")
    junkf, b_junkf = T_([128, 128], F32, "junkf")
    stat, b_stat = T_([128, 4], F32, "stat")
    hT, b_hT = T_([128, KD, 512], BF16, "hT")
    cstg = [kb.sb([128, 515], F32, "cstg%d" % g) for g in range(3)]
    b_cstg = S.bufs(3, "cstg")
    cacc = [kb.sb([128, 512], F32, "cacc%d" % g) for g in range(3)]
    b_cacc = S.bufs(3, "cacc")
    sil = [kb.sb([128, 512], F32, "sil%d" % g) for g in range(2)]
    b_sil = S.bufs(2, "sil")
    sq, b_sq = T_([128, 512], F32, "sq")
    rs, b_rs = T_([128, 512], F32, "rs")
    qnT, b_qnT = T_([128, 512], BF16, "qnT")
    knT, b_knT = T_([128, 512], BF16, "knT")
    vsT, b_vsT = T_([128, 512], BF16, "vsT")
    QdT, b_QdT = T_([128, 512], BF16, "QdT")
    qgT, b_qgT = T_([128, 512], BF16, "qgT")
    kvt, b_kvt = T_([128, 8, 128], BF16, "kvt")
    zs, b_zs = T_([128, 4, 128], BF16, "zs")
    ba, b_ba = T_([128, 4, 2], F32, "ba")
    for g in range(3):
        S.op("dve", lambda e, g=g: e.memset(cstg[g][:, 0:3], 0.0), W=[b_cstg[g]])
    pt = {}
    for nm in ("beta", "eb", "g", "gc", "egc", "bg", "glt", "ekd", "tmpa"):
        pt[nm] = T_([128, 4], F32, "pt_" + nm)
    egl, b_egl = T_([128, 8], F32, "egl")
    NS = 4
    gset = []
    for s_ in range(NS):
        d = {}
        for nm, shp, dt in (("dg", [128, 128], F32), ("Eb", [128, 128], F32), ("Ds", [128, 128], F32),
                            ("EA", [128, 128], F32), ("t1", [128, 128], F32), ("t2", [128, 128], F32),
                            ("MPa", [128, 256], F32), ("MPb", [128, 256], F32),
                            ("MTa", [128, 128], F32), ("MTb", [128, 128], F32),
                            ("TT", [128, 128], BF16), ("vb", [128, 128], BF16), ("kbg", [128, 128], BF16)):
            d[nm] = T_(shp, dt, "%s_%d" % (nm, s_))
        gset.append(d)
    u_sb, b_u = T_([128, 4, 128], F32, "u_sb")
    wT_sb, b_wT = T_([128, 512], BF16, "wT_sb")
    apT, b_apT = T_([128, 4, 128], BF16, "apT")
    kd_sb, b_kd = T_([128, 4, 128], BF16, "kd_sb")
    S_f, b_Sf = T_([128, 128], F32, "S_f")
    Sb, b_Sb = T_([128, 128], BF16, "Sb")
    vn, b_vn = T_([128, 128], BF16, "vn")
    o_sb = [kb.sb([128, 128], F32, "o_sb%d" % i) for i in range(2)]
    b_osb = S.bufs(2, "o_sb")
    on_, b_on = T_([128, 128], F32, "on")
    ola_st = [kb.sb([128, 4, 128], BF16, "ola_st%d" % i) for i in range(2)]
    b_olast = S.bufs(2, "ola_st")
    S.op("dve", lambda e: e.memset(S_f[:], 0.0), W=[b_Sf])
    S.op("dve", lambda e: e.memset(Sb[:], 0.0), W=[b_Sb])
    pT = [[kb.sb([128, 256], BF16, "pT%d%d" % (m, p)) for p in range(2)] for m in range(2)]
    b_pT = [[S.buf("pT%d%d" % (m, p)) for p in range(2)] for m in range(2)]
    s2 = [kb.sb([128, 256], F32, "s2_%d" % m) for m in range(2)]
    b_s2 = S.bufs(2, "s2")
    rden, b_rden = T_([128, 4], F32, "rden")
    O1, b_O1 = T_([128, 128], F32, "O1")
    odf, b_odf = T_([128, 128], F32, "odf")
    od_st = [kb.sb([128, 2, 128], BF16, "od_st%d" % i) for i in range(2)]
    b_odst = S.bufs(2, "od_st")

    if ext:
        pb, b_pb = ext["pb"], ext["b_pb"]
    else:
        pb = [kb.ps([128, 512], F32, "pb%d" % i) for i in range(8)]
        b_pb = S.bufs(8, "pb", excl=True)
    pb0_bf = pb[0][:].bitcast(BF16)

    x_v = x_d.rearrange("(t p) d -> t p d", p=128)
    ola_v = ola_d.rearrange("(c t p) e -> c p t e", p=128, t=4)
    od_v = od_d.rearrange("(c t p) e -> c p t e", p=128, t=2)

    def rstd_from_ss(col, n, epsc):
        S.op("act", lambda e: e.activation(out=stat[:, col:col + 1], in_=stat[:, col:col + 1], func=AF.Sqrt,
                                           scale=1.0 / n, bias=epsc), R=[b_stat, b_cst], W=[b_stat])
        S.op("dve", lambda e: e.reciprocal(out=stat[:, col:col + 1], in_=stat[:, col:col + 1]),
             R=[b_stat], W=[b_stat])

    hT2 = [hT, kb.sb([128, KD, 512], BF16, "hT_b")]
    b_hT2 = [b_hT, S.buf("hT_b")]
    QdT2 = [QdT, kb.sb([128, 512], BF16, "QdT_b")]
    b_QdT2 = [b_QdT, S.buf("QdT_b")]
    zs2 = [zs, kb.sb([128, 4, 128], BF16, "zs_b")]
    b_zs2 = [b_zs, S.buf("zs_b")]
    pend = []

    def pull(n):
        for _ in range(min(n, len(pend))):
            pend.pop(0)()

    def front(j):
        if dbg <= 0:
            return
        for tt in range(4):
            t = 4 * j + tt
            i = t % 2
            S.dma("sp" if i == 0 else "pool", xt[i][:],
                  (ext["x_tile"](t) if (ext and ext.get("x_tile") is not None) else x_v[t]), b_xt[i], W=[b_xt[i]],
                  R=(list(ext["b_x"]) if (ext and ext.get("b_x") is not None) else []))
            S.op("act", lambda e, i=i: e.activation(out=junk[:], in_=xt[i][:], func=AF.Square,
                                                    accum_out=stat[:, 0:1]), R=[b_xt[i]], W=[b_junk, b_stat])
            rstd_from_ss(0, D_MODEL, C_EPS6)
            S.op("act", lambda e, i=i: e.activation(out=xn[:], in_=xt[i][:], func=AF.Copy, scale=stat[:, 0:1]),
                 R=[b_xt[i], b_stat], W=[b_xn])

            def trx(e):
                ins = None
                for k in range(KD):
                    ins = e.transpose(out=pb0_bf[:, k * 128:(k + 1) * 128], in_=xn[:, k * 128:(k + 1) * 128],
                                      identity=idb[:])
                return ins
            S.op("pe", trx, R=[b_xn, b_idb], W=[b_pb[0]])
            S.op("dve", lambda e, tt=tt: e.tensor_copy(out=hT2[j % 2][:, :, tt * 128:(tt + 1) * 128],
                                                        in_=pb0_bf[:, 0:1024].rearrange("p (k t) -> p k t", k=KD)),
                 R=[b_pb[0]], W=[b_hT2[j % 2]])
        if dbg <= 1:
            return
        for g in range(5 if 'f' not in SKIP else 0):
            bk = 1 + (g % 2)

            def mmf(e, g=g, bk=bk):
                ins = None
                for k in range(KD):
                    ins = e.matmul(out=pb[bk][:], lhsT=Wb[:, k, g * 128:(g + 1) * 128], rhs=hT2[j % 2][:, k, :],
                                   start=(k == 0), stop=(k == KD - 1))
                return ins
            S.op("pe", mmf, R=[b_Wb, b_hT2[j % 2]], W=[b_pb[bk]])
            if g < 3:
                S.op("act", lambda e, g=g, bk=bk: e.copy(out=cstg[g][:, 3:515], in_=pb[bk][:]),
                     R=[b_pb[bk]], W=[b_cstg[g]])
            elif g == 3:
                S.op("act", lambda e, bk=bk: e.copy(out=QdT2[j % 2][:], in_=pb[bk][:]), R=[b_pb[bk]], W=[b_QdT2[j % 2]])
            else:
                S.op("act", lambda e, bk=bk, j=j: e.copy(out=KdT[:, j * 512:(j + 1) * 512], in_=pb[bk][:]),
                     R=[b_pb[bk]], W=[b_KdT])
        for tt in range(4 if 't' not in SKIP else 0):
            t = 4 * j + tt

            def mmt(e, tt=tt):
                ins = None
                for k in range(KD):
                    NN = 256 if 'n' in SKIP else 258
                    ins = e.matmul(out=pb[3][:, 0:NN], lhsT=hT2[j % 2][:, k, tt * 128:(tt + 1) * 128],
                                   rhs=Wb[:, k, 640:640 + NN], start=(k == 0), stop=(k == KD - 1))
                return ins
            S.op("pe", mmt, R=[b_Wb, b_hT2[j % 2]], W=[b_pb[3]])
            S.op("act", lambda e, tt=tt: e.activation(out=zs2[j % 2][:, tt, :], in_=pb[3][:, 0:128], func=AF.Silu),
                 R=[b_pb[3]], W=[b_zs2[j % 2]])
            S.op("dve", lambda e, t=t: e.tensor_copy(out=Vaug[:, t, 0:128], in_=pb[3][:, 128:256]),
                 R=[b_pb[3]], W=[b_Vaug])
            S.op("dve", lambda e, tt=tt: e.tensor_copy(out=ba[:, tt, :], in_=pb[3][:, 256:258]),
                 R=[b_pb[3]], W=[b_ba])

    front(0)
    for j in range(NCH):
        if dbg <= 2:
            continue
        pend.clear()
        if j + 1 < NCH:
            S.defer = pend
            front(j + 1)
            S.defer = None
        for g in range(3):
            ce = "dve"
            S.op(ce, lambda e, g=g: e.tensor_scalar(out=cacc[g][:], in0=cstg[g][:, 3:515],
                                                    scalar1=cw[:, g * 4 + 3:g * 4 + 4], scalar2=None, op0=ALU.mult),
                 R=[b_cstg[g], b_cw], W=[b_cacc[g]])
            for tap in (2, 1, 0):
                S.op(ce, lambda e, g=g, tap=tap: e.scalar_tensor_tensor(
                    out=cacc[g][:], in0=cstg[g][:, tap:tap + 512], scalar=cw[:, g * 4 + tap:g * 4 + tap + 1],
                    in1=cacc[g][:], op0=ALU.mult, op1=ALU.add),
                    R=[b_cstg[g], b_cw, b_cacc[g]], W=[b_cacc[g]])
            S.op(ce, lambda e, g=g: e.tensor_copy(out=cstg[g][:, 0:3], in_=cstg[g][:, 512:515]),
                 R=[b_cstg[g]], W=[b_cstg[g]])
            if g < 2:
                S.op("act", lambda e, g=g: e.activation(out=sil[g][:], in_=cacc[g][:], func=AF.Silu),
                     R=[b_cacc[g]], W=[b_sil[g]])
            else:
                S.op("act", lambda e, g=g: e.activation(out=vsT[:], in_=cacc[g][:], func=AF.Silu),
                     R=[b_cacc[g]], W=[b_vsT])
        if dbg <= 3:
            continue
        for g in range(2):
            S.op("pool", lambda e, g=g: e.tensor_tensor(out=sq[:], in0=sil[g][:], in1=sil[g][:], op=ALU.mult),
                 R=[b_sil[g]], W=[b_sq])
            bk = 1 + g
            S.op("pe", lambda e, bk=bk: e.matmul(out=pb[bk][:], lhsT=ONES, rhs=sq[:], start=True, stop=True),
                 R=[b_sq, b_cf], W=[b_pb[bk]])
            S.op("act", lambda e, bk=bk: e.activation(out=rs[:], in_=pb[bk][:], func=AF.Sqrt, bias=C_EPS6),
                 R=[b_pb[bk], b_cst], W=[b_rs])
            S.op("dve", lambda e: e.reciprocal(out=rs[:], in_=rs[:]), R=[b_rs], W=[b_rs])
            if g == 0:
                S.op("dve", lambda e: e.scalar_tensor_tensor(out=qnT[:], in0=sil[0][:], scalar=float(128 ** -0.5),
                                                             in1=rs[:], op0=ALU.mult, op1=ALU.mult),
                     R=[b_sil[0], b_rs], W=[b_qnT])
            else:
                S.op("dve", lambda e: e.tensor_tensor(out=knT[:], in0=sil[1][:], in1=rs[:], op=ALU.mult),
                     R=[b_sil[1], b_rs], W=[b_knT])
        if dbg <= 4:
            continue
        def trkv(e):
            ins = None
            for tt in range(4):
                ins = e.transpose(out=pb0_bf[:, (2 * tt) * 128:(2 * tt + 1) * 128],
                                  in_=knT[:, tt * 128:(tt + 1) * 128], identity=idb[:])
                ins = e.transpose(out=pb0_bf[:, (2 * tt + 1) * 128:(2 * tt + 2) * 128],
                                  in_=vsT[:, tt * 128:(tt + 1) * 128], identity=idb[:])
            return ins
        S.op("pe", trkv, R=[b_knT, b_vsT, b_idb], W=[b_pb[0]])
        S.op("dve", lambda e: e.tensor_copy(out=kvt[:].rearrange("p a d -> p (a d)"), in_=pb0_bf[:, 0:1024]),
             R=[b_pb[0]], W=[b_kvt])
        if dbg <= 5:
            continue
        P = lambda nm: pt[nm][0]
        B = lambda nm: pt[nm][1]
        S.op("act", lambda e: e.activation(out=P("eb")[:], in_=ba[:, :, 0], func=AF.Exp, scale=-1.0),
             R=[b_ba], W=[B("eb")])
        S.op("dve", lambda e: e.tensor_scalar(out=P("eb")[:], in0=P("eb")[:], scalar1=1.0, scalar2=None,
                                              op0=ALU.add), R=[B("eb")], W=[B("eb")])
        S.op("dve", lambda e: e.reciprocal(out=P("beta")[:], in_=P("eb")[:]), R=[B("eb")], W=[B("beta")])
        S.op("act", lambda e: e.activation(out=P("tmpa")[:], in_=ba[:, :, 1], func=AF.Exp, bias=sc[:, 1:2]),
             R=[b_ba, b_sc], W=[B("tmpa")])
        S.op("act", lambda e: e.activation(out=P("tmpa")[:], in_=P("tmpa")[:], func=AF.Ln, bias=C_ONE),
             R=[B("tmpa"), b_cst], W=[B("tmpa")])
        S.op("dve", lambda e: e.tensor_scalar(out=P("g")[:], in0=P("tmpa")[:], scalar1=C_NA, scalar2=None,
                                              op0=ALU.mult), R=[B("tmpa"), b_cst], W=[B("g")])
        S.op("pe", lambda e: e.matmul(out=pb[3][:, 0:4], lhsT=TRI, rhs=P("g")[:], start=True, stop=True),
             R=[B("g"), b_cf], W=[b_pb[3]])
        S.op("act", lambda e: e.copy(out=P("gc")[:], in_=pb[3][:, 0:4]), R=[b_pb[3]], W=[B("gc")])

        def mmgl(e):
            e.matmul(out=pb[3][:, 0:4], lhsT=SEL63, rhs=P("gc")[:], start=True, stop=True)
            return e.matmul(out=pb[3][:, 4:8], lhsT=SEL127, rhs=P("gc")[:], start=True, stop=True)
        S.op("pe", mmgl, R=[B("gc"), b_cf], W=[b_pb[3]])
        eglv = egl[:].rearrange("p (t h) -> p t h", h=2)
        S.op("act", lambda e: e.activation(out=eglv[:, :, 0], in_=pb[3][:, 0:4], func=AF.Exp),
             R=[b_pb[3]], W=[b_egl])
        S.op("act", lambda e: e.activation(out=eglv[:, :, 1], in_=pb[3][:, 4:8], func=AF.Exp),
             R=[b_pb[3], b_egl], W=[b_egl])
        S.op("dve", lambda e: e.tensor_copy(out=P("glt")[0:64, :], in_=pb[3][0:64, 0:4]),
             R=[b_pb[3]], W=[B("glt")])
        S.op("dve", lambda e: e.tensor_copy(out=P("glt")[64:128, :], in_=pb[3][64:128, 4:8]),
             R=[b_pb[3], B("glt")], W=[B("glt")])
        S.op("dve", lambda e: e.tensor_tensor(out=P("ekd")[:], in0=P("glt")[:], in1=P("gc")[:], op=ALU.subtract),
             R=[B("glt"), B("gc")], W=[B("ekd")])
        S.op("act", lambda e: e.activation(out=P("ekd")[:], in_=P("ekd")[:], func=AF.Exp),
             R=[B("ekd")], W=[B("ekd")])
        S.op("act", lambda e: e.activation(out=P("egc")[:], in_=P("gc")[:], func=AF.Exp),
             R=[B("gc")], W=[B("egc")])
        S.op("dve", lambda e: e.tensor_tensor(out=P("bg")[:], in0=P("beta")[:], in1=P("egc")[:], op=ALU.mult),
             R=[B("beta"), B("egc")], W=[B("bg")])

        if dbg <= 6:
            continue
        def tile_gen(tt):
            gs = gset[tt % NS]
            bk = 4 + tt
            G = lambda nm, gs=gs: gs[nm][0]
            GB = lambda nm, gs=gs: gs[nm][1]
            csl = slice(tt * 128, (tt + 1) * 128)
            S.op("dve", lambda e, G=G, tt=tt: e.tensor_scalar(out=G("dg")[:], in0=IDF, scalar1=P("gc")[:, tt:tt + 1],
                                                                scalar2=None, op0=ALU.mult),
                 R=[b_cf, B("gc")], W=[GB("dg")])

            def mm_abb(e, G=G, bk=bk, csl=csl):
                e.matmul(out=pb[bk][:, 0:128], lhsT=ONES, rhs=G("dg")[:], start=True, stop=True)
                e.matmul(out=pb[bk][:, 128:256], lhsT=knT[:, csl], rhs=knT[:, csl], start=True, stop=True)
                return e.matmul(out=pb[bk][:, 256:384], lhsT=knT[:, csl], rhs=qnT[:, csl], start=True, stop=True)
            yield
            S.op("pe", mm_abb, R=[b_cf, GB("dg"), b_knT, b_qnT], W=[b_pb[bk]])
            S.op("dve", lambda e, G=G, bk=bk, tt=tt: e.tensor_scalar(
                out=G("Eb")[:], in0=pb[bk][:, 0:128], scalar1=P("gc")[:, tt:tt + 1], scalar2=None,
                op0=ALU.subtract), R=[b_pb[bk], B("gc")], W=[GB("Eb")])
            S.op("act", lambda e, G=G: e.activation(out=G("Eb")[:], in_=G("Eb")[:], func=AF.Abs),
                 R=[GB("Eb")], W=[GB("Eb")])
            S.op("act", lambda e, G=G: e.activation(out=G("Ds")[:], in_=G("Eb")[:], func=AF.Exp, scale=-1.0),
                 R=[GB("Eb")], W=[GB("Ds")])
            S.op("act", lambda e, G=G, bk=bk: e.activation(out=G("EA")[:], in_=pb[bk][:, 0:128], func=AF.Exp),
                 R=[b_pb[bk]], W=[GB("EA")])
            S.op("dve", lambda e, G=G, bk=bk, tt=tt: e.scalar_tensor_tensor(
                out=G("t1")[:], in0=pb[bk][:, 128:256], scalar=P("beta")[:, tt:tt + 1], in1=G("Ds")[:],
                op0=ALU.mult, op1=ALU.mult), R=[b_pb[bk], B("beta"), GB("Ds")], W=[GB("t1")])
            S.op("dve", lambda e, G=G: e.tensor_tensor(out=G("MTa")[:], in0=G("t1")[:], in1=MASKL, op=ALU.mult),
                 R=[GB("t1"), b_cf], W=[GB("MTa")])
            S.op("dve", lambda e, G=G, bk=bk: e.tensor_tensor(out=G("t2")[:], in0=pb[bk][:, 256:384], in1=G("Ds")[:],
                                                              op=ALU.mult), R=[b_pb[bk], GB("Ds")], W=[GB("t2")])
            S.op("pool", lambda e, G=G, tt=tt: e.tensor_tensor(out=apT[:, tt, :], in0=G("t2")[:], in1=MASKU,
                                                               op=ALU.mult), R=[GB("t2"), b_cf], W=[b_apT])
            S.op("pool", lambda e, G=G, csl=csl: e.tensor_tensor(out=qgT[:, csl], in0=qnT[:, csl], in1=G("EA")[:],
                                                                 op=ALU.mult), R=[b_qnT, GB("EA")], W=[b_qgT])
            yield
            S.op("pe", lambda e, G=G, bk=bk: e.transpose(out=pb[bk][:, 384:512], in_=G("MTa")[:], identity=IDF),
                 R=[GB("MTa"), b_cf], W=[b_pb[bk]])
            S.op("act", lambda e, G=G, bk=bk: e.copy(out=G("MPa")[:, 0:128], in_=pb[bk][:, 384:512]),
                 R=[b_pb[bk]], W=[GB("MPa")])
            S.op("dve", lambda e, G=G, bk=bk: e.tensor_tensor(out=G("MPb")[:, 128:256], in0=pb[bk][:, 384:512],
                                                              in1=IDF, op=ALU.add),
                 R=[b_pb[bk], b_cf], W=[GB("MPb")])

            def st0(e, G=G, bk=bk):
                e.matmul(out=pb[bk][:, 0:128], lhsT=G("MTa")[:], rhs=G("MPa")[:, 0:128], start=True, stop=True)
                return e.matmul(out=pb[bk][:, 128:256], lhsT=G("MPa")[:, 0:128], rhs=G("MTa")[:], start=True,
                                stop=True)
            yield
            S.op("pe", st0, R=[GB("MTa"), GB("MPa")], W=[b_pb[bk]])
            S.op("act", lambda e, G=G, bk=bk: e.copy(out=G("MPb")[:, 0:128], in_=pb[bk][:, 0:128]),
                 R=[b_pb[bk]], W=[GB("MPb")])
            S.op("dve", lambda e, G=G, bk=bk: e.tensor_copy(out=G("MTb")[:], in_=pb[bk][:, 128:256]),
                 R=[b_pb[bk]], W=[GB("MTb")])
            cur, nxt = ("MPb", "MTb"), ("MPa", "MTa")
            for stp in range(1, 5):
                def stj(e, G=G, bk=bk, cur=cur):
                    e.matmul(out=pb[bk][:, 0:256], lhsT=G(cur[1])[:], rhs=G(cur[0])[:, 0:256], start=True, stop=True)
                    return e.matmul(out=pb[bk][:, 256:384], lhsT=G(cur[0])[:, 0:128], rhs=G(cur[1])[:], start=True,
                                    stop=True)
                yield
                S.op("pe", stj, R=[GB(cur[0]), GB(cur[1])], W=[b_pb[bk]])
                S.op("act", lambda e, G=G, bk=bk, nxt=nxt: e.copy(out=G(nxt[0])[:, 0:128], in_=pb[bk][:, 0:128]),
                     R=[b_pb[bk]], W=[GB(nxt[0])])
                S.op("dve", lambda e, G=G, bk=bk, cur=cur, nxt=nxt: e.tensor_tensor(
                    out=G(nxt[0])[:, 128:256], in0=pb[bk][:, 128:256], in1=G(cur[0])[:, 128:256], op=ALU.add),
                    R=[b_pb[bk], GB(cur[0])], W=[GB(nxt[0])])
                S.op("act", lambda e, G=G, bk=bk, nxt=nxt: e.copy(out=G(nxt[1])[:], in_=pb[bk][:, 256:384]),
                     R=[b_pb[bk]], W=[GB(nxt[1])])
                cur, nxt = nxt, cur
            yield
            S.op("pe", lambda e, G=G, bk=bk, cur=cur: e.matmul(out=pb[bk][:, 0:128], lhsT=G(cur[1])[:],
                                                               rhs=G(cur[0])[:, 128:256], start=True, stop=True),
                 R=[GB(cur[0]), GB(cur[1])], W=[b_pb[bk]])
            S.op("dve", lambda e, G=G, bk=bk, cur=cur: e.tensor_tensor(out=G("TT")[:], in0=pb[bk][:, 0:128],
                                                                       in1=G(cur[0])[:, 128:256], op=ALU.add),
                 R=[b_pb[bk], GB(cur[0])], W=[GB("TT")])
            S.op("pool", lambda e, G=G, tt=tt: e.tensor_scalar(out=G("vb")[:], in0=kvt[:, 2 * tt + 1, :],
                                                               scalar1=P("beta")[:, tt:tt + 1], scalar2=None,
                                                               op0=ALU.mult), R=[b_kvt, B("beta")], W=[GB("vb")])
            S.op("pool", lambda e, G=G, tt=tt: e.tensor_scalar(out=G("kbg")[:], in0=kvt[:, 2 * tt, :],
                                                               scalar1=P("bg")[:, tt:tt + 1], scalar2=None,
                                                               op0=ALU.mult), R=[b_kvt, B("bg")], W=[GB("kbg")])
            S.op("pool", lambda e, tt=tt: e.tensor_scalar(out=kd_sb[:, tt, :], in0=kvt[:, 2 * tt, :],
                                                          scalar1=P("ekd")[:, tt:tt + 1], scalar2=None,
                                                          op0=ALU.mult), R=[b_kvt, B("ekd")], W=[b_kd])

            def mm_uw(e, G=G, bk=bk):
                e.matmul(out=pb[bk][:, 0:128], lhsT=G("TT")[:], rhs=G("vb")[:], start=True, stop=True)
                return e.matmul(out=pb[bk][:, 128:256], lhsT=G("kbg")[:], rhs=G("TT")[:], start=True, stop=True)
            yield
            S.op("pe", mm_uw, R=[GB("TT"), GB("vb"), GB("kbg")], W=[b_pb[bk]])
            S.op("act", lambda e, bk=bk, tt=tt: e.copy(out=u_sb[:, tt, :], in_=pb[bk][:, 0:128]),
                 R=[b_pb[bk]], W=[b_u])
            S.op("act", lambda e, bk=bk, csl=csl: e.copy(out=wT_sb[:, csl], in_=pb[bk][:, 128:256]),
                 R=[b_pb[bk]], W=[b_wT])

        gens = [tile_gen(tt) for tt in range(4)]
        while gens:
            for g_ in gens[:]:
                try:
                    next(g_)
                except StopIteration:
                    gens.remove(g_)
            pull(NPULL)
        if dbg <= 7:
            continue
        oi = j % 2
        for tt in range(4):
            csl = slice(tt * 128, (tt + 1) * 128)
            osb, b_o = o_sb[tt % 2], b_osb[tt % 2]
            for hh in range(2):
                r = slice(hh * 64, hh * 64 + 64)
                nl = 2 * tt + hh

                def mm1(e, csl=csl):
                    e.matmul(out=pb[6][:, 0:128], lhsT=wT_sb[:, csl], rhs=Sb[:], start=True, stop=True)
                    return e.matmul(out=pb[7][:, 0:128], lhsT=qgT[:, csl], rhs=Sb[:], start=True, stop=False)
                S.op("pe", mm1, R=[b_wT, b_qgT, b_Sb], W=[b_pb[6], b_pb[7]])
                S.op("dve", lambda e, r=r, tt=tt: e.tensor_tensor(out=vn[r, :], in0=u_sb[r, tt, :],
                                                                  in1=pb[6][r, 0:128], op=ALU.subtract),
                     R=[b_u, b_pb[6]], W=[b_vn])

                def mm2(e, r=r, tt=tt):
                    e.matmul(out=pb[7][:, 0:128], lhsT=apT[r, tt, :], rhs=vn[r, :], start=False, stop=True)
                    return e.matmul(out=pb[7][:, 128:256], lhsT=kd_sb[r, tt, :], rhs=vn[r, :], start=True, stop=True)
                S.op("pe", mm2, R=[b_apT, b_kd, b_vn], W=[b_pb[7]])
                S.op("act", lambda e, r=r, osb=osb: e.copy(out=osb[r, :], in_=pb[7][r, 0:128]),
                     R=[b_pb[7]], W=[b_o])
                S.op("dve", lambda e, nl=nl: e.scalar_tensor_tensor(out=S_f[:], in0=S_f[:], scalar=egl[:, nl:nl + 1],
                                                                    in1=pb[7][:, 128:256], op0=ALU.mult,
                                                                    op1=ALU.add),
                     R=[b_Sf, b_egl, b_pb[7]], W=[b_Sf])
                S.op("act", lambda e: e.copy(out=Sb[:], in_=S_f[:]), R=[b_Sf], W=[b_Sb])
                pull(NPULL)
            S.op("act", lambda e, osb=osb: e.activation(out=junkf[:], in_=osb[:], func=AF.Square,
                                                        accum_out=stat[:, 1:2]), R=[b_o], W=[b_junkf, b_stat])
            rstd_from_ss(1, 128, C_EPS6)
            S.op("dve", lambda e, osb=osb: e.scalar_tensor_tensor(out=on_[:], in0=osb[:], scalar=stat[:, 1:2],
                                                                  in1=lnw[:], op0=ALU.mult, op1=ALU.mult),
                 R=[b_o, b_stat, b_lnw], W=[b_on])
            S.op("dve", lambda e, tt=tt, oi=oi, zz=zs2[j % 2]: e.tensor_tensor(out=ola_st[oi][:, tt, :], in0=on_[:], in1=zz[:, tt, :],
                                                                op=ALU.mult), R=[b_on, b_zs2[j % 2]], W=[b_olast[oi]])
        S.dma("sp", ola_v[j], ola_st[oi][:], b_olast[oi], R=[b_olast[oi]])

        if dbg <= 8:
            continue
        pull(len(pend))
        for qq in range(2):
            qc = 2 * j + qq
            q0 = qc * 256
            qsl = slice(qq * 256, qq * 256 + 256)
            oi2 = qc % 2
            nkt = 2 * qc + 2
            def emit_qk(kt, qsl=qsl, qd=QdT2[j % 2], bq=b_QdT2[j % 2]):
                k0 = kt * 128
                par = kt % 2
                for m in range(2):
                    bk = 4 + 2 * m + par
                    ms = slice(m * 64, m * 64 + 64)
                    S.op("pe", lambda e, bk=bk, ms=ms, k0=k0, qsl=qsl, qd=qd: e.matmul(
                        out=pb[bk][:, 0:256], lhsT=KdT[ms, k0:k0 + 128], rhs=qd[ms, qsl], start=True, stop=True),
                        R=[b_KdT, bq], W=[b_pb[bk]])

            def emit_exp(kt, q0=q0):
                k0 = kt * 128
                d = q0 - k0
                far = d >= 256
                par = kt % 2
                for m in range(2):
                    bk = 4 + 2 * m + par
                    if far:
                        S.op("act", lambda e, bk=bk, m=m, par=par: e.activation(
                            out=pT[m][par][:], in_=pb[bk][:, 0:256], func=AF.Exp, scale=0.125, bias=sc[:, 2:3]),
                            R=[b_pb[bk], b_sc], W=[b_pT[m][par]])
                    else:
                        S.op("dve", lambda e, bk=bk, m=m, d=d: e.scalar_tensor_tensor(
                            out=s2[m][:], in0=pb[bk][:, 0:256], scalar=0.125, in1=bt[:, d + 128:d + 128 + 256],
                            op0=ALU.mult, op1=ALU.add), R=[b_pb[bk], b_bt], W=[b_s2[m]])
                        S.op("act", lambda e, m=m, par=par: e.activation(out=pT[m][par][:], in_=s2[m][:],
                                                                         func=AF.Exp),
                             R=[b_s2[m]], W=[b_pT[m][par]])

            def emit_pv(kt, qc=qc):
                par = kt % 2
                for m in range(2):
                    for qb in range(2):
                        klast = 2 * qc + qb
                        if kt > klast:
                            continue
                        ab = qb * 2 + m
                        S.op("pe", lambda e, ab=ab, m=m, par=par, qb=qb, kt=kt, klast=klast: e.matmul(
                            out=pb[ab][:, 0:129], lhsT=pT[m][par][:, qb * 128:(qb + 1) * 128],
                            rhs=Vaug[:, kt, 0:129], start=(kt == 0), stop=(kt == klast)),
                            R=[b_pT[m][par], b_Vaug], W=[b_pb[ab]])

            emit_qk(0)
            for kt in range(nkt):
                if kt + 1 < nkt:
                    emit_qk(kt + 1)
                emit_exp(kt)
                emit_pv(kt)
            for qb in range(2):
                a1, a2 = qb * 2, qb * 2 + 1
                S.op("dve", lambda e, a1=a1: e.reciprocal(out=rden[:, 0:1], in_=pb[a1][:, 128:129]),
                     R=[b_pb[a1]], W=[b_rden])
                S.op("dve", lambda e, a2=a2: e.reciprocal(out=rden[:, 1:2], in_=pb[a2][:, 128:129]),
                     R=[b_pb[a2], b_rden], W=[b_rden])
                S.op("dve", lambda e: e.tensor_scalar(out=rden[:, 2:3], in0=rden[:, 1:2], scalar1=C_NLAM,
                                                      scalar2=None, op0=ALU.mult), R=[b_rden, b_cst], W=[b_rden])
                S.op("act", lambda e, a1=a1: e.activation(out=O1[:], in_=pb[a1][:, 0:128], func=AF.Copy,
                                                          scale=rden[:, 0:1]), R=[b_pb[a1], b_rden], W=[b_O1])
                S.op("dve", lambda e, a2=a2: e.scalar_tensor_tensor(out=odf[:], in0=pb[a2][:, 0:128],
                                                                    scalar=rden[:, 2:3], in1=O1[:], op0=ALU.mult,
                                                                    op1=ALU.add),
                     R=[b_pb[a2], b_rden, b_O1], W=[b_odf])
                S.op("act", lambda e: e.activation(out=junkf[:], in_=odf[:], func=AF.Square,
                                                   accum_out=stat[:, 2:3]), R=[b_odf], W=[b_junkf, b_stat])
                rstd_from_ss(2, 128, C_EPS5)
                S.op("dve", lambda e, qb=qb, oi2=oi2: e.scalar_tensor_tensor(
                    out=od_st[oi2][:, qb, :], in0=odf[:], scalar=stat[:, 2:3], in1=dnw[:], op0=ALU.mult,
                    op1=ALU.mult), R=[b_odf, b_stat, b_dnw], W=[b_odst[oi2]])
            S.dma("sp", od_v[qc], od_st[oi2][:], b_odst[oi2], R=[b_odst[oi2]])

    if ext:
        return None
    S.barrier_wait("sp", b_olast + b_odst)
    return kb.done()


def _t5_bucket_np(rel):
    n = np.maximum(rel, 0)
    nf = np.maximum(n, 1).astype(np.float32)
    large = 16 + (np.log(nf / np.float32(16)) / np.float32(math.log(128 / 16)) * np.float32(16)).astype(np.int32)
    large = np.minimum(large, 31)
    return np.where(n < 16, n, large)


def _mixer_consts():
    p = np.arange(128)
    same = (p[:, None] // 64) == (p[None, :] // 64)
    ident = np.eye(128, dtype=np.float32)
    ones = np.ones((128, 128), np.float32)
    maskl = np.where(same & (p[:, None] > p[None, :]), -1.0, 0.0).astype(np.float32)
    masku = np.where(same & (p[:, None] <= p[None, :]), 1.0, 0.0).astype(np.float32)
    tri = masku.copy()
    sel63 = np.zeros((128, 128), np.float32)
    sel63[63, :] = 1.0
    sel127 = np.zeros((128, 128), np.float32)
    sel127[127, :] = 1.0
    return np.ascontiguousarray(np.concatenate([ident, ones, maskl, masku, tri, sel63, sel127], axis=1))


def mixer_inputs(xb, l, h, P):
    w_in = P["w_in"][l]
    cols = np.concatenate([
        np.arange(h * 128, (h + 1) * 128),
        512 + np.arange(h * 128, (h + 1) * 128),
        1024 + np.arange(h * 128, (h + 1) * 128),
        2056 + np.arange(h * 128, (h + 1) * 128),
        2568 + np.arange(h * 128, (h + 1) * 128),
        1536 + np.arange(h * 128, (h + 1) * 128),
        3080 + np.arange(h * 128, (h + 1) * 128),
        np.array([2048 + h]),
        np.array([2052 + h]),
    ])
    wh = np.ascontiguousarray(w_in[:, cols])
    anw = np.ascontiguousarray(P["attn_norm_w"][l].reshape(8, 128).T)
    cwl = P["conv_w"][l]
    cw = np.concatenate([cwl[:, g * 512 + h * 128: g * 512 + (h + 1) * 128].T for g in range(3)], axis=1)
    sc = np.zeros((128, 4), np.float32)
    sc[:, 0] = P["a_log"][l, h]
    sc[:, 1] = P["dt_bias"][l, h]
    sc[:, 2] = P["rel_bias"][31, h]
    lamv = np.concatenate([P["lambda_q1"][l], P["lambda_k1"][l], P["lambda_q2"][l], P["lambda_k2"][l]])
    lamv = np.broadcast_to(lamv[None, :], (128, 256))
    lnw = np.broadcast_to(P["la_norm_w"][l][None, :], (128, 128))
    dnw = np.broadcast_to(P["diff_norm_w"][l][None, :], (128, 128))
    kl = np.arange(128)[:, None]
    jj = np.arange(512)[None, :]
    rel = jj - 128 - kl
    bt = np.where(rel >= 0, P["rel_bias"][_t5_bucket_np(rel), h], np.float32(-30000.0)).astype(np.float32)
    c = np.ascontiguousarray
    return {"x": c(xb), "wh": wh, "anw": anw, "cw": c(cw.astype(np.float32)), "sc": sc,
            "lamv": c(lamv.astype(np.float32)), "lnw": c(lnw.astype(np.float32)),
            "dnw": c(dnw.astype(np.float32)), "btoep": c(bt), "ident_bf": _ident_bf(), "cf": _mixer_consts()}


CC_GROUPS = [[0, 1, 2, 3], [4, 5, 6, 7]]
_MIX_KEYS = ("wh", "anw", "cw", "sc", "lamv", "lnw", "dnw")


def build_fused(T):
    kb = KB()
    nc, S = kb.nc, kb.S
    NT = T // NH
    NTL = NT // 128
    I32 = mybir.dt.int32
    shp = {"wh": [D_MODEL, NW], "anw": [128, 8], "cw": [128, 12], "sc": [128, 4], "lamv": [128, 256],
           "lnw": [128, 128], "dnw": [128, 128]}
    x_d = kb.din("x", [T, D_MODEL], F32)
    xs_d = kb.din("xs", [NT, D_MODEL], F32)
    idx_d = kb.din("idx", [128, NH * NTL], I32)
    bt_d = kb.din("btoep", [128, 512], F32)
    idb_d = kb.din("ident_bf", [128, 128], BF16)
    cf_d = kb.din("cf", [128, 7 * 128], F32)
    fin_d = kb.din("final_w_bc", [128, D_MODEL], F32)
    lay = []
    for l in range(DEPTH):
        d = {k: kb.din("%s%d" % (k, l), shp[k], F32) for k in _MIX_KEYS}
        d["w_out"] = kb.din("w_out%d" % l, [D_MODEL, D_MODEL], F32)
        d["w_gu"] = kb.din("w_gu%d" % l, [D_MODEL, 2 * D_FF], F32)
        d["w_down"] = kb.din("w_down%d" % l, [D_FF, D_MODEL], F32)
        d["ffn_norm_w"] = kb.din("fnw%d" % l, [128, 8], F32)
        lay.append(d)
    y_d = kb.dout("y", [NT, D_MODEL], F32)
    og_in = kb.dint("og_in", [T, 256], BF16)
    og_all = kb.dint("og_all", [NH * T, 256], BF16)
    xs_in = kb.dint("xs_in", [NT, D_MODEL], F32)
    x1_all = kb.dint("x1_all", [T, D_MODEL], F32)

    pb = [kb.ps([128, 512], F32, "pb%d" % i) for i in range(8)]
    b_pb = S.bufs(8, "pb", excl=True)
    ORC = min(T, 2048)
    NKO = T // ORC
    XRC = min(NT, 256)
    NKX = NT // XRC
    b_og_all = S.bufs(NKO, "og_all")
    b_x1all = S.bufs(NKX, "x1_all")
    kb.persist = b_pb + b_og_all + b_x1all

    def allgather(src, dst, b_dst, name):
        S.dma_fn("pool", lambda e: e.collective_compute("AllGather", ALU.bypass, replica_groups=CC_GROUPS,
                                                         ins=[src], outs=[dst]),
                 S.buf(name), W=[b_dst], inc=None)

    def x1_tile(t):
        tok = t * 128
        r, w = divmod(tok, NT)
        k, ww = divmod(w, XRC)
        base = k * (NH * XRC) + r * XRC + ww
        return x1_all[base:base + 128, :]

    for l in range(DEPTH):
        lam_init = 0.8 - 0.6 * math.exp(-0.3 * l)
        kb.begin_phase()
        if l > 0:
            for k in range(NKX):
                allgather(xs_in[k * XRC:(k + 1) * XRC, :], x1_all[k * NH * XRC:(k + 1) * NH * XRC, :], b_x1all[k],
                          "ccx%d_%d" % (l, k))
        ext = {"kb": kb, "pb": pb, "b_pb": b_pb, "x": x_d, "x_tile": (None if l == 0 else x1_tile),
               "b_x": (None if l == 0 else b_x1all), "og": og_in,
               "btoep": bt_d, "ident_bf": idb_d, "cf": cf_d}
        for k in _MIX_KEYS:
            ext[k] = lay[l][k]
        build_mixer(T, lam_init, ext=ext)
        kb.end_phase()
        kb.begin_phase()
        for k in range(NKO):
            allgather(og_in[k * ORC:(k + 1) * ORC, :], og_all[k * NH * ORC:(k + 1) * NH * ORC, :], b_og_all[k],
                      "cco%d_%d" % (l, k))
        last = (l == DEPTH - 1)
        ext = {"kb": kb, "pb": pb, "b_pb": b_pb, "x": (xs_d if l == 0 else xs_in), "y": (y_d if last else xs_in),
               "og_all": og_all, "b_og_all": b_og_all, "idx": idx_d, "T": T,
               "w_out": lay[l]["w_out"], "w_gu": lay[l]["w_gu"], "w_down": lay[l]["w_down"],
               "ffn_norm_w": lay[l]["ffn_norm_w"], "ident_bf": idb_d, "final_w_bc": fin_d}
        build_ffn(NT, last, ext=ext)
        kb.end_phase()
    kb.es.close()
    return nc


_FUSED_CACHE = {}


def _gather_idx(T, h):
    NT = T // NH
    NTL = NT // 128
    ORC = min(T, 2048)
    tok = h * NT + np.arange(NTL)[None, :] * 128 + np.arange(128)[:, None]
    k, w = tok // ORC, tok % ORC
    cols = [k * (NH * ORC) + r * ORC + w for r in range(NH)]
    return np.ascontiguousarray(np.concatenate(cols, axis=1).astype(np.int32))


def fused_inputs(P, T):
    NT = T // NH
    NTL = NT // 128
    perm = np.concatenate([np.concatenate([np.arange(r * 128, (r + 1) * 128),
                                           512 + np.arange(r * 128, (r + 1) * 128)]) for r in range(NH)])
    in_maps = []
    for c in range(NCORES):
        b, h = divmod(c, NH)
        xb = P["x"][b, :T]
        m = {"x": np.ascontiguousarray(xb), "xs": np.ascontiguousarray(xb[h * NT:(h + 1) * NT]),
             "idx": _gather_idx(T, h),
             "final_w_bc": np.ascontiguousarray(np.broadcast_to(P["final_norm_w"][None, :], (128, D_MODEL)))}
        for l in range(DEPTH):
            mi = mixer_inputs(xb, l, h, P)
            for k in _MIX_KEYS:
                m["%s%d" % (k, l)] = mi[k]
            if l == 0:
                m["btoep"], m["ident_bf"], m["cf"] = mi["btoep"], mi["ident_bf"], mi["cf"]
            m["w_out%d" % l] = np.ascontiguousarray(P["w_out"][l][perm])
            m["w_gu%d" % l] = P["w_gate_up"][l]
            m["w_down%d" % l] = P["w_down"][l]
            m["fnw%d" % l] = np.ascontiguousarray(P["ffn_norm_w"][l].reshape(8, 128).T)
        in_maps.append(m)
    return in_maps


def kernel_fused(P, T):
    if T not in _FUSED_CACHE:
        _FUSED_CACHE[T] = build_fused(T)
    nc = _FUSED_CACHE[T]
    NT = T // NH
    res = run_bass_kernel_spmd(nc, fused_inputs(P, T), core_ids=list(range(NCORES)))
    out = np.empty((BATCH, T, D_MODEL), np.float32)
    for c in range(NCORES):
        b, h = divmod(c, NH)
        out[b, h * NT:(h + 1) * NT] = np.asarray(res.results[c]["y"])
    return out


_MIX_CACHE = {}


def kernel(**inputs):
    P = {k: np.ascontiguousarray(np.asarray(v, dtype=np.float32)) for k, v in inputs.items()}
    return kernel_fused(P, P["x"].shape[1])


def kernel_unfused(**inputs):
    P = {k: np.ascontiguousarray(np.asarray(v, dtype=np.float32)) for k, v in inputs.items()}
    x = P["x"]
    B, T, D = x.shape
    NTOK = B * T
    per = NTOK // NCORES
    for l in range(DEPTH):
        lam_init = 0.8 - 0.6 * math.exp(-0.3 * l)
        key = (T, l)
        if key not in _MIX_CACHE:
            _MIX_CACHE[key] = build_mixer(T, lam_init)
        nc = _MIX_CACHE[key]
        in_maps = [mixer_inputs(x[c // NH], l, c % NH, P) for c in range(NCORES)]
        res = run_bass_kernel_spmd(nc, in_maps, core_ids=list(range(NCORES)))
        o = np.empty((B, T, D), dtype=ml_dtypes.bfloat16)
        for c in range(NCORES):
            b, h = divmod(c, NH)
            o[b, :, h * 128:(h + 1) * 128] = np.asarray(res.results[c]["o_la"])
            o[b, :, 512 + h * 128:512 + (h + 1) * 128] = np.asarray(res.results[c]["o_d"])
        xs = x.reshape(NTOK, D)
        os_ = o.reshape(NTOK, D)
        ys = run_ffn([xs[c * per:(c + 1) * per] for c in range(NCORES)],
                     [os_[c * per:(c + 1) * per] for c in range(NCORES)],
                     P["w_out"][l], P["ffn_norm_w"][l], P["w_gate_up"][l], P["w_down"][l],
                     P["final_norm_w"] if l == DEPTH - 1 else None)
        x = np.concatenate([np.asarray(y) for y in ys], axis=0).reshape(B, T, D)
    return np.ascontiguousarray(x.astype(np.float32))
```

```python
import math
from contextlib import ExitStack

import numpy as np
import ml_dtypes
import concourse.bass as bass
import concourse.mybir as mybir
from concourse.bass_utils import run_bass_kernel_spmd

F32 = mybir.dt.float32
BF16 = mybir.dt.bfloat16
AF = mybir.ActivationFunctionType
ALU = mybir.AluOpType
AX = mybir.AxisListType

D_MODEL = 1024
SEQ = 8192
BATCH = 2
DEPTH = 2
NH = 4
D_FF = 2816
IN_DIM = 3592
NORM_EPS = 1e-6
NCORES = 8

ENGS = ("pe", "act", "dve", "pool", "sp")


class Buf:
    __slots__ = ("name", "w", "r", "dsem", "dcnt", "excl")

    def __init__(self, name, excl=False):
        self.name = name
        self.excl = excl
        self.w = None
        self.r = []
        self.dsem = None
        self.dcnt = 0


class Op:
    __slots__ = ("eng", "fn", "deps", "dma", "sem", "val", "needed", "inc")

    def __init__(self, eng, fn, dma=False):
        self.eng = eng
        self.fn = fn
        self.deps = []
        self.dma = dma
        self.sem = None
        self.val = 0
        self.needed = False
        self.inc = 16


class Sched:
    def __init__(self, nc, es):
        self.nc = nc
        self.es = es
        self.ops = {e: [] for e in ENGS}
        self.esem = {e: es.enter_context(nc.semaphore("s_" + e)) for e in ENGS}
        self.nbuf = 0
        self.cnt = {e: 0 for e in ENGS}
        self.phase_dmas = []
        self.nsem = 0
        self.defer = None

    def buf(self, name=None, excl=False):
        self.nbuf += 1
        return Buf(name or ("b%d" % self.nbuf), excl)

    def bufs(self, n, name="b", excl=False):
        return [self.buf("%s%d" % (name, i), excl) for i in range(n)]

    def _link(self, o, R, W):
        deps = []
        for b in R:
            if b.w is not None:
                d = b.w
                if d.dma or o.dma or d.eng != o.eng or o.eng != "pe":
                    deps.append(d)
            if b.excl:
                for d in b.r:
                    if d.eng != o.eng:
                        deps.append(d)
        for b in W:
            if b.w is not None:
                d = b.w
                if d.dma or o.dma or d.eng != o.eng or o.eng != "pe":
                    deps.append(d)
            for d in b.r:
                if d.dma or o.dma or d.eng != o.eng or o.eng != "pe":
                    deps.append(d)
        o.deps = deps
        for b in R:
            if b in W:
                continue
            if b.excl:
                b.r = []
            elif not o.dma:
                b.r = [x for x in b.r if x.dma or x.eng != o.eng]
            b.r.append(o)
        for b in W:
            b.w = o
            b.r = []

    def op(self, eng, fn, R=(), W=()):
        if self.defer is not None:
            self.defer.append(lambda: self.op_now(eng, fn, R, W))
            return None
        return self.op_now(eng, fn, R, W)

    def op_now(self, eng, fn, R=(), W=()):
        o = Op(eng, fn)
        self._link(o, R, W)
        self.ops[eng].append(o)
        return o

    def dma(self, q, out, in_, sb, R=(), W=()):
        return self.dma_fn(q, lambda e, out=out, in_=in_: e.dma_start(out=out, in_=in_), sb, R, W)

    def dma_fn(self, q, fn, sb, R=(), W=(), inc=16):
        if self.defer is not None:
            self.defer.append(lambda: self.dma_fn_now(q, fn, sb, R, W, inc))
            return None
        return self.dma_fn_now(q, fn, sb, R, W, inc)

    def dma_fn_now(self, q, fn, sb, R=(), W=(), inc=16):
        if sb.dsem is None:
            self.nsem += 1
            sb.dsem = self.es.enter_context(self.nc.semaphore("d%d_%s" % (self.nsem, sb.name)))
        o = Op(q, fn, dma=True)
        o.inc = inc
        sb.dcnt += (inc if inc else 1)
        o.sem = sb.dsem
        o.val = sb.dcnt
        self._link(o, R, W)
        self.ops[q].append(o)
        self.phase_dmas.append(o)
        return o

    def phase_barrier(self):
        lasts = []
        for e in ENGS:
            real = [o for o in self.ops[e] if o.fn is not None and not o.dma]
            if real:
                lasts.append(real[-1])
        deps = lasts + list(self.phase_dmas)
        for e in ENGS:
            o = Op(e, None)
            o.deps = [d for d in deps if d.dma or d.eng != e]
            self.ops[e].append(o)
        self.phase_dmas = []

    def barrier_wait(self, eng, R):
        o = Op(eng, None)
        self._link(o, (), R)
        self.ops[eng].append(o)
        return o

    def finalize(self):
        for e in ENGS:
            for o in self.ops[e]:
                for d in o.deps:
                    d.needed = True
        for e in ENGS:
            c = self.cnt[e]
            for o in self.ops[e]:
                if not o.dma and o.needed and o.fn is not None:
                    c += 1
                    o.sem = self.esem[e]
                    o.val = c
            self.cnt[e] = c
        ops = self.ops
        self.ops = {e: [] for e in ENGS}

        def run(eng, lst):
            seen = {}
            for o in lst:
                waits = {}
                for d in o.deps:
                    k = id(d.sem)
                    if k not in waits or waits[k][1] < d.val:
                        waits[k] = (d.sem, d.val)
                for k, (sem, val) in waits.items():
                    if seen.get(k, 0) < val:
                        eng.wait_ge(sem, val)
                        seen[k] = val
                if o.fn is None:
                    continue
                ins = o.fn(eng)
                if o.dma:
                    if o.inc:
                        ins.then_inc(o.sem, o.inc)
                    else:
                        ins.then_inc(o.sem)
                elif o.needed:
                    ins.then_inc(o.sem, 1)

        with self.nc.Block() as block:
            @block.tensor
            def _(e):
                run(e, ops["pe"])

            @block.scalar
            def _(e):
                run(e, ops["act"])

            @block.vector
            def _(e):
                run(e, ops["dve"])

            @block.gpsimd
            def _(e):
                run(e, ops["pool"])

            @block.sync
            def _(e):
                run(e, ops["sp"])


class KB:
    def __init__(self):
        self.nc = bass.Bass("TRN2", target_bir_lowering=False)
        self.es = ExitStack()
        self.S = Sched(self.nc, self.es)
        self.n = 0
        self.pes = None
        self.phase = 0

    def begin_phase(self):
        self.phase += 1
        self.pes = ExitStack()

    def end_phase(self):
        self.S.phase_barrier()
        self.S.finalize()
        self.pes.close()
        self.pes = None
        for b in getattr(self, "persist", []):
            b.w = None
            b.r = []

    def sb(self, shape, dt, name=None):
        self.n += 1
        st = self.pes if self.pes is not None else self.es
        return st.enter_context(self.nc.sbuf_tensor("sb%d_" % self.phase + (name or ("t%d" % self.n)), list(shape), dt))

    def dint(self, name, shape, dt):
        return self.nc.dram_tensor(name, list(shape), dt).ap()

    def ps(self, shape, dt, name=None):
        self.n += 1
        return self.es.enter_context(self.nc.psum_tensor("ps_" + (name or ("p%d" % self.n)), list(shape), dt))

    def din(self, name, shape, dt):
        return self.nc.dram_tensor(name, list(shape), dt, kind="ExternalInput").ap()

    def dout(self, name, shape, dt):
        return self.nc.dram_tensor(name, list(shape), dt, kind="ExternalOutput").ap()

    def done(self):
        self.S.finalize()
        self.es.close()
        return self.nc


def build_ffn(NT, final_norm, ext=None):
    kb = ext["kb"] if ext else KB()
    nc, S = kb.nc, kb.S
    NTL = NT // 128
    KD = D_MODEL // 128
    JF = D_FF // 128

    if ext:
        x_d, wout_d, wgu_d, wdn_d, fnw_d, idb_d, y_d = (
            ext[k] for k in ("x", "w_out", "w_gu", "w_down", "ffn_norm_w", "ident_bf", "y"))
        if final_norm:
            fin_d = ext["final_w_bc"]
        o_d = None
    else:
        x_d = kb.din("x", [NT, D_MODEL], F32)
        o_d = kb.din("o", [NT, D_MODEL], BF16)
        wout_d = kb.din("w_out", [D_MODEL, D_MODEL], F32)
        wgu_d = kb.din("w_gu", [D_MODEL, 2 * D_FF], F32)
        wdn_d = kb.din("w_down", [D_FF, D_MODEL], F32)
        fnw_d = kb.din("ffn_norm_w", [128, KD], F32)
        idb_d = kb.din("ident_bf", [128, 128], BF16)
        if final_norm:
            fin_d = kb.din("final_w_bc", [128, D_MODEL], F32)
        y_d = kb.dout("y", [NT, D_MODEL], F32)

    wout = kb.sb([128, KD, D_MODEL], BF16, "wout")
    wgu = kb.sb([128, KD, 2 * D_FF], BF16, "wgu")
    wdn = kb.sb([128, JF, D_MODEL], BF16, "wdn")
    fnw = kb.sb([128, KD], F32, "fnw")
    idb = kb.sb([128, 128], BF16, "idb")
    b_wout, b_wgu, b_wdn, b_fnw, b_idb = S.bufs(5, "wres")
    if final_norm:
        finw = kb.sb([128, D_MODEL], F32, "finw")
        b_finw = S.buf("finw")
        S.dma("sp", finw[:], fin_d, b_finw, W=[b_finw])
    S.dma("sp", fnw[:], fnw_d, b_fnw, W=[b_fnw])
    S.dma("sp", idb[:], idb_d, b_idb, W=[b_idb])

    STG = 1408
    NSTG = 3
    stg = [kb.sb([128, STG], F32, "stg%d" % i) for i in range(NSTG)]
    b_stg = S.bufs(NSTG, "stg")
    cnt = [0]
    cast_engs = ("dve", "pool")

    def load_cast(dst_ap, src_ap, n, wbuf, scale_ap=None):
        i = cnt[0] % NSTG
        q = "sp" if (cnt[0] % 2 == 0) else "act"
        S.dma(q, stg[i][:, 0:n], src_ap, b_stg[i], W=[b_stg[i]])
        ce = cast_engs[cnt[0] % 2]
        if scale_ap is None:
            S.op(ce, lambda e, d=dst_ap, s=stg[i][:, 0:n]: e.tensor_copy(out=d, in_=s),
                 R=[b_stg[i]], W=[wbuf])
        else:
            S.op(ce, lambda e, d=dst_ap, s=stg[i][:, 0:n], sc=scale_ap:
                 e.tensor_scalar(out=d, in0=s, scalar1=sc, scalar2=None, op0=ALU.mult),
                 R=[b_stg[i], b_fnw], W=[wbuf])
        cnt[0] += 1

    wout_v = wout_d.rearrange("(ko p) n -> p ko n", p=128)
    for ko in range(KD):
        load_cast(wout[:, ko, :], wout_v[:, ko, :], D_MODEL, b_wout)
    wgu_v = wgu_d.rearrange("(ko p) n -> p ko n", p=128)
    for ko in range(KD):
        for c in range(4):
            load_cast(wgu[:, ko, c * STG:(c + 1) * STG], wgu_v[:, ko, c * STG:(c + 1) * STG], STG,
                      b_wgu, scale_ap=fnw[:, ko:ko + 1])
    wdn_v = wdn_d.rearrange("(j p) n -> p j n", p=128)
    for j in range(JF):
        load_cast(wdn[:, j, :], wdn_v[:, j, :], D_MODEL, b_wdn)

    xin = [kb.sb([128, D_MODEL], F32, "xin%d" % i) for i in range(2)]
    oin = [kb.sb([128, D_MODEL], BF16, "oin%d" % i) for i in range(2)]
    b_xin = S.bufs(2, "xin")
    b_oin = S.bufs(2, "oin")
    tbuf = kb.sb([128, KD, 128], BF16, "tbuf")
    b_tbuf = S.buf("tbuf")
    hn = kb.sb([128, D_MODEL], BF16, "hn")
    b_hn = S.buf("hn")
    junk = kb.sb([128, D_MODEL], BF16, "junk")
    b_junk = S.buf("junk")
    aT = kb.sb([128, JF, 128], BF16, "aT")
    b_aT = S.buf("aT")
    sg = [kb.sb([128, 128], F32, "sg%d" % i) for i in range(2)]
    b_sg = S.bufs(2, "sg")
    stat = kb.sb([128, 8], F32, "stat")
    b_stat = S.buf("stat")
    epsc = kb.sb([128, 1], F32, "epsc")
    b_epsc = S.buf("epsc")
    S.op("dve", lambda e: e.memset(epsc[:], NORM_EPS), W=[b_epsc])

    if ext:
        pbank, b_pb = ext["pb"], ext["b_pb"]
        ptb_t = pbank[0][:].bitcast(BF16)
        idx_sb = kb.sb([128, 4 * NTL], mybir.dt.int32, "idx")
        b_idx = S.buf("idx")
        S.dma("sp", idx_sb[:], ext["idx"], b_idx, W=[b_idx])
        TT_ = ext["T"]
    else:
        pbank = [None] + [kb.ps([128, 512], F32, "pb%d" % i) for i in range(1, 8)]
        b_pb = S.bufs(8, "pb", excl=True)
        ptb_t = kb.ps([128, D_MODEL], BF16, "ptb")

    x_v = x_d.rearrange("(t p) d -> t p d", p=128)
    o_v = o_d.rearrange("(t p) d -> t p d", p=128) if o_d is not None else None
    y_v = y_d.rearrange("(t p) d -> t p d", p=128)

    def rms_scale(src, b_src, col):
        S.op("act", lambda e: e.activation(out=junk[:], in_=src, func=AF.Square,
                                           accum_out=stat[:, col:col + 1]),
             R=[b_src], W=[b_junk, b_stat])
        S.op("act", lambda e: e.activation(out=stat[:, col:col + 1], in_=stat[:, col:col + 1], func=AF.Sqrt,
                                           scale=1.0 / D_MODEL, bias=epsc[:, 0:1]),
             R=[b_stat, b_epsc], W=[b_stat])
        S.op("dve", lambda e: e.reciprocal(out=stat[:, col:col + 1], in_=stat[:, col:col + 1]),
             R=[b_stat], W=[b_stat])

    hT2 = [kb.sb([128, KD, 128], BF16, "hT2_%d" % i) for i in range(2)]
    b_hT2 = S.bufs(2, "hT2")
    aT2 = [aT, kb.sb([128, JF, 128], BF16, "aT_1")]
    b_aT2 = [b_aT, S.buf("aT_1")]
    ptb = ptb_t

    def stageA(t):
        i = t % 2
        xt, ot = xin[i], oin[i]
        S.dma("sp", xt[:], x_v[t], b_xin[i], W=[b_xin[i]])
        if ext:
            for r_ in range(4):
                S.dma_fn("pool", lambda e, ot=ot, r_=r_, t=t: e.indirect_dma_start(
                    out=ot[:, r_ * 256:(r_ + 1) * 256], out_offset=None,
                    in_=ext["og_all"],
                    in_offset=bass.IndirectOffsetOnAxis(ap=idx_sb[:, r_ * NTL + t:r_ * NTL + t + 1], axis=0)),
                    b_oin[i], R=[b_idx] + list(ext["b_og_all"]), W=[b_oin[i]])
        else:
            S.dma("pool", ot[:], o_v[t], b_oin[i], W=[b_oin[i]])

        def tr_group(e, src=ot):
            ins = None
            for k in range(KD):
                ins = e.transpose(out=ptb[:, k * 128:(k + 1) * 128], in_=src[:, k * 128:(k + 1) * 128],
                                  identity=idb[:])
            return ins
        S.op("pe", tr_group, R=[b_oin[i], b_idb], W=[b_pb[0]])
        S.op("act", lambda e: e.copy(out=tbuf[:].rearrange("p k t -> p (k t)"), in_=ptb[:, 0:KD * 128]),
             R=[b_pb[0]], W=[b_tbuf])
        for nchunk in range(2):
            bk = 1 + nchunk

            def mm_out(e, bk=bk, nchunk=nchunk):
                ins = None
                for k in range(KD):
                    ins = e.matmul(out=pbank[bk][:], lhsT=tbuf[:, k, :],
                                   rhs=wout[:, k, nchunk * 512:(nchunk + 1) * 512],
                                   start=(k == 0), stop=(k == KD - 1))
                return ins
            S.op("pe", mm_out, R=[b_tbuf, b_wout], W=[b_pb[bk]])
            S.op("dve", lambda e, bk=bk, nchunk=nchunk, xt=xt:
                 e.tensor_tensor(out=xt[:, nchunk * 512:(nchunk + 1) * 512],
                                 in0=pbank[bk][:], in1=xt[:, nchunk * 512:(nchunk + 1) * 512], op=ALU.add),
                 R=[b_pb[bk], b_xin[i]], W=[b_xin[i]])
        rms_scale(xt[:], b_xin[i], 0)
        S.op("act", lambda e, xt=xt: e.activation(out=hn[:], in_=xt[:], func=AF.Copy, scale=stat[:, 0:1]),
             R=[b_xin[i], b_stat], W=[b_hn])

        def tr_group2(e):
            ins = None
            for k in range(KD):
                ins = e.transpose(out=ptb[:, k * 128:(k + 1) * 128], in_=hn[:, k * 128:(k + 1) * 128],
                                  identity=idb[:])
            return ins
        S.op("pe", tr_group2, R=[b_hn, b_idb], W=[b_pb[0]])
        S.op("act", lambda e, i=i: e.copy(out=hT2[i][:].rearrange("p k t -> p (k t)"), in_=ptb[:, 0:KD * 128]),
             R=[b_pb[0]], W=[b_hT2[i]])

    def stageB(t):
        i = t % 2
        for j in range(JF):
            bk = 3 + (j % 2)

            def mm_gu(e, bk=bk, j=j, i=i):
                ins = None
                for half in range(2):
                    for k in range(KD):
                        c0 = half * D_FF + j * 128
                        ins = e.matmul(out=pbank[bk][:, half * 128:(half + 1) * 128],
                                       lhsT=wgu[:, k, c0:c0 + 128], rhs=hT2[i][:, k, :],
                                       start=(k == 0), stop=(k == KD - 1))
                return ins
            S.op("pe", mm_gu, R=[b_hT2[i], b_wgu], W=[b_pb[bk]])
            s_ = j % 2
            S.op("act", lambda e, bk=bk, s_=s_: e.activation(out=sg[s_][:], in_=pbank[bk][:, 0:128], func=AF.Silu),
                 R=[b_pb[bk]], W=[b_sg[s_]])
            S.op("dve", lambda e, bk=bk, s_=s_, j=j, i=i: e.tensor_tensor(out=aT2[i][:, j, :],
                                                                          in0=pbank[bk][:, 128:256],
                                                                          in1=sg[s_][:], op=ALU.mult),
                 R=[b_pb[bk], b_sg[s_]], W=[b_aT2[i]])

    def stageC(t):
        i = t % 2
        xt = xin[i]
        for nchunk in range(2):
            bk = 5 + nchunk

            def mm_dn(e, bk=bk, nchunk=nchunk, i=i):
                ins = None
                for j in range(JF):
                    ins = e.matmul(out=pbank[bk][:], lhsT=aT2[i][:, j, :],
                                   rhs=wdn[:, j, nchunk * 512:(nchunk + 1) * 512],
                                   start=(j == 0), stop=(j == JF - 1))
                return ins
            S.op("pe", mm_dn, R=[b_aT2[i], b_wdn], W=[b_pb[bk]])
            S.op("dve", lambda e, bk=bk, nchunk=nchunk, xt=xt:
                 e.tensor_tensor(out=xt[:, nchunk * 512:(nchunk + 1) * 512],
                                 in0=pbank[bk][:], in1=xt[:, nchunk * 512:(nchunk + 1) * 512], op=ALU.add),
                 R=[b_pb[bk], b_xin[i]], W=[b_xin[i]])
        if final_norm:
            rms_scale(xt[:], b_xin[i], 1)
            S.op("dve", lambda e, xt=xt: e.scalar_tensor_tensor(out=xt[:], in0=xt[:], scalar=stat[:, 1:2],
                                                                in1=finw[:], op0=ALU.mult, op1=ALU.mult),
                 R=[b_xin[i], b_stat, b_finw], W=[b_xin[i]])
        S.dma("sp", y_v[t], xt[:], b_xin[i], R=[b_xin[i]])

    stageA(0)
    for t in range(NTL):
        if t + 1 < NTL:
            stageA(t + 1)
        stageB(t)
        stageC(t)

    if ext:
        return None
    S.barrier_wait("sp", b_xin)
    return kb.done()


def _ident_bf():
    return np.eye(128, dtype=np.float32).astype(ml_dtypes.bfloat16)


def run_ffn(x_sl, o_sl, w_out, ffn_norm_w, w_gu, w_down, final_w, nc_cache={}):
    NT = x_sl[0].shape[0]
    key = (NT, final_w is not None)
    if key not in nc_cache:
        nc_cache[key] = build_ffn(NT, final_w is not None)
    nc = nc_cache[key]
    fnw = np.ascontiguousarray(ffn_norm_w.reshape(D_MODEL // 128, 128).T)
    in_maps = []
    for c in range(len(x_sl)):
        m = {"x": np.ascontiguousarray(x_sl[c]), "o": np.ascontiguousarray(o_sl[c]),
             "w_out": w_out, "w_gu": w_gu, "w_down": w_down, "ffn_norm_w": fnw,
             "ident_bf": _ident_bf()}
        if final_w is not None:
            m["final_w_bc"] = np.ascontiguousarray(np.broadcast_to(final_w[None, :], (128, D_MODEL)))
        in_maps.append(m)
    res = run_bass_kernel_spmd(nc, in_maps, core_ids=list(range(len(x_sl))))
    return [r["y"] for r in res.results]


NW = 898


NPULL = 3


def build_mixer(T, lam_init, dbg=99, ext=None):
    SKIP = ''
    kb = ext["kb"] if ext else KB()
    nc, S = kb.nc, kb.S
    NCH = T // 512
    NTL = T // 128
    KD = D_MODEL // 128

    if ext:
        x_d = ext["x"]
        wh_d, anw_d, cw_d, sc_d, lamv_d, lnw_d, dnw_d, bt_d, idb_d, cf_d = (
            ext[k] for k in ("wh", "anw", "cw", "sc", "lamv", "lnw", "dnw", "btoep", "ident_bf", "cf"))
        ola_d = ext["og"][:, 0:128]
        od_d = ext["og"][:, 128:256]
    else:
        x_d = kb.din("x", [T, D_MODEL], F32)
        wh_d = kb.din("wh", [D_MODEL, NW], F32)
        anw_d = kb.din("anw", [128, KD], F32)
        cw_d = kb.din("cw", [128, 12], F32)
        sc_d = kb.din("sc", [128, 4], F32)
        lamv_d = kb.din("lamv", [128, 4 * 64], F32)
        lnw_d = kb.din("lnw", [128, 128], F32)
        dnw_d = kb.din("dnw", [128, 128], F32)
        bt_d = kb.din("btoep", [128, 512], F32)
        idb_d = kb.din("ident_bf", [128, 128], BF16)
        cf_d = kb.din("cf", [128, 7 * 128], F32)
        ola_d = kb.dout("o_la", [T, 128], BF16)
        od_d = kb.dout("o_d", [T, 128], BF16)

    def T_(shape, dt, name):
        return kb.sb(shape, dt, name), S.buf(name)

    anw, b_anw = T_([128, KD], F32, "anw")
    cw, b_cw = T_([128, 12], F32, "cw")
    sc, b_sc = T_([128, 4], F32, "sc")
    lamv, b_lamv = T_([128, 256], F32, "lamv")
    lnw, b_lnw = T_([128, 128], F32, "lnw")
    dnw, b_dnw = T_([128, 128], F32, "dnw")
    bt, b_bt = T_([128, 512], F32, "bt")
    idb, b_idb = T_([128, 128], BF16, "idb")
    cf, b_cf = T_([128, 7 * 128], F32, "cf")
    for (t_, d_, b_) in ((anw, anw_d, b_anw), (cw, cw_d, b_cw), (sc, sc_d, b_sc), (lamv, lamv_d, b_lamv),
                         (lnw, lnw_d, b_lnw), (dnw, dnw_d, b_dnw), (bt, bt_d, b_bt), (idb, idb_d, b_idb),
                         (cf, cf_d, b_cf)):
        S.dma("sp", t_[:], d_, b_, W=[b_])
    IDF = cf[:, 0:128]
    ONES = cf[:, 128:256]
    MASKL = cf[:, 256:384]
    MASKU = cf[:, 384:512]
    TRI = cf[:, 512:640]
    SEL63 = cf[:, 640:768]
    SEL127 = cf[:, 768:896]

    cst_, b_cst = T_([128, 8], F32, "cst")
    S.op("dve", lambda e: e.memset(cst_[:, 0:1], 1.0), W=[b_cst])
    S.op("dve", lambda e: e.memset(cst_[:, 1:2], 1e-6), W=[b_cst])
    S.op("dve", lambda e: e.memset(cst_[:, 2:3], 1e-5), W=[b_cst])
    S.op("dve", lambda e: e.memset(cst_[:, 5:6], 0.0), W=[b_cst])
    C_ONE, C_EPS6, C_EPS5, C_NA, C_NLAM, C_ZERO = (cst_[:, i:i + 1] for i in range(6))
    S.op("act", lambda e: e.activation(out=cst_[:, 3:4], in_=sc[:, 0:1], func=AF.Exp), R=[b_sc], W=[b_cst])
    S.op("dve", lambda e: e.tensor_scalar(out=cst_[:, 3:4], in0=cst_[:, 3:4], scalar1=-1.0, scalar2=None,
                                          op0=ALU.mult), R=[b_cst], W=[b_cst])
    lt, b_lt = T_([128, 128], F32, "lamtmp")
    ls, b_ls = T_([128, 4], F32, "lamsum")
    S.op("dve", lambda e: e.tensor_tensor(out=lt[:, 0:64], in0=lamv[:, 0:64], in1=lamv[:, 64:128], op=ALU.mult),
         R=[b_lamv], W=[b_lt])
    S.op("dve", lambda e: e.tensor_tensor(out=lt[:, 64:128], in0=lamv[:, 128:192], in1=lamv[:, 192:256],
                                          op=ALU.mult), R=[b_lamv, b_lt], W=[b_lt])
    if 'r' not in SKIP:
        S.op("dve", lambda e: e.reduce_sum(out=ls[:, 0:1], in_=lt[:, 0:64], axis=AX.X), R=[b_lt], W=[b_ls])
        S.op("dve", lambda e: e.reduce_sum(out=ls[:, 1:2], in_=lt[:, 64:128], axis=AX.X), R=[b_lt, b_ls], W=[b_ls])
    S.op("act", lambda e: e.activation(out=ls[:, 2:4], in_=ls[:, 0:2], func=AF.Exp), R=[b_ls], W=[b_ls])
    S.op("dve", lambda e: e.scalar_tensor_tensor(out=cst_[:, 4:5], in0=ls[:, 3:4], scalar=float(-lam_init),
                                                 in1=ls[:, 2:3], op0=ALU.add, op1=ALU.subtract),
         R=[b_ls, b_cst], W=[b_cst])
    S.op("dve", lambda e: e.tensor_scalar(out=dnw[:], in0=dnw[:], scalar1=float(1.0 - lam_init), scalar2=None,
                                          op0=ALU.mult), R=[b_dnw], W=[b_dnw])

    Wb, b_Wb = T_([128, KD, 1024], BF16, "Wb")
    wst = [kb.sb([128, NW], F32, "wst%d" % i) for i in range(2)]
    b_wst = S.bufs(2, "wst")
    wh_v = wh_d.rearrange("(ko p) n -> p ko n", p=128)
    for ko in range(KD):
        i = ko % 2
        S.dma("sp", wst[i][:], wh_v[:, ko, :], b_wst[i], W=[b_wst[i]])
        S.op("dve" if i == 0 else "pool",
             lambda e, i=i, ko=ko: e.tensor_scalar(out=Wb[:, ko, 0:NW], in0=wst[i][:], scalar1=anw[:, ko:ko + 1],
                                                   scalar2=None, op0=ALU.mult),
             R=[b_wst[i], b_anw], W=[b_Wb])

    KdT, b_KdT = T_([128, T], BF16, "KdT")
    Vaug, b_Vaug = T_([128, NTL, 144], BF16, "Vaug")
    if 'v' not in SKIP:
        S.op("pool", lambda e: e.memset(Vaug[:, :, 128:129], 1.0), W=[b_Vaug])

    xt = [kb.sb([128, D_MODEL], F32, "xt%d" % i) for i in range(4)]
    b_xt = S.bufs(4, "xt")
    xn, b_xn = T_([128, D_MODEL], BF16, "xn")
    junk, b_junk = T_([128, D_MODEL], BF16, "junk")
    junkf, b_junkf = T_([128, 128], F32, "junkf")
    stat, b_stat = T_([128, 4], F32, "stat")
    hT, b_hT = T_([128, KD, 512], BF16, "hT")
    cstg = [kb.sb([128, 515], F32, "cstg%d" % g) for g in range(3)]
    b_cstg = S.bufs(3, "cstg")
    cacc = [kb.sb([128, 512], F32, "cacc%d" % g) for g in range(3)]
    b_cacc = S.bufs(3, "cacc")
    sil = [kb.sb([128, 512], F32, "sil%d" % g) for g in range(2)]
    b_sil = S.bufs(2, "sil")
    sq, b_sq = T_([128, 512], F32, "sq")
    rs, b_rs = T_([128, 512], F32, "rs")
    qnT, b_qnT = T_([128, 512], BF16, "qnT")
    knT, b_knT = T_([128, 512], BF16, "knT")
    vsT, b_vsT = T_([128, 512], BF16, "vsT")
    QdT, b_QdT = T_([128, 512], BF16, "QdT")
    qgT, b_qgT = T_([128, 512], BF16, "qgT")
    kvt, b_kvt = T_([128, 8, 128], BF16, "kvt")
    zs, b_zs = T_([128, 4, 128], BF16, "zs")
    ba, b_ba = T_([128, 4, 2], F32, "ba")
    for g in range(3):
        S.op("dve", lambda e, g=g: e.memset(cstg[g][:, 0:3], 0.0), W=[b_cstg[g]])
    pt = {}
    for nm in ("beta", "eb", "g", "gc", "egc", "bg", "glt", "ekd", "tmpa"):
        pt[nm] = T_([128, 4], F32, "pt_" + nm)
    egl, b_egl = T_([128, 8], F32, "egl")
    NS = 4
    gset = []
    for s_ in range(NS):
        d = {}
        for nm, shp, dt in (("dg", [128, 128], F32), ("Eb", [128, 128], F32), ("Ds", [128, 128], F32),
                            ("EA", [128, 128], F32), ("t1", [128, 128], F32), ("t2", [128, 128], F32),
                            ("MPa", [128, 256], F32), ("MPb", [128, 256], F32),
                            ("MTa", [128, 128], F32), ("MTb", [128, 128], F32),
                            ("TT", [128, 128], BF16), ("vb", [128, 128], BF16), ("kbg", [128, 128], BF16)):
            d[nm] = T_(shp, dt, "%s_%d" % (nm, s_))
        gset.append(d)
    u_sb, b_u = T_([128, 4, 128], F32, "u_sb")
    wT_sb, b_wT = T_([128, 512], BF16, "wT_sb")
    apT, b_apT = T_([128, 4, 128], BF16, "apT")
    kd_sb, b_kd = T_([128, 4, 128], BF16, "kd_sb")
    S_f, b_Sf = T_([128, 128], F32, "S_f")
    Sb, b_Sb = T_([128, 128], BF16, "Sb")
    vn, b_vn = T_([128, 128], BF16, "vn")
    o_sb = [kb.sb([128, 128], F32, "o_sb%d" % i) for i in range(2)]
    b_osb = S.bufs(2, "o_sb")
    on_, b_on = T_([128, 128], F32, "on")
    ola_st = [kb.sb([128, 4, 128], BF16, "ola_st%d" % i) for i in range(2)]
    b_olast = S.bufs(2, "ola_st")
    S.op("dve", lambda e: e.memset(S_f[:], 0.0), W=[b_Sf])
    S.op("dve", lambda e: e.memset(Sb[:], 0.0), W=[b_Sb])
    pT = [[kb.sb([128, 256], BF16, "pT%d%d" % (m, p)) for p in range(2)] for m in range(2)]
    b_pT = [[S.buf("pT%d%d" % (m, p)) for p in range(2)] for m in range(2)]
    s2 = [kb.sb([128, 256], F32, "s2_%d" % m) for m in range(2)]
    b_s2 = S.bufs(2, "s2")
    rden, b_rden = T_([128, 4], F32, "rden")
    O1, b_O1 = T_([128, 128], F32, "O1")
    odf, b_odf = T_([128, 128], F32, "odf")
    od_st = [kb.sb([128, 2, 128], BF16, "od_st%d" % i) for i in range(2)]
    b_odst = S.bufs(2, "od_st")

    if ext:
        pb, b_pb = ext["pb"], ext["b_pb"]
    else:
        pb = [kb.ps([128, 512], F32, "pb%d" % i) for i in range(8)]
        b_pb = S.bufs(8, "pb", excl=True)
    pb0_bf = pb[0][:].bitcast(BF16)

    x_v = x_d.rearrange("(t p) d -> t p d", p=128)
    ola_v = ola_d.rearrange("(c t p) e -> c p t e", p=128, t=4)
    od_v = od_d.rearrange("(c t p) e -> c p t e", p=128, t=2)

    def rstd_from_ss(col, n, epsc):
        S.op("act", lambda e: e.activation(out=stat[:, col:col + 1], in_=stat[:, col:col + 1], func=AF.Ln,
                                           scale=1.0 / n, bias=epsc), R=[b_stat, b_cst], W=[b_stat])
        S.op("act", lambda e: e.activation(out=stat[:, col:col + 1], in_=stat[:, col:col + 1], func=AF.Exp,
                                           scale=-0.5), R=[b_stat], W=[b_stat])

    def silu_via_exp(src_ap, R_src, tmp_ap, b_tmp, out_ap, W_out, mul_eng="dve"):
        S.op("act", lambda e: e.activation(out=tmp_ap, in_=src_ap, func=AF.Exp, scale=-1.0), R=R_src, W=[b_tmp])
        S.op("act", lambda e: e.activation(out=tmp_ap, in_=tmp_ap, func=AF.Ln, bias=C_ONE), R=[b_tmp, b_cst],
             W=[b_tmp])
        S.op("act", lambda e: e.activation(out=tmp_ap, in_=tmp_ap, func=AF.Exp, scale=-1.0), R=[b_tmp], W=[b_tmp])
        S.op(mul_eng, lambda e: e.tensor_tensor(out=out_ap, in0=src_ap, in1=tmp_ap, op=ALU.mult),
             R=list(R_src) + [b_tmp], W=W_out)

    stmp, b_stmp = T_([128, 512], F32, "stmp")
    zraw, b_zraw = T_([128, 4, 128], F32, "zraw")

    hT2 = [hT, kb.sb([128, KD, 512], BF16, "hT_b")]
    b_hT2 = [b_hT, S.buf("hT_b")]
    QdT2 = [QdT, kb.sb([128, 512], BF16, "QdT_b")]
    b_QdT2 = [b_QdT, S.buf("QdT_b")]
    zs2 = [zs, kb.sb([128, 4, 128], BF16, "zs_b")]
    b_zs2 = [b_zs, S.buf("zs_b")]
    pend = []

    def pull(n):
        for _ in range(min(n, len(pend))):
            pend.pop(0)()

    def front(j):
        if dbg <= 0:
            return
        for tt in range(4):
            t = 4 * j + tt
            i = tt
            S.dma("sp" if i % 2 == 0 else "pool", xt[i][:],
                  (ext["x_tile"](t) if (ext and ext.get("x_tile") is not None) else x_v[t]), b_xt[i], W=[b_xt[i]],
                  R=(list(ext["b_x"](t)) if (ext and ext.get("b_x") is not None) else []))
        for tt in range(4):
            t = 4 * j + tt
            i = tt
            S.op("act", lambda e, i=i: e.activation(out=junk[:], in_=xt[i][:], func=AF.Square,
                                                    accum_out=stat[:, 0:1]), R=[b_xt[i]], W=[b_junk, b_stat])
            rstd_from_ss(0, D_MODEL, C_EPS6)
            S.op("act", lambda e, i=i: e.activation(out=xn[:], in_=xt[i][:], func=AF.Copy, scale=stat[:, 0:1]),
                 R=[b_xt[i], b_stat], W=[b_xn])

            def trx(e):
                ins = None
                for k in range(KD):
                    ins = e.transpose(out=pb0_bf[:, k * 128:(k + 1) * 128], in_=xn[:, k * 128:(k + 1) * 128],
                                      identity=idb[:])
                return ins
            S.op("pe", trx, R=[b_xn, b_idb], W=[b_pb[0]])
            S.op("dve", lambda e, tt=tt: e.tensor_copy(out=hT2[j % 2][:, :, tt * 128:(tt + 1) * 128],
                                                        in_=pb0_bf[:, 0:1024].rearrange("p (k t) -> p k t", k=KD)),
                 R=[b_pb[0]], W=[b_hT2[j % 2]])
        if dbg <= 1:
            return
        for g in range(5 if 'f' not in SKIP else 0):
            bk = 1 + (g % 2)

            def mmf(e, g=g, bk=bk):
                ins = None
                for k in range(KD):
                    ins = e.matmul(out=pb[bk][:], lhsT=Wb[:, k, g * 128:(g + 1) * 128], rhs=hT2[j % 2][:, k, :],
                                   start=(k == 0), stop=(k == KD - 1))
                return ins
            S.op("pe", mmf, R=[b_Wb, b_hT2[j % 2]], W=[b_pb[bk]])
            if g < 3:
                S.op("act", lambda e, g=g, bk=bk: e.copy(out=cstg[g][:, 3:515], in_=pb[bk][:]),
                     R=[b_pb[bk]], W=[b_cstg[g]])
            elif g == 3:
                S.op("act", lambda e, bk=bk: e.copy(out=QdT2[j % 2][:], in_=pb[bk][:]), R=[b_pb[bk]], W=[b_QdT2[j % 2]])
            else:
                S.op("act", lambda e, bk=bk, j=j: e.copy(out=KdT[:, j * 512:(j + 1) * 512], in_=pb[bk][:]),
                     R=[b_pb[bk]], W=[b_KdT])
        for tt in range(4 if 't' not in SKIP else 0):
            t = 4 * j + tt

            def mmt(e, tt=tt):
                ins = None
                for k in range(KD):
                    NN = 256 if 'n' in SKIP else 258
                    ins = e.matmul(out=pb[3][:, 0:NN], lhsT=hT2[j % 2][:, k, tt * 128:(tt + 1) * 128],
                                   rhs=Wb[:, k, 640:640 + NN], start=(k == 0), stop=(k == KD - 1))
                return ins
            S.op("pe", mmt, R=[b_Wb, b_hT2[j % 2]], W=[b_pb[3]])
            S.op("dve", lambda e, tt=tt: e.tensor_copy(out=zraw[:, tt, :], in_=pb[3][:, 0:128]),
                 R=[b_pb[3]], W=[b_zraw])
            S.op("dve", lambda e, t=t: e.tensor_copy(out=Vaug[:, t, 0:128], in_=pb[3][:, 128:256]),
                 R=[b_pb[3]], W=[b_Vaug])
            S.op("dve", lambda e, tt=tt: e.tensor_copy(out=ba[:, tt, :], in_=pb[3][:, 256:258]),
                 R=[b_pb[3]], W=[b_ba])
        silu_via_exp(zraw[:].rearrange("p a d -> p (a d)"), [b_zraw], stmp[:], b_stmp,
                     zs2[j % 2][:].rearrange("p a d -> p (a d)"), [b_zs2[j % 2]], mul_eng="pool")

    front(0)
    for j in range(NCH):
        if dbg <= 2:
            continue
        pend.clear()
        if j + 1 < NCH:
            S.defer = pend
            front(j + 1)
            S.defer = None
            pull(4)
        for g in range(3):
            ce = "dve"
            S.op(ce, lambda e, g=g: e.tensor_scalar(out=cacc[g][:], in0=cstg[g][:, 3:515],
                                                    scalar1=cw[:, g * 4 + 3:g * 4 + 4], scalar2=None, op0=ALU.mult),
                 R=[b_cstg[g], b_cw], W=[b_cacc[g]])
            for tap in (2, 1, 0):
                S.op(ce, lambda e, g=g, tap=tap: e.scalar_tensor_tensor(
                    out=cacc[g][:], in0=cstg[g][:, tap:tap + 512], scalar=cw[:, g * 4 + tap:g * 4 + tap + 1],
                    in1=cacc[g][:], op0=ALU.mult, op1=ALU.add),
                    R=[b_cstg[g], b_cw, b_cacc[g]], W=[b_cacc[g]])
            S.op(ce, lambda e, g=g: e.tensor_copy(out=cstg[g][:, 0:3], in_=cstg[g][:, 512:515]),
                 R=[b_cstg[g]], W=[b_cstg[g]])
            if g < 2:
                silu_via_exp(cacc[g][:], [b_cacc[g]], stmp[:], b_stmp, sil[g][:], [b_sil[g]])
            else:
                silu_via_exp(cacc[g][:], [b_cacc[g]], stmp[:], b_stmp, vsT[:], [b_vsT])
        if dbg <= 3:
            continue
        for g in range(2):
            S.op("pool", lambda e, g=g: e.tensor_tensor(out=sq[:], in0=sil[g][:], in1=sil[g][:], op=ALU.mult),
                 R=[b_sil[g]], W=[b_sq])
            bk = 1 + g
            S.op("pe", lambda e, bk=bk: e.matmul(out=pb[bk][:], lhsT=ONES, rhs=sq[:], start=True, stop=True),
                 R=[b_sq, b_cf], W=[b_pb[bk]])
            S.op("act", lambda e, bk=bk: e.activation(out=rs[:], in_=pb[bk][:], func=AF.Ln, bias=C_EPS6),
                 R=[b_pb[bk], b_cst], W=[b_rs])
            S.op("act", lambda e: e.activation(out=rs[:], in_=rs[:], func=AF.Exp, scale=-0.5), R=[b_rs], W=[b_rs])
            if g == 0:
                S.op("dve", lambda e: e.scalar_tensor_tensor(out=qnT[:], in0=sil[0][:], scalar=float(128 ** -0.5),
                                                             in1=rs[:], op0=ALU.mult, op1=ALU.mult),
                     R=[b_sil[0], b_rs], W=[b_qnT])
            else:
                S.op("dve", lambda e: e.tensor_tensor(out=knT[:], in0=sil[1][:], in1=rs[:], op=ALU.mult),
                     R=[b_sil[1], b_rs], W=[b_knT])
        if dbg <= 4:
            continue
        def trkv(e):
            ins = None
            for tt in range(4):
                ins = e.transpose(out=pb0_bf[:, (2 * tt) * 128:(2 * tt + 1) * 128],
                                  in_=knT[:, tt * 128:(tt + 1) * 128], identity=idb[:])
                ins = e.transpose(out=pb0_bf[:, (2 * tt + 1) * 128:(2 * tt + 2) * 128],
                                  in_=vsT[:, tt * 128:(tt + 1) * 128], identity=idb[:])
            return ins
        S.op("pe", trkv, R=[b_knT, b_vsT, b_idb], W=[b_pb[0]])
        S.op("dve", lambda e: e.tensor_copy(out=kvt[:].rearrange("p a d -> p (a d)"), in_=pb0_bf[:, 0:1024]),
             R=[b_pb[0]], W=[b_kvt])
        if dbg <= 5:
            continue
        P = lambda nm: pt[nm][0]
        B = lambda nm: pt[nm][1]
        S.op("act", lambda e: e.activation(out=P("eb")[:], in_=ba[:, :, 0], func=AF.Exp, scale=-1.0),
             R=[b_ba], W=[B("eb")])
        S.op("dve", lambda e: e.tensor_scalar(out=P("eb")[:], in0=P("eb")[:], scalar1=1.0, scalar2=None,
                                              op0=ALU.add), R=[B("eb")], W=[B("eb")])
        S.op("dve", lambda e: e.reciprocal(out=P("beta")[:], in_=P("eb")[:]), R=[B("eb")], W=[B("beta")])
        S.op("act", lambda e: e.activation(out=P("tmpa")[:], in_=ba[:, :, 1], func=AF.Exp, bias=sc[:, 1:2]),
             R=[b_ba, b_sc], W=[B("tmpa")])
        S.op("act", lambda e: e.activation(out=P("tmpa")[:], in_=P("tmpa")[:], func=AF.Ln, bias=C_ONE),
             R=[B("tmpa"), b_cst], W=[B("tmpa")])
        S.op("dve", lambda e: e.tensor_scalar(out=P("g")[:], in0=P("tmpa")[:], scalar1=C_NA, scalar2=None,
                                              op0=ALU.mult), R=[B("tmpa"), b_cst], W=[B("g")])
        S.op("pe", lambda e: e.matmul(out=pb[3][:, 0:4], lhsT=TRI, rhs=P("g")[:], start=True, stop=True),
             R=[B("g"), b_cf], W=[b_pb[3]])
        S.op("act", lambda e: e.copy(out=P("gc")[:], in_=pb[3][:, 0:4]), R=[b_pb[3]], W=[B("gc")])

        def mmgl(e):
            e.matmul(out=pb[3][:, 0:4], lhsT=SEL63, rhs=P("gc")[:], start=True, stop=True)
            return e.matmul(out=pb[3][:, 4:8], lhsT=SEL127, rhs=P("gc")[:], start=True, stop=True)
        S.op("pe", mmgl, R=[B("gc"), b_cf], W=[b_pb[3]])
        eglv = egl[:].rearrange("p (t h) -> p t h", h=2)
        S.op("act", lambda e: e.activation(out=eglv[:, :, 0], in_=pb[3][:, 0:4], func=AF.Exp),
             R=[b_pb[3]], W=[b_egl])
        S.op("act", lambda e: e.activation(out=eglv[:, :, 1], in_=pb[3][:, 4:8], func=AF.Exp),
             R=[b_pb[3], b_egl], W=[b_egl])
        S.op("dve", lambda e: e.tensor_copy(out=P("glt")[0:64, :], in_=pb[3][0:64, 0:4]),
             R=[b_pb[3]], W=[B("glt")])
        S.op("dve", lambda e: e.tensor_copy(out=P("glt")[64:128, :], in_=pb[3][64:128, 4:8]),
             R=[b_pb[3], B("glt")], W=[B("glt")])
        S.op("dve", lambda e: e.tensor_tensor(out=P("ekd")[:], in0=P("glt")[:], in1=P("gc")[:], op=ALU.subtract),
             R=[B("glt"), B("gc")], W=[B("ekd")])
        S.op("act", lambda e: e.activation(out=P("ekd")[:], in_=P("ekd")[:], func=AF.Exp),
             R=[B("ekd")], W=[B("ekd")])
        S.op("act", lambda e: e.activation(out=P("egc")[:], in_=P("gc")[:], func=AF.Exp),
             R=[B("gc")], W=[B("egc")])
        S.op("dve", lambda e: e.tensor_tensor(out=P("bg")[:], in0=P("beta")[:], in1=P("egc")[:], op=ALU.mult),
             R=[B("beta"), B("egc")], W=[B("bg")])

        if dbg <= 6:
            continue
        def tile_gen(tt):
            gs = gset[tt % NS]
            bk = 4 + tt
            G = lambda nm, gs=gs: gs[nm][0]
            GB = lambda nm, gs=gs: gs[nm][1]
            csl = slice(tt * 128, (tt + 1) * 128)
            S.op("dve", lambda e, G=G, tt=tt: e.tensor_scalar(out=G("dg")[:], in0=IDF, scalar1=P("gc")[:, tt:tt + 1],
                                                                scalar2=None, op0=ALU.mult),
                 R=[b_cf, B("gc")], W=[GB("dg")])

            def mm_abb(e, G=G, bk=bk, csl=csl):
                e.matmul(out=pb[bk][:, 0:128], lhsT=ONES, rhs=G("dg")[:], start=True, stop=True)
                e.matmul(out=pb[bk][:, 128:256], lhsT=knT[:, csl], rhs=knT[:, csl], start=True, stop=True)
                return e.matmul(out=pb[bk][:, 256:384], lhsT=knT[:, csl], rhs=qnT[:, csl], start=True, stop=True)
            yield
            S.op("pe", mm_abb, R=[b_cf, GB("dg"), b_knT, b_qnT], W=[b_pb[bk]])
            S.op("dve", lambda e, G=G, bk=bk, tt=tt: e.tensor_scalar(
                out=G("Eb")[:], in0=pb[bk][:, 0:128], scalar1=P("gc")[:, tt:tt + 1], scalar2=None,
                op0=ALU.subtract), R=[b_pb[bk], B("gc")], W=[GB("Eb")])
            S.op("act", lambda e, G=G: e.activation(out=G("Eb")[:], in_=G("Eb")[:], func=AF.Abs),
                 R=[GB("Eb")], W=[GB("Eb")])
            S.op("act", lambda e, G=G: e.activation(out=G("Ds")[:], in_=G("Eb")[:], func=AF.Exp, scale=-1.0),
                 R=[GB("Eb")], W=[GB("Ds")])
            S.op("act", lambda e, G=G, bk=bk: e.activation(out=G("EA")[:], in_=pb[bk][:, 0:128], func=AF.Exp),
                 R=[b_pb[bk]], W=[GB("EA")])
            S.op("dve", lambda e, G=G, bk=bk, tt=tt: e.scalar_tensor_tensor(
                out=G("t1")[:], in0=pb[bk][:, 128:256], scalar=P("beta")[:, tt:tt + 1], in1=G("Ds")[:],
                op0=ALU.mult, op1=ALU.mult), R=[b_pb[bk], B("beta"), GB("Ds")], W=[GB("t1")])
            S.op("dve", lambda e, G=G: e.tensor_tensor(out=G("MTa")[:], in0=G("t1")[:], in1=MASKL, op=ALU.mult),
                 R=[GB("t1"), b_cf], W=[GB("MTa")])
            S.op("dve", lambda e, G=G, bk=bk: e.tensor_tensor(out=G("t2")[:], in0=pb[bk][:, 256:384], in1=G("Ds")[:],
                                                              op=ALU.mult), R=[b_pb[bk], GB("Ds")], W=[GB("t2")])
            S.op("pool", lambda e, G=G, tt=tt: e.tensor_tensor(out=apT[:, tt, :], in0=G("t2")[:], in1=MASKU,
                                                               op=ALU.mult), R=[GB("t2"), b_cf], W=[b_apT])
            S.op("pool", lambda e, G=G, csl=csl: e.tensor_tensor(out=qgT[:, csl], in0=qnT[:, csl], in1=G("EA")[:],
                                                                 op=ALU.mult), R=[b_qnT, GB("EA")], W=[b_qgT])
            yield
            S.op("pe", lambda e, G=G, bk=bk: e.transpose(out=pb[bk][:, 384:512], in_=G("MTa")[:], identity=IDF),
                 R=[GB("MTa"), b_cf], W=[b_pb[bk]])
            S.op("act", lambda e, G=G, bk=bk: e.copy(out=G("MPa")[:, 0:128], in_=pb[bk][:, 384:512]),
                 R=[b_pb[bk]], W=[GB("MPa")])
            S.op("dve", lambda e, G=G, bk=bk: e.tensor_tensor(out=G("MPb")[:, 128:256], in0=pb[bk][:, 384:512],
                                                              in1=IDF, op=ALU.add),
                 R=[b_pb[bk], b_cf], W=[GB("MPb")])

            def st0(e, G=G, bk=bk):
                e.matmul(out=pb[bk][:, 0:128], lhsT=G("MTa")[:], rhs=G("MPa")[:, 0:128], start=True, stop=True)
                return e.matmul(out=pb[bk][:, 128:256], lhsT=G("MPa")[:, 0:128], rhs=G("MTa")[:], start=True,
                                stop=True)
            yield
            S.op("pe", st0, R=[GB("MTa"), GB("MPa")], W=[b_pb[bk]])
            S.op("act", lambda e, G=G, bk=bk: e.copy(out=G("MPb")[:, 0:128], in_=pb[bk][:, 0:128]),
                 R=[b_pb[bk]], W=[GB("MPb")])
            S.op("dve", lambda e, G=G, bk=bk: e.tensor_copy(out=G("MTb")[:], in_=pb[bk][:, 128:256]),
                 R=[b_pb[bk]], W=[GB("MTb")])
            cur, nxt = ("MPb", "MTb"), ("MPa", "MTa")
            for stp in range(1, 5):
                def stj(e, G=G, bk=bk, cur=cur):
                    e.matmul(out=pb[bk][:, 0:256], lhsT=G(cur[1])[:], rhs=G(cur[0])[:, 0:256], start=True, stop=True)
                    return e.matmul(out=pb[bk][:, 256:384], lhsT=G(cur[0])[:, 0:128], rhs=G(cur[1])[:], start=True,
                                    stop=True)
                yield
                S.op("pe", stj, R=[GB(cur[0]), GB(cur[1])], W=[b_pb[bk]])
                S.op("act", lambda e, G=G, bk=bk, nxt=nxt: e.copy(out=G(nxt[0])[:, 0:128], in_=pb[bk][:, 0:128]),
                     R=[b_pb[bk]], W=[GB(nxt[0])])
                S.op("dve", lambda e, G=G, bk=bk, cur=cur, nxt=nxt: e.tensor_tensor(
                    out=G(nxt[0])[:, 128:256], in0=pb[bk][:, 128:256], in1=G(cur[0])[:, 128:256], op=ALU.add),
                    R=[b_pb[bk], GB(cur[0])], W=[GB(nxt[0])])
                S.op("act", lambda e, G=G, bk=bk, nxt=nxt: e.copy(out=G(nxt[1])[:], in_=pb[bk][:, 256:384]),
                     R=[b_pb[bk]], W=[GB(nxt[1])])
                cur, nxt = nxt, cur
            yield
            S.op("pe", lambda e, G=G, bk=bk, cur=cur: e.matmul(out=pb[bk][:, 0:128], lhsT=G(cur[1])[:],
                                                               rhs=G(cur[0])[:, 128:256], start=True, stop=True),
                 R=[GB(cur[0]), GB(cur[1])], W=[b_pb[bk]])
            S.op("dve", lambda e, G=G, bk=bk, cur=cur: e.tensor_tensor(out=G("TT")[:], in0=pb[bk][:, 0:128],
                                                                       in1=G(cur[0])[:, 128:256], op=ALU.add),
                 R=[b_pb[bk], GB(cur[0])], W=[GB("TT")])
            S.op("pool", lambda e, G=G, tt=tt: e.tensor_scalar(out=G("vb")[:], in0=kvt[:, 2 * tt + 1, :],
                                                               scalar1=P("beta")[:, tt:tt + 1], scalar2=None,
                                                               op0=ALU.mult), R=[b_kvt, B("beta")], W=[GB("vb")])
            S.op("pool", lambda e, G=G, tt=tt: e.tensor_scalar(out=G("kbg")[:], in0=kvt[:, 2 * tt, :],
                                                               scalar1=P("bg")[:, tt:tt + 1], scalar2=None,
                                                               op0=ALU.mult), R=[b_kvt, B("bg")], W=[GB("kbg")])
            S.op("pool", lambda e, tt=tt: e.tensor_scalar(out=kd_sb[:, tt, :], in0=kvt[:, 2 * tt, :],
                                                          scalar1=P("ekd")[:, tt:tt + 1], scalar2=None,
                                                          op0=ALU.mult), R=[b_kvt, B("ekd")], W=[b_kd])

            def mm_uw(e, G=G, bk=bk):
                e.matmul(out=pb[bk][:, 0:128], lhsT=G("TT")[:], rhs=G("vb")[:], start=True, stop=True)
                return e.matmul(out=pb[bk][:, 128:256], lhsT=G("kbg")[:], rhs=G("TT")[:], start=True, stop=True)
            yield
            S.op("pe", mm_uw, R=[GB("TT"), GB("vb"), GB("kbg")], W=[b_pb[bk]])
            S.op("act", lambda e, bk=bk, tt=tt: e.copy(out=u_sb[:, tt, :], in_=pb[bk][:, 0:128]),
                 R=[b_pb[bk]], W=[b_u])
            S.op("act", lambda e, bk=bk, csl=csl: e.copy(out=wT_sb[:, csl], in_=pb[bk][:, 128:256]),
                 R=[b_pb[bk]], W=[b_wT])

        gens = [tile_gen(tt) for tt in range(4)]
        while gens:
            for g_ in gens[:]:
                try:
                    next(g_)
                except StopIteration:
                    gens.remove(g_)
            pull(NPULL)
        if dbg <= 7:
            continue
        oi = j % 2
        for tt in range(4):
            csl = slice(tt * 128, (tt + 1) * 128)
            osb, b_o = o_sb[tt % 2], b_osb[tt % 2]
            for hh in range(2):
                r = slice(hh * 64, hh * 64 + 64)
                nl = 2 * tt + hh

                def mm1(e, csl=csl):
                    e.matmul(out=pb[6][:, 0:128], lhsT=wT_sb[:, csl], rhs=Sb[:], start=True, stop=True)
                    return e.matmul(out=pb[7][:, 0:128], lhsT=qgT[:, csl], rhs=Sb[:], start=True, stop=False)
                S.op("pe", mm1, R=[b_wT, b_qgT, b_Sb], W=[b_pb[6], b_pb[7]])
                S.op("dve", lambda e, r=r, tt=tt: e.tensor_tensor(out=vn[r, :], in0=u_sb[r, tt, :],
                                                                  in1=pb[6][r, 0:128], op=ALU.subtract),
                     R=[b_u, b_pb[6]], W=[b_vn])

                def mm2(e, r=r, tt=tt):
                    e.matmul(out=pb[7][:, 0:128], lhsT=apT[r, tt, :], rhs=vn[r, :], start=False, stop=True)
                    return e.matmul(out=pb[7][:, 128:256], lhsT=kd_sb[r, tt, :], rhs=vn[r, :], start=True, stop=True)
                S.op("pe", mm2, R=[b_apT, b_kd, b_vn], W=[b_pb[7]])
                S.op("act", lambda e, r=r, osb=osb: e.copy(out=osb[r, :], in_=pb[7][r, 0:128]),
                     R=[b_pb[7]], W=[b_o])
                S.op("dve", lambda e, nl=nl: e.scalar_tensor_tensor(out=S_f[:], in0=S_f[:], scalar=egl[:, nl:nl + 1],
                                                                    in1=pb[7][:, 128:256], op0=ALU.mult,
                                                                    op1=ALU.add),
                     R=[b_Sf, b_egl, b_pb[7]], W=[b_Sf])
                S.op("act", lambda e: e.copy(out=Sb[:], in_=S_f[:]), R=[b_Sf], W=[b_Sb])
                pull(NPULL)
            S.op("act", lambda e, osb=osb: e.activation(out=junkf[:], in_=osb[:], func=AF.Square,
                                                        accum_out=stat[:, 1:2]), R=[b_o], W=[b_junkf, b_stat])
            rstd_from_ss(1, 128, C_EPS6)
            S.op("dve", lambda e, osb=osb: e.scalar_tensor_tensor(out=on_[:], in0=osb[:], scalar=stat[:, 1:2],
                                                                  in1=lnw[:], op0=ALU.mult, op1=ALU.mult),
                 R=[b_o, b_stat, b_lnw], W=[b_on])
            S.op("dve", lambda e, tt=tt, oi=oi, zz=zs2[j % 2]: e.tensor_tensor(out=ola_st[oi][:, tt, :], in0=on_[:], in1=zz[:, tt, :],
                                                                op=ALU.mult), R=[b_on, b_zs2[j % 2]], W=[b_olast[oi]])
        S.dma("sp", ola_v[j], ola_st[oi][:], b_olast[oi], R=[b_olast[oi]])

        if dbg <= 8:
            continue
        pull(len(pend))
        for qq in range(2):
            qc = 2 * j + qq
            q0 = qc * 256
            qsl = slice(qq * 256, qq * 256 + 256)
            oi2 = qc % 2
            nkt = 2 * qc + 2
            def emit_qk(kt, qsl=qsl, qd=QdT2[j % 2], bq=b_QdT2[j % 2]):
                k0 = kt * 128
                par = kt % 2
                for m in range(2):
                    bk = 4 + 2 * m + par
                    ms = slice(m * 64, m * 64 + 64)
                    S.op("pe", lambda e, bk=bk, ms=ms, k0=k0, qsl=qsl, qd=qd: e.matmul(
                        out=pb[bk][:, 0:256], lhsT=KdT[ms, k0:k0 + 128], rhs=qd[ms, qsl], start=True, stop=True),
                        R=[b_KdT, bq], W=[b_pb[bk]])

            def emit_exp(kt, q0=q0):
                k0 = kt * 128
                d = q0 - k0
                far = d >= 256
                par = kt % 2
                for m in range(2):
                    bk = 4 + 2 * m + par
                    if far:
                        S.op("act", lambda e, bk=bk, m=m, par=par: e.activation(
                            out=pT[m][par][:], in_=pb[bk][:, 0:256], func=AF.Exp, scale=0.125, bias=sc[:, 2:3]),
                            R=[b_pb[bk], b_sc], W=[b_pT[m][par]])
                    else:
                        S.op("dve", lambda e, bk=bk, m=m, d=d: e.scalar_tensor_tensor(
                            out=s2[m][:], in0=pb[bk][:, 0:256], scalar=0.125, in1=bt[:, d + 128:d + 128 + 256],
                            op0=ALU.mult, op1=ALU.add), R=[b_pb[bk], b_bt], W=[b_s2[m]])
                        S.op("act", lambda e, m=m, par=par: e.activation(out=pT[m][par][:], in_=s2[m][:],
                                                                         func=AF.Exp),
                             R=[b_s2[m]], W=[b_pT[m][par]])

            def emit_pv(kt, qc=qc):
                par = kt % 2
                for m in range(2):
                    for qb in range(2):
                        klast = 2 * qc + qb
                        if kt > klast:
                            continue
                        ab = qb * 2 + m
                        S.op("pe", lambda e, ab=ab, m=m, par=par, qb=qb, kt=kt, klast=klast: e.matmul(
                            out=pb[ab][:, 0:129], lhsT=pT[m][par][:, qb * 128:(qb + 1) * 128],
                            rhs=Vaug[:, kt, 0:129], start=(kt == 0), stop=(kt == klast)),
                            R=[b_pT[m][par], b_Vaug], W=[b_pb[ab]])

            emit_qk(0)
            for kt in range(nkt):
                if kt + 1 < nkt:
                    emit_qk(kt + 1)
                emit_exp(kt)
                emit_pv(kt)
            for qb in range(2):
                a1, a2 = qb * 2, qb * 2 + 1
                S.op("dve", lambda e, a1=a1: e.reciprocal(out=rden[:, 0:1], in_=pb[a1][:, 128:129]),
                     R=[b_pb[a1]], W=[b_rden])
                S.op("dve", lambda e, a2=a2: e.reciprocal(out=rden[:, 1:2], in_=pb[a2][:, 128:129]),
                     R=[b_pb[a2], b_rden], W=[b_rden])
                S.op("dve", lambda e: e.tensor_scalar(out=rden[:, 2:3], in0=rden[:, 1:2], scalar1=C_NLAM,
                                                      scalar2=None, op0=ALU.mult), R=[b_rden, b_cst], W=[b_rden])
                S.op("act", lambda e, a1=a1: e.activation(out=O1[:], in_=pb[a1][:, 0:128], func=AF.Copy,
                                                          scale=rden[:, 0:1]), R=[b_pb[a1], b_rden], W=[b_O1])
                S.op("dve", lambda e, a2=a2: e.scalar_tensor_tensor(out=odf[:], in0=pb[a2][:, 0:128],
                                                                    scalar=rden[:, 2:3], in1=O1[:], op0=ALU.mult,
                                                                    op1=ALU.add),
                     R=[b_pb[a2], b_rden, b_O1], W=[b_odf])
                S.op("act", lambda e: e.activation(out=junkf[:], in_=odf[:], func=AF.Square,
                                                   accum_out=stat[:, 2:3]), R=[b_odf], W=[b_junkf, b_stat])
                rstd_from_ss(2, 128, C_EPS5)
                S.op("dve", lambda e, qb=qb, oi2=oi2: e.scalar_tensor_tensor(
                    out=od_st[oi2][:, qb, :], in0=odf[:], scalar=stat[:, 2:3], in1=dnw[:], op0=ALU.mult,
                    op1=ALU.mult), R=[b_odf, b_stat, b_dnw], W=[b_odst[oi2]])
            S.dma("sp", od_v[qc], od_st[oi2][:], b_odst[oi2], R=[b_odst[oi2]])

    if ext:
        return None
    S.barrier_wait("sp", b_olast + b_odst)
    return kb.done()


def _t5_bucket_np(rel):
    n = np.maximum(rel, 0)
    nf = np.maximum(n, 1).astype(np.float32)
    large = 16 + (np.log(nf / np.float32(16)) / np.float32(math.log(128 / 16)) * np.float32(16)).astype(np.int32)
    large = np.minimum(large, 31)
    return np.where(n < 16, n, large)


def _mixer_consts():
    p = np.arange(128)
    same = (p[:, None] // 64) == (p[None, :] // 64)
    ident = np.eye(128, dtype=np.float32)
    ones = np.ones((128, 128), np.float32)
    maskl = np.where(same & (p[:, None] > p[None, :]), -1.0, 0.0).astype(np.float32)
    masku = np.where(same & (p[:, None] <= p[None, :]), 1.0, 0.0).astype(np.float32)
    tri = masku.copy()
    sel63 = np.zeros((128, 128), np.float32)
    sel63[63, :] = 1.0
    sel127 = np.zeros((128, 128), np.float32)
    sel127[127, :] = 1.0
    return np.ascontiguousarray(np.concatenate([ident, ones, maskl, masku, tri, sel63, sel127], axis=1))


def mixer_inputs(xb, l, h, P):
    w_in = P["w_in"][l]
    cols = np.concatenate([
        np.arange(h * 128, (h + 1) * 128),
        512 + np.arange(h * 128, (h + 1) * 128),
        1024 + np.arange(h * 128, (h + 1) * 128),
        2056 + np.arange(h * 128, (h + 1) * 128),
        2568 + np.arange(h * 128, (h + 1) * 128),
        1536 + np.arange(h * 128, (h + 1) * 128),
        3080 + np.arange(h * 128, (h + 1) * 128),
        np.array([2048 + h]),
        np.array([2052 + h]),
    ])
    wh = np.ascontiguousarray(w_in[:, cols])
    anw = np.ascontiguousarray(P["attn_norm_w"][l].reshape(8, 128).T)
    cwl = P["conv_w"][l]
    cw = np.concatenate([cwl[:, g * 512 + h * 128: g * 512 + (h + 1) * 128].T for g in range(3)], axis=1)
    sc = np.zeros((128, 4), np.float32)
    sc[:, 0] = P["a_log"][l, h]
    sc[:, 1] = P["dt_bias"][l, h]
    sc[:, 2] = P["rel_bias"][31, h]
    lamv = np.concatenate([P["lambda_q1"][l], P["lambda_k1"][l], P["lambda_q2"][l], P["lambda_k2"][l]])
    lamv = np.broadcast_to(lamv[None, :], (128, 256))
    lnw = np.broadcast_to(P["la_norm_w"][l][None, :], (128, 128))
    dnw = np.broadcast_to(P["diff_norm_w"][l][None, :], (128, 128))
    kl = np.arange(128)[:, None]
    jj = np.arange(512)[None, :]
    rel = jj - 128 - kl
    bt = np.where(rel >= 0, P["rel_bias"][_t5_bucket_np(rel), h], np.float32(-30000.0)).astype(np.float32)
    c = np.ascontiguousarray
    return {"x": c(xb), "wh": wh, "anw": anw, "cw": c(cw.astype(np.float32)), "sc": sc,
            "lamv": c(lamv.astype(np.float32)), "lnw": c(lnw.astype(np.float32)),
            "dnw": c(dnw.astype(np.float32)), "btoep": c(bt), "ident_bf": _ident_bf(), "cf": _mixer_consts()}


CC_GROUPS = [[0, 1, 2, 3], [4, 5, 6, 7]]
_MIX_KEYS = ("wh", "anw", "cw", "sc", "lamv", "lnw", "dnw")


def build_fused(T):
    kb = KB()
    nc, S = kb.nc, kb.S
    NT = T // NH
    NTL = NT // 128
    I32 = mybir.dt.int32
    shp = {"wh": [D_MODEL, NW], "anw": [128, 8], "cw": [128, 12], "sc": [128, 4], "lamv": [128, 256],
           "lnw": [128, 128], "dnw": [128, 128]}
    x_d = kb.din("x", [T, D_MODEL], F32)
    xs_d = kb.din("xs", [NT, D_MODEL], F32)
    idx_d = kb.din("idx", [128, NH * NTL], I32)
    bt_d = kb.din("btoep", [128, 512], F32)
    idb_d = kb.din("ident_bf", [128, 128], BF16)
    cf_d = kb.din("cf", [128, 7 * 128], F32)
    fin_d = kb.din("final_w_bc", [128, D_MODEL], F32)
    lay = []
    for l in range(DEPTH):
        d = {k: kb.din("%s%d" % (k, l), shp[k], F32) for k in _MIX_KEYS}
        d["w_out"] = kb.din("w_out%d" % l, [D_MODEL, D_MODEL], F32)
        d["w_gu"] = kb.din("w_gu%d" % l, [D_MODEL, 2 * D_FF], F32)
        d["w_down"] = kb.din("w_down%d" % l, [D_FF, D_MODEL], F32)
        d["ffn_norm_w"] = kb.din("fnw%d" % l, [128, 8], F32)
        lay.append(d)
    y_d = kb.dout("y", [NT, D_MODEL], F32)
    og_in = kb.dint("og_in", [T, 256], BF16)
    og_all = kb.dint("og_all", [NH * T, 256], BF16)
    xs_in = kb.dint("xs_in", [NT, D_MODEL], F32)
    x1_all = kb.dint("x1_all", [T, D_MODEL], F32)

    pb = [kb.ps([128, 512], F32, "pb%d" % i) for i in range(8)]
    b_pb = S.bufs(8, "pb", excl=True)
    ORC = min(T, 2048)
    NKO = T // ORC
    XRC = min(NT, 256)
    NKX = NT // XRC
    b_og_all = S.bufs(NKO, "og_all")
    b_x1all = S.bufs(NKX, "x1_all")
    kb.persist = b_pb + b_og_all + b_x1all

    def allgather(src, dst, b_dst, name):
        S.dma_fn("pool", lambda e: e.collective_compute("AllGather", ALU.bypass, replica_groups=CC_GROUPS,
                                                         ins=[src], outs=[dst]),
                 S.buf(name), W=[b_dst], inc=None)

    def x1_tile(t):
        tok = t * 128
        r, w = divmod(tok, NT)
        k, ww = divmod(w, XRC)
        base = k * (NH * XRC) + r * XRC + ww
        return x1_all[base:base + 128, :]

    for l in range(DEPTH):
        lam_init = 0.8 - 0.6 * math.exp(-0.3 * l)
        kb.begin_phase()
        if l > 0:
            for k in range(NKX):
                allgather(xs_in[k * XRC:(k + 1) * XRC, :], x1_all[k * NH * XRC:(k + 1) * NH * XRC, :], b_x1all[k],
                          "ccx%d_%d" % (l, k))
        ext = {"kb": kb, "pb": pb, "b_pb": b_pb, "x": x_d, "x_tile": (None if l == 0 else x1_tile),
               "b_x": (None if l == 0 else (lambda t: [b_x1all[((t * 128) % NT) // XRC]])), "og": og_in,
               "btoep": bt_d, "ident_bf": idb_d, "cf": cf_d}
        for k in _MIX_KEYS:
            ext[k] = lay[l][k]
        build_mixer(T, lam_init, ext=ext)
        kb.end_phase()
        kb.begin_phase()
        for k in range(NKO):
            allgather(og_in[k * ORC:(k + 1) * ORC, :], og_all[k * NH * ORC:(k + 1) * NH * ORC, :], b_og_all[k],
                      "cco%d_%d" % (l, k))
        last = (l == DEPTH - 1)
        ext = {"kb": kb, "pb": pb, "b_pb": b_pb, "x": (xs_d if l == 0 else xs_in), "y": (y_d if last else xs_in),
               "og_all": og_all, "b_og_all": b_og_all, "idx": idx_d, "T": T,
               "w_out": lay[l]["w_out"], "w_gu": lay[l]["w_gu"], "w_down": lay[l]["w_down"],
               "ffn_norm_w": lay[l]["ffn_norm_w"], "ident_bf": idb_d, "final_w_bc": fin_d}
        build_ffn(NT, last, ext=ext)
        kb.end_phase()
    kb.es.close()
    return nc


_FUSED_CACHE = {}


def _gather_idx(T, h):
    NT = T // NH
    NTL = NT // 128
    ORC = min(T, 2048)
    tok = h * NT + np.arange(NTL)[None, :] * 128 + np.arange(128)[:, None]
    k, w = tok // ORC, tok % ORC
    cols = [k * (NH * ORC) + r * ORC + w for r in range(NH)]
    return np.ascontiguousarray(np.concatenate(cols, axis=1).astype(np.int32))


def fused_inputs(P, T):
    NT = T // NH
    NTL = NT // 128
    perm = np.concatenate([np.concatenate([np.arange(r * 128, (r + 1) * 128),
                                           512 + np.arange(r * 128, (r + 1) * 128)]) for r in range(NH)])
    in_maps = []
    for c in range(NCORES):
        b, h = divmod(c, NH)
        xb = P["x"][b, :T]
        m = {"x": np.ascontiguousarray(xb), "xs": np.ascontiguousarray(xb[h * NT:(h + 1) * NT]),
             "idx": _gather_idx(T, h),
             "final_w_bc": np.ascontiguousarray(np.broadcast_to(P["final_norm_w"][None, :], (128, D_MODEL)))}
        for l in range(DEPTH):
            mi = mixer_inputs(xb, l, h, P)
            for k in _MIX_KEYS:
                m["%s%d" % (k, l)] = mi[k]
            if l == 0:
                m["btoep"], m["ident_bf"], m["cf"] = mi["btoep"], mi["ident_bf"], mi["cf"]
            m["w_out%d" % l] = np.ascontiguousarray(P["w_out"][l][perm])
            m["w_gu%d" % l] = P["w_gate_up"][l]
            m["w_down%d" % l] = P["w_down"][l]
            m["fnw%d" % l] = np.ascontiguousarray(P["ffn_norm_w"][l].reshape(8, 128).T)
        in_maps.append(m)
    return in_maps


def kernel_fused(P, T):
    if T not in _FUSED_CACHE:
        _FUSED_CACHE[T] = build_fused(T)
    nc = _FUSED_CACHE[T]
    NT = T // NH
    res = run_bass_kernel_spmd(nc, fused_inputs(P, T), core_ids=list(range(NCORES)))
    out = np.empty((BATCH, T, D_MODEL), np.float32)
    for c in range(NCORES):
        b, h = divmod(c, NH)
        out[b, h * NT:(h + 1) * NT] = np.asarray(res.results[c]["y"])
    return out


_MIX_CACHE = {}


def kernel(**inputs):
    P = {k: np.ascontiguousarray(np.asarray(v, dtype=np.float32)) for k, v in inputs.items()}
    return kernel_fused(P, P["x"].shape[1])


def kernel_unfused(**inputs):
    P = {k: np.ascontiguousarray(np.asarray(v, dtype=np.float32)) for k, v in inputs.items()}
    x = P["x"]
    B, T, D = x.shape
    NTOK = B * T
    per = NTOK // NCORES
    for l in range(DEPTH):
        lam_init = 0.8 - 0.6 * math.exp(-0.3 * l)
        key = (T, l)
        if key not in _MIX_CACHE:
            _MIX_CACHE[key] = build_mixer(T, lam_init)
        nc = _MIX_CACHE[key]
        in_maps = [mixer_inputs(x[c // NH], l, c % NH, P) for c in range(NCORES)]
        res = run_bass_kernel_spmd(nc, in_maps, core_ids=list(range(NCORES)))
        o = np.empty((B, T, D), dtype=ml_dtypes.bfloat16)
        for c in range(NCORES):
            b, h = divmod(c, NH)
            o[b, :, h * 128:(h + 1) * 128] = np.asarray(res.results[c]["o_la"])
            o[b, :, 512 + h * 128:512 + (h + 1) * 128] = np.asarray(res.results[c]["o_d"])
        xs = x.reshape(NTOK, D)
        os_ = o.reshape(NTOK, D)
        ys = run_ffn([xs[c * per:(c + 1) * per] for c in range(NCORES)],
                     [os_[c * per:(c + 1) * per] for c in range(NCORES)],
                     P["w_out"][l], P["ffn_norm_w"][l], P["w_gate_up"][l], P["w_down"][l],
                     P["final_norm_w"] if l == DEPTH - 1 else None)
        x = np.concatenate([np.asarray(y) for y in ys], axis=0).reshape(B, T, D)
    return np.ascontiguousarray(x.astype(np.float32))
```

```python
import math
from contextlib import ExitStack

import numpy as np
import ml_dtypes
import concourse.bass as bass
import concourse.mybir as mybir
from concourse.bass_utils import run_bass_kernel_spmd

F32 = mybir.dt.float32
BF16 = mybir.dt.bfloat16
AF = mybir.ActivationFunctionType
ALU = mybir.AluOpType
AX = mybir.AxisListType

D_MODEL = 1024
SEQ = 8192
BATCH = 2
DEPTH = 2
NH = 4
D_FF = 2816
IN_DIM = 3592
NORM_EPS = 1e-6
NCORES = 8

ENGS = ("pe", "act", "dve", "pool", "sp")


class Buf:
    __slots__ = ("name", "w", "r", "dsem", "dcnt", "excl")

    def __init__(self, name, excl=False):
        self.name = name
        self.excl = excl
        self.w = None
        self.r = []
        self.dsem = None
        self.dcnt = 0


class Op:
    __slots__ = ("eng", "fn", "deps", "dma", "sem", "val", "needed", "inc")

    def __init__(self, eng, fn, dma=False):
        self.eng = eng
        self.fn = fn
        self.deps = []
        self.dma = dma
        self.sem = None
        self.val = 0
        self.needed = False
        self.inc = 16


class Sched:
    def __init__(self, nc, es):
        self.nc = nc
        self.es = es
        self.ops = {e: [] for e in ENGS}
        self.esem = {e: es.enter_context(nc.semaphore("s_" + e)) for e in ENGS}
        self.nbuf = 0
        self.cnt = {e: 0 for e in ENGS}
        self.phase_dmas = []
        self.nsem = 0
        self.defer = None

    def buf(self, name=None, excl=False):
        self.nbuf += 1
        return Buf(name or ("b%d" % self.nbuf), excl)

    def bufs(self, n, name="b", excl=False):
        return [self.buf("%s%d" % (name, i), excl) for i in range(n)]

    def _link(self, o, R, W):
        deps = []
        for b in R:
            if b.w is not None:
                d = b.w
                if d.dma or o.dma or d.eng != o.eng or o.eng != "pe":
                    deps.append(d)
            if b.excl:
                for d in b.r:
                    if d.eng != o.eng:
                        deps.append(d)
        for b in W:
            if b.w is not None:
                d = b.w
                if d.dma or o.dma or d.eng != o.eng or o.eng != "pe":
                    deps.append(d)
            for d in b.r:
                if d.dma or o.dma or d.eng != o.eng or o.eng != "pe":
                    deps.append(d)
        o.deps = deps
        for b in R:
            if b in W:
                continue
            if b.excl:
                b.r = []
            elif not o.dma:
                b.r = [x for x in b.r if x.dma or x.eng != o.eng]
            b.r.append(o)
        for b in W:
            b.w = o
            b.r = []

    def op(self, eng, fn, R=(), W=()):
        if self.defer is not None:
            self.defer.append(lambda: self.op_now(eng, fn, R, W))
            return None
        return self.op_now(eng, fn, R, W)

    def op_now(self, eng, fn, R=(), W=()):
        o = Op(eng, fn)
        self._link(o, R, W)
        self.ops[eng].append(o)
        return o

    def dma(self, q, out, in_, sb, R=(), W=()):
        return self.dma_fn(q, lambda e, out=out, in_=in_: e.dma_start(out=out, in_=in_), sb, R, W)

    def dma_fn(self, q, fn, sb, R=(), W=(), inc=16):
        if self.defer is not None:
            self.defer.append(lambda: self.dma_fn_now(q, fn, sb, R, W, inc))
            return None
        return self.dma_fn_now(q, fn, sb, R, W, inc)

    def dma_fn_now(self, q, fn, sb, R=(), W=(), inc=16):
        if sb.dsem is None:
            self.nsem += 1
            sb.dsem = self.es.enter_context(self.nc.semaphore("d%d_%s" % (self.nsem, sb.name)))
        o = Op(q, fn, dma=True)
        o.inc = inc
        sb.dcnt += (inc if inc else 1)
        o.sem = sb.dsem
        o.val = sb.dcnt
        self._link(o, R, W)
        self.ops[q].append(o)
        self.phase_dmas.append(o)
        return o

    def phase_barrier(self):
        lasts = []
        for e in ENGS:
            real = [o for o in self.ops[e] if o.fn is not None and not o.dma]
            if real:
                lasts.append(real[-1])
        deps = lasts + list(self.phase_dmas)
        for e in ENGS:
            o = Op(e, None)
            o.deps = [d for d in deps if d.dma or d.eng != e]
            self.ops[e].append(o)
        self.phase_dmas = []

    def barrier_wait(self, eng, R):
        o = Op(eng, None)
        self._link(o, (), R)
        self.ops[eng].append(o)
        return o

    def finalize(self):
        for e in ENGS:
            for o in self.ops[e]:
                for d in o.deps:
                    d.needed = True
        for e in ENGS:
            c = self.cnt[e]
            for o in self.ops[e]:
                if not o.dma and o.needed and o.fn is not None:
                    c += 1
                    o.sem = self.esem[e]
                    o.val = c
            self.cnt[e] = c
        ops = self.ops
        self.ops = {e: [] for e in ENGS}

        def run(eng, lst):
            seen = {}
            for o in lst:
                waits = {}
                for d in o.deps:
                    k = id(d.sem)
                    if k not in waits or waits[k][1] < d.val:
                        waits[k] = (d.sem, d.val)
                for k, (sem, val) in waits.items():
                    if seen.get(k, 0) < val:
                        eng.wait_ge(sem, val)
                        seen[k] = val
                if o.fn is None:
                    continue
                ins = o.fn(eng)
                if o.dma:
                    if o.inc:
                        ins.then_inc(o.sem, o.inc)
                    else:
                        ins.then_inc(o.sem)
                elif o.needed:
                    ins.then_inc(o.sem, 1)

        with self.nc.Block() as block:
            @block.tensor
            def _(e):
                run(e, ops["pe"])

            @block.scalar
            def _(e):
                run(e, ops["act"])

            @block.vector
            def _(e):
                run(e, ops["dve"])

            @block.gpsimd
            def _(e):
                run(e, ops["pool"])

            @block.sync
            def _(e):
                run(e, ops["sp"])


class KB:
    def __init__(self):
        self.nc = bass.Bass("TRN2", target_bir_lowering=False)
        self.es = ExitStack()
        self.S = Sched(self.nc, self.es)
        self.n = 0
        self.pes = None
        self.phase = 0

    def begin_phase(self):
        self.phase += 1
        self.pes = ExitStack()

    def end_phase(self):
        self.S.phase_barrier()
        self.S.finalize()
        self.pes.close()
        self.pes = None
        for b in getattr(self, "persist", []):
            b.w = None
            b.r = []

    def sb(self, shape, dt, name=None):
        self.n += 1
        st = self.pes if self.pes is not None else self.es
        return st.enter_context(self.nc.sbuf_tensor("sb%d_" % self.phase + (name or ("t%d" % self.n)), list(shape), dt))

    def dint(self, name, shape, dt):
        return self.nc.dram_tensor(name, list(shape), dt).ap()

    def ps(self, shape, dt, name=None):
        self.n += 1
        return self.es.enter_context(self.nc.psum_tensor("ps_" + (name or ("p%d" % self.n)), list(shape), dt))

    def din(self, name, shape, dt):
        return self.nc.dram_tensor(name, list(shape), dt, kind="ExternalInput").ap()

    def dout(self, name, shape, dt):
        return self.nc.dram_tensor(name, list(shape), dt, kind="ExternalOutput").ap()

    def done(self):
        self.S.finalize()
        self.es.close()
        return self.nc


def build_ffn(NT, final_norm, ext=None):
    kb = ext["kb"] if ext else KB()
    nc, S = kb.nc, kb.S
    NTL = NT // 128
    KD = D_MODEL // 128
    JF = D_FF // 128

    if ext:
        x_d, wout_d, wgu_d, wdn_d, fnw_d, idb_d, y_d = (
            ext[k] for k in ("x", "w_out", "w_gu", "w_down", "ffn_norm_w", "ident_bf", "y"))
        if final_norm:
            fin_d = ext["final_w_bc"]
        o_d = None
    else:
        x_d = kb.din("x", [NT, D_MODEL], F32)
        o_d = kb.din("o", [NT, D_MODEL], BF16)
        wout_d = kb.din("w_out", [D_MODEL, D_MODEL], F32)
        wgu_d = kb.din("w_gu", [D_MODEL, 2 * D_FF], F32)
        wdn_d = kb.din("w_down", [D_FF, D_MODEL], F32)
        fnw_d = kb.din("ffn_norm_w", [128, KD], F32)
        idb_d = kb.din("ident_bf", [128, 128], BF16)
        if final_norm:
            fin_d = kb.din("final_w_bc", [128, D_MODEL], F32)
        y_d = kb.dout("y", [NT, D_MODEL], F32)

    wout = kb.sb([128, KD, D_MODEL], BF16, "wout")
    wgu = kb.sb([128, KD, 2 * D_FF], BF16, "wgu")
    wdn = kb.sb([128, JF, D_MODEL], BF16, "wdn")
    fnw = kb.sb([128, KD], F32, "fnw")
    idb = kb.sb([128, 128], BF16, "idb")
    b_wout, b_wgu, b_wdn, b_fnw, b_idb = S.bufs(5, "wres")
    if final_norm:
        finw = kb.sb([128, D_MODEL], F32, "finw")
        b_finw = S.buf("finw")
        S.dma("sp", finw[:], fin_d, b_finw, W=[b_finw])
    S.dma("sp", fnw[:], fnw_d, b_fnw, W=[b_fnw])
    S.dma("sp", idb[:], idb_d, b_idb, W=[b_idb])

    STG = 1408
    NSTG = 3
    stg = [kb.sb([128, STG], F32, "stg%d" % i) for i in range(NSTG)]
    b_stg = S.bufs(NSTG, "stg")
    cnt = [0]
    cast_engs = ("dve", "pool")

    def load_cast(dst_ap, src_ap, n, wbuf, scale_ap=None):
        i = cnt[0] % NSTG
        q = "sp" if (cnt[0] % 2 == 0) else "act"
        S.dma(q, stg[i][:, 0:n], src_ap, b_stg[i], W=[b_stg[i]])
        ce = cast_engs[cnt[0] % 2]
        if scale_ap is None:
            S.op(ce, lambda e, d=dst_ap, s=stg[i][:, 0:n]: e.tensor_copy(out=d, in_=s),
                 R=[b_stg[i]], W=[wbuf])
        else:
            S.op(ce, lambda e, d=dst_ap, s=stg[i][:, 0:n], sc=scale_ap:
                 e.tensor_scalar(out=d, in0=s, scalar1=sc, scalar2=None, op0=ALU.mult),
                 R=[b_stg[i], b_fnw], W=[wbuf])
        cnt[0] += 1

    wout_v = wout_d.rearrange("(ko p) n -> p ko n", p=128)
    for ko in range(KD):
        load_cast(wout[:, ko, :], wout_v[:, ko, :], D_MODEL, b_wout)
    wgu_v = wgu_d.rearrange("(ko p) n -> p ko n", p=128)
    for ko in range(KD):
        for c in range(4):
            load_cast(wgu[:, ko, c * STG:(c + 1) * STG], wgu_v[:, ko, c * STG:(c + 1) * STG], STG,
                      b_wgu, scale_ap=fnw[:, ko:ko + 1])
    wdn_v = wdn_d.rearrange("(j p) n -> p j n", p=128)
    for j in range(JF):
        load_cast(wdn[:, j, :], wdn_v[:, j, :], D_MODEL, b_wdn)

    xin = [kb.sb([128, D_MODEL], F32, "xin%d" % i) for i in range(2)]
    oin = [kb.sb([128, D_MODEL], BF16, "oin%d" % i) for i in range(2)]
    b_xin = S.bufs(2, "xin")
    b_oin = S.bufs(2, "oin")
    tbuf = kb.sb([128, KD, 128], BF16, "tbuf")
    b_tbuf = S.buf("tbuf")
    hn = kb.sb([128, D_MODEL], BF16, "hn")
    b_hn = S.buf("hn")
    junk = kb.sb([128, D_MODEL], BF16, "junk")
    b_junk = S.buf("junk")
    aT = kb.sb([128, JF, 128], BF16, "aT")
    b_aT = S.buf("aT")
    sg = [kb.sb([128, 128], F32, "sg%d" % i) for i in range(2)]
    b_sg = S.bufs(2, "sg")
    stat = kb.sb([128, 8], F32, "stat")
    b_stat = S.buf("stat")
    epsc = kb.sb([128, 1], F32, "epsc")
    b_epsc = S.buf("epsc")
    S.op("dve", lambda e: e.memset(epsc[:], NORM_EPS), W=[b_epsc])

    if ext:
        pbank, b_pb = ext["pb"], ext["b_pb"]
        ptb_t = pbank[0][:].bitcast(BF16)
        idx_sb = kb.sb([128, 4 * NTL], mybir.dt.int32, "idx")
        b_idx = S.buf("idx")
        S.dma("sp", idx_sb[:], ext["idx"], b_idx, W=[b_idx])
        TT_ = ext["T"]
    else:
        pbank = [None] + [kb.ps([128, 512], F32, "pb%d" % i) for i in range(1, 8)]
        b_pb = S.bufs(8, "pb", excl=True)
        ptb_t = kb.ps([128, D_MODEL], BF16, "ptb")

    x_v = x_d.rearrange("(t p) d -> t p d", p=128)
    o_v = o_d.rearrange("(t p) d -> t p d", p=128) if o_d is not None else None
    y_v = y_d.rearrange("(t p) d -> t p d", p=128)

    def rms_scale(src, b_src, col):
        S.op("act", lambda e: e.activation(out=junk[:], in_=src, func=AF.Square,
                                           accum_out=stat[:, col:col + 1]),
             R=[b_src], W=[b_junk, b_stat])
        S.op("act", lambda e: e.activation(out=stat[:, col:col + 1], in_=stat[:, col:col + 1], func=AF.Sqrt,
                                           scale=1.0 / D_MODEL, bias=epsc[:, 0:1]),
             R=[b_stat, b_epsc], W=[b_stat])
        S.op("dve", lambda e: e.reciprocal(out=stat[:, col:col + 1], in_=stat[:, col:col + 1]),
             R=[b_stat], W=[b_stat])

    hT2 = [kb.sb([128, KD, 128], BF16, "hT2_%d" % i) for i in range(2)]
    b_hT2 = S.bufs(2, "hT2")
    aT2 = [aT, kb.sb([128, JF, 128], BF16, "aT_1")]
    b_aT2 = [b_aT, S.buf("aT_1")]
    ptb = ptb_t

    def stageA(t):
        i = t % 2
        xt, ot = xin[i], oin[i]
        S.dma("sp", xt[:], x_v[t], b_xin[i], W=[b_xin[i]])
        if ext:
            for r_ in range(4):
                S.dma_fn("pool", lambda e, ot=ot, r_=r_, t=t: e.indirect_dma_start(
                    out=ot[:, r_ * 256:(r_ + 1) * 256], out_offset=None,
                    in_=ext["og_all"],
                    in_offset=bass.IndirectOffsetOnAxis(ap=idx_sb[:, r_ * NTL + t:r_ * NTL + t + 1], axis=0)),
                    b_oin[i], R=[b_idx] + list(ext["b_og_all"]), W=[b_oin[i]])
        else:
            S.dma("pool", ot[:], o_v[t], b_oin[i], W=[b_oin[i]])

        def tr_group(e, src=ot):
            ins = None
            for k in range(KD):
                ins = e.transpose(out=ptb[:, k * 128:(k + 1) * 128], in_=src[:, k * 128:(k + 1) * 128],
                                  identity=idb[:])
            return ins
        S.op("pe", tr_group, R=[b_oin[i], b_idb], W=[b_pb[0]])
        S.op("act", lambda e: e.copy(out=tbuf[:].rearrange("p k t -> p (k t)"), in_=ptb[:, 0:KD * 128]),
             R=[b_pb[0]], W=[b_tbuf])
        for nchunk in range(2):
            bk = 1 + nchunk

            def mm_out(e, bk=bk, nchunk=nchunk):
                ins = None
                for k in range(KD):
                    ins = e.matmul(out=pbank[bk][:], lhsT=tbuf[:, k, :],
                                   rhs=wout[:, k, nchunk * 512:(nchunk + 1) * 512],
                                   start=(k == 0), stop=(k == KD - 1))
                return ins
            S.op("pe", mm_out, R=[b_tbuf, b_wout], W=[b_pb[bk]])
            S.op("dve", lambda e, bk=bk, nchunk=nchunk, xt=xt:
                 e.tensor_tensor(out=xt[:, nchunk * 512:(nchunk + 1) * 512],
                                 in0=pbank[bk][:], in1=xt[:, nchunk * 512:(nchunk + 1) * 512], op=ALU.add),
                 R=[b_pb[bk], b_xin[i]], W=[b_xin[i]])
        rms_scale(xt[:], b_xin[i], 0)
        S.op("act", lambda e, xt=xt: e.activation(out=hn[:], in_=xt[:], func=AF.Copy, scale=stat[:, 0:1]),
             R=[b_xin[i], b_stat], W=[b_hn])

        def tr_group2(e):
            ins = None
            for k in range(KD):
                ins = e.transpose(out=ptb[:, k * 128:(k + 1) * 128], in_=hn[:, k * 128:(k + 1) * 128],
                                  identity=idb[:])
            return ins
        S.op("pe", tr_group2, R=[b_hn, b_idb], W=[b_pb[0]])
        S.op("act", lambda e, i=i: e.copy(out=hT2[i][:].rearrange("p k t -> p (k t)"), in_=ptb[:, 0:KD * 128]),
             R=[b_pb[0]], W=[b_hT2[i]])

    def stageB(t):
        i = t % 2
        for j in range(JF):
            bk = 3 + (j % 2)

            def mm_gu(e, bk=bk, j=j, i=i):
                ins = None
                for half in range(2):
                    for k in range(KD):
                        c0 = half * D_FF + j * 128
                        ins = e.matmul(out=pbank[bk][:, half * 128:(half + 1) * 128],
                                       lhsT=wgu[:, k, c0:c0 + 128], rhs=hT2[i][:, k, :],
                                       start=(k == 0), stop=(k == KD - 1))
                return ins
            S.op("pe", mm_gu, R=[b_hT2[i], b_wgu], W=[b_pb[bk]])
            s_ = j % 2
            S.op("act", lambda e, bk=bk, s_=s_: e.activation(out=sg[s_][:], in_=pbank[bk][:, 0:128], func=AF.Silu),
                 R=[b_pb[bk]], W=[b_sg[s_]])
            S.op("dve", lambda e, bk=bk, s_=s_, j=j, i=i: e.tensor_tensor(out=aT2[i][:, j, :],
                                                                          in0=pbank[bk][:, 128:256],
                                                                          in1=sg[s_][:], op=ALU.mult),
                 R=[b_pb[bk], b_sg[s_]], W=[b_aT2[i]])

    def stageC(t):
        i = t % 2
        xt = xin[i]
        for nchunk in range(2):
            bk = 5 + nchunk

            def mm_dn(e, bk=bk, nchunk=nchunk, i=i):
                ins = None
                for j in range(JF):
                    ins = e.matmul(out=pbank[bk][:], lhsT=aT2[i][:, j, :],
                                   rhs=wdn[:, j, nchunk * 512:(nchunk + 1) * 512],
                                   start=(j == 0), stop=(j == JF - 1))
                return ins
            S.op("pe", mm_dn, R=[b_aT2[i], b_wdn], W=[b_pb[bk]])
            S.op("dve", lambda e, bk=bk, nchunk=nchunk, xt=xt:
                 e.tensor_tensor(out=xt[:, nchunk * 512:(nchunk + 1) * 512],
                                 in0=pbank[bk][:], in1=xt[:, nchunk * 512:(nchunk + 1) * 512], op=ALU.add),
                 R=[b_pb[bk], b_xin[i]], W=[b_xin[i]])
        if final_norm:
            rms_scale(xt[:], b_xin[i], 1)
            S.op("dve", lambda e, xt=xt: e.scalar_tensor_tensor(out=xt[:], in0=xt[:], scalar=stat[:, 1:2],
                                                                in1=finw[:], op0=ALU.mult, op1=ALU.mult),
                 R=[b_xin[i], b_stat, b_finw], W=[b_xin[i]])
        S.dma("sp", y_v[t], xt[:], b_xin[i], R=[b_xin[i]])

    stageA(0)
    for t in range(NTL):
        if t + 1 < NTL:
            stageA(t + 1)
        stageB(t)
        stageC(t)

    if ext:
        return None
    S.barrier_wait("sp", b_xin)
    return kb.done()


def _ident_bf():
    return np.eye(128, dtype=np.float32).astype(ml_dtypes.bfloat16)


def run_ffn(x_sl, o_sl, w_out, ffn_norm_w, w_gu, w_down, final_w, nc_cache={}):
    NT = x_sl[0].shape[0]
    key = (NT, final_w is not None)
    if key not in nc_cache:
        nc_cache[key] = build_ffn(NT, final_w is not None)
    nc = nc_cache[key]
    fnw = np.ascontiguousarray(ffn_norm_w.reshape(D_MODEL // 128, 128).T)
    in_maps = []
    for c in range(len(x_sl)):
        m = {"x": np.ascontiguousarray(x_sl[c]), "o": np.ascontiguousarray(o_sl[c]),
             "w_out": w_out, "w_gu": w_gu, "w_down": w_down, "ffn_norm_w": fnw,
             "ident_bf": _ident_bf()}
        if final_w is not None:
            m["final_w_bc"] = np.ascontiguousarray(np.broadcast_to(final_w[None, :], (128, D_MODEL)))
        in_maps.append(m)
    res = run_bass_kernel_spmd(nc, in_maps, core_ids=list(range(len(x_sl))))
    return [r["y"] for r in res.results]


NW = 898


NPULL = 6


def build_mixer(T, lam_init, dbg=99, ext=None):
    SKIP = ''
    kb = ext["kb"] if ext else KB()
    nc, S = kb.nc, kb.S
    NCH = T // 512
    NTL = T // 128
    KD = D_MODEL // 128

    if ext:
        x_d = ext["x"]
        wh_d, anw_d, cw_d, sc_d, lamv_d, lnw_d, dnw_d, bt_d, idb_d, cf_d = (
            ext[k] for k in ("wh", "anw", "cw", "sc", "lamv", "lnw", "dnw", "btoep", "ident_bf", "cf"))
        ola_d = ext["og"][:, 0:128]
        od_d = ext["og"][:, 128:256]
    else:
        x_d = kb.din("x", [T, D_MODEL], F32)
        wh_d = kb.din("wh", [D_MODEL, NW], F32)
        anw_d = kb.din("anw", [128, KD], F32)
        cw_d = kb.din("cw", [128, 12], F32)
        sc_d = kb.din("sc", [128, 4], F32)
        lamv_d = kb.din("lamv", [128, 4 * 64], F32)
        lnw_d = kb.din("lnw", [128, 128], F32)
        dnw_d = kb.din("dnw", [128, 128], F32)
        bt_d = kb.din("btoep", [128, 512], F32)
        idb_d = kb.din("ident_bf", [128, 128], BF16)
        cf_d = kb.din("cf", [128, 7 * 128], F32)
        ola_d = kb.dout("o_la", [T, 128], BF16)
        od_d = kb.dout("o_d", [T, 128], BF16)

    def T_(shape, dt, name):
        return kb.sb(shape, dt, name), S.buf(name)

    anw, b_anw = T_([128, KD], F32, "anw")
    cw, b_cw = T_([128, 12], F32, "cw")
    sc, b_sc = T_([128, 4], F32, "sc")
    lamv, b_lamv = T_([128, 256], F32, "lamv")
    lnw, b_lnw = T_([128, 128], F32, "lnw")
    dnw, b_dnw = T_([128, 128], F32, "dnw")
    bt, b_bt = T_([128, 512], F32, "bt")
    idb, b_idb = T_([128, 128], BF16, "idb")
    cf, b_cf = T_([128, 7 * 128], F32, "cf")
    for (t_, d_, b_) in ((anw, anw_d, b_anw), (cw, cw_d, b_cw), (sc, sc_d, b_sc), (lamv, lamv_d, b_lamv),
                         (lnw, lnw_d, b_lnw), (dnw, dnw_d, b_dnw), (bt, bt_d, b_bt), (idb, idb_d, b_idb),
                         (cf, cf_d, b_cf)):
        S.dma("sp", t_[:], d_, b_, W=[b_])
    IDF = cf[:, 0:128]
    ONES = cf[:, 128:256]
    MASKL = cf[:, 256:384]
    MASKU = cf[:, 384:512]
    TRI = cf[:, 512:640]
    SEL63 = cf[:, 640:768]
    SEL127 = cf[:, 768:896]

    cst_, b_cst = T_([128, 8], F32, "cst")
    S.op("dve", lambda e: e.memset(cst_[:, 0:1], 1.0), W=[b_cst])
    S.op("dve", lambda e: e.memset(cst_[:, 1:2], 1e-6), W=[b_cst])
    S.op("dve", lambda e: e.memset(cst_[:, 2:3], 1e-5), W=[b_cst])
    S.op("dve", lambda e: e.memset(cst_[:, 5:6], 0.0), W=[b_cst])
    C_ONE, C_EPS6, C_EPS5, C_NA, C_NLAM, C_ZERO = (cst_[:, i:i + 1] for i in range(6))
    S.op("act", lambda e: e.activation(out=cst_[:, 3:4], in_=sc[:, 0:1], func=AF.Exp), R=[b_sc], W=[b_cst])
    S.op("dve", lambda e: e.tensor_scalar(out=cst_[:, 3:4], in0=cst_[:, 3:4], scalar1=-1.0, scalar2=None,
                                          op0=ALU.mult), R=[b_cst], W=[b_cst])
    lt, b_lt = T_([128, 128], F32, "lamtmp")
    ls, b_ls = T_([128, 4], F32, "lamsum")
    S.op("dve", lambda e: e.tensor_tensor(out=lt[:, 0:64], in0=lamv[:, 0:64], in1=lamv[:, 64:128], op=ALU.mult),
         R=[b_lamv], W=[b_lt])
    S.op("dve", lambda e: e.tensor_tensor(out=lt[:, 64:128], in0=lamv[:, 128:192], in1=lamv[:, 192:256],
                                          op=ALU.mult), R=[b_lamv, b_lt], W=[b_lt])
    if 'r' not in SKIP:
        S.op("dve", lambda e: e.reduce_sum(out=ls[:, 0:1], in_=lt[:, 0:64], axis=AX.X), R=[b_lt], W=[b_ls])
        S.op("dve", lambda e: e.reduce_sum(out=ls[:, 1:2], in_=lt[:, 64:128], axis=AX.X), R=[b_lt, b_ls], W=[b_ls])
    S.op("act", lambda e: e.activation(out=ls[:, 2:4], in_=ls[:, 0:2], func=AF.Exp), R=[b_ls], W=[b_ls])
    S.op("dve", lambda e: e.scalar_tensor_tensor(out=cst_[:, 4:5], in0=ls[:, 3:4], scalar=float(-lam_init),
                                                 in1=ls[:, 2:3], op0=ALU.add, op1=ALU.subtract),
         R=[b_ls, b_cst], W=[b_cst])
    S.op("dve", lambda e: e.tensor_scalar(out=dnw[:], in0=dnw[:], scalar1=float(1.0 - lam_init), scalar2=None,
                                          op0=ALU.mult), R=[b_dnw], W=[b_dnw])

    Wb, b_Wb = T_([128, KD, 1024], BF16, "Wb")
    wst = [kb.sb([128, NW], F32, "wst%d" % i) for i in range(2)]
    b_wst = S.bufs(2, "wst")
    wh_v = wh_d.rearrange("(ko p) n -> p ko n", p=128)
    for ko in range(KD):
        i = ko % 2
        S.dma("sp", wst[i][:], wh_v[:, ko, :], b_wst[i], W=[b_wst[i]])
        S.op("dve" if i == 0 else "pool",
             lambda e, i=i, ko=ko: e.tensor_scalar(out=Wb[:, ko, 0:NW], in0=wst[i][:], scalar1=anw[:, ko:ko + 1],
                                                   scalar2=None, op0=ALU.mult),
             R=[b_wst[i], b_anw], W=[b_Wb])

    KdT, b_KdT = T_([128, T], BF16, "KdT")
    Vaug, b_Vaug = T_([128, NTL, 144], BF16, "Vaug")
    if 'v' not in SKIP:
        S.op("pool", lambda e: e.memset(Vaug[:, :, 128:129], 1.0), W=[b_Vaug])

    xt = [kb.sb([128, D_MODEL], F32, "xt%d" % i) for i in range(4)]
    b_xt = S.bufs(4, "xt")
    xn, b_xn = T_([128, D_MODEL], BF16, "xn")
    junk, b_junk = T_([128, D_MODEL], BF16, "junk")
    junkf, b_junkf = T_([128, 128], F32, "junkf")
    stat, b_stat = T_([128, 4], F32, "stat")
    hT, b_hT = T_([128, KD, 512], BF16, "hT")
    cstg = [kb.sb([128, 515], F32, "cstg%d" % g) for g in range(3)]
    b_cstg = S.bufs(3, "cstg")
    cacc = [kb.sb([128, 512], F32, "cacc%d" % g) for g in range(3)]
    b_cacc = S.bufs(3, "cacc")
    sil = [kb.sb([128, 512], F32, "sil%d" % g) for g in range(2)]
    b_sil = S.bufs(2, "sil")
    sq, b_sq = T_([128, 512], F32, "sq")
    rs, b_rs = T_([128, 512], F32, "rs")
    qnT, b_qnT = T_([128, 512], BF16, "qnT")
    knT, b_knT = T_([128, 512], BF16, "knT")
    vsT, b_vsT = T_([128, 512], BF16, "vsT")
    QdT, b_QdT = T_([128, 512], BF16, "QdT")
    qgT, b_qgT = T_([128, 512], BF16, "qgT")
    kvt, b_kvt = T_([128, 8, 128], BF16, "kvt")
    zs, b_zs = T_([128, 4, 128], BF16, "zs")
    ba, b_ba = T_([128, 4, 2], F32, "ba")
    for g in range(3):
        S.op("dve", lambda e, g=g: e.memset(cstg[g][:, 0:3], 0.0), W=[b_cstg[g]])
    pt = {}
    for nm in ("beta", "eb", "g", "gc", "egc", "bg", "glt", "ekd", "tmpa"):
        pt[nm] = T_([128, 4], F32, "pt_" + nm)
    egl, b_egl = T_([128, 8], F32, "egl")
    NS = 4
    gset = []
    for s_ in range(NS):
        d = {}
        for nm, shp, dt in (("dg", [128, 128], F32), ("Eb", [128, 128], F32), ("Ds", [128, 128], F32),
                            ("EA", [128, 128], F32), ("t1", [128, 128], F32), ("t2", [128, 128], F32),
                            ("MPa", [128, 256], F32), ("MPb", [128, 256], F32),
                            ("MTa", [128, 128], F32), ("MTb", [128, 128], F32),
                            ("TT", [128, 128], BF16), ("vb", [128, 128], BF16), ("kbg", [128, 128], BF16)):
            d[nm] = T_(shp, dt, "%s_%d" % (nm, s_))
        gset.append(d)
    u_sb, b_u = T_([128, 4, 128], F32, "u_sb")
    wT_sb, b_wT = T_([128, 512], BF16, "wT_sb")
    apT, b_apT = T_([128, 4, 128], BF16, "apT")
    kd_sb, b_kd = T_([128, 4, 128], BF16, "kd_sb")
    S_f, b_Sf = T_([128, 128], F32, "S_f")
    Sb, b_Sb = T_([128, 128], BF16, "Sb")
    vn, b_vn = T_([128, 128], BF16, "vn")
    o_sb = [kb.sb([128, 128], F32, "o_sb%d" % i) for i in range(2)]
    b_osb = S.bufs(2, "o_sb")
    on_, b_on = T_([128, 128], F32, "on")
    ola_st = [kb.sb([128, 4, 128], BF16, "ola_st%d" % i) for i in range(2)]
    b_olast = S.bufs(2, "ola_st")
    S.op("dve", lambda e: e.memset(S_f[:], 0.0), W=[b_Sf])
    S.op("dve", lambda e: e.memset(Sb[:], 0.0), W=[b_Sb])
    pT = [[kb.sb([128, 256], BF16, "pT%d%d" % (m, p)) for p in range(2)] for m in range(2)]
    b_pT = [[S.buf("pT%d%d" % (m, p)) for p in range(2)] for m in range(2)]
    s2 = [kb.sb([128, 256], F32, "s2_%d" % m) for m in range(2)]
    b_s2 = S.bufs(2, "s2")
    rden, b_rden = T_([128, 4], F32, "rden")
    O1, b_O1 = T_([128, 128], F32, "O1")
    odf, b_odf = T_([128, 128], F32, "odf")
    od_st = [kb.sb([128, 2, 128], BF16, "od_st%d" % i) for i in range(2)]
    b_odst = S.bufs(2, "od_st")

    if ext:
        pb, b_pb = ext["pb"], ext["b_pb"]
    else:
        pb = [kb.ps([128, 512], F32, "pb%d" % i) for i in range(8)]
        b_pb = S.bufs(8, "pb", excl=True)
    pb0_bf = pb[0][:].bitcast(BF16)

    x_v = x_d.rearrange("(t p) d -> t p d", p=128)
    ola_v = ola_d.rearrange("(c t p) e -> c p t e", p=128, t=4)
    od_v = od_d.rearrange("(c t p) e -> c p t e", p=128, t=2)

    def rstd_from_ss(col, n, epsc):
        S.op("act", lambda e: e.activation(out=stat[:, col:col + 1], in_=stat[:, col:col + 1], func=AF.Ln,
                                           scale=1.0 / n, bias=epsc), R=[b_stat, b_cst], W=[b_stat])
        S.op("act", lambda e: e.activation(out=stat[:, col:col + 1], in_=stat[:, col:col + 1], func=AF.Exp,
                                           scale=-0.5), R=[b_stat], W=[b_stat])

    def silu_via_exp(src_ap, R_src, tmp_ap, b_tmp, out_ap, W_out, mul_eng="dve"):
        S.op("act", lambda e: e.activation(out=tmp_ap, in_=src_ap, func=AF.Exp, scale=-1.0), R=R_src, W=[b_tmp])
        S.op("act", lambda e: e.activation(out=tmp_ap, in_=tmp_ap, func=AF.Ln, bias=C_ONE), R=[b_tmp, b_cst],
             W=[b_tmp])
        S.op("act", lambda e: e.activation(out=tmp_ap, in_=tmp_ap, func=AF.Exp, scale=-1.0), R=[b_tmp], W=[b_tmp])
        S.op(mul_eng, lambda e: e.tensor_tensor(out=out_ap, in0=src_ap, in1=tmp_ap, op=ALU.mult),
             R=list(R_src) + [b_tmp], W=W_out)

    stmp, b_stmp = T_([128, 512], F32, "stmp")
    zraw, b_zraw = T_([128, 4, 128], F32, "zraw")

    hT2 = [hT, kb.sb([128, KD, 512], BF16, "hT_b")]
    b_hT2 = [b_hT, S.buf("hT_b")]
    Qblk = [kb.sb([128, 2, 512], BF16, "Qblk%d" % i) for i in range(2)]
    b_Qblk = S.bufs(2, "Qblk")
    for i_ in range(2):
        S.op("pool", lambda e, i_=i_: e.memset(Qblk[i_][:], 0.0), W=[b_Qblk[i_]])
    pT2 = [kb.sb([128, 512], BF16, "pT2_%d" % i) for i in range(2)]
    b_pT2 = S.bufs(2, "pT2")
    s2w, b_s2w = T_([128, 512], F32, "s2w")
    scan_l = []

    def pull_scan(n):
        for _ in range(min(n, len(scan_l))):
            scan_l.pop(0)()

    zs2 = [zs, kb.sb([128, 4, 128], BF16, "zs_b")]
    b_zs2 = [b_zs, S.buf("zs_b")]
    pend = []

    def pull(n):
        for _ in range(min(n, len(pend))):
            pend.pop(0)()

    def front(j):
        if dbg <= 0:
            return
        for tt in range(4):
            t = 4 * j + tt
            i = tt
            S.dma("sp" if i % 2 == 0 else "pool", xt[i][:],
                  (ext["x_tile"](t) if (ext and ext.get("x_tile") is not None) else x_v[t]), b_xt[i], W=[b_xt[i]],
                  R=(list(ext["b_x"](t)) if (ext and ext.get("b_x") is not None) else []))
        for tt in range(4):
            t = 4 * j + tt
            i = tt
            S.op("act", lambda e, i=i: e.activation(out=junk[:], in_=xt[i][:], func=AF.Square,
                                                    accum_out=stat[:, 0:1]), R=[b_xt[i]], W=[b_junk, b_stat])
            rstd_from_ss(0, D_MODEL, C_EPS6)
            S.op("act", lambda e, i=i: e.activation(out=xn[:], in_=xt[i][:], func=AF.Copy, scale=stat[:, 0:1]),
                 R=[b_xt[i], b_stat], W=[b_xn])

            def trx(e):
                ins = None
                for k in range(KD):
                    ins = e.transpose(out=pb0_bf[:, k * 128:(k + 1) * 128], in_=xn[:, k * 128:(k + 1) * 128],
                                      identity=idb[:])
                return ins
            S.op("pe", trx, R=[b_xn, b_idb], W=[b_pb[0]])
            S.op("dve", lambda e, tt=tt: e.tensor_copy(out=hT2[j % 2][:, :, tt * 128:(tt + 1) * 128],
                                                        in_=pb0_bf[:, 0:1024].rearrange("p (k t) -> p k t", k=KD)),
                 R=[b_pb[0]], W=[b_hT2[j % 2]])
        if dbg <= 1:
            return
        for g in range(5 if 'f' not in SKIP else 0):
            bk = 1 + (g % 2)

            def mmf(e, g=g, bk=bk):
                ins = None
                for k in range(KD):
                    ins = e.matmul(out=pb[bk][:], lhsT=Wb[:, k, g * 128:(g + 1) * 128], rhs=hT2[j % 2][:, k, :],
                                   start=(k == 0), stop=(k == KD - 1))
                return ins
            S.op("pe", mmf, R=[b_Wb, b_hT2[j % 2]], W=[b_pb[bk]])
            if g < 3:
                S.op("act", lambda e, g=g, bk=bk: e.copy(out=cstg[g][:, 3:515], in_=pb[bk][:]),
                     R=[b_pb[bk]], W=[b_cstg[g]])
            elif g == 3:
                S.op("act", lambda e, bk=bk: e.copy(out=Qblk[j % 2][0:64, :, 0:256],
                                                    in_=pb[bk][0:64, :].rearrange("p (q c) -> p q c", q=2)),
                     R=[b_pb[bk]], W=[b_Qblk[j % 2]])
                S.op("act", lambda e, bk=bk: e.copy(out=Qblk[j % 2][64:128, :, 256:512],
                                                    in_=pb[bk][64:128, :].rearrange("p (q c) -> p q c", q=2)),
                     R=[b_pb[bk]], W=[b_Qblk[j % 2]])
            else:
                S.op("act", lambda e, bk=bk, j=j: e.copy(out=KdT[:, j * 512:(j + 1) * 512], in_=pb[bk][:]),
                     R=[b_pb[bk]], W=[b_KdT])
        for tt in range(4 if 't' not in SKIP else 0):
            t = 4 * j + tt

            def mmt(e, tt=tt):
                ins = None
                for k in range(KD):
                    NN = 256 if 'n' in SKIP else 258
                    ins = e.matmul(out=pb[3][:, 0:NN], lhsT=hT2[j % 2][:, k, tt * 128:(tt + 1) * 128],
                                   rhs=Wb[:, k, 640:640 + NN], start=(k == 0), stop=(k == KD - 1))
                return ins
            S.op("pe", mmt, R=[b_Wb, b_hT2[j % 2]], W=[b_pb[3]])
            S.op("dve", lambda e, tt=tt: e.tensor_copy(out=zraw[:, tt, :], in_=pb[3][:, 0:128]),
                 R=[b_pb[3]], W=[b_zraw])
            S.op("dve", lambda e, t=t: e.tensor_copy(out=Vaug[:, t, 0:128], in_=pb[3][:, 128:256]),
                 R=[b_pb[3]], W=[b_Vaug])
            S.op("dve", lambda e, tt=tt: e.tensor_copy(out=ba[:, tt, :], in_=pb[3][:, 256:258]),
                 R=[b_pb[3]], W=[b_ba])
        silu_via_exp(zraw[:].rearrange("p a d -> p (a d)"), [b_zraw], stmp[:], b_stmp,
                     zs2[j % 2][:].rearrange("p a d -> p (a d)"), [b_zs2[j % 2]], mul_eng="pool")

    front(0)
    for j in range(NCH):
        if dbg <= 2:
            continue
        pend.clear()
        if j + 1 < NCH:
            S.defer = pend
            front(j + 1)
            S.defer = None
            pull(4)
        for g in range(3):
            ce = "dve"
            S.op(ce, lambda e, g=g: e.tensor_scalar(out=cacc[g][:], in0=cstg[g][:, 3:515],
                                                    scalar1=cw[:, g * 4 + 3:g * 4 + 4], scalar2=None, op0=ALU.mult),
                 R=[b_cstg[g], b_cw], W=[b_cacc[g]])
            for tap in (2, 1, 0):
                S.op(ce, lambda e, g=g, tap=tap: e.scalar_tensor_tensor(
                    out=cacc[g][:], in0=cstg[g][:, tap:tap + 512], scalar=cw[:, g * 4 + tap:g * 4 + tap + 1],
                    in1=cacc[g][:], op0=ALU.mult, op1=ALU.add),
                    R=[b_cstg[g], b_cw, b_cacc[g]], W=[b_cacc[g]])
            S.op(ce, lambda e, g=g: e.tensor_copy(out=cstg[g][:, 0:3], in_=cstg[g][:, 512:515]),
                 R=[b_cstg[g]], W=[b_cstg[g]])
            if g < 2:
                silu_via_exp(cacc[g][:], [b_cacc[g]], stmp[:], b_stmp, sil[g][:], [b_sil[g]])
            else:
                silu_via_exp(cacc[g][:], [b_cacc[g]], stmp[:], b_stmp, vsT[:], [b_vsT])
        if dbg <= 3:
            continue
        for g in range(2):
            S.op("pool", lambda e, g=g: e.tensor_tensor(out=sq[:], in0=sil[g][:], in1=sil[g][:], op=ALU.mult),
                 R=[b_sil[g]], W=[b_sq])
            bk = 1 + g
            S.op("pe", lambda e, bk=bk: e.matmul(out=pb[bk][:], lhsT=ONES, rhs=sq[:], start=True, stop=True),
                 R=[b_sq, b_cf], W=[b_pb[bk]])
            S.op("act", lambda e, bk=bk: e.activation(out=rs[:], in_=pb[bk][:], func=AF.Ln, bias=C_EPS6),
                 R=[b_pb[bk], b_cst], W=[b_rs])
            S.op("act", lambda e: e.activation(out=rs[:], in_=rs[:], func=AF.Exp, scale=-0.5), R=[b_rs], W=[b_rs])
            if g == 0:
                S.op("dve", lambda e: e.scalar_tensor_tensor(out=qnT[:], in0=sil[0][:], scalar=float(128 ** -0.5),
                                                             in1=rs[:], op0=ALU.mult, op1=ALU.mult),
                     R=[b_sil[0], b_rs], W=[b_qnT])
            else:
                S.op("dve", lambda e: e.tensor_tensor(out=knT[:], in0=sil[1][:], in1=rs[:], op=ALU.mult),
                     R=[b_sil[1], b_rs], W=[b_knT])
        if dbg <= 4:
            continue
        def trkv(e):
            ins = None
            for tt in range(4):
                ins = e.transpose(out=pb0_bf[:, (2 * tt) * 128:(2 * tt + 1) * 128],
                                  in_=knT[:, tt * 128:(tt + 1) * 128], identity=idb[:])
                ins = e.transpose(out=pb0_bf[:, (2 * tt + 1) * 128:(2 * tt + 2) * 128],
                                  in_=vsT[:, tt * 128:(tt + 1) * 128], identity=idb[:])
            return ins
        S.op("pe", trkv, R=[b_knT, b_vsT, b_idb], W=[b_pb[0]])
        S.op("dve", lambda e: e.tensor_copy(out=kvt[:].rearrange("p a d -> p (a d)"), in_=pb0_bf[:, 0:1024]),
             R=[b_pb[0]], W=[b_kvt])
        if dbg <= 5:
            continue
        P = lambda nm: pt[nm][0]
        B = lambda nm: pt[nm][1]
        S.op("act", lambda e: e.activation(out=P("eb")[:], in_=ba[:, :, 0], func=AF.Exp, scale=-1.0),
             R=[b_ba], W=[B("eb")])
        S.op("dve", lambda e: e.tensor_scalar(out=P("eb")[:], in0=P("eb")[:], scalar1=1.0, scalar2=None,
                                              op0=ALU.add), R=[B("eb")], W=[B("eb")])
        S.op("dve", lambda e: e.reciprocal(out=P("beta")[:], in_=P("eb")[:]), R=[B("eb")], W=[B("beta")])
        S.op("act", lambda e: e.activation(out=P("tmpa")[:], in_=ba[:, :, 1], func=AF.Exp, bias=sc[:, 1:2]),
             R=[b_ba, b_sc], W=[B("tmpa")])
        S.op("act", lambda e: e.activation(out=P("tmpa")[:], in_=P("tmpa")[:], func=AF.Ln, bias=C_ONE),
             R=[B("tmpa"), b_cst], W=[B("tmpa")])
        S.op("dve", lambda e: e.tensor_scalar(out=P("g")[:], in0=P("tmpa")[:], scalar1=C_NA, scalar2=None,
                                              op0=ALU.mult), R=[B("tmpa"), b_cst], W=[B("g")])
        S.op("pe", lambda e: e.matmul(out=pb[3][:, 0:4], lhsT=TRI, rhs=P("g")[:], start=True, stop=True),
             R=[B("g"), b_cf], W=[b_pb[3]])
        S.op("act", lambda e: e.copy(out=P("gc")[:], in_=pb[3][:, 0:4]), R=[b_pb[3]], W=[B("gc")])

        def mmgl(e):
            e.matmul(out=pb[3][:, 0:4], lhsT=SEL63, rhs=P("gc")[:], start=True, stop=True)
            return e.matmul(out=pb[3][:, 4:8], lhsT=SEL127, rhs=P("gc")[:], start=True, stop=True)
        S.op("pe", mmgl, R=[B("gc"), b_cf], W=[b_pb[3]])
        eglv = egl[:].rearrange("p (t h) -> p t h", h=2)
        S.op("act", lambda e: e.activation(out=eglv[:, :, 0], in_=pb[3][:, 0:4], func=AF.Exp),
             R=[b_pb[3]], W=[b_egl])
        S.op("act", lambda e: e.activation(out=eglv[:, :, 1], in_=pb[3][:, 4:8], func=AF.Exp),
             R=[b_pb[3], b_egl], W=[b_egl])
        S.op("dve", lambda e: e.tensor_copy(out=P("glt")[0:64, :], in_=pb[3][0:64, 0:4]),
             R=[b_pb[3]], W=[B("glt")])
        S.op("dve", lambda e: e.tensor_copy(out=P("glt")[64:128, :], in_=pb[3][64:128, 4:8]),
             R=[b_pb[3], B("glt")], W=[B("glt")])
        S.op("dve", lambda e: e.tensor_tensor(out=P("ekd")[:], in0=P("glt")[:], in1=P("gc")[:], op=ALU.subtract),
             R=[B("glt"), B("gc")], W=[B("ekd")])
        S.op("act", lambda e: e.activation(out=P("ekd")[:], in_=P("ekd")[:], func=AF.Exp),
             R=[B("ekd")], W=[B("ekd")])
        S.op("act", lambda e: e.activation(out=P("egc")[:], in_=P("gc")[:], func=AF.Exp),
             R=[B("gc")], W=[B("egc")])
        S.op("dve", lambda e: e.tensor_tensor(out=P("bg")[:], in0=P("beta")[:], in1=P("egc")[:], op=ALU.mult),
             R=[B("beta"), B("egc")], W=[B("bg")])

        if dbg <= 6:
            continue
        def tile_gen(tt):
            gs = gset[tt % NS]
            bk = 4 + tt
            G = lambda nm, gs=gs: gs[nm][0]
            GB = lambda nm, gs=gs: gs[nm][1]
            csl = slice(tt * 128, (tt + 1) * 128)
            S.op("dve", lambda e, G=G, tt=tt: e.tensor_scalar(out=G("dg")[:], in0=IDF, scalar1=P("gc")[:, tt:tt + 1],
                                                                scalar2=None, op0=ALU.mult),
                 R=[b_cf, B("gc")], W=[GB("dg")])

            def mm_abb(e, G=G, bk=bk, csl=csl):
                e.matmul(out=pb[bk][:, 0:128], lhsT=ONES, rhs=G("dg")[:], start=True, stop=True)
                e.matmul(out=pb[bk][:, 128:256], lhsT=knT[:, csl], rhs=knT[:, csl], start=True, stop=True)
                return e.matmul(out=pb[bk][:, 256:384], lhsT=knT[:, csl], rhs=qnT[:, csl], start=True, stop=True)
            yield
            S.op("pe", mm_abb, R=[b_cf, GB("dg"), b_knT, b_qnT], W=[b_pb[bk]])
            S.op("dve", lambda e, G=G, bk=bk, tt=tt: e.tensor_scalar(
                out=G("Eb")[:], in0=pb[bk][:, 0:128], scalar1=P("gc")[:, tt:tt + 1], scalar2=None,
                op0=ALU.subtract), R=[b_pb[bk], B("gc")], W=[GB("Eb")])
            S.op("act", lambda e, G=G: e.activation(out=G("Eb")[:], in_=G("Eb")[:], func=AF.Abs),
                 R=[GB("Eb")], W=[GB("Eb")])
            S.op("act", lambda e, G=G: e.activation(out=G("Ds")[:], in_=G("Eb")[:], func=AF.Exp, scale=-1.0),
                 R=[GB("Eb")], W=[GB("Ds")])
            S.op("act", lambda e, G=G, bk=bk: e.activation(out=G("EA")[:], in_=pb[bk][:, 0:128], func=AF.Exp),
                 R=[b_pb[bk]], W=[GB("EA")])
            S.op("dve", lambda e, G=G, bk=bk, tt=tt: e.scalar_tensor_tensor(
                out=G("t1")[:], in0=pb[bk][:, 128:256], scalar=P("beta")[:, tt:tt + 1], in1=G("Ds")[:],
                op0=ALU.mult, op1=ALU.mult), R=[b_pb[bk], B("beta"), GB("Ds")], W=[GB("t1")])
            S.op("dve", lambda e, G=G: e.tensor_tensor(out=G("MTa")[:], in0=G("t1")[:], in1=MASKL, op=ALU.mult),
                 R=[GB("t1"), b_cf], W=[GB("MTa")])
            S.op("dve", lambda e, G=G, bk=bk: e.tensor_tensor(out=G("t2")[:], in0=pb[bk][:, 256:384], in1=G("Ds")[:],
                                                              op=ALU.mult), R=[b_pb[bk], GB("Ds")], W=[GB("t2")])
            S.op("pool", lambda e, G=G, tt=tt: e.tensor_tensor(out=apT[:, tt, :], in0=G("t2")[:], in1=MASKU,
                                                               op=ALU.mult), R=[GB("t2"), b_cf], W=[b_apT])
            S.op("pool", lambda e, G=G, csl=csl: e.tensor_tensor(out=qgT[:, csl], in0=qnT[:, csl], in1=G("EA")[:],
                                                                 op=ALU.mult), R=[b_qnT, GB("EA")], W=[b_qgT])
            yield
            S.op("pe", lambda e, G=G, bk=bk: e.transpose(out=pb[bk][:, 384:512], in_=G("MTa")[:], identity=IDF),
                 R=[GB("MTa"), b_cf], W=[b_pb[bk]])
            S.op("act", lambda e, G=G, bk=bk: e.copy(out=G("MPa")[:, 0:128], in_=pb[bk][:, 384:512]),
                 R=[b_pb[bk]], W=[GB("MPa")])
            S.op("dve", lambda e, G=G, bk=bk: e.tensor_tensor(out=G("MPb")[:, 128:256], in0=pb[bk][:, 384:512],
                                                              in1=IDF, op=ALU.add),
                 R=[b_pb[bk], b_cf], W=[GB("MPb")])

            def st0(e, G=G, bk=bk):
                e.matmul(out=pb[bk][:, 0:128], lhsT=G("MTa")[:], rhs=G("MPa")[:, 0:128], start=True, stop=True)
                return e.matmul(out=pb[bk][:, 128:256], lhsT=G("MPa")[:, 0:128], rhs=G("MTa")[:], start=True,
                                stop=True)
            yield
            S.op("pe", st0, R=[GB("MTa"), GB("MPa")], W=[b_pb[bk]])
            S.op("act", lambda e, G=G, bk=bk: e.copy(out=G("MPb")[:, 0:128], in_=pb[bk][:, 0:128]),
                 R=[b_pb[bk]], W=[GB("MPb")])
            S.op("dve", lambda e, G=G, bk=bk: e.tensor_copy(out=G("MTb")[:], in_=pb[bk][:, 128:256]),
                 R=[b_pb[bk]], W=[GB("MTb")])
            cur, nxt = ("MPb", "MTb"), ("MPa", "MTa")
            for stp in range(1, 5):
                def stj(e, G=G, bk=bk, cur=cur):
                    e.matmul(out=pb[bk][:, 0:256], lhsT=G(cur[1])[:], rhs=G(cur[0])[:, 0:256], start=True, stop=True)
                    return e.matmul(out=pb[bk][:, 256:384], lhsT=G(cur[0])[:, 0:128], rhs=G(cur[1])[:], start=True,
                                    stop=True)
                yield
                S.op("pe", stj, R=[GB(cur[0]), GB(cur[1])], W=[b_pb[bk]])
                S.op("act", lambda e, G=G, bk=bk, nxt=nxt: e.copy(out=G(nxt[0])[:, 0:128], in_=pb[bk][:, 0:128]),
                     R=[b_pb[bk]], W=[GB(nxt[0])])
                S.op("dve", lambda e, G=G, bk=bk, cur=cur, nxt=nxt: e.tensor_tensor(
                    out=G(nxt[0])[:, 128:256], in0=pb[bk][:, 128:256], in1=G(cur[0])[:, 128:256], op=ALU.add),
                    R=[b_pb[bk], GB(cur[0])], W=[GB(nxt[0])])
                S.op("act", lambda e, G=G, bk=bk, nxt=nxt: e.copy(out=G(nxt[1])[:], in_=pb[bk][:, 256:384]),
                     R=[b_pb[bk]], W=[GB(nxt[1])])
                cur, nxt = nxt, cur
            yield
            S.op("pe", lambda e, G=G, bk=bk, cur=cur: e.matmul(out=pb[bk][:, 0:128], lhsT=G(cur[1])[:],
                                                               rhs=G(cur[0])[:, 128:256], start=True, stop=True),
                 R=[GB(cur[0]), GB(cur[1])], W=[b_pb[bk]])
            S.op("dve", lambda e, G=G, bk=bk, cur=cur: e.tensor_tensor(out=G("TT")[:], in0=pb[bk][:, 0:128],
                                                                       in1=G(cur[0])[:, 128:256], op=ALU.add),
                 R=[b_pb[bk], GB(cur[0])], W=[GB("TT")])
            S.op("pool", lambda e, G=G, tt=tt: e.tensor_scalar(out=G("vb")[:], in0=kvt[:, 2 * tt + 1, :],
                                                               scalar1=P("beta")[:, tt:tt + 1], scalar2=None,
                                                               op0=ALU.mult), R=[b_kvt, B("beta")], W=[GB("vb")])
            S.op("pool", lambda e, G=G, tt=tt: e.tensor_scalar(out=G("kbg")[:], in0=kvt[:, 2 * tt, :],
                                                               scalar1=P("bg")[:, tt:tt + 1], scalar2=None,
                                                               op0=ALU.mult), R=[b_kvt, B("bg")], W=[GB("kbg")])
            S.op("pool", lambda e, tt=tt: e.tensor_scalar(out=kd_sb[:, tt, :], in0=kvt[:, 2 * tt, :],
                                                          scalar1=P("ekd")[:, tt:tt + 1], scalar2=None,
                                                          op0=ALU.mult), R=[b_kvt, B("ekd")], W=[b_kd])

            def mm_uw(e, G=G, bk=bk):
                e.matmul(out=pb[bk][:, 0:128], lhsT=G("TT")[:], rhs=G("vb")[:], start=True, stop=True)
                return e.matmul(out=pb[bk][:, 128:256], lhsT=G("kbg")[:], rhs=G("TT")[:], start=True, stop=True)
            yield
            S.op("pe", mm_uw, R=[GB("TT"), GB("vb"), GB("kbg")], W=[b_pb[bk]])
            S.op("act", lambda e, bk=bk, tt=tt: e.copy(out=u_sb[:, tt, :], in_=pb[bk][:, 0:128]),
                 R=[b_pb[bk]], W=[b_u])
            S.op("act", lambda e, bk=bk, csl=csl: e.copy(out=wT_sb[:, csl], in_=pb[bk][:, 128:256]),
                 R=[b_pb[bk]], W=[b_wT])

        gens = [tile_gen(tt) for tt in range(4)]
        while gens:
            for g_ in gens[:]:
                try:
                    next(g_)
                except StopIteration:
                    gens.remove(g_)
            pull(NPULL)
        if dbg <= 7:
            continue
        S.defer = scan_l
        oi = j % 2
        for tt in range(4):
            csl = slice(tt * 128, (tt + 1) * 128)
            osb, b_o = o_sb[tt % 2], b_osb[tt % 2]
            for hh in range(2):
                r = slice(hh * 64, hh * 64 + 64)
                nl = 2 * tt + hh

                def mm1(e, csl=csl):
                    e.matmul(out=pb[6][:, 0:128], lhsT=wT_sb[:, csl], rhs=Sb[:], start=True, stop=True)
                    return e.matmul(out=pb[7][:, 0:128], lhsT=qgT[:, csl], rhs=Sb[:], start=True, stop=False)
                S.op("pe", mm1, R=[b_wT, b_qgT, b_Sb], W=[b_pb[6], b_pb[7]])
                S.op("dve", lambda e, r=r, tt=tt: e.tensor_tensor(out=vn[r, :], in0=u_sb[r, tt, :],
                                                                  in1=pb[6][r, 0:128], op=ALU.subtract),
                     R=[b_u, b_pb[6]], W=[b_vn])

                def mm2(e, r=r, tt=tt):
                    e.matmul(out=pb[7][:, 0:128], lhsT=apT[r, tt, :], rhs=vn[r, :], start=False, stop=True)
                    return e.matmul(out=pb[7][:, 128:256], lhsT=kd_sb[r, tt, :], rhs=vn[r, :], start=True, stop=True)
                S.op("pe", mm2, R=[b_apT, b_kd, b_vn], W=[b_pb[7]])
                S.op("act", lambda e, r=r, osb=osb: e.copy(out=osb[r, :], in_=pb[7][r, 0:128]),
                     R=[b_pb[7]], W=[b_o])
                S.op("dve", lambda e, nl=nl: e.scalar_tensor_tensor(out=S_f[:], in0=S_f[:], scalar=egl[:, nl:nl + 1],
                                                                    in1=pb[7][:, 128:256], op0=ALU.mult,
                                                                    op1=ALU.add),
                     R=[b_Sf, b_egl, b_pb[7]], W=[b_Sf])
                S.op("act", lambda e: e.copy(out=Sb[:], in_=S_f[:]), R=[b_Sf], W=[b_Sb])
            S.op("act", lambda e, osb=osb: e.activation(out=junkf[:], in_=osb[:], func=AF.Square,
                                                        accum_out=stat[:, 1:2]), R=[b_o], W=[b_junkf, b_stat])
            rstd_from_ss(1, 128, C_EPS6)
            S.op("dve", lambda e, osb=osb: e.scalar_tensor_tensor(out=on_[:], in0=osb[:], scalar=stat[:, 1:2],
                                                                  in1=lnw[:], op0=ALU.mult, op1=ALU.mult),
                 R=[b_o, b_stat, b_lnw], W=[b_on])
            S.op("dve", lambda e, tt=tt, oi=oi, zz=zs2[j % 2]: e.tensor_tensor(out=ola_st[oi][:, tt, :], in0=on_[:], in1=zz[:, tt, :],
                                                                op=ALU.mult), R=[b_on, b_zs2[j % 2]], W=[b_olast[oi]])
        S.dma("sp", ola_v[j], ola_st[oi][:], b_olast[oi], R=[b_olast[oi]])
        S.defer = None

        if dbg <= 8:
            pull_scan(len(scan_l))
            continue
        pull(len(pend))
        nscan = max(1, -(-len(scan_l) // (8 * j + 6)))
        for qq in range(2):
            qc = 2 * j + qq
            q0 = qc * 256
            qsl = slice(qq * 256, qq * 256 + 256)
            oi2 = qc % 2
            nkt = 2 * qc + 2
            def emit_qk(kt, qq=qq, qd=Qblk[j % 2], bq=b_Qblk[j % 2]):
                k0 = kt * 128
                bk = 4 + (kt % 2)
                S.op("pe", lambda e, bk=bk, k0=k0, qq=qq, qd=qd: e.matmul(
                    out=pb[bk][:, 0:512], lhsT=KdT[:, k0:k0 + 128], rhs=qd[:, qq, :], start=True, stop=True),
                    R=[b_KdT, bq], W=[b_pb[bk]])

            def emit_exp(kt, q0=q0):
                k0 = kt * 128
                d = q0 - k0
                par = kt % 2
                bk = 4 + par
                if d >= 256:
                    S.op("act", lambda e, bk=bk, par=par: e.activation(
                        out=pT2[par][:], in_=pb[bk][:, 0:512], func=AF.Exp, scale=0.125, bias=sc[:, 2:3]),
                        R=[b_pb[bk], b_sc], W=[b_pT2[par]])
                else:
                    for m in range(2):
                        S.op("dve", lambda e, bk=bk, m=m, d=d: e.scalar_tensor_tensor(
                            out=s2w[:, m * 256:(m + 1) * 256], in0=pb[bk][:, m * 256:(m + 1) * 256], scalar=0.125,
                            in1=bt[:, d + 128:d + 128 + 256], op0=ALU.mult, op1=ALU.add),
                            R=[b_pb[bk], b_bt], W=[b_s2w])
                    S.op("act", lambda e, par=par: e.activation(out=pT2[par][:], in_=s2w[:], func=AF.Exp),
                         R=[b_s2w], W=[b_pT2[par]])

            def emit_pv(kt, qc=qc):
                par = kt % 2
                for m in range(2):
                    for qb in range(2):
                        klast = 2 * qc + qb
                        if kt > klast:
                            continue
                        ab = qb * 2 + m
                        c0 = m * 256 + qb * 128
                        S.op("pe", lambda e, ab=ab, par=par, c0=c0, kt=kt, klast=klast: e.matmul(
                            out=pb[ab][:, 0:129], lhsT=pT2[par][:, c0:c0 + 128],
                            rhs=Vaug[:, kt, 0:129], start=(kt == 0), stop=(kt == klast)),
                            R=[b_pT2[par], b_Vaug], W=[b_pb[ab]])

            emit_qk(0)
            for kt in range(nkt):
                if kt + 1 < nkt:
                    emit_qk(kt + 1)
                emit_exp(kt)
                emit_pv(kt)
                pull_scan(nscan)
            for qb in range(2):
                a1, a2 = qb * 2, qb * 2 + 1
                S.op("dve", lambda e, a1=a1: e.reciprocal(out=rden[:, 0:1], in_=pb[a1][:, 128:129]),
                     R=[b_pb[a1]], W=[b_rden])
                S.op("dve", lambda e, a2=a2: e.reciprocal(out=rden[:, 1:2], in_=pb[a2][:, 128:129]),
                     R=[b_pb[a2], b_rden], W=[b_rden])
                S.op("dve", lambda e: e.tensor_scalar(out=rden[:, 2:3], in0=rden[:, 1:2], scalar1=C_NLAM,
                                                      scalar2=None, op0=ALU.mult), R=[b_rden, b_cst], W=[b_rden])
                S.op("act", lambda e, a1=a1: e.activation(out=O1[:], in_=pb[a1][:, 0:128], func=AF.Copy,
                                                          scale=rden[:, 0:1]), R=[b_pb[a1], b_rden], W=[b_O1])
                S.op("dve", lambda e, a2=a2: e.scalar_tensor_tensor(out=odf[:], in0=pb[a2][:, 0:128],
                                                                    scalar=rden[:, 2:3], in1=O1[:], op0=ALU.mult,
                                                                    op1=ALU.add),
                     R=[b_pb[a2], b_rden, b_O1], W=[b_odf])
                S.op("act", lambda e: e.activation(out=junkf[:], in_=odf[:], func=AF.Square,
                                                   accum_out=stat[:, 2:3]), R=[b_odf], W=[b_junkf, b_stat])
                rstd_from_ss(2, 128, C_EPS5)
                S.op("dve", lambda e, qb=qb, oi2=oi2: e.scalar_tensor_tensor(
                    out=od_st[oi2][:, qb, :], in0=odf[:], scalar=stat[:, 2:3], in1=dnw[:], op0=ALU.mult,
                    op1=ALU.mult), R=[b_odf, b_stat, b_dnw], W=[b_odst[oi2]])
            S.dma("sp", od_v[qc], od_st[oi2][:], b_odst[oi2], R=[b_odst[oi2]])
        pull_scan(len(scan_l))

    if ext:
        return None
    S.barrier_wait("sp", b_olast + b_odst)
    return kb.done()


def _t5_bucket_np(rel):
    n = np.maximum(rel, 0)
    nf = np.maximum(n, 1).astype(np.float32)
    large = 16 + (np.log(nf / np.float32(16)) / np.float32(math.log(128 / 16)) * np.float32(16)).astype(np.int32)
    large = np.minimum(large, 31)
    return np.where(n < 16, n, large)


def _mixer_consts():
    p = np.arange(128)
    same = (p[:, None] // 64) == (p[None, :] // 64)
    ident = np.eye(128, dtype=np.float32)
    ones = np.ones((128, 128), np.float32)
    maskl = np.where(same & (p[:, None] > p[None, :]), -1.0, 0.0).astype(np.float32)
    masku = np.where(same & (p[:, None] <= p[None, :]), 1.0, 0.0).astype(np.float32)
    tri = masku.copy()
    sel63 = np.zeros((128, 128), np.float32)
    sel63[63, :] = 1.0
    sel127 = np.zeros((128, 128), np.float32)
    sel127[127, :] = 1.0
    return np.ascontiguousarray(np.concatenate([ident, ones, maskl, masku, tri, sel63, sel127], axis=1))


def mixer_inputs(xb, l, h, P):
    w_in = P["w_in"][l]
    cols = np.concatenate([
        np.arange(h * 128, (h + 1) * 128),
        512 + np.arange(h * 128, (h + 1) * 128),
        1024 + np.arange(h * 128, (h + 1) * 128),
        2056 + np.arange(h * 128, (h + 1) * 128),
        2568 + np.arange(h * 128, (h + 1) * 128),
        1536 + np.arange(h * 128, (h + 1) * 128),
        3080 + np.arange(h * 128, (h + 1) * 128),
        np.array([2048 + h]),
        np.array([2052 + h]),
    ])
    wh = np.ascontiguousarray(w_in[:, cols])
    anw = np.ascontiguousarray(P["attn_norm_w"][l].reshape(8, 128).T)
    cwl = P["conv_w"][l]
    cw = np.concatenate([cwl[:, g * 512 + h * 128: g * 512 + (h + 1) * 128].T for g in range(3)], axis=1)
    sc = np.zeros((128, 4), np.float32)
    sc[:, 0] = P["a_log"][l, h]
    sc[:, 1] = P["dt_bias"][l, h]
    sc[:, 2] = P["rel_bias"][31, h]
    lamv = np.concatenate([P["lambda_q1"][l], P["lambda_k1"][l], P["lambda_q2"][l], P["lambda_k2"][l]])
    lamv = np.broadcast_to(lamv[None, :], (128, 256))
    lnw = np.broadcast_to(P["la_norm_w"][l][None, :], (128, 128))
    dnw = np.broadcast_to(P["diff_norm_w"][l][None, :], (128, 128))
    kl = np.arange(128)[:, None]
    jj = np.arange(512)[None, :]
    rel = jj - 128 - kl
    bt = np.where(rel >= 0, P["rel_bias"][_t5_bucket_np(rel), h], np.float32(-30000.0)).astype(np.float32)
    c = np.ascontiguousarray
    return {"x": c(xb), "wh": wh, "anw": anw, "cw": c(cw.astype(np.float32)), "sc": sc,
            "lamv": c(lamv.astype(np.float32)), "lnw": c(lnw.astype(np.float32)),
            "dnw": c(dnw.astype(np.float32)), "btoep": c(bt), "ident_bf": _ident_bf(), "cf": _mixer_consts()}


CC_GROUPS = [[0, 1, 2, 3], [4, 5, 6, 7]]
_MIX_KEYS = ("wh", "anw", "cw", "sc", "lamv", "lnw", "dnw")


def build_fused(T):
    kb = KB()
    nc, S = kb.nc, kb.S
    NT = T // NH
    NTL = NT // 128
    I32 = mybir.dt.int32
    shp = {"wh": [D_MODEL, NW], "anw": [128, 8], "cw": [128, 12], "sc": [128, 4], "lamv": [128, 256],
           "lnw": [128, 128], "dnw": [128, 128]}
    x_d = kb.din("x", [T, D_MODEL], F32)
    xs_d = kb.din("xs", [NT, D_MODEL], F32)
    idx_d = kb.din("idx", [128, NH * NTL], I32)
    bt_d = kb.din("btoep", [128, 512], F32)
    idb_d = kb.din("ident_bf", [128, 128], BF16)
    cf_d = kb.din("cf", [128, 7 * 128], F32)
    fin_d = kb.din("final_w_bc", [128, D_MODEL], F32)
    lay = []
    for l in range(DEPTH):
        d = {k: kb.din("%s%d" % (k, l), shp[k], F32) for k in _MIX_KEYS}
        d["w_out"] = kb.din("w_out%d" % l, [D_MODEL, D_MODEL], F32)
        d["w_gu"] = kb.din("w_gu%d" % l, [D_MODEL, 2 * D_FF], F32)
        d["w_down"] = kb.din("w_down%d" % l, [D_FF, D_MODEL], F32)
        d["ffn_norm_w"] = kb.din("fnw%d" % l, [128, 8], F32)
        lay.append(d)
    y_d = kb.dout("y", [NT, D_MODEL], F32)
    og_in = kb.dint("og_in", [T, 256], BF16)
    og_all = kb.dint("og_all", [NH * T, 256], BF16)
    xs_in = kb.dint("xs_in", [NT, D_MODEL], F32)
    x1_all = kb.dint("x1_all", [T, D_MODEL], F32)

    pb = [kb.ps([128, 512], F32, "pb%d" % i) for i in range(8)]
    b_pb = S.bufs(8, "pb", excl=True)
    ORC = min(T, 2048)
    NKO = T // ORC
    XRC = min(NT, 256)
    NKX = NT // XRC
    b_og_all = S.bufs(NKO, "og_all")
    b_x1all = S.bufs(NKX, "x1_all")
    kb.persist = b_pb + b_og_all + b_x1all

    def allgather(src, dst, b_dst, name):
        S.dma_fn("pool", lambda e: e.collective_compute("AllGather", ALU.bypass, replica_groups=CC_GROUPS,
                                                         ins=[src], outs=[dst]),
                 S.buf(name), W=[b_dst], inc=None)

    def x1_tile(t):
        tok = t * 128
        r, w = divmod(tok, NT)
        k, ww = divmod(w, XRC)
        base = k * (NH * XRC) + r * XRC + ww
        return x1_all[base:base + 128, :]

    for l in range(DEPTH):
        lam_init = 0.8 - 0.6 * math.exp(-0.3 * l)
        kb.begin_phase()
        if l > 0:
            for k in range(NKX):
                allgather(xs_in[k * XRC:(k + 1) * XRC, :], x1_all[k * NH * XRC:(k + 1) * NH * XRC, :], b_x1all[k],
                          "ccx%d_%d" % (l, k))
        ext = {"kb": kb, "pb": pb, "b_pb": b_pb, "x": x_d, "x_tile": (None if l == 0 else x1_tile),
               "b_x": (None if l == 0 else (lambda t: [b_x1all[((t * 128) % NT) // XRC]])), "og": og_in,
               "btoep": bt_d, "ident_bf": idb_d, "cf": cf_d}
        for k in _MIX_KEYS:
            ext[k] = lay[l][k]
        build_mixer(T, lam_init, ext=ext)
        kb.end_phase()
        kb.begin_phase()
        for k in range(NKO):
            allgather(og_in[k * ORC:(k + 1) * ORC, :], og_all[k * NH * ORC:(k + 1) * NH * ORC, :], b_og_all[k],
                      "cco%d_%d" % (l, k))
        last = (l == DEPTH - 1)
        ext = {"kb": kb, "pb": pb, "b_pb": b_pb, "x": (xs_d if l == 0 else xs_in), "y": (y_d if last else xs_in),
               "og_all": og_all, "b_og_all": b_og_all, "idx": idx_d, "T": T,
               "w_out": lay[l]["w_out"], "w_gu": lay[l]["w_gu"], "w_down": lay[l]["w_down"],
               "ffn_norm_w": lay[l]["ffn_norm_w"], "ident_bf": idb_d, "final_w_bc": fin_d}
        build_ffn(NT, last, ext=ext)
        kb.end_phase()
    kb.es.close()
    return nc


_FUSED_CACHE = {}


def _gather_idx(T, h):
    NT = T // NH
    NTL = NT // 128
    ORC = min(T, 2048)
    tok = h * NT + np.arange(NTL)[None, :] * 128 + np.arange(128)[:, None]
    k, w = tok // ORC, tok % ORC
    cols = [k * (NH * ORC) + r * ORC + w for r in range(NH)]
    return np.ascontiguousarray(np.concatenate(cols, axis=1).astype(np.int32))


def fused_inputs(P, T):
    NT = T // NH
    NTL = NT // 128
    perm = np.concatenate([np.concatenate([np.arange(r * 128, (r + 1) * 128),
                                           512 + np.arange(r * 128, (r + 1) * 128)]) for r in range(NH)])
    in_maps = []
    for c in range(NCORES):
        b, h = divmod(c, NH)
        xb = P["x"][b, :T]
        m = {"x": np.ascontiguousarray(xb), "xs": np.ascontiguousarray(xb[h * NT:(h + 1) * NT]),
             "idx": _gather_idx(T, h),
             "final_w_bc": np.ascontiguousarray(np.broadcast_to(P["final_norm_w"][None, :], (128, D_MODEL)))}
        for l in range(DEPTH):
            mi = mixer_inputs(xb, l, h, P)
            for k in _MIX_KEYS:
                m["%s%d" % (k, l)] = mi[k]
            if l == 0:
                m["btoep"], m["ident_bf"], m["cf"] = mi["btoep"], mi["ident_bf"], mi["cf"]
            m["w_out%d" % l] = np.ascontiguousarray(P["w_out"][l][perm])
            m["w_gu%d" % l] = P["w_gate_up"][l]
            m["w_down%d" % l] = P["w_down"][l]
            m["fnw%d" % l] = np.ascontiguousarray(P["ffn_norm_w"][l].reshape(8, 128).T)
        in_maps.append(m)
    return in_maps


def kernel_fused(P, T):
    if T not in _FUSED_CACHE:
        _FUSED_CACHE[T] = build_fused(T)
    nc = _FUSED_CACHE[T]
    NT = T // NH
    res = run_bass_kernel_spmd(nc, fused_inputs(P, T), core_ids=list(range(NCORES)))
    out = np.empty((BATCH, T, D_MODEL), np.float32)
    for c in range(NCORES):
        b, h = divmod(c, NH)
        out[b, h * NT:(h + 1) * NT] = np.asarray(res.results[c]["y"])
    return out


_MIX_CACHE = {}


def kernel(**inputs):
    P = {k: np.ascontiguousarray(np.asarray(v, dtype=np.float32)) for k, v in inputs.items()}
    return kernel_fused(P, P["x"].shape[1])


def kernel_unfused(**inputs):
    P = {k: np.ascontiguousarray(np.asarray(v, dtype=np.float32)) for k, v in inputs.items()}
    x = P["x"]
    B, T, D = x.shape
    NTOK = B * T
    per = NTOK // NCORES
    for l in range(DEPTH):
        lam_init = 0.8 - 0.6 * math.exp(-0.3 * l)
        key = (T, l)
        if key not in _MIX_CACHE:
            _MIX_CACHE[key] = build_mixer(T, lam_init)
        nc = _MIX_CACHE[key]
        in_maps = [mixer_inputs(x[c // NH], l, c % NH, P) for c in range(NCORES)]
        res = run_bass_kernel_spmd(nc, in_maps, core_ids=list(range(NCORES)))
        o = np.empty((B, T, D), dtype=ml_dtypes.bfloat16)
        for c in range(NCORES):
            b, h = divmod(c, NH)
            o[b, :, h * 128:(h + 1) * 128] = np.asarray(res.results[c]["o_la"])
            o[b, :, 512 + h * 128:512 + (h + 1) * 128] = np.asarray(res.results[c]["o_d"])
        xs = x.reshape(NTOK, D)
        os_ = o.reshape(NTOK, D)
        ys = run_ffn([xs[c * per:(c + 1) * per] for c in range(NCORES)],
                     [os_[c * per:(c + 1) * per] for c in range(NCORES)],
                     P["w_out"][l], P["ffn_norm_w"][l], P["w_gate_up"][l], P["w_down"][l],
                     P["final_norm_w"] if l == DEPTH - 1 else None)
        x = np.concatenate([np.asarray(y) for y in ys], axis=0).reshape(B, T, D)
    return np.ascontiguousarray(x.astype(np.float32))
```

```python
import math
from contextlib import ExitStack

import numpy as np
import ml_dtypes
import concourse.bass as bass
import concourse.mybir as mybir
from concourse.bass_utils import run_bass_kernel_spmd

F32 = mybir.dt.float32
BF16 = mybir.dt.bfloat16
AF = mybir.ActivationFunctionType
ALU = mybir.AluOpType
AX = mybir.AxisListType

D_MODEL = 1024
SEQ = 8192
BATCH = 2
DEPTH = 2
NH = 4
D_FF = 2816
IN_DIM = 3592
NORM_EPS = 1e-6
NCORES = 8

ENGS = ("pe", "act", "dve", "pool", "sp")


class Buf:
    __slots__ = ("name", "w", "r", "dsem", "dcnt", "excl")

    def __init__(self, name, excl=False):
        self.name = name
        self.excl = excl
        self.w = None
        self.r = []
        self.dsem = None
        self.dcnt = 0


class Op:
    __slots__ = ("eng", "fn", "deps", "dma", "sem", "val", "needed", "inc")

    def __init__(self, eng, fn, dma=False):
        self.eng = eng
        self.fn = fn
        self.deps = []
        self.dma = dma
        self.sem = None
        self.val = 0
        self.needed = False
        self.inc = 16


class Sched:
    def __init__(self, nc, es):
        self.nc = nc
        self.es = es
        self.ops = {e: [] for e in ENGS}
        self.esem = {e: es.enter_context(nc.semaphore("s_" + e)) for e in ENGS}
        self.nbuf = 0
        self.cnt = {e: 0 for e in ENGS}
        self.phase_dmas = []
        self.nsem = 0
        self.defer = None

    def buf(self, name=None, excl=False):
        self.nbuf += 1
        return Buf(name or ("b%d" % self.nbuf), excl)

    def bufs(self, n, name="b", excl=False):
        return [self.buf("%s%d" % (name, i), excl) for i in range(n)]

    def _link(self, o, R, W):
        deps = []
        for b in R:
            if b.w is not None:
                d = b.w
                if d.dma or o.dma or d.eng != o.eng or o.eng != "pe":
                    deps.append(d)
            if b.excl:
                for d in b.r:
                    if d.eng != o.eng:
                        deps.append(d)
        for b in W:
            if b.w is not None:
                d = b.w
                if d.dma or o.dma or d.eng != o.eng or o.eng != "pe":
                    deps.append(d)
            for d in b.r:
                if d.dma or o.dma or d.eng != o.eng or o.eng != "pe":
                    deps.append(d)
        o.deps = deps
        for b in R:
            if b in W:
                continue
            if b.excl:
                b.r = []
            elif not o.dma:
                b.r = [x for x in b.r if x.dma or x.eng != o.eng]
            b.r.append(o)
        for b in W:
            b.w = o
            b.r = []

    def op(self, eng, fn, R=(), W=()):
        if self.defer is not None:
            self.defer.append(lambda: self.op_now(eng, fn, R, W))
            return None
        return self.op_now(eng, fn, R, W)

    def op_now(self, eng, fn, R=(), W=()):
        o = Op(eng, fn)
        self._link(o, R, W)
        self.ops[eng].append(o)
        return o

    def dma(self, q, out, in_, sb, R=(), W=()):
        return self.dma_fn(q, lambda e, out=out, in_=in_: e.dma_start(out=out, in_=in_), sb, R, W)

    def dma_fn(self, q, fn, sb, R=(), W=(), inc=16):
        if self.defer is not None:
            self.defer.append(lambda: self.dma_fn_now(q, fn, sb, R, W, inc))
            return None
        return self.dma_fn_now(q, fn, sb, R, W, inc)

    def dma_fn_now(self, q, fn, sb, R=(), W=(), inc=16):
        if sb.dsem is None:
            self.nsem += 1
            sb.dsem = self.es.enter_context(self.nc.semaphore("d%d_%s" % (self.nsem, sb.name)))
        o = Op(q, fn, dma=True)
        o.inc = inc
        sb.dcnt += (inc if inc else 1)
        o.sem = sb.dsem
        o.val = sb.dcnt
        self._link(o, R, W)
        self.ops[q].append(o)
        self.phase_dmas.append(o)
        return o

    def phase_barrier(self):
        lasts = []
        for e in ENGS:
            real = [o for o in self.ops[e] if o.fn is not None and not o.dma]
            if real:
                lasts.append(real[-1])
        deps = lasts + list(self.phase_dmas)
        for e in ENGS:
            o = Op(e, None)
            o.deps = [d for d in deps if d.dma or d.eng != e]
            self.ops[e].append(o)
        self.phase_dmas = []

    def barrier_wait(self, eng, R):
        o = Op(eng, None)
        self._link(o, (), R)
        self.ops[eng].append(o)
        return o

    def finalize(self):
        for e in ENGS:
            for o in self.ops[e]:
                for d in o.deps:
                    d.needed = True
        for e in ENGS:
            c = self.cnt[e]
            for o in self.ops[e]:
                if not o.dma and o.needed and o.fn is not None:
                    c += 1
                    o.sem = self.esem[e]
                    o.val = c
            self.cnt[e] = c
        ops = self.ops
        self.ops = {e: [] for e in ENGS}

        def run(eng, lst):
            seen = {}
            for o in lst:
                waits = {}
                for d in o.deps:
                    k = id(d.sem)
                    if k not in waits or waits[k][1] < d.val:
                        waits[k] = (d.sem, d.val)
                for k, (sem, val) in waits.items():
                    if seen.get(k, 0) < val:
                        eng.wait_ge(sem, val)
                        seen[k] = val
                if o.fn is None:
                    continue
                ins = o.fn(eng)
                if o.dma:
                    if o.inc:
                        ins.then_inc(o.sem, o.inc)
                    else:
                        ins.then_inc(o.sem)
                elif o.needed:
                    ins.then_inc(o.sem, 1)

        with self.nc.Block() as block:
            @block.tensor
            def _(e):
                run(e, ops["pe"])

            @block.scalar
            def _(e):
                run(e, ops["act"])

            @block.vector
            def _(e):
                run(e, ops["dve"])

            @block.gpsimd
            def _(e):
                run(e, ops["pool"])

            @block.sync
            def _(e):
                run(e, ops["sp"])


class KB:
    def __init__(self):
        self.nc = bass.Bass("TRN2", target_bir_lowering=False)
        self.es = ExitStack()
        self.S = Sched(self.nc, self.es)
        self.n = 0
        self.pes = None
        self.phase = 0

    def begin_phase(self):
        self.phase += 1
        self.pes = ExitStack()

    def end_phase(self):
        self.S.phase_barrier()
        self.S.finalize()
        self.pes.close()
        self.pes = None
        for b in getattr(self, "persist", []):
            b.w = None
            b.r = []

    def sb(self, shape, dt, name=None):
        self.n += 1
        st = self.pes if self.pes is not None else self.es
        return st.enter_context(self.nc.sbuf_tensor("sb%d_" % self.phase + (name or ("t%d" % self.n)), list(shape), dt))

    def dint(self, name, shape, dt):
        return self.nc.dram_tensor(name, list(shape), dt).ap()

    def ps(self, shape, dt, name=None):
        self.n += 1
        return self.es.enter_context(self.nc.psum_tensor("ps_" + (name or ("p%d" % self.n)), list(shape), dt))

    def din(self, name, shape, dt):
        return self.nc.dram_tensor(name, list(shape), dt, kind="ExternalInput").ap()

    def dout(self, name, shape, dt):
        return self.nc.dram_tensor(name, list(shape), dt, kind="ExternalOutput").ap()

    def done(self):
        self.S.finalize()
        self.es.close()
        return self.nc


def build_ffn(NT, final_norm, ext=None):
    kb = ext["kb"] if ext else KB()
    nc, S = kb.nc, kb.S
    NTL = NT // 128
    KD = D_MODEL // 128
    JF = D_FF // 128

    if ext:
        x_d, wout_d, wgu_d, wdn_d, fnw_d, idb_d, y_d = (
            ext[k] for k in ("x", "w_out", "w_gu", "w_down", "ffn_norm_w", "ident_bf", "y"))
        if final_norm:
            fin_d = ext["final_w_bc"]
        o_d = None
    else:
        x_d = kb.din("x", [NT, D_MODEL], F32)
        o_d = kb.din("o", [NT, D_MODEL], BF16)
        wout_d = kb.din("w_out", [D_MODEL, D_MODEL], F32)
        wgu_d = kb.din("w_gu", [D_MODEL, 2 * D_FF], F32)
        wdn_d = kb.din("w_down", [D_FF, D_MODEL], F32)
        fnw_d = kb.din("ffn_norm_w", [128, KD], F32)
        idb_d = kb.din("ident_bf", [128, 128], BF16)
        if final_norm:
            fin_d = kb.din("final_w_bc", [128, D_MODEL], F32)
        y_d = kb.dout("y", [NT, D_MODEL], F32)

    wout = kb.sb([128, KD, D_MODEL], BF16, "wout")
    wgu = kb.sb([128, KD, 2 * D_FF], BF16, "wgu")
    wdn = kb.sb([128, JF, D_MODEL], BF16, "wdn")
    fnw = kb.sb([128, KD], F32, "fnw")
    idb = kb.sb([128, 128], BF16, "idb")
    b_wout, b_wgu, b_wdn, b_fnw, b_idb = S.bufs(5, "wres")
    if final_norm:
        finw = kb.sb([128, D_MODEL], F32, "finw")
        b_finw = S.buf("finw")
        S.dma("sp", finw[:], fin_d, b_finw, W=[b_finw])
    S.dma("sp", fnw[:], fnw_d, b_fnw, W=[b_fnw])
    S.dma("sp", idb[:], idb_d, b_idb, W=[b_idb])

    STG = 1408
    NSTG = 3
    stg = [kb.sb([128, STG], F32, "stg%d" % i) for i in range(NSTG)]
    b_stg = S.bufs(NSTG, "stg")
    cnt = [0]
    cast_engs = ("dve", "act")

    def load_cast(dst_ap, src_ap, n, wbuf, scale_ap=None):
        i = cnt[0] % NSTG
        q = "sp" if (cnt[0] % 2 == 0) else "pool"
        S.dma(q, stg[i][:, 0:n], src_ap, b_stg[i], W=[b_stg[i]])
        ce = cast_engs[cnt[0] % 2]
        if ce == "act":
            if scale_ap is None:
                S.op("act", lambda e, d=dst_ap, s=stg[i][:, 0:n]: e.copy(out=d, in_=s), R=[b_stg[i]], W=[wbuf])
            else:
                S.op("act", lambda e, d=dst_ap, s=stg[i][:, 0:n], sc=scale_ap:
                     e.activation(out=d, in_=s, func=AF.Copy, scale=sc), R=[b_stg[i], b_fnw], W=[wbuf])
        elif scale_ap is None:
            S.op(ce, lambda e, d=dst_ap, s=stg[i][:, 0:n]: e.tensor_copy(out=d, in_=s),
                 R=[b_stg[i]], W=[wbuf])
        else:
            S.op(ce, lambda e, d=dst_ap, s=stg[i][:, 0:n], sc=scale_ap:
                 e.tensor_scalar(out=d, in0=s, scalar1=sc, scalar2=None, op0=ALU.mult),
                 R=[b_stg[i], b_fnw], W=[wbuf])
        cnt[0] += 1

    wout_v = wout_d.rearrange("(ko p) n -> p ko n", p=128)
    for ko in range(KD):
        load_cast(wout[:, ko, :], wout_v[:, ko, :], D_MODEL, b_wout)
    wgu_v = wgu_d.rearrange("(ko p) n -> p ko n", p=128)
    for ko in range(KD):
        for c in range(4):
            load_cast(wgu[:, ko, c * STG:(c + 1) * STG], wgu_v[:, ko, c * STG:(c + 1) * STG], STG,
                      b_wgu, scale_ap=fnw[:, ko:ko + 1])
    wdn_v = wdn_d.rearrange("(j p) n -> p j n", p=128)
    for j in range(JF):
        load_cast(wdn[:, j, :], wdn_v[:, j, :], D_MODEL, b_wdn)

    xin = [kb.sb([128, D_MODEL], F32, "xin%d" % i) for i in range(2)]
    oin = [kb.sb([128, D_MODEL], BF16, "oin%d" % i) for i in range(2)]
    b_xin = S.bufs(2, "xin")
    b_oin = S.bufs(2, "oin")
    tbuf = kb.sb([128, KD, 128], BF16, "tbuf")
    b_tbuf = S.buf("tbuf")
    hn = kb.sb([128, D_MODEL], BF16, "hn")
    b_hn = S.buf("hn")
    junk = kb.sb([128, D_MODEL], BF16, "junk")
    b_junk = S.buf("junk")
    aT = kb.sb([128, JF, 128], BF16, "aT")
    b_aT = S.buf("aT")
    sg = [kb.sb([128, 128], F32, "sg%d" % i) for i in range(3)]
    b_sg = S.bufs(3, "sg")
    stat = kb.sb([128, 8], F32, "stat")
    b_stat = S.buf("stat")
    epsc = kb.sb([128, 1], F32, "epsc")
    b_epsc = S.buf("epsc")
    S.op("dve", lambda e: e.memset(epsc[:], NORM_EPS), W=[b_epsc])

    if ext:
        pbank, b_pb = ext["pb"], ext["b_pb"]
        ptb_t = pbank[0][:].bitcast(BF16)
        idx_sb = kb.sb([128, 4 * NTL], mybir.dt.int32, "idx")
        b_idx = S.buf("idx")
        S.dma("sp", idx_sb[:], ext["idx"], b_idx, W=[b_idx])
        TT_ = ext["T"]
    else:
        pbank = [None] + [kb.ps([128, 512], F32, "pb%d" % i) for i in range(1, 8)]
        b_pb = S.bufs(8, "pb", excl=True)
        ptb_t = kb.ps([128, D_MODEL], BF16, "ptb")

    x_v = x_d.rearrange("(t p) d -> t p d", p=128)
    o_v = o_d.rearrange("(t p) d -> t p d", p=128) if o_d is not None else None
    y_v = y_d.rearrange("(t p) d -> t p d", p=128)

    def rms_scale(src, b_src, col):
        S.op("act", lambda e: e.activation(out=junk[:], in_=src, func=AF.Square,
                                           accum_out=stat[:, col:col + 1]),
             R=[b_src], W=[b_junk, b_stat])
        S.op("act", lambda e: e.activation(out=stat[:, col:col + 1], in_=stat[:, col:col + 1], func=AF.Sqrt,
                                           scale=1.0 / D_MODEL, bias=epsc[:, 0:1]),
             R=[b_stat, b_epsc], W=[b_stat])
        S.op("dve", lambda e: e.reciprocal(out=stat[:, col:col + 1], in_=stat[:, col:col + 1]),
             R=[b_stat], W=[b_stat])

    hT2 = [kb.sb([128, KD, 128], BF16, "hT2_%d" % i) for i in range(2)]
    b_hT2 = S.bufs(2, "hT2")
    aT2 = [aT, kb.sb([128, JF, 128], BF16, "aT_1")]
    b_aT2 = [b_aT, S.buf("aT_1")]
    ptb = ptb_t

    def stageA(t):
        i = t % 2
        xt, ot = xin[i], oin[i]
        S.dma("sp", xt[:], x_v[t], b_xin[i], W=[b_xin[i]])
        if ext:
            for r_ in range(4):
                S.dma_fn("pool", lambda e, ot=ot, r_=r_, t=t: e.indirect_dma_start(
                    out=ot[:, r_ * 256:(r_ + 1) * 256], out_offset=None,
                    in_=ext["og_all"],
                    in_offset=bass.IndirectOffsetOnAxis(ap=idx_sb[:, r_ * NTL + t:r_ * NTL + t + 1], axis=0)),
                    b_oin[i], R=[b_idx] + list(ext["b_og_all"]), W=[b_oin[i]])
        else:
            S.dma("pool", ot[:], o_v[t], b_oin[i], W=[b_oin[i]])

        def tr_group(e, src=ot):
            ins = None
            for k in range(KD):
                ins = e.transpose(out=ptb[:, k * 128:(k + 1) * 128], in_=src[:, k * 128:(k + 1) * 128],
                                  identity=idb[:])
            return ins
        S.op("pe", tr_group, R=[b_oin[i], b_idb], W=[b_pb[0]])
        S.op("act", lambda e: e.copy(out=tbuf[:].rearrange("p k t -> p (k t)"), in_=ptb[:, 0:KD * 128]),
             R=[b_pb[0]], W=[b_tbuf])
        for nchunk in range(2):
            bk = 1 + nchunk

            def mm_out(e, bk=bk, nchunk=nchunk):
                ins = None
                for k in range(KD):
                    ins = e.matmul(out=pbank[bk][:], lhsT=tbuf[:, k, :],
                                   rhs=wout[:, k, nchunk * 512:(nchunk + 1) * 512],
                                   start=(k == 0), stop=(k == KD - 1))
                return ins
            S.op("pe", mm_out, R=[b_tbuf, b_wout], W=[b_pb[bk]])
            S.op("dve", lambda e, bk=bk, nchunk=nchunk, xt=xt:
                 e.tensor_tensor(out=xt[:, nchunk * 512:(nchunk + 1) * 512],
                                 in0=pbank[bk][:], in1=xt[:, nchunk * 512:(nchunk + 1) * 512], op=ALU.add),
                 R=[b_pb[bk], b_xin[i]], W=[b_xin[i]])
        rms_scale(xt[:], b_xin[i], 0)
        S.op("act", lambda e, xt=xt: e.activation(out=hn[:], in_=xt[:], func=AF.Copy, scale=stat[:, 0:1]),
             R=[b_xin[i], b_stat], W=[b_hn])

        def tr_group2(e):
            ins = None
            for k in range(KD):
                ins = e.transpose(out=ptb[:, k * 128:(k + 1) * 128], in_=hn[:, k * 128:(k + 1) * 128],
                                  identity=idb[:])
            return ins
        S.op("pe", tr_group2, R=[b_hn, b_idb], W=[b_pb[0]])
        S.op("act", lambda e, i=i: e.copy(out=hT2[i][:].rearrange("p k t -> p (k t)"), in_=ptb[:, 0:KD * 128]),
             R=[b_pb[0]], W=[b_hT2[i]])

    def stageB(t):
        i = t % 2
        for j in range(JF):
            bk = (3, 4, 7)[j % 3]

            def mm_gu(e, bk=bk, j=j, i=i):
                ins = None
                for half in range(2):
                    for k in range(KD):
                        c0 = half * D_FF + j * 128
                        ins = e.matmul(out=pbank[bk][:, half * 128:(half + 1) * 128],
                                       lhsT=wgu[:, k, c0:c0 + 128], rhs=hT2[i][:, k, :],
                                       start=(k == 0), stop=(k == KD - 1))
                return ins
            S.op("pe", mm_gu, R=[b_hT2[i], b_wgu], W=[b_pb[bk]])
            s_ = j % 3
            S.op("act", lambda e, bk=bk, s_=s_: e.activation(out=sg[s_][:], in_=pbank[bk][:, 0:128], func=AF.Silu),
                 R=[b_pb[bk]], W=[b_sg[s_]])
            S.op("dve", lambda e, bk=bk, s_=s_, j=j, i=i: e.tensor_tensor(out=aT2[i][:, j, :],
                                                                          in0=pbank[bk][:, 128:256],
                                                                          in1=sg[s_][:], op=ALU.mult),
                 R=[b_pb[bk], b_sg[s_]], W=[b_aT2[i]])

    def stageC(t):
        i = t % 2
        xt = xin[i]
        for nchunk in range(2):
            bk = 5 + nchunk

            def mm_dn(e, bk=bk, nchunk=nchunk, i=i):
                ins = None
                for j in range(JF):
                    ins = e.matmul(out=pbank[bk][:], lhsT=aT2[i][:, j, :],
                                   rhs=wdn[:, j, nchunk * 512:(nchunk + 1) * 512],
                                   start=(j == 0), stop=(j == JF - 1))
                return ins
            S.op("pe", mm_dn, R=[b_aT2[i], b_wdn], W=[b_pb[bk]])
            S.op("dve", lambda e, bk=bk, nchunk=nchunk, xt=xt:
                 e.tensor_tensor(out=xt[:, nchunk * 512:(nchunk + 1) * 512],
                                 in0=pbank[bk][:], in1=xt[:, nchunk * 512:(nchunk + 1) * 512], op=ALU.add),
                 R=[b_pb[bk], b_xin[i]], W=[b_xin[i]])
        if final_norm:
            rms_scale(xt[:], b_xin[i], 1)
            S.op("dve", lambda e, xt=xt: e.scalar_tensor_tensor(out=xt[:], in0=xt[:], scalar=stat[:, 1:2],
                                                                in1=finw[:], op0=ALU.mult, op1=ALU.mult),
                 R=[b_xin[i], b_stat, b_finw], W=[b_xin[i]])
        S.dma("sp", y_v[t], xt[:], b_xin[i], R=[b_xin[i]])

    stageA(0)
    for t in range(NTL):
        if t + 1 < NTL:
            stageA(t + 1)
        stageB(t)
        stageC(t)

    if ext:
        return None
    S.barrier_wait("sp", b_xin)
    return kb.done()


def _ident_bf():
    return np.eye(128, dtype=np.float32).astype(ml_dtypes.bfloat16)


def run_ffn(x_sl, o_sl, w_out, ffn_norm_w, w_gu, w_down, final_w, nc_cache={}):
    NT = x_sl[0].shape[0]
    key = (NT, final_w is not None)
    if key not in nc_cache:
        nc_cache[key] = build_ffn(NT, final_w is not None)
    nc = nc_cache[key]
    fnw = np.ascontiguousarray(ffn_norm_w.reshape(D_MODEL // 128, 128).T)
    in_maps = []
    for c in range(len(x_sl)):
        m = {"x": np.ascontiguousarray(x_sl[c]), "o": np.ascontiguousarray(o_sl[c]),
             "w_out": w_out, "w_gu": w_gu, "w_down": w_down, "ffn_norm_w": fnw,
             "ident_bf": _ident_bf()}
        if final_w is not None:
            m["final_w_bc"] = np.ascontiguousarray(np.broadcast_to(final_w[None, :], (128, D_MODEL)))
        in_maps.append(m)
    res = run_bass_kernel_spmd(nc, in_maps, core_ids=list(range(len(x_sl))))
    return [r["y"] for r in res.results]


NW = 898


NPULL = 6


def build_mixer(T, lam_init, dbg=99, ext=None):
    SKIP = ''
    kb = ext["kb"] if ext else KB()
    nc, S = kb.nc, kb.S
    NCH = T // 512
    NTL = T // 128
    KD = D_MODEL // 128

    if ext:
        x_d = ext["x"]
        wh_d, anw_d, cw_d, sc_d, lamv_d, lnw_d, dnw_d, bt_d, idb_d, cf_d = (
            ext[k] for k in ("wh", "anw", "cw", "sc", "lamv", "lnw", "dnw", "btoep", "ident_bf", "cf"))
        ola_d = ext["og"][:, 0:128]
        od_d = ext["og"][:, 128:256]
    else:
        x_d = kb.din("x", [T, D_MODEL], F32)
        wh_d = kb.din("wh", [D_MODEL, NW], F32)
        anw_d = kb.din("anw", [128, KD], F32)
        cw_d = kb.din("cw", [128, 12], F32)
        sc_d = kb.din("sc", [128, 4], F32)
        lamv_d = kb.din("lamv", [128, 4 * 64], F32)
        lnw_d = kb.din("lnw", [128, 128], F32)
        dnw_d = kb.din("dnw", [128, 128], F32)
        bt_d = kb.din("btoep", [128, 512], F32)
        idb_d = kb.din("ident_bf", [128, 128], BF16)
        cf_d = kb.din("cf", [128, 7 * 128], F32)
        ola_d = kb.dout("o_la", [T, 128], BF16)
        od_d = kb.dout("o_d", [T, 128], BF16)

    def T_(shape, dt, name):
        return kb.sb(shape, dt, name), S.buf(name)

    anw, b_anw = T_([128, KD], F32, "anw")
    cw, b_cw = T_([128, 12], F32, "cw")
    sc, b_sc = T_([128, 4], F32, "sc")
    lamv, b_lamv = T_([128, 256], F32, "lamv")
    lnw, b_lnw = T_([128, 128], F32, "lnw")
    dnw, b_dnw = T_([128, 128], F32, "dnw")
    bt, b_bt = T_([128, 512], F32, "bt")
    idb, b_idb = T_([128, 128], BF16, "idb")
    cf, b_cf = T_([128, 7 * 128], F32, "cf")
    for (t_, d_, b_) in ((anw, anw_d, b_anw), (cw, cw_d, b_cw), (sc, sc_d, b_sc), (lamv, lamv_d, b_lamv),
                         (lnw, lnw_d, b_lnw), (dnw, dnw_d, b_dnw), (bt, bt_d, b_bt), (idb, idb_d, b_idb),
                         (cf, cf_d, b_cf)):
        S.dma("sp", t_[:], d_, b_, W=[b_])
    IDF = cf[:, 0:128]
    ONES = cf[:, 128:256]
    MASKL = cf[:, 256:384]
    MASKU = cf[:, 384:512]
    TRI = cf[:, 512:640]
    SEL63 = cf[:, 640:768]
    SEL127 = cf[:, 768:896]

    cst_, b_cst = T_([128, 8], F32, "cst")
    S.op("dve", lambda e: e.memset(cst_[:, 0:1], 1.0), W=[b_cst])
    S.op("dve", lambda e: e.memset(cst_[:, 1:2], 1e-6), W=[b_cst])
    S.op("dve", lambda e: e.memset(cst_[:, 2:3], 1e-5), W=[b_cst])
    S.op("dve", lambda e: e.memset(cst_[:, 5:6], 0.0), W=[b_cst])
    C_ONE, C_EPS6, C_EPS5, C_NA, C_NLAM, C_ZERO = (cst_[:, i:i + 1] for i in range(6))
    S.op("act", lambda e: e.activation(out=cst_[:, 3:4], in_=sc[:, 0:1], func=AF.Exp), R=[b_sc], W=[b_cst])
    S.op("dve", lambda e: e.tensor_scalar(out=cst_[:, 3:4], in0=cst_[:, 3:4], scalar1=-1.0, scalar2=None,
                                          op0=ALU.mult), R=[b_cst], W=[b_cst])
    lt, b_lt = T_([128, 128], F32, "lamtmp")
    ls, b_ls = T_([128, 4], F32, "lamsum")
    S.op("dve", lambda e: e.tensor_tensor(out=lt[:, 0:64], in0=lamv[:, 0:64], in1=lamv[:, 64:128], op=ALU.mult),
         R=[b_lamv], W=[b_lt])
    S.op("dve", lambda e: e.tensor_tensor(out=lt[:, 64:128], in0=lamv[:, 128:192], in1=lamv[:, 192:256],
                                          op=ALU.mult), R=[b_lamv, b_lt], W=[b_lt])
    if 'r' not in SKIP:
        S.op("dve", lambda e: e.reduce_sum(out=ls[:, 0:1], in_=lt[:, 0:64], axis=AX.X), R=[b_lt], W=[b_ls])
        S.op("dve", lambda e: e.reduce_sum(out=ls[:, 1:2], in_=lt[:, 64:128], axis=AX.X), R=[b_lt, b_ls], W=[b_ls])
    S.op("act", lambda e: e.activation(out=ls[:, 2:4], in_=ls[:, 0:2], func=AF.Exp), R=[b_ls], W=[b_ls])
    S.op("dve", lambda e: e.scalar_tensor_tensor(out=cst_[:, 4:5], in0=ls[:, 3:4], scalar=float(-lam_init),
                                                 in1=ls[:, 2:3], op0=ALU.add, op1=ALU.subtract),
         R=[b_ls, b_cst], W=[b_cst])
    S.op("dve", lambda e: e.tensor_scalar(out=dnw[:], in0=dnw[:], scalar1=float(1.0 - lam_init), scalar2=None,
                                          op0=ALU.mult), R=[b_dnw], W=[b_dnw])

    Wb, b_Wb = T_([128, KD, 1024], BF16, "Wb")
    wst = [kb.sb([128, NW], F32, "wst%d" % i) for i in range(2)]
    b_wst = S.bufs(2, "wst")
    wh_v = wh_d.rearrange("(ko p) n -> p ko n", p=128)
    for ko in range(KD):
        i = ko % 2
        S.dma("sp", wst[i][:], wh_v[:, ko, :], b_wst[i], W=[b_wst[i]])
        S.op("dve" if i == 0 else "pool",
             lambda e, i=i, ko=ko: e.tensor_scalar(out=Wb[:, ko, 0:NW], in0=wst[i][:], scalar1=anw[:, ko:ko + 1],
                                                   scalar2=None, op0=ALU.mult),
             R=[b_wst[i], b_anw], W=[b_Wb])

    KdT, b_KdT = T_([128, T], BF16, "KdT")
    Vaug, b_Vaug = T_([128, NTL, 144], BF16, "Vaug")
    if 'v' not in SKIP:
        S.op("pool", lambda e: e.memset(Vaug[:, :, 128:129], 1.0), W=[b_Vaug])

    xt = [kb.sb([128, D_MODEL], F32, "xt%d" % i) for i in range(4)]
    b_xt = S.bufs(4, "xt")
    xn, b_xn = T_([128, D_MODEL], BF16, "xn")
    junk, b_junk = T_([128, D_MODEL], BF16, "junk")
    junkf, b_junkf = T_([128, 128], F32, "junkf")
    stat, b_stat = T_([128, 4], F32, "stat")
    hT, b_hT = T_([128, KD, 512], BF16, "hT")
    cstg = [kb.sb([128, 515], F32, "cstg%d" % g) for g in range(3)]
    b_cstg = S.bufs(3, "cstg")
    cacc = [kb.sb([128, 512], F32, "cacc%d" % g) for g in range(3)]
    b_cacc = S.bufs(3, "cacc")
    sil = [kb.sb([128, 512], F32, "sil%d" % g) for g in range(2)]
    b_sil = S.bufs(2, "sil")
    sq, b_sq = T_([128, 512], F32, "sq")
    rs, b_rs = T_([128, 512], F32, "rs")
    qnT, b_qnT = T_([128, 512], BF16, "qnT")
    knT, b_knT = T_([128, 512], BF16, "knT")
    vsT, b_vsT = T_([128, 512], BF16, "vsT")
    QdT, b_QdT = T_([128, 512], BF16, "QdT")
    qgT, b_qgT = T_([128, 512], BF16, "qgT")
    kvt, b_kvt = T_([128, 8, 128], BF16, "kvt")
    zs, b_zs = T_([128, 4, 128], BF16, "zs")
    ba, b_ba = T_([128, 4, 2], F32, "ba")
    for g in range(3):
        S.op("dve", lambda e, g=g: e.memset(cstg[g][:, 0:3], 0.0), W=[b_cstg[g]])
    pt = {}
    for nm in ("beta", "eb", "g", "gc", "egc", "bg", "glt", "ekd", "tmpa"):
        pt[nm] = T_([128, 4], F32, "pt_" + nm)
    egl, b_egl = T_([128, 8], F32, "egl")
    NS = 4
    gset = []
    for s_ in range(NS):
        d = {}
        for nm, shp, dt in (("dg", [128, 128], F32), ("Eb", [128, 128], F32), ("Ds", [128, 128], F32),
                            ("EA", [128, 128], F32), ("t1", [128, 128], F32), ("t2", [128, 128], F32),
                            ("MPa", [128, 256], F32), ("MPb", [128, 256], F32),
                            ("MTa", [128, 128], F32), ("MTb", [128, 128], F32),
                            ("TT", [128, 128], BF16), ("vb", [128, 128], BF16), ("kbg", [128, 128], BF16)):
            d[nm] = T_(shp, dt, "%s_%d" % (nm, s_))
        gset.append(d)
    u_sb, b_u = T_([128, 4, 128], F32, "u_sb")
    wT_sb, b_wT = T_([128, 512], BF16, "wT_sb")
    apT, b_apT = T_([128, 4, 128], BF16, "apT")
    kd_sb, b_kd = T_([128, 4, 128], BF16, "kd_sb")
    S_f, b_Sf = T_([128, 128], F32, "S_f")
    Sb, b_Sb = T_([128, 128], BF16, "Sb")
    vn, b_vn = T_([128, 128], BF16, "vn")
    o_sb = [kb.sb([128, 128], F32, "o_sb%d" % i) for i in range(2)]
    b_osb = S.bufs(2, "o_sb")
    on_, b_on = T_([128, 128], F32, "on")
    ola_st = [kb.sb([128, 4, 128], BF16, "ola_st%d" % i) for i in range(2)]
    b_olast = S.bufs(2, "ola_st")
    S.op("dve", lambda e: e.memset(S_f[:], 0.0), W=[b_Sf])
    S.op("dve", lambda e: e.memset(Sb[:], 0.0), W=[b_Sb])
    pT = [[kb.sb([128, 256], BF16, "pT%d%d" % (m, p)) for p in range(2)] for m in range(2)]
    b_pT = [[S.buf("pT%d%d" % (m, p)) for p in range(2)] for m in range(2)]
    s2 = [kb.sb([128, 256], F32, "s2_%d" % m) for m in range(2)]
    b_s2 = S.bufs(2, "s2")
    rden, b_rden = T_([128, 4], F32, "rden")
    O1, b_O1 = T_([128, 128], F32, "O1")
    odf, b_odf = T_([128, 128], F32, "odf")
    od_st = [kb.sb([128, 2, 128], BF16, "od_st%d" % i) for i in range(2)]
    b_odst = S.bufs(2, "od_st")

    if ext:
        pb, b_pb = ext["pb"], ext["b_pb"]
    else:
        pb = [kb.ps([128, 512], F32, "pb%d" % i) for i in range(8)]
        b_pb = S.bufs(8, "pb", excl=True)
    pb0_bf = pb[0][:].bitcast(BF16)

    x_v = x_d.rearrange("(t p) d -> t p d", p=128)
    ola_v = ola_d.rearrange("(c t p) e -> c p t e", p=128, t=4)
    od_v = od_d.rearrange("(c t p) e -> c p t e", p=128, t=2)

    def rstd_from_ss(col, n, epsc):
        S.op("act", lambda e: e.activation(out=stat[:, col:col + 1], in_=stat[:, col:col + 1], func=AF.Ln,
                                           scale=1.0 / n, bias=epsc), R=[b_stat, b_cst], W=[b_stat])
        S.op("act", lambda e: e.activation(out=stat[:, col:col + 1], in_=stat[:, col:col + 1], func=AF.Exp,
                                           scale=-0.5), R=[b_stat], W=[b_stat])

    def silu_via_exp(src_ap, R_src, tmp_ap, b_tmp, out_ap, W_out, mul_eng="dve"):
        S.op("act", lambda e: e.activation(out=tmp_ap, in_=src_ap, func=AF.Exp, scale=-1.0), R=R_src, W=[b_tmp])
        S.op("act", lambda e: e.activation(out=tmp_ap, in_=tmp_ap, func=AF.Ln, bias=C_ONE), R=[b_tmp, b_cst],
             W=[b_tmp])
        S.op("act", lambda e: e.activation(out=tmp_ap, in_=tmp_ap, func=AF.Exp, scale=-1.0), R=[b_tmp], W=[b_tmp])
        S.op(mul_eng, lambda e: e.tensor_tensor(out=out_ap, in0=src_ap, in1=tmp_ap, op=ALU.mult),
             R=list(R_src) + [b_tmp], W=W_out)

    stmp, b_stmp = T_([128, 512], F32, "stmp")
    zraw, b_zraw = T_([128, 4, 128], F32, "zraw")

    hT2 = [hT, kb.sb([128, KD, 512], BF16, "hT_b")]
    b_hT2 = [b_hT, S.buf("hT_b")]
    Qblk = [kb.sb([128, 2, 512], BF16, "Qblk%d" % i) for i in range(2)]
    b_Qblk = S.bufs(2, "Qblk")
    for i_ in range(2):
        S.op("pool", lambda e, i_=i_: e.memset(Qblk[i_][:], 0.0), W=[b_Qblk[i_]])
    pT2 = [kb.sb([128, 512], BF16, "pT2_%d" % i) for i in range(2)]
    b_pT2 = S.bufs(2, "pT2")
    s2w, b_s2w = T_([128, 512], F32, "s2w")
    scan_l = []

    def pull_scan(n):
        for _ in range(min(n, len(scan_l))):
            scan_l.pop(0)()

    zs2 = [zs, kb.sb([128, 4, 128], BF16, "zs_b")]
    b_zs2 = [b_zs, S.buf("zs_b")]
    pend = []

    def pull(n):
        for _ in range(min(n, len(pend))):
            pend.pop(0)()

    def front(j):
        if dbg <= 0:
            return
        for tt in range(4):
            t = 4 * j + tt
            i = tt
            S.dma("sp" if i % 2 == 0 else "pool", xt[i][:],
                  (ext["x_tile"](t) if (ext and ext.get("x_tile") is not None) else x_v[t]), b_xt[i], W=[b_xt[i]],
                  R=(list(ext["b_x"](t)) if (ext and ext.get("b_x") is not None) else []))
        for tt in range(4):
            t = 4 * j + tt
            i = tt
            S.op("act", lambda e, i=i: e.activation(out=junk[:], in_=xt[i][:], func=AF.Square,
                                                    accum_out=stat[:, 0:1]), R=[b_xt[i]], W=[b_junk, b_stat])
            rstd_from_ss(0, D_MODEL, C_EPS6)
            S.op("act", lambda e, i=i: e.activation(out=xn[:], in_=xt[i][:], func=AF.Copy, scale=stat[:, 0:1]),
                 R=[b_xt[i], b_stat], W=[b_xn])

            def trx(e):
                ins = None
                for k in range(KD):
                    ins = e.transpose(out=pb0_bf[:, k * 128:(k + 1) * 128], in_=xn[:, k * 128:(k + 1) * 128],
                                      identity=idb[:])
                return ins
            S.op("pe", trx, R=[b_xn, b_idb], W=[b_pb[0]])
            S.op("dve", lambda e, tt=tt: e.tensor_copy(out=hT2[j % 2][:, :, tt * 128:(tt + 1) * 128],
                                                        in_=pb0_bf[:, 0:1024].rearrange("p (k t) -> p k t", k=KD)),
                 R=[b_pb[0]], W=[b_hT2[j % 2]])
        if dbg <= 1:
            return
        for g in range(5 if 'f' not in SKIP else 0):
            bk = 1 + (g % 2)

            def mmf(e, g=g, bk=bk):
                ins = None
                for k in range(KD):
                    ins = e.matmul(out=pb[bk][:], lhsT=Wb[:, k, g * 128:(g + 1) * 128], rhs=hT2[j % 2][:, k, :],
                                   start=(k == 0), stop=(k == KD - 1))
                return ins
            S.op("pe", mmf, R=[b_Wb, b_hT2[j % 2]], W=[b_pb[bk]])
            if g < 3:
                S.op("act", lambda e, g=g, bk=bk: e.copy(out=cstg[g][:, 3:515], in_=pb[bk][:]),
                     R=[b_pb[bk]], W=[b_cstg[g]])
            elif g == 3:
                S.op("act", lambda e, bk=bk: e.copy(out=Qblk[j % 2][0:64, :, 0:256],
                                                    in_=pb[bk][0:64, :].rearrange("p (q c) -> p q c", q=2)),
                     R=[b_pb[bk]], W=[b_Qblk[j % 2]])
                S.op("act", lambda e, bk=bk: e.copy(out=Qblk[j % 2][64:128, :, 256:512],
                                                    in_=pb[bk][64:128, :].rearrange("p (q c) -> p q c", q=2)),
                     R=[b_pb[bk]], W=[b_Qblk[j % 2]])
            else:
                S.op("act", lambda e, bk=bk, j=j: e.copy(out=KdT[:, j * 512:(j + 1) * 512], in_=pb[bk][:]),
                     R=[b_pb[bk]], W=[b_KdT])
        for tt in range(4 if 't' not in SKIP else 0):
            t = 4 * j + tt

            def mmt(e, tt=tt):
                ins = None
                for k in range(KD):
                    NN = 256 if 'n' in SKIP else 258
                    ins = e.matmul(out=pb[3][:, 0:NN], lhsT=hT2[j % 2][:, k, tt * 128:(tt + 1) * 128],
                                   rhs=Wb[:, k, 640:640 + NN], start=(k == 0), stop=(k == KD - 1))
                return ins
            S.op("pe", mmt, R=[b_Wb, b_hT2[j % 2]], W=[b_pb[3]])
            S.op("dve", lambda e, tt=tt: e.tensor_copy(out=zraw[:, tt, :], in_=pb[3][:, 0:128]),
                 R=[b_pb[3]], W=[b_zraw])
            S.op("dve", lambda e, t=t: e.tensor_copy(out=Vaug[:, t, 0:128], in_=pb[3][:, 128:256]),
                 R=[b_pb[3]], W=[b_Vaug])
            S.op("dve", lambda e, tt=tt: e.tensor_copy(out=ba[:, tt, :], in_=pb[3][:, 256:258]),
                 R=[b_pb[3]], W=[b_ba])
        silu_via_exp(zraw[:].rearrange("p a d -> p (a d)"), [b_zraw], stmp[:], b_stmp,
                     zs2[j % 2][:].rearrange("p a d -> p (a d)"), [b_zs2[j % 2]], mul_eng="pool")

    front(0)
    for j in range(NCH):
        if dbg <= 2:
            continue
        pend.clear()
        if j + 1 < NCH:
            S.defer = pend
            front(j + 1)
            S.defer = None
            pull(4)
        for g in range(3):
            ce = "dve"
            S.op(ce, lambda e, g=g: e.tensor_scalar(out=cacc[g][:], in0=cstg[g][:, 3:515],
                                                    scalar1=cw[:, g * 4 + 3:g * 4 + 4], scalar2=None, op0=ALU.mult),
                 R=[b_cstg[g], b_cw], W=[b_cacc[g]])
            for tap in (2, 1, 0):
                S.op(ce, lambda e, g=g, tap=tap: e.scalar_tensor_tensor(
                    out=cacc[g][:], in0=cstg[g][:, tap:tap + 512], scalar=cw[:, g * 4 + tap:g * 4 + tap + 1],
                    in1=cacc[g][:], op0=ALU.mult, op1=ALU.add),
                    R=[b_cstg[g], b_cw, b_cacc[g]], W=[b_cacc[g]])
            S.op(ce, lambda e, g=g: e.tensor_copy(out=cstg[g][:, 0:3], in_=cstg[g][:, 512:515]),
                 R=[b_cstg[g]], W=[b_cstg[g]])
            if g < 2:
                silu_via_exp(cacc[g][:], [b_cacc[g]], stmp[:], b_stmp, sil[g][:], [b_sil[g]])
            else:
                silu_via_exp(cacc[g][:], [b_cacc[g]], stmp[:], b_stmp, vsT[:], [b_vsT])
        if dbg <= 3:
            continue
        for g in range(2):
            S.op("pool", lambda e, g=g: e.tensor_tensor(out=sq[:], in0=sil[g][:], in1=sil[g][:], op=ALU.mult),
                 R=[b_sil[g]], W=[b_sq])
            bk = 1 + g
            S.op("pe", lambda e, bk=bk: e.matmul(out=pb[bk][:], lhsT=ONES, rhs=sq[:], start=True, stop=True),
                 R=[b_sq, b_cf], W=[b_pb[bk]])
            S.op("act", lambda e, bk=bk: e.activation(out=rs[:], in_=pb[bk][:], func=AF.Ln, bias=C_EPS6),
                 R=[b_pb[bk], b_cst], W=[b_rs])
            S.op("act", lambda e: e.activation(out=rs[:], in_=rs[:], func=AF.Exp, scale=-0.5), R=[b_rs], W=[b_rs])
            if g == 0:
                S.op("dve", lambda e: e.scalar_tensor_tensor(out=qnT[:], in0=sil[0][:], scalar=float(128 ** -0.5),
                                                             in1=rs[:], op0=ALU.mult, op1=ALU.mult),
                     R=[b_sil[0], b_rs], W=[b_qnT])
            else:
                S.op("dve", lambda e: e.tensor_tensor(out=knT[:], in0=sil[1][:], in1=rs[:], op=ALU.mult),
                     R=[b_sil[1], b_rs], W=[b_knT])
        if dbg <= 4:
            continue
        def trkv(e):
            ins = None
            for tt in range(4):
                ins = e.transpose(out=pb0_bf[:, (2 * tt) * 128:(2 * tt + 1) * 128],
                                  in_=knT[:, tt * 128:(tt + 1) * 128], identity=idb[:])
                ins = e.transpose(out=pb0_bf[:, (2 * tt + 1) * 128:(2 * tt + 2) * 128],
                                  in_=vsT[:, tt * 128:(tt + 1) * 128], identity=idb[:])
            return ins
        S.op("pe", trkv, R=[b_knT, b_vsT, b_idb], W=[b_pb[0]])
        S.op("dve", lambda e: e.tensor_copy(out=kvt[:].rearrange("p a d -> p (a d)"), in_=pb0_bf[:, 0:1024]),
             R=[b_pb[0]], W=[b_kvt])
        if dbg <= 5:
            continue
        P = lambda nm: pt[nm][0]
        B = lambda nm: pt[nm][1]
        S.op("act", lambda e: e.activation(out=P("eb")[:], in_=ba[:, :, 0], func=AF.Exp, scale=-1.0),
             R=[b_ba], W=[B("eb")])
        S.op("dve", lambda e: e.tensor_scalar(out=P("eb")[:], in0=P("eb")[:], scalar1=1.0, scalar2=None,
                                              op0=ALU.add), R=[B("eb")], W=[B("eb")])
        S.op("dve", lambda e: e.reciprocal(out=P("beta")[:], in_=P("eb")[:]), R=[B("eb")], W=[B("beta")])
        S.op("act", lambda e: e.activation(out=P("tmpa")[:], in_=ba[:, :, 1], func=AF.Exp, bias=sc[:, 1:2]),
             R=[b_ba, b_sc], W=[B("tmpa")])
        S.op("act", lambda e: e.activation(out=P("tmpa")[:], in_=P("tmpa")[:], func=AF.Ln, bias=C_ONE),
             R=[B("tmpa"), b_cst], W=[B("tmpa")])
        S.op("dve", lambda e: e.tensor_scalar(out=P("g")[:], in0=P("tmpa")[:], scalar1=C_NA, scalar2=None,
                                              op0=ALU.mult), R=[B("tmpa"), b_cst], W=[B("g")])
        S.op("pe", lambda e: e.matmul(out=pb[3][:, 0:4], lhsT=TRI, rhs=P("g")[:], start=True, stop=True),
             R=[B("g"), b_cf], W=[b_pb[3]])
        S.op("act", lambda e: e.copy(out=P("gc")[:], in_=pb[3][:, 0:4]), R=[b_pb[3]], W=[B("gc")])

        def mmgl(e):
            e.matmul(out=pb[3][:, 0:4], lhsT=SEL63, rhs=P("gc")[:], start=True, stop=True)
            return e.matmul(out=pb[3][:, 4:8], lhsT=SEL127, rhs=P("gc")[:], start=True, stop=True)
        S.op("pe", mmgl, R=[B("gc"), b_cf], W=[b_pb[3]])
        eglv = egl[:].rearrange("p (t h) -> p t h", h=2)
        S.op("act", lambda e: e.activation(out=eglv[:, :, 0], in_=pb[3][:, 0:4], func=AF.Exp),
             R=[b_pb[3]], W=[b_egl])
        S.op("act", lambda e: e.activation(out=eglv[:, :, 1], in_=pb[3][:, 4:8], func=AF.Exp),
             R=[b_pb[3], b_egl], W=[b_egl])
        S.op("dve", lambda e: e.tensor_copy(out=P("glt")[0:64, :], in_=pb[3][0:64, 0:4]),
             R=[b_pb[3]], W=[B("glt")])
        S.op("dve", lambda e: e.tensor_copy(out=P("glt")[64:128, :], in_=pb[3][64:128, 4:8]),
             R=[b_pb[3], B("glt")], W=[B("glt")])
        S.op("dve", lambda e: e.tensor_tensor(out=P("ekd")[:], in0=P("glt")[:], in1=P("gc")[:], op=ALU.subtract),
             R=[B("glt"), B("gc")], W=[B("ekd")])
        S.op("act", lambda e: e.activation(out=P("ekd")[:], in_=P("ekd")[:], func=AF.Exp),
             R=[B("ekd")], W=[B("ekd")])
        S.op("act", lambda e: e.activation(out=P("egc")[:], in_=P("gc")[:], func=AF.Exp),
             R=[B("gc")], W=[B("egc")])
        S.op("dve", lambda e: e.tensor_tensor(out=P("bg")[:], in0=P("beta")[:], in1=P("egc")[:], op=ALU.mult),
             R=[B("beta"), B("egc")], W=[B("bg")])

        if dbg <= 6:
            continue
        def tile_gen(tt):
            gs = gset[tt % NS]
            bk = 4 + tt
            G = lambda nm, gs=gs: gs[nm][0]
            GB = lambda nm, gs=gs: gs[nm][1]
            csl = slice(tt * 128, (tt + 1) * 128)
            S.op("dve", lambda e, G=G, tt=tt: e.tensor_scalar(out=G("dg")[:], in0=IDF, scalar1=P("gc")[:, tt:tt + 1],
                                                                scalar2=None, op0=ALU.mult),
                 R=[b_cf, B("gc")], W=[GB("dg")])

            def mm_abb(e, G=G, bk=bk, csl=csl):
                e.matmul(out=pb[bk][:, 0:128], lhsT=ONES, rhs=G("dg")[:], start=True, stop=True)
                e.matmul(out=pb[bk][:, 128:256], lhsT=knT[:, csl], rhs=knT[:, csl], start=True, stop=True)
                return e.matmul(out=pb[bk][:, 256:384], lhsT=knT[:, csl], rhs=qnT[:, csl], start=True, stop=True)
            yield
            S.op("pe", mm_abb, R=[b_cf, GB("dg"), b_knT, b_qnT], W=[b_pb[bk]])
            S.op("dve", lambda e, G=G, bk=bk, tt=tt: e.tensor_scalar(
                out=G("Eb")[:], in0=pb[bk][:, 0:128], scalar1=P("gc")[:, tt:tt + 1], scalar2=None,
                op0=ALU.subtract), R=[b_pb[bk], B("gc")], W=[GB("Eb")])
            S.op("act", lambda e, G=G: e.activation(out=G("Eb")[:], in_=G("Eb")[:], func=AF.Abs),
                 R=[GB("Eb")], W=[GB("Eb")])
            S.op("act", lambda e, G=G: e.activation(out=G("Ds")[:], in_=G("Eb")[:], func=AF.Exp, scale=-1.0),
                 R=[GB("Eb")], W=[GB("Ds")])
            S.op("act", lambda e, G=G, bk=bk: e.activation(out=G("EA")[:], in_=pb[bk][:, 0:128], func=AF.Exp),
                 R=[b_pb[bk]], W=[GB("EA")])
            S.op("dve", lambda e, G=G, bk=bk, tt=tt: e.scalar_tensor_tensor(
                out=G("t1")[:], in0=pb[bk][:, 128:256], scalar=P("beta")[:, tt:tt + 1], in1=G("Ds")[:],
                op0=ALU.mult, op1=ALU.mult), R=[b_pb[bk], B("beta"), GB("Ds")], W=[GB("t1")])
            S.op("dve", lambda e, G=G: e.tensor_tensor(out=G("MTa")[:], in0=G("t1")[:], in1=MASKL, op=ALU.mult),
                 R=[GB("t1"), b_cf], W=[GB("MTa")])
            S.op("dve", lambda e, G=G, bk=bk: e.tensor_tensor(out=G("t2")[:], in0=pb[bk][:, 256:384], in1=G("Ds")[:],
                                                              op=ALU.mult), R=[b_pb[bk], GB("Ds")], W=[GB("t2")])
            S.op("pool", lambda e, G=G, tt=tt: e.tensor_tensor(out=apT[:, tt, :], in0=G("t2")[:], in1=MASKU,
                                                               op=ALU.mult), R=[GB("t2"), b_cf], W=[b_apT])
            S.op("pool", lambda e, G=G, csl=csl: e.tensor_tensor(out=qgT[:, csl], in0=qnT[:, csl], in1=G("EA")[:],
                                                                 op=ALU.mult), R=[b_qnT, GB("EA")], W=[b_qgT])
            yield
            S.op("pe", lambda e, G=G, bk=bk: e.transpose(out=pb[bk][:, 384:512], in_=G("MTa")[:], identity=IDF),
                 R=[GB("MTa"), b_cf], W=[b_pb[bk]])
            S.op("act", lambda e, G=G, bk=bk: e.copy(out=G("MPa")[:, 0:128], in_=pb[bk][:, 384:512]),
                 R=[b_pb[bk]], W=[GB("MPa")])
            S.op("dve", lambda e, G=G, bk=bk: e.tensor_tensor(out=G("MPb")[:, 128:256], in0=pb[bk][:, 384:512],
                                                              in1=IDF, op=ALU.add),
                 R=[b_pb[bk], b_cf], W=[GB("MPb")])

            def st0(e, G=G, bk=bk):
                e.matmul(out=pb[bk][:, 0:128], lhsT=G("MTa")[:], rhs=G("MPa")[:, 0:128], start=True, stop=True)
                return e.matmul(out=pb[bk][:, 128:256], lhsT=G("MPa")[:, 0:128], rhs=G("MTa")[:], start=True,
                                stop=True)
            yield
            S.op("pe", st0, R=[GB("MTa"), GB("MPa")], W=[b_pb[bk]])
            S.op("act", lambda e, G=G, bk=bk: e.copy(out=G("MPb")[:, 0:128], in_=pb[bk][:, 0:128]),
                 R=[b_pb[bk]], W=[GB("MPb")])
            S.op("dve", lambda e, G=G, bk=bk: e.tensor_copy(out=G("MTb")[:], in_=pb[bk][:, 128:256]),
                 R=[b_pb[bk]], W=[GB("MTb")])
            cur, nxt = ("MPb", "MTb"), ("MPa", "MTa")
            for stp in range(1, 5):
                def stj(e, G=G, bk=bk, cur=cur):
                    e.matmul(out=pb[bk][:, 0:256], lhsT=G(cur[1])[:], rhs=G(cur[0])[:, 0:256], start=True, stop=True)
                    return e.matmul(out=pb[bk][:, 256:384], lhsT=G(cur[0])[:, 0:128], rhs=G(cur[1])[:], start=True,
                                    stop=True)
                yield
                S.op("pe", stj, R=[GB(cur[0]), GB(cur[1])], W=[b_pb[bk]])
                S.op("act", lambda e, G=G, bk=bk, nxt=nxt: e.copy(out=G(nxt[0])[:, 0:128], in_=pb[bk][:, 0:128]),
                     R=[b_pb[bk]], W=[GB(nxt[0])])
                S.op("dve", lambda e, G=G, bk=bk, cur=cur, nxt=nxt: e.tensor_tensor(
                    out=G(nxt[0])[:, 128:256], in0=pb[bk][:, 128:256], in1=G(cur[0])[:, 128:256], op=ALU.add),
                    R=[b_pb[bk], GB(cur[0])], W=[GB(nxt[0])])
                S.op("act", lambda e, G=G, bk=bk, nxt=nxt: e.copy(out=G(nxt[1])[:], in_=pb[bk][:, 256:384]),
                     R=[b_pb[bk]], W=[GB(nxt[1])])
                cur, nxt = nxt, cur
            yield
            S.op("pe", lambda e, G=G, bk=bk, cur=cur: e.matmul(out=pb[bk][:, 0:128], lhsT=G(cur[1])[:],
                                                               rhs=G(cur[0])[:, 128:256], start=True, stop=True),
                 R=[GB(cur[0]), GB(cur[1])], W=[b_pb[bk]])
            S.op("dve", lambda e, G=G, bk=bk, cur=cur: e.tensor_tensor(out=G("TT")[:], in0=pb[bk][:, 0:128],
                                                                       in1=G(cur[0])[:, 128:256], op=ALU.add),
                 R=[b_pb[bk], GB(cur[0])], W=[GB("TT")])
            S.op("pool", lambda e, G=G, tt=tt: e.tensor_scalar(out=G("vb")[:], in0=kvt[:, 2 * tt + 1, :],
                                                               scalar1=P("beta")[:, tt:tt + 1], scalar2=None,
                                                               op0=ALU.mult), R=[b_kvt, B("beta")], W=[GB("vb")])
            S.op("pool", lambda e, G=G, tt=tt: e.tensor_scalar(out=G("kbg")[:], in0=kvt[:, 2 * tt, :],
                                                               scalar1=P("bg")[:, tt:tt + 1], scalar2=None,
                                                               op0=ALU.mult), R=[b_kvt, B("bg")], W=[GB("kbg")])
            S.op("pool", lambda e, tt=tt: e.tensor_scalar(out=kd_sb[:, tt, :], in0=kvt[:, 2 * tt, :],
                                                          scalar1=P("ekd")[:, tt:tt + 1], scalar2=None,
                                                          op0=ALU.mult), R=[b_kvt, B("ekd")], W=[b_kd])

            def mm_uw(e, G=G, bk=bk):
                e.matmul(out=pb[bk][:, 0:128], lhsT=G("TT")[:], rhs=G("vb")[:], start=True, stop=True)
                return e.matmul(out=pb[bk][:, 128:256], lhsT=G("kbg")[:], rhs=G("TT")[:], start=True, stop=True)
            yield
            S.op("pe", mm_uw, R=[GB("TT"), GB("vb"), GB("kbg")], W=[b_pb[bk]])
            S.op("act", lambda e, bk=bk, tt=tt: e.copy(out=u_sb[:, tt, :], in_=pb[bk][:, 0:128]),
                 R=[b_pb[bk]], W=[b_u])
            S.op("act", lambda e, bk=bk, csl=csl: e.copy(out=wT_sb[:, csl], in_=pb[bk][:, 128:256]),
                 R=[b_pb[bk]], W=[b_wT])

        gens = [tile_gen(tt) for tt in range(4)]
        while gens:
            for g_ in gens[:]:
                try:
                    next(g_)
                except StopIteration:
                    gens.remove(g_)
            pull(NPULL)
        if dbg <= 7:
            continue
        S.defer = scan_l
        oi = j % 2
        for tt in range(4):
            csl = slice(tt * 128, (tt + 1) * 128)
            osb, b_o = o_sb[tt % 2], b_osb[tt % 2]
            for hh in range(2):
                r = slice(hh * 64, hh * 64 + 64)
                nl = 2 * tt + hh

                def mm1(e, csl=csl):
                    e.matmul(out=pb[6][:, 0:128], lhsT=wT_sb[:, csl], rhs=Sb[:], start=True, stop=True)
                    return e.matmul(out=pb[7][:, 0:128], lhsT=qgT[:, csl], rhs=Sb[:], start=True, stop=False)
                S.op("pe", mm1, R=[b_wT, b_qgT, b_Sb], W=[b_pb[6], b_pb[7]])
                S.op("dve", lambda e, r=r, tt=tt: e.tensor_tensor(out=vn[r, :], in0=u_sb[r, tt, :],
                                                                  in1=pb[6][r, 0:128], op=ALU.subtract),
                     R=[b_u, b_pb[6]], W=[b_vn])

                def mm2(e, r=r, tt=tt):
                    e.matmul(out=pb[7][:, 0:128], lhsT=apT[r, tt, :], rhs=vn[r, :], start=False, stop=True)
                    return e.matmul(out=pb[7][:, 128:256], lhsT=kd_sb[r, tt, :], rhs=vn[r, :], start=True, stop=True)
                S.op("pe", mm2, R=[b_apT, b_kd, b_vn], W=[b_pb[7]])
                S.op("act", lambda e, r=r, osb=osb: e.copy(out=osb[r, :], in_=pb[7][r, 0:128]),
                     R=[b_pb[7]], W=[b_o])
                S.op("dve", lambda e, nl=nl: e.scalar_tensor_tensor(out=S_f[:], in0=S_f[:], scalar=egl[:, nl:nl + 1],
                                                                    in1=pb[7][:, 128:256], op0=ALU.mult,
                                                                    op1=ALU.add),
                     R=[b_Sf, b_egl, b_pb[7]], W=[b_Sf])
                S.op("act", lambda e: e.copy(out=Sb[:], in_=S_f[:]), R=[b_Sf], W=[b_Sb])
            S.op("act", lambda e, osb=osb: e.activation(out=junkf[:], in_=osb[:], func=AF.Square,
                                                        accum_out=stat[:, 1:2]), R=[b_o], W=[b_junkf, b_stat])
            rstd_from_ss(1, 128, C_EPS6)
            S.op("dve", lambda e, osb=osb: e.scalar_tensor_tensor(out=on_[:], in0=osb[:], scalar=stat[:, 1:2],
                                                                  in1=lnw[:], op0=ALU.mult, op1=ALU.mult),
                 R=[b_o, b_stat, b_lnw], W=[b_on])
            S.op("dve", lambda e, tt=tt, oi=oi, zz=zs2[j % 2]: e.tensor_tensor(out=ola_st[oi][:, tt, :], in0=on_[:], in1=zz[:, tt, :],
                                                                op=ALU.mult), R=[b_on, b_zs2[j % 2]], W=[b_olast[oi]])
        S.dma("sp", ola_v[j], ola_st[oi][:], b_olast[oi], R=[b_olast[oi]])
        S.defer = None

        if dbg <= 8:
            pull_scan(len(scan_l))
            continue
        pull(len(pend))
        nscan = max(1, -(-len(scan_l) // (8 * j + 6)))
        for qq in range(2):
            qc = 2 * j + qq
            q0 = qc * 256
            qsl = slice(qq * 256, qq * 256 + 256)
            oi2 = qc % 2
            nkt = 2 * qc + 2
            def emit_qk(kt, qq=qq, qd=Qblk[j % 2], bq=b_Qblk[j % 2]):
                k0 = kt * 128
                bk = 4 + (kt % 2)
                S.op("pe", lambda e, bk=bk, k0=k0, qq=qq, qd=qd: e.matmul(
                    out=pb[bk][:, 0:512], lhsT=KdT[:, k0:k0 + 128], rhs=qd[:, qq, :], start=True, stop=True),
                    R=[b_KdT, bq], W=[b_pb[bk]])

            def emit_exp(kt, q0=q0):
                k0 = kt * 128
                d = q0 - k0
                par = kt % 2
                bk = 4 + par
                if d >= 256:
                    S.op("act", lambda e, bk=bk, par=par: e.activation(
                        out=pT2[par][:], in_=pb[bk][:, 0:512], func=AF.Exp, scale=0.125, bias=sc[:, 2:3]),
                        R=[b_pb[bk], b_sc], W=[b_pT2[par]])
                else:
                    for m in range(2):
                        S.op("dve", lambda e, bk=bk, m=m, d=d: e.scalar_tensor_tensor(
                            out=s2w[:, m * 256:(m + 1) * 256], in0=pb[bk][:, m * 256:(m + 1) * 256], scalar=0.125,
                            in1=bt[:, d + 128:d + 128 + 256], op0=ALU.mult, op1=ALU.add),
                            R=[b_pb[bk], b_bt], W=[b_s2w])
                    S.op("act", lambda e, par=par: e.activation(out=pT2[par][:], in_=s2w[:], func=AF.Exp),
                         R=[b_s2w], W=[b_pT2[par]])

            def emit_pv(kt, qc=qc):
                par = kt % 2
                for m in range(2):
                    for qb in range(2):
                        klast = 2 * qc + qb
                        if kt > klast:
                            continue
                        ab = qb * 2 + m
                        c0 = m * 256 + qb * 128
                        S.op("pe", lambda e, ab=ab, par=par, c0=c0, kt=kt, klast=klast: e.matmul(
                            out=pb[ab][:, 0:129], lhsT=pT2[par][:, c0:c0 + 128],
                            rhs=Vaug[:, kt, 0:129], start=(kt == 0), stop=(kt == klast)),
                            R=[b_pT2[par], b_Vaug], W=[b_pb[ab]])

            emit_qk(0)
            for kt in range(nkt):
                if kt + 1 < nkt:
                    emit_qk(kt + 1)
                emit_exp(kt)
                emit_pv(kt)
                pull_scan(nscan)
            for qb in range(2):
                a1, a2 = qb * 2, qb * 2 + 1
                S.op("dve", lambda e, a1=a1: e.reciprocal(out=rden[:, 0:1], in_=pb[a1][:, 128:129]),
                     R=[b_pb[a1]], W=[b_rden])
                S.op("dve", lambda e, a2=a2: e.reciprocal(out=rden[:, 1:2], in_=pb[a2][:, 128:129]),
                     R=[b_pb[a2], b_rden], W=[b_rden])
                S.op("dve", lambda e: e.tensor_scalar(out=rden[:, 2:3], in0=rden[:, 1:2], scalar1=C_NLAM,
                                                      scalar2=None, op0=ALU.mult), R=[b_rden, b_cst], W=[b_rden])
                S.op("act", lambda e, a1=a1: e.activation(out=O1[:], in_=pb[a1][:, 0:128], func=AF.Copy,
                                                          scale=rden[:, 0:1]), R=[b_pb[a1], b_rden], W=[b_O1])
                S.op("dve", lambda e, a2=a2: e.scalar_tensor_tensor(out=odf[:], in0=pb[a2][:, 0:128],
                                                                    scalar=rden[:, 2:3], in1=O1[:], op0=ALU.mult,
                                                                    op1=ALU.add),
                     R=[b_pb[a2], b_rden, b_O1], W=[b_odf])
                S.op("act", lambda e: e.activation(out=junkf[:], in_=odf[:], func=AF.Square,
                                                   accum_out=stat[:, 2:3]), R=[b_odf], W=[b_junkf, b_stat])
                rstd_from_ss(2, 128, C_EPS5)
                S.op("dve", lambda e, qb=qb, oi2=oi2: e.scalar_tensor_tensor(
                    out=od_st[oi2][:, qb, :], in0=odf[:], scalar=stat[:, 2:3], in1=dnw[:], op0=ALU.mult,
                    op1=ALU.mult), R=[b_odf, b_stat, b_dnw], W=[b_odst[oi2]])
            S.dma("sp", od_v[qc], od_st[oi2][:], b_odst[oi2], R=[b_odst[oi2]])
        pull_scan(len(scan_l))

    if ext:
        return None
    S.barrier_wait("sp", b_olast + b_odst)
    return kb.done()


def _t5_bucket_np(rel):
    n = np.maximum(rel, 0)
    nf = np.maximum(n, 1).astype(np.float32)
    large = 16 + (np.log(nf / np.float32(16)) / np.float32(math.log(128 / 16)) * np.float32(16)).astype(np.int32)
    large = np.minimum(large, 31)
    return np.where(n < 16, n, large)


def _mixer_consts():
    p = np.arange(128)
    same = (p[:, None] // 64) == (p[None, :] // 64)
    ident = np.eye(128, dtype=np.float32)
    ones = np.ones((128, 128), np.float32)
    maskl = np.where(same & (p[:, None] > p[None, :]), -1.0, 0.0).astype(np.float32)
    masku = np.where(same & (p[:, None] <= p[None, :]), 1.0, 0.0).astype(np.float32)
    tri = masku.copy()
    sel63 = np.zeros((128, 128), np.float32)
    sel63[63, :] = 1.0
    sel127 = np.zeros((128, 128), np.float32)
    sel127[127, :] = 1.0
    return np.ascontiguousarray(np.concatenate([ident, ones, maskl, masku, tri, sel63, sel127], axis=1))


def mixer_inputs(xb, l, h, P):
    w_in = P["w_in"][l]
    cols = np.concatenate([
        np.arange(h * 128, (h + 1) * 128),
        512 + np.arange(h * 128, (h + 1) * 128),
        1024 + np.arange(h * 128, (h + 1) * 128),
        2056 + np.arange(h * 128, (h + 1) * 128),
        2568 + np.arange(h * 128, (h + 1) * 128),
        1536 + np.arange(h * 128, (h + 1) * 128),
        3080 + np.arange(h * 128, (h + 1) * 128),
        np.array([2048 + h]),
        np.array([2052 + h]),
    ])
    wh = np.ascontiguousarray(w_in[:, cols])
    anw = np.ascontiguousarray(P["attn_norm_w"][l].reshape(8, 128).T)
    cwl = P["conv_w"][l]
    cw = np.concatenate([cwl[:, g * 512 + h * 128: g * 512 + (h + 1) * 128].T for g in range(3)], axis=1)
    sc = np.zeros((128, 4), np.float32)
    sc[:, 0] = P["a_log"][l, h]
    sc[:, 1] = P["dt_bias"][l, h]
    sc[:, 2] = P["rel_bias"][31, h]
    lamv = np.concatenate([P["lambda_q1"][l], P["lambda_k1"][l], P["lambda_q2"][l], P["lambda_k2"][l]])
    lamv = np.broadcast_to(lamv[None, :], (128, 256))
    lnw = np.broadcast_to(P["la_norm_w"][l][None, :], (128, 128))
    dnw = np.broadcast_to(P["diff_norm_w"][l][None, :], (128, 128))
    kl = np.arange(128)[:, None]
    jj = np.arange(512)[None, :]
    rel = jj - 128 - kl
    bt = np.where(rel >= 0, P["rel_bias"][_t5_bucket_np(rel), h], np.float32(-30000.0)).astype(np.float32)
    c = np.ascontiguousarray
    return {"x": c(xb), "wh": wh, "anw": anw, "cw": c(cw.astype(np.float32)), "sc": sc,
            "lamv": c(lamv.astype(np.float32)), "lnw": c(lnw.astype(np.float32)),
            "dnw": c(dnw.astype(np.float32)), "btoep": c(bt), "ident_bf": _ident_bf(), "cf": _mixer_consts()}


CC_GROUPS = [[0, 1, 2, 3], [4, 5, 6, 7]]
_MIX_KEYS = ("wh", "anw", "cw", "sc", "lamv", "lnw", "dnw")


def build_fused(T):
    kb = KB()
    nc, S = kb.nc, kb.S
    NT = T // NH
    NTL = NT // 128
    I32 = mybir.dt.int32
    shp = {"wh": [D_MODEL, NW], "anw": [128, 8], "cw": [128, 12], "sc": [128, 4], "lamv": [128, 256],
           "lnw": [128, 128], "dnw": [128, 128]}
    x_d = kb.din("x", [T, D_MODEL], F32)
    xs_d = kb.din("xs", [NT, D_MODEL], F32)
    idx_d = kb.din("idx", [128, NH * NTL], I32)
    bt_d = kb.din("btoep", [128, 512], F32)
    idb_d = kb.din("ident_bf", [128, 128], BF16)
    cf_d = kb.din("cf", [128, 7 * 128], F32)
    fin_d = kb.din("final_w_bc", [128, D_MODEL], F32)
    lay = []
    for l in range(DEPTH):
        d = {k: kb.din("%s%d" % (k, l), shp[k], F32) for k in _MIX_KEYS}
        d["w_out"] = kb.din("w_out%d" % l, [D_MODEL, D_MODEL], F32)
        d["w_gu"] = kb.din("w_gu%d" % l, [D_MODEL, 2 * D_FF], F32)
        d["w_down"] = kb.din("w_down%d" % l, [D_FF, D_MODEL], F32)
        d["ffn_norm_w"] = kb.din("fnw%d" % l, [128, 8], F32)
        lay.append(d)
    y_d = kb.dout("y", [NT, D_MODEL], F32)
    og_in = kb.dint("og_in", [T, 256], BF16)
    og_all = kb.dint("og_all", [NH * T, 256], BF16)
    xs_in = kb.dint("xs_in", [NT, D_MODEL], F32)
    x1_all = kb.dint("x1_all", [T, D_MODEL], F32)

    pb = [kb.ps([128, 512], F32, "pb%d" % i) for i in range(8)]
    b_pb = S.bufs(8, "pb", excl=True)
    ORC = min(T, 2048)
    NKO = T // ORC
    XRC = min(NT, 256)
    NKX = NT // XRC
    b_og_all = S.bufs(NKO, "og_all")
    b_x1all = S.bufs(NKX, "x1_all")
    kb.persist = b_pb + b_og_all + b_x1all

    def allgather(src, dst, b_dst, name):
        S.dma_fn("pool", lambda e: e.collective_compute("AllGather", ALU.bypass, replica_groups=CC_GROUPS,
                                                         ins=[src], outs=[dst]),
                 S.buf(name), W=[b_dst], inc=None)

    def x1_tile(t):
        tok = t * 128
        r, w = divmod(tok, NT)
        k, ww = divmod(w, XRC)
        base = k * (NH * XRC) + r * XRC + ww
        return x1_all[base:base + 128, :]

    for l in range(DEPTH):
        lam_init = 0.8 - 0.6 * math.exp(-0.3 * l)
        kb.begin_phase()
        if l > 0:
            for k in range(NKX):
                allgather(xs_in[k * XRC:(k + 1) * XRC, :], x1_all[k * NH * XRC:(k + 1) * NH * XRC, :], b_x1all[k],
                          "ccx%d_%d" % (l, k))
        ext = {"kb": kb, "pb": pb, "b_pb": b_pb, "x": x_d, "x_tile": (None if l == 0 else x1_tile),
               "b_x": (None if l == 0 else (lambda t: [b_x1all[((t * 128) % NT) // XRC]])), "og": og_in,
               "btoep": bt_d, "ident_bf": idb_d, "cf": cf_d}
        for k in _MIX_KEYS:
            ext[k] = lay[l][k]
        build_mixer(T, lam_init, ext=ext)
        kb.end_phase()
        kb.begin_phase()
        for k in range(NKO):
            allgather(og_in[k * ORC:(k + 1) * ORC, :], og_all[k * NH * ORC:(k + 1) * NH * ORC, :], b_og_all[k],
                      "cco%d_%d" % (l, k))
        last = (l == DEPTH - 1)
        ext = {"kb": kb, "pb": pb, "b_pb": b_pb, "x": (xs_d if l == 0 else xs_in), "y": (y_d if last else xs_in),
               "og_all": og_all, "b_og_all": b_og_all, "idx": idx_d, "T": T,
               "w_out": lay[l]["w_out"], "w_gu": lay[l]["w_gu"], "w_down": lay[l]["w_down"],
               "ffn_norm_w": lay[l]["ffn_norm_w"], "ident_bf": idb_d, "final_w_bc": fin_d}
        build_ffn(NT, last, ext=ext)
        kb.end_phase()
    kb.es.close()
    return nc


_FUSED_CACHE = {}


def _gather_idx(T, h):
    NT = T // NH
    NTL = NT // 128
    ORC = min(T, 2048)
    tok = h * NT + np.arange(NTL)[None, :] * 128 + np.arange(128)[:, None]
    k, w = tok // ORC, tok % ORC
    cols = [k * (NH * ORC) + r * ORC + w for r in range(NH)]
    return np.ascontiguousarray(np.concatenate(cols, axis=1).astype(np.int32))


def fused_inputs(P, T):
    NT = T // NH
    NTL = NT // 128
    perm = np.concatenate([np.concatenate([np.arange(r * 128, (r + 1) * 128),
                                           512 + np.arange(r * 128, (r + 1) * 128)]) for r in range(NH)])
    in_maps = []
    for c in range(NCORES):
        b, h = divmod(c, NH)
        xb = P["x"][b, :T]
        m = {"x": np.ascontiguousarray(xb), "xs": np.ascontiguousarray(xb[h * NT:(h + 1) * NT]),
             "idx": _gather_idx(T, h),
             "final_w_bc": np.ascontiguousarray(np.broadcast_to(P["final_norm_w"][None, :], (128, D_MODEL)))}
        for l in range(DEPTH):
            mi = mixer_inputs(xb, l, h, P)
            for k in _MIX_KEYS:
                m["%s%d" % (k, l)] = mi[k]
            if l == 0:
                m["btoep"], m["ident_bf"], m["cf"] = mi["btoep"], mi["ident_bf"], mi["cf"]
            m["w_out%d" % l] = np.ascontiguousarray(P["w_out"][l][perm])
            m["w_gu%d" % l] = P["w_gate_up"][l]
            m["w_down%d" % l] = P["w_down"][l]
            m["fnw%d" % l] = np.ascontiguousarray(P["ffn_norm_w"][l].reshape(8, 128).T)
        in_maps.append(m)
    return in_maps


def kernel_fused(P, T):
    if T not in _FUSED_CACHE:
        _FUSED_CACHE[T] = build_fused(T)
    nc = _FUSED_CACHE[T]
    NT = T // NH
    res = run_bass_kernel_spmd(nc, fused_inputs(P, T), core_ids=list(range(NCORES)))
    out = np.empty((BATCH, T, D_MODEL), np.float32)
    for c in range(NCORES):
        b, h = divmod(c, NH)
        out[b, h * NT:(h + 1) * NT] = np.asarray(res.results[c]["y"])
    return out


_MIX_CACHE = {}


def kernel(**inputs):
    P = {k: np.ascontiguousarray(np.asarray(v, dtype=np.float32)) for k, v in inputs.items()}
    return kernel_fused(P, P["x"].shape[1])


def kernel_unfused(**inputs):
    P = {k: np.ascontiguousarray(np.asarray(v, dtype=np.float32)) for k, v in inputs.items()}
    x = P["x"]
    B, T, D = x.shape
    NTOK = B * T
    per = NTOK // NCORES
    for l in range(DEPTH):
        lam_init = 0.8 - 0.6 * math.exp(-0.3 * l)
        key = (T, l)
        if key not in _MIX_CACHE:
            _MIX_CACHE[key] = build_mixer(T, lam_init)
        nc = _MIX_CACHE[key]
        in_maps = [mixer_inputs(x[c // NH], l, c % NH, P) for c in range(NCORES)]
        res = run_bass_kernel_spmd(nc, in_maps, core_ids=list(range(NCORES)))
        o = np.empty((B, T, D), dtype=ml_dtypes.bfloat16)
        for c in range(NCORES):
            b, h = divmod(c, NH)
            o[b, :, h * 128:(h + 1) * 128] = np.asarray(res.results[c]["o_la"])
            o[b, :, 512 + h * 128:512 + (h + 1) * 128] = np.asarray(res.results[c]["o_d"])
        xs = x.reshape(NTOK, D)
        os_ = o.reshape(NTOK, D)
        ys = run_ffn([xs[c * per:(c + 1) * per] for c in range(NCORES)],
                     [os_[c * per:(c + 1) * per] for c in range(NCORES)],
                     P["w_out"][l], P["ffn_norm_w"][l], P["w_gate_up"][l], P["w_down"][l],
                     P["final_norm_w"] if l == DEPTH - 1 else None)
        x = np.concatenate([np.asarray(y) for y in ys], axis=0).reshape(B, T, D)
    return np.ascontiguousarray(x.astype(np.float32))
```

```python
import math
from contextlib import ExitStack

import numpy as np
import ml_dtypes
import concourse.bass as bass
import concourse.mybir as mybir
from concourse.bass_utils import run_bass_kernel_spmd

F32 = mybir.dt.float32
BF16 = mybir.dt.bfloat16
AF = mybir.ActivationFunctionType
ALU = mybir.AluOpType
AX = mybir.AxisListType

D_MODEL = 1024
SEQ = 8192
BATCH = 2
DEPTH = 2
NH = 4
D_FF = 2816
IN_DIM = 3592
NORM_EPS = 1e-6
NCORES = 8

ENGS = ("pe", "act", "dve", "pool", "sp")


class Buf:
    __slots__ = ("name", "w", "r", "dsem", "dcnt", "excl")

    def __init__(self, name, excl=False):
        self.name = name
        self.excl = excl
        self.w = None
        self.r = []
        self.dsem = None
        self.dcnt = 0


class Op:
    __slots__ = ("eng", "fn", "deps", "dma", "sem", "val", "needed", "inc")

    def __init__(self, eng, fn, dma=False):
        self.eng = eng
        self.fn = fn
        self.deps = []
        self.dma = dma
        self.sem = None
        self.val = 0
        self.needed = False
        self.inc = 16


class Sched:
    def __init__(self, nc, es):
        self.nc = nc
        self.es = es
        self.ops = {e: [] for e in ENGS}
        self.esem = {e: es.enter_context(nc.semaphore("s_" + e)) for e in ENGS}
        self.nbuf = 0
        self.cnt = {e: 0 for e in ENGS}
        self.phase_dmas = []
        self.nsem = 0
        self.defer = None

    def buf(self, name=None, excl=False):
        self.nbuf += 1
        return Buf(name or ("b%d" % self.nbuf), excl)

    def bufs(self, n, name="b", excl=False):
        return [self.buf("%s%d" % (name, i), excl) for i in range(n)]

    def _link(self, o, R, W):
        deps = []
        for b in R:
            if b.w is not None:
                d = b.w
                if d.dma or o.dma or d.eng != o.eng or o.eng != "pe":
                    deps.append(d)
            if b.excl:
                for d in b.r:
                    if d.eng != o.eng:
                        deps.append(d)
        for b in W:
            if b.w is not None:
                d = b.w
                if d.dma or o.dma or d.eng != o.eng or o.eng != "pe":
                    deps.append(d)
            for d in b.r:
                if d.dma or o.dma or d.eng != o.eng or o.eng != "pe":
                    deps.append(d)
        o.deps = deps
        for b in R:
            if b in W:
                continue
            if b.excl:
                b.r = []
            elif not o.dma:
                b.r = [x for x in b.r if x.dma or x.eng != o.eng]
            b.r.append(o)
        for b in W:
            b.w = o
            b.r = []

    def op(self, eng, fn, R=(), W=()):
        if self.defer is not None:
            self.defer.append(lambda: self.op_now(eng, fn, R, W))
            return None
        return self.op_now(eng, fn, R, W)

    def op_now(self, eng, fn, R=(), W=()):
        o = Op(eng, fn)
        self._link(o, R, W)
        self.ops[eng].append(o)
        return o

    def dma(self, q, out, in_, sb, R=(), W=()):
        return self.dma_fn(q, lambda e, out=out, in_=in_: e.dma_start(out=out, in_=in_), sb, R, W)

    def dma_fn(self, q, fn, sb, R=(), W=(), inc=16):
        if self.defer is not None:
            self.defer.append(lambda: self.dma_fn_now(q, fn, sb, R, W, inc))
            return None
        return self.dma_fn_now(q, fn, sb, R, W, inc)

    def dma_fn_now(self, q, fn, sb, R=(), W=(), inc=16):
        if sb.dsem is None:
            self.nsem += 1
            sb.dsem = self.es.enter_context(self.nc.semaphore("d%d_%s" % (self.nsem, sb.name)))
        o = Op(q, fn, dma=True)
        o.inc = inc
        sb.dcnt += (inc if inc else 1)
        o.sem = sb.dsem
        o.val = sb.dcnt
        self._link(o, R, W)
        self.ops[q].append(o)
        self.phase_dmas.append(o)
        return o

    def phase_barrier(self):
        lasts = []
        for e in ENGS:
            real = [o for o in self.ops[e] if o.fn is not None and not o.dma]
            if real:
                lasts.append(real[-1])
        deps = lasts + list(self.phase_dmas)
        for e in ENGS:
            o = Op(e, None)
            o.deps = [d for d in deps if d.dma or d.eng != e]
            self.ops[e].append(o)
        self.phase_dmas = []

    def barrier_wait(self, eng, R):
        o = Op(eng, None)
        self._link(o, (), R)
        self.ops[eng].append(o)
        return o

    def finalize(self):
        for e in ENGS:
            for o in self.ops[e]:
                for d in o.deps:
                    d.needed = True
        for e in ENGS:
            c = self.cnt[e]
            for o in self.ops[e]:
                if not o.dma and o.needed and o.fn is not None:
                    c += 1
                    o.sem = self.esem[e]
                    o.val = c
            self.cnt[e] = c
        ops = self.ops
        self.ops = {e: [] for e in ENGS}

        def run(eng, lst):
            seen = {}
            for o in lst:
                waits = {}
                for d in o.deps:
                    k = id(d.sem)
                    if k not in waits or waits[k][1] < d.val:
                        waits[k] = (d.sem, d.val)
                for k, (sem, val) in waits.items():
                    if seen.get(k, 0) < val:
                        eng.wait_ge(sem, val)
                        seen[k] = val
                if o.fn is None:
                    continue
                ins = o.fn(eng)
                if o.dma:
                    if o.inc:
                        ins.then_inc(o.sem, o.inc)
                    else:
                        ins.then_inc(o.sem)
                elif o.needed:
                    ins.then_inc(o.sem, 1)

        with self.nc.Block() as block:
            @block.tensor
            def _(e):
                run(e, ops["pe"])

            @block.scalar
            def _(e):
                run(e, ops["act"])

            @block.vector
            def _(e):
                run(e, ops["dve"])

            @block.gpsimd
            def _(e):
                run(e, ops["pool"])

            @block.sync
            def _(e):
                run(e, ops["sp"])


class KB:
    def __init__(self):
        self.nc = bass.Bass("TRN2", target_bir_lowering=False)
        self.es = ExitStack()
        self.S = Sched(self.nc, self.es)
        self.n = 0
        self.pes = None
        self.phase = 0

    def begin_phase(self):
        self.phase += 1
        self.pes = ExitStack()

    def end_phase(self):
        self.S.phase_barrier()
        self.S.finalize()
        self.pes.close()
        self.pes = None
        for b in getattr(self, "persist", []):
            b.w = None
            b.r = []

    def sb(self, shape, dt, name=None):
        self.n += 1
        st = self.pes if self.pes is not None else self.es
        return st.enter_context(self.nc.sbuf_tensor("sb%d_" % self.phase + (name or ("t%d" % self.n)), list(shape), dt))

    def dint(self, name, shape, dt):
        return self.nc.dram_tensor(name, list(shape), dt).ap()

    def ps(self, shape, dt, name=None):
        self.n += 1
        return self.es.enter_context(self.nc.psum_tensor("ps_" + (name or ("p%d" % self.n)), list(shape), dt))

    def din(self, name, shape, dt):
        return self.nc.dram_tensor(name, list(shape), dt, kind="ExternalInput").ap()

    def dout(self, name, shape, dt):
        return self.nc.dram_tensor(name, list(shape), dt, kind="ExternalOutput").ap()

    def done(self):
        self.S.finalize()
        self.es.close()
        return self.nc


def build_ffn(NT, final_norm, ext=None):
    kb = ext["kb"] if ext else KB()
    nc, S = kb.nc, kb.S
    NTL = NT // 128
    KD = D_MODEL // 128
    JF = D_FF // 128

    if ext:
        x_d, wout_d, wgu_d, wdn_d, fnw_d, idb_d, y_d = (
            ext[k] for k in ("x", "w_out", "w_gu", "w_down", "ffn_norm_w", "ident_bf", "y"))
        if final_norm:
            fin_d = ext["final_w_bc"]
        o_d = None
    else:
        x_d = kb.din("x", [NT, D_MODEL], F32)
        o_d = kb.din("o", [NT, D_MODEL], BF16)
        wout_d = kb.din("w_out", [D_MODEL, D_MODEL], F32)
        wgu_d = kb.din("w_gu", [D_MODEL, 2 * D_FF], F32)
        wdn_d = kb.din("w_down", [D_FF, D_MODEL], F32)
        fnw_d = kb.din("ffn_norm_w", [128, KD], F32)
        idb_d = kb.din("ident_bf", [128, 128], BF16)
        if final_norm:
            fin_d = kb.din("final_w_bc", [128, D_MODEL], F32)
        y_d = kb.dout("y", [NT, D_MODEL], F32)

    wout = kb.sb([128, KD, D_MODEL], BF16, "wout")
    wgu = kb.sb([128, KD, 2 * D_FF], BF16, "wgu")
    wdn = kb.sb([128, JF, D_MODEL], BF16, "wdn")
    fnw = kb.sb([128, KD], F32, "fnw")
    idb = kb.sb([128, 128], BF16, "idb")
    b_wout, b_wgu, b_wdn, b_fnw, b_idb = S.bufs(5, "wres")
    if final_norm:
        finw = kb.sb([128, D_MODEL], F32, "finw")
        b_finw = S.buf("finw")
        S.dma("sp", finw[:], fin_d, b_finw, W=[b_finw])
    S.dma("sp", fnw[:], fnw_d, b_fnw, W=[b_fnw])
    S.dma("sp", idb[:], idb_d, b_idb, W=[b_idb])

    STG = 1408
    NSTG = 3
    stg = [kb.sb([128, STG], F32, "stg%d" % i) for i in range(NSTG)]
    b_stg = S.bufs(NSTG, "stg")
    cnt = [0]
    cast_engs = ("dve", "act")

    def load_cast(dst_ap, src_ap, n, wbuf, scale_ap=None):
        i = cnt[0] % NSTG
        q = "sp" if (cnt[0] % 2 == 0) else "pool"
        S.dma(q, stg[i][:, 0:n], src_ap, b_stg[i], W=[b_stg[i]])
        ce = cast_engs[cnt[0] % 2]
        if ce == "act":
            if scale_ap is None:
                S.op("act", lambda e, d=dst_ap, s=stg[i][:, 0:n]: e.copy(out=d, in_=s), R=[b_stg[i]], W=[wbuf])
            else:
                S.op("act", lambda e, d=dst_ap, s=stg[i][:, 0:n], sc=scale_ap:
                     e.activation(out=d, in_=s, func=AF.Copy, scale=sc), R=[b_stg[i], b_fnw], W=[wbuf])
        elif scale_ap is None:
            S.op(ce, lambda e, d=dst_ap, s=stg[i][:, 0:n]: e.tensor_copy(out=d, in_=s),
                 R=[b_stg[i]], W=[wbuf])
        else:
            S.op(ce, lambda e, d=dst_ap, s=stg[i][:, 0:n], sc=scale_ap:
                 e.tensor_scalar(out=d, in0=s, scalar1=sc, scalar2=None, op0=ALU.mult),
                 R=[b_stg[i], b_fnw], W=[wbuf])
        cnt[0] += 1

    wout_v = wout_d.rearrange("(ko p) n -> p ko n", p=128)
    for ko in range(KD):
        load_cast(wout[:, ko, :], wout_v[:, ko, :], D_MODEL, b_wout)
    wgu_v = wgu_d.rearrange("(ko p) n -> p ko n", p=128)
    for ko in range(KD):
        for c in range(4):
            load_cast(wgu[:, ko, c * STG:(c + 1) * STG], wgu_v[:, ko, c * STG:(c + 1) * STG], STG,
                      b_wgu, scale_ap=fnw[:, ko:ko + 1])
    wdn_v = wdn_d.rearrange("(j p) n -> p j n", p=128)
    for j in range(JF):
        load_cast(wdn[:, j, :], wdn_v[:, j, :], D_MODEL, b_wdn)

    xin = [kb.sb([128, D_MODEL], F32, "xin%d" % i) for i in range(2)]
    oin = [kb.sb([128, D_MODEL], BF16, "oin%d" % i) for i in range(2)]
    b_xin = S.bufs(2, "xin")
    b_oin = S.bufs(2, "oin")
    tbuf = kb.sb([128, KD, 128], BF16, "tbuf")
    b_tbuf = S.buf("tbuf")
    hn = kb.sb([128, D_MODEL], BF16, "hn")
    b_hn = S.buf("hn")
    junk = kb.sb([128, D_MODEL], BF16, "junk")
    b_junk = S.buf("junk")
    aT = kb.sb([128, JF, 128], BF16, "aT")
    b_aT = S.buf("aT")
    sg = [kb.sb([128, 128], F32, "sg%d" % i) for i in range(3)]
    b_sg = S.bufs(3, "sg")
    stat = kb.sb([128, 8], F32, "stat")
    b_stat = S.buf("stat")
    epsc = kb.sb([128, 1], F32, "epsc")
    b_epsc = S.buf("epsc")
    S.op("dve", lambda e: e.memset(epsc[:], NORM_EPS), W=[b_epsc])

    if ext:
        pbank, b_pb = ext["pb"], ext["b_pb"]
        ptb_t = pbank[0][:].bitcast(BF16)
        idx_sb = kb.sb([128, 4 * NTL], mybir.dt.int32, "idx")
        b_idx = S.buf("idx")
        S.dma("sp", idx_sb[:], ext["idx"], b_idx, W=[b_idx])
        TT_ = ext["T"]
    else:
        pbank = [None] + [kb.ps([128, 512], F32, "pb%d" % i) for i in range(1, 8)]
        b_pb = S.bufs(8, "pb", excl=True)
        ptb_t = kb.ps([128, D_MODEL], BF16, "ptb")

    x_v = x_d.rearrange("(t p) d -> t p d", p=128)
    o_v = o_d.rearrange("(t p) d -> t p d", p=128) if o_d is not None else None
    y_v = y_d.rearrange("(t p) d -> t p d", p=128)

    def rms_scale(src, b_src, col):
        S.op("act", lambda e: e.activation(out=junk[:], in_=src, func=AF.Square,
                                           accum_out=stat[:, col:col + 1]),
             R=[b_src], W=[b_junk, b_stat])
        S.op("act", lambda e: e.activation(out=stat[:, col:col + 1], in_=stat[:, col:col + 1], func=AF.Sqrt,
                                           scale=1.0 / D_MODEL, bias=epsc[:, 0:1]),
             R=[b_stat, b_epsc], W=[b_stat])
        S.op("dve", lambda e: e.reciprocal(out=stat[:, col:col + 1], in_=stat[:, col:col + 1]),
             R=[b_stat], W=[b_stat])

    hT2 = [kb.sb([128, KD, 128], BF16, "hT2_%d" % i) for i in range(2)]
    b_hT2 = S.bufs(2, "hT2")
    aT2 = [aT, kb.sb([128, JF, 128], BF16, "aT_1")]
    b_aT2 = [b_aT, S.buf("aT_1")]
    ptb = ptb_t

    def stageA(t):
        i = t % 2
        xt, ot = xin[i], oin[i]
        S.dma("sp", xt[:], x_v[t], b_xin[i], W=[b_xin[i]])
        if ext:
            for r_ in range(4):
                S.dma_fn("pool", lambda e, ot=ot, r_=r_, t=t: e.indirect_dma_start(
                    out=ot[:, r_ * 256:(r_ + 1) * 256], out_offset=None,
                    in_=ext["og_all"],
                    in_offset=bass.IndirectOffsetOnAxis(ap=idx_sb[:, r_ * NTL + t:r_ * NTL + t + 1], axis=0)),
                    b_oin[i], R=[b_idx] + list(ext["b_og_all"]), W=[b_oin[i]])
        else:
            S.dma("pool", ot[:], o_v[t], b_oin[i], W=[b_oin[i]])

        def tr_group(e, src=ot):
            ins = None
            for k in range(KD):
                ins = e.transpose(out=ptb[:, k * 128:(k + 1) * 128], in_=src[:, k * 128:(k + 1) * 128],
                                  identity=idb[:])
            return ins
        S.op("pe", tr_group, R=[b_oin[i], b_idb], W=[b_pb[0]])
        S.op("act", lambda e: e.copy(out=tbuf[:].rearrange("p k t -> p (k t)"), in_=ptb[:, 0:KD * 128]),
             R=[b_pb[0]], W=[b_tbuf])
        for nchunk in range(2):
            bk = 1 + nchunk

            def mm_out(e, bk=bk, nchunk=nchunk):
                ins = None
                for k in range(KD):
                    ins = e.matmul(out=pbank[bk][:], lhsT=tbuf[:, k, :],
                                   rhs=wout[:, k, nchunk * 512:(nchunk + 1) * 512],
                                   start=(k == 0), stop=(k == KD - 1))
                return ins
            S.op("pe", mm_out, R=[b_tbuf, b_wout], W=[b_pb[bk]])
            S.op("dve", lambda e, bk=bk, nchunk=nchunk, xt=xt:
                 e.tensor_tensor(out=xt[:, nchunk * 512:(nchunk + 1) * 512],
                                 in0=pbank[bk][:], in1=xt[:, nchunk * 512:(nchunk + 1) * 512], op=ALU.add),
                 R=[b_pb[bk], b_xin[i]], W=[b_xin[i]])
        rms_scale(xt[:], b_xin[i], 0)
        S.op("act", lambda e, xt=xt: e.activation(out=hn[:], in_=xt[:], func=AF.Copy, scale=stat[:, 0:1]),
             R=[b_xin[i], b_stat], W=[b_hn])

        def tr_group2(e):
            ins = None
            for k in range(KD):
                ins = e.transpose(out=ptb[:, k * 128:(k + 1) * 128], in_=hn[:, k * 128:(k + 1) * 128],
                                  identity=idb[:])
            return ins
        S.op("pe", tr_group2, R=[b_hn, b_idb], W=[b_pb[0]])
        S.op("act", lambda e, i=i: e.copy(out=hT2[i][:].rearrange("p k t -> p (k t)"), in_=ptb[:, 0:KD * 128]),
             R=[b_pb[0]], W=[b_hT2[i]])

    def stageB(t):
        i = t % 2
        for j in range(JF):
            bk = (3, 4, 7)[j % 3]

            def mm_gu(e, bk=bk, j=j, i=i):
                ins = None
                for half in range(2):
                    for k in range(KD):
                        c0 = half * D_FF + j * 128
                        ins = e.matmul(out=pbank[bk][:, half * 128:(half + 1) * 128],
                                       lhsT=wgu[:, k, c0:c0 + 128], rhs=hT2[i][:, k, :],
                                       start=(k == 0), stop=(k == KD - 1))
                return ins
            S.op("pe", mm_gu, R=[b_hT2[i], b_wgu], W=[b_pb[bk]])
            s_ = j % 3
            S.op("act", lambda e, bk=bk, s_=s_: e.activation(out=sg[s_][:], in_=pbank[bk][:, 0:128], func=AF.Silu),
                 R=[b_pb[bk]], W=[b_sg[s_]])
            S.op("dve", lambda e, bk=bk, s_=s_, j=j, i=i: e.tensor_tensor(out=aT2[i][:, j, :],
                                                                          in0=pbank[bk][:, 128:256],
                                                                          in1=sg[s_][:], op=ALU.mult),
                 R=[b_pb[bk], b_sg[s_]], W=[b_aT2[i]])

    def stageC(t):
        i = t % 2
        xt = xin[i]
        for nchunk in range(2):
            bk = 5 + nchunk

            def mm_dn(e, bk=bk, nchunk=nchunk, i=i):
                ins = None
                for j in range(JF):
                    ins = e.matmul(out=pbank[bk][:], lhsT=aT2[i][:, j, :],
                                   rhs=wdn[:, j, nchunk * 512:(nchunk + 1) * 512],
                                   start=(j == 0), stop=(j == JF - 1))
                return ins
            S.op("pe", mm_dn, R=[b_aT2[i], b_wdn], W=[b_pb[bk]])
            S.op("dve", lambda e, bk=bk, nchunk=nchunk, xt=xt:
                 e.tensor_tensor(out=xt[:, nchunk * 512:(nchunk + 1) * 512],
                                 in0=pbank[bk][:], in1=xt[:, nchunk * 512:(nchunk + 1) * 512], op=ALU.add),
                 R=[b_pb[bk], b_xin[i]], W=[b_xin[i]])
        if final_norm:
            rms_scale(xt[:], b_xin[i], 1)
            S.op("dve", lambda e, xt=xt: e.scalar_tensor_tensor(out=xt[:], in0=xt[:], scalar=stat[:, 1:2],
                                                                in1=finw[:], op0=ALU.mult, op1=ALU.mult),
                 R=[b_xin[i], b_stat, b_finw], W=[b_xin[i]])
        S.dma("sp", y_v[t], xt[:], b_xin[i], R=[b_xin[i]])

    stageA(0)
    for t in range(NTL):
        if t + 1 < NTL:
            stageA(t + 1)
        stageB(t)
        stageC(t)

    if ext:
        return None
    S.barrier_wait("sp", b_xin)
    return kb.done()


def _ident_bf():
    return np.eye(128, dtype=np.float32).astype(ml_dtypes.bfloat16)


def run_ffn(x_sl, o_sl, w_out, ffn_norm_w, w_gu, w_down, final_w, nc_cache={}):
    NT = x_sl[0].shape[0]
    key = (NT, final_w is not None)
    if key not in nc_cache:
        nc_cache[key] = build_ffn(NT, final_w is not None)
    nc = nc_cache[key]
    fnw = np.ascontiguousarray(ffn_norm_w.reshape(D_MODEL // 128, 128).T)
    in_maps = []
    for c in range(len(x_sl)):
        m = {"x": np.ascontiguousarray(x_sl[c]), "o": np.ascontiguousarray(o_sl[c]),
             "w_out": w_out, "w_gu": w_gu, "w_down": w_down, "ffn_norm_w": fnw,
             "ident_bf": _ident_bf()}
        if final_w is not None:
            m["final_w_bc"] = np.ascontiguousarray(np.broadcast_to(final_w[None, :], (128, D_MODEL)))
        in_maps.append(m)
    res = run_bass_kernel_spmd(nc, in_maps, core_ids=list(range(len(x_sl))))
    return [r["y"] for r in res.results]


NW = 898


NPULL = 6


def build_mixer(T, lam_init, dbg=99, ext=None):
    SKIP = ''
    kb = ext["kb"] if ext else KB()
    nc, S = kb.nc, kb.S
    NCH = T // 512
    NTL = T // 128
    KD = D_MODEL // 128

    if ext:
        x_d = ext["x"]
        wh_d, anw_d, cw_d, sc_d, lamv_d, lnw_d, dnw_d, bt_d, idb_d, cf_d = (
            ext[k] for k in ("wh", "anw", "cw", "sc", "lamv", "lnw", "dnw", "btoep", "ident_bf", "cf"))
        ola_d = ext["og"][:, 0:128]
        od_d = ext["og"][:, 128:256]
    else:
        x_d = kb.din("x", [T, D_MODEL], F32)
        wh_d = kb.din("wh", [D_MODEL, NW], F32)
        anw_d = kb.din("anw", [128, KD], F32)
        cw_d = kb.din("cw", [128, 12], F32)
        sc_d = kb.din("sc", [128, 4], F32)
        lamv_d = kb.din("lamv", [128, 4 * 64], F32)
        lnw_d = kb.din("lnw", [128, 128], F32)
        dnw_d = kb.din("dnw", [128, 128], F32)
        bt_d = kb.din("btoep", [128, 512], F32)
        idb_d = kb.din("ident_bf", [128, 128], BF16)
        cf_d = kb.din("cf", [128, 7 * 128], F32)
        ola_d = kb.dout("o_la", [T, 128], BF16)
        od_d = kb.dout("o_d", [T, 128], BF16)

    def T_(shape, dt, name):
        return kb.sb(shape, dt, name), S.buf(name)

    anw, b_anw = T_([128, KD], F32, "anw")
    cw, b_cw = T_([128, 12], F32, "cw")
    sc, b_sc = T_([128, 4], F32, "sc")
    lamv, b_lamv = T_([128, 256], F32, "lamv")
    lnw, b_lnw = T_([128, 128], F32, "lnw")
    dnw, b_dnw = T_([128, 128], F32, "dnw")
    bt, b_bt = T_([128, 512], F32, "bt")
    idb, b_idb = T_([128, 128], BF16, "idb")
    cf, b_cf = T_([128, 7 * 128], F32, "cf")
    for (t_, d_, b_) in ((anw, anw_d, b_anw), (cw, cw_d, b_cw), (sc, sc_d, b_sc), (lamv, lamv_d, b_lamv),
                         (lnw, lnw_d, b_lnw), (dnw, dnw_d, b_dnw), (bt, bt_d, b_bt), (idb, idb_d, b_idb),
                         (cf, cf_d, b_cf)):
        S.dma("sp", t_[:], d_, b_, W=[b_])
    IDF = cf[:, 0:128]
    ONES = cf[:, 128:256]
    MASKL = cf[:, 256:384]
    MASKU = cf[:, 384:512]
    TRI = cf[:, 512:640]
    SEL63 = cf[:, 640:768]
    SEL127 = cf[:, 768:896]

    cst_, b_cst = T_([128, 8], F32, "cst")
    S.op("dve", lambda e: e.memset(cst_[:, 0:1], 1.0), W=[b_cst])
    S.op("dve", lambda e: e.memset(cst_[:, 1:2], 1e-6), W=[b_cst])
    S.op("dve", lambda e: e.memset(cst_[:, 2:3], 1e-5), W=[b_cst])
    S.op("dve", lambda e: e.memset(cst_[:, 5:6], 0.0), W=[b_cst])
    C_ONE, C_EPS6, C_EPS5, C_NA, C_NLAM, C_ZERO = (cst_[:, i:i + 1] for i in range(6))
    S.op("act", lambda e: e.activation(out=cst_[:, 3:4], in_=sc[:, 0:1], func=AF.Exp), R=[b_sc], W=[b_cst])
    S.op("dve", lambda e: e.tensor_scalar(out=cst_[:, 3:4], in0=cst_[:, 3:4], scalar1=-1.0, scalar2=None,
                                          op0=ALU.mult), R=[b_cst], W=[b_cst])
    lt, b_lt = T_([128, 128], F32, "lamtmp")
    ls, b_ls = T_([128, 4], F32, "lamsum")
    S.op("dve", lambda e: e.tensor_tensor(out=lt[:, 0:64], in0=lamv[:, 0:64], in1=lamv[:, 64:128], op=ALU.mult),
         R=[b_lamv], W=[b_lt])
    S.op("dve", lambda e: e.tensor_tensor(out=lt[:, 64:128], in0=lamv[:, 128:192], in1=lamv[:, 192:256],
                                          op=ALU.mult), R=[b_lamv, b_lt], W=[b_lt])
    if 'r' not in SKIP:
        S.op("dve", lambda e: e.reduce_sum(out=ls[:, 0:1], in_=lt[:, 0:64], axis=AX.X), R=[b_lt], W=[b_ls])
        S.op("dve", lambda e: e.reduce_sum(out=ls[:, 1:2], in_=lt[:, 64:128], axis=AX.X), R=[b_lt, b_ls], W=[b_ls])
    S.op("act", lambda e: e.activation(out=ls[:, 2:4], in_=ls[:, 0:2], func=AF.Exp), R=[b_ls], W=[b_ls])
    S.op("dve", lambda e: e.scalar_tensor_tensor(out=cst_[:, 4:5], in0=ls[:, 3:4], scalar=float(-lam_init),
                                                 in1=ls[:, 2:3], op0=ALU.add, op1=ALU.subtract),
         R=[b_ls, b_cst], W=[b_cst])
    S.op("dve", lambda e: e.tensor_scalar(out=dnw[:], in0=dnw[:], scalar1=float(1.0 - lam_init), scalar2=None,
                                          op0=ALU.mult), R=[b_dnw], W=[b_dnw])

    Wb, b_Wb = T_([128, KD, 1024], BF16, "Wb")
    wst = [kb.sb([128, NW], F32, "wst%d" % i) for i in range(2)]
    b_wst = S.bufs(2, "wst")
    wh_v = wh_d.rearrange("(ko p) n -> p ko n", p=128)
    for ko in range(KD):
        i = ko % 2
        S.dma("sp", wst[i][:], wh_v[:, ko, :], b_wst[i], W=[b_wst[i]])
        if i == 0:
            S.op("dve", lambda e, i=i, ko=ko: e.tensor_scalar(out=Wb[:, ko, 0:NW], in0=wst[i][:],
                                                              scalar1=anw[:, ko:ko + 1], scalar2=None, op0=ALU.mult),
                 R=[b_wst[i], b_anw], W=[b_Wb])
        else:
            S.op("act", lambda e, i=i, ko=ko: e.activation(out=Wb[:, ko, 0:NW], in_=wst[i][:], func=AF.Copy,
                                                           scale=anw[:, ko:ko + 1]),
                 R=[b_wst[i], b_anw], W=[b_Wb])

    KdT, b_KdT = T_([128, T], BF16, "KdT")
    Vaug, b_Vaug = T_([128, NTL, 144], BF16, "Vaug")
    if 'v' not in SKIP:
        S.op("pool", lambda e: e.memset(Vaug[:, :, 128:129], 1.0), W=[b_Vaug])

    xt = [kb.sb([128, D_MODEL], F32, "xt%d" % i) for i in range(4)]
    b_xt = S.bufs(4, "xt")
    xn, b_xn = T_([128, D_MODEL], BF16, "xn")
    junk, b_junk = T_([128, D_MODEL], BF16, "junk")
    junkf, b_junkf = T_([128, 128], F32, "junkf")
    stat, b_stat = T_([128, 4], F32, "stat")
    hT, b_hT = T_([128, KD, 512], BF16, "hT")
    cstg = [kb.sb([128, 515], F32, "cstg%d" % g) for g in range(3)]
    b_cstg = S.bufs(3, "cstg")
    cacc = [kb.sb([128, 512], F32, "cacc%d" % g) for g in range(3)]
    b_cacc = S.bufs(3, "cacc")
    sil = [kb.sb([128, 512], F32, "sil%d" % g) for g in range(2)]
    b_sil = S.bufs(2, "sil")
    sq, b_sq = T_([128, 512], F32, "sq")
    rs, b_rs = T_([128, 512], F32, "rs")
    qnT, b_qnT = T_([128, 512], BF16, "qnT")
    knT, b_knT = T_([128, 512], BF16, "knT")
    vsT, b_vsT = T_([128, 512], BF16, "vsT")
    QdT, b_QdT = T_([128, 512], BF16, "QdT")
    qgT, b_qgT = T_([128, 512], BF16, "qgT")
    kvt, b_kvt = T_([128, 8, 128], BF16, "kvt")
    zs, b_zs = T_([128, 4, 128], BF16, "zs")
    ba, b_ba = T_([128, 4, 2], F32, "ba")
    for g in range(3):
        S.op("dve", lambda e, g=g: e.memset(cstg[g][:, 0:3], 0.0), W=[b_cstg[g]])
    pt = {}
    for nm in ("beta", "eb", "g", "gc", "egc", "bg", "glt", "ekd", "tmpa"):
        pt[nm] = T_([128, 4], F32, "pt_" + nm)
    egl, b_egl = T_([128, 8], F32, "egl")
    NS = 4
    gset = []
    for s_ in range(NS):
        d = {}
        for nm, shp, dt in (("dg", [128, 128], F32), ("Eb", [128, 128], F32), ("Ds", [128, 128], F32),
                            ("EA", [128, 128], F32), ("t1", [128, 128], F32), ("t2", [128, 128], F32),
                            ("MPa", [128, 256], F32), ("MPb", [128, 256], F32),
                            ("MTa", [128, 128], F32), ("MTb", [128, 128], F32),
                            ("TT", [128, 128], BF16), ("vb", [128, 128], BF16), ("kbg", [128, 128], BF16)):
            d[nm] = T_(shp, dt, "%s_%d" % (nm, s_))
        gset.append(d)
    u_sb, b_u = T_([128, 4, 128], F32, "u_sb")
    wT_sb, b_wT = T_([128, 512], BF16, "wT_sb")
    apT, b_apT = T_([128, 4, 128], BF16, "apT")
    kd_sb, b_kd = T_([128, 4, 128], BF16, "kd_sb")
    S_f, b_Sf = T_([128, 128], F32, "S_f")
    Sb, b_Sb = T_([128, 128], BF16, "Sb")
    vn, b_vn = T_([128, 128], BF16, "vn")
    o_sb = [kb.sb([128, 128], F32, "o_sb%d" % i) for i in range(2)]
    b_osb = S.bufs(2, "o_sb")
    on_, b_on = T_([128, 128], F32, "on")
    ola_st = [kb.sb([128, 4, 128], BF16, "ola_st%d" % i) for i in range(2)]
    b_olast = S.bufs(2, "ola_st")
    S.op("dve", lambda e: e.memset(S_f[:], 0.0), W=[b_Sf])
    S.op("dve", lambda e: e.memset(Sb[:], 0.0), W=[b_Sb])
    pT = [[kb.sb([128, 256], BF16, "pT%d%d" % (m, p)) for p in range(2)] for m in range(2)]
    b_pT = [[S.buf("pT%d%d" % (m, p)) for p in range(2)] for m in range(2)]
    s2 = [kb.sb([128, 256], F32, "s2_%d" % m) for m in range(2)]
    b_s2 = S.bufs(2, "s2")
    rden, b_rden = T_([128, 4], F32, "rden")
    O1, b_O1 = T_([128, 128], F32, "O1")
    odf, b_odf = T_([128, 128], F32, "odf")
    od_st = [kb.sb([128, 2, 128], BF16, "od_st%d" % i) for i in range(2)]
    b_odst = S.bufs(2, "od_st")

    if ext:
        pb, b_pb = ext["pb"], ext["b_pb"]
    else:
        pb = [kb.ps([128, 512], F32, "pb%d" % i) for i in range(8)]
        b_pb = S.bufs(8, "pb", excl=True)
    pb0_bf = pb[0][:].bitcast(BF16)

    x_v = x_d.rearrange("(t p) d -> t p d", p=128)
    ola_v = ola_d.rearrange("(c t p) e -> c p t e", p=128, t=4)
    od_v = od_d.rearrange("(c t p) e -> c p t e", p=128, t=2)

    def rstd_from_ss(col, n, epsc):
        S.op("act", lambda e: e.activation(out=stat[:, col:col + 1], in_=stat[:, col:col + 1], func=AF.Ln,
                                           scale=1.0 / n, bias=epsc), R=[b_stat, b_cst], W=[b_stat])
        S.op("act", lambda e: e.activation(out=stat[:, col:col + 1], in_=stat[:, col:col + 1], func=AF.Exp,
                                           scale=-0.5), R=[b_stat], W=[b_stat])

    def silu_via_exp(src_ap, R_src, tmp_ap, b_tmp, out_ap, W_out, mul_eng="dve"):
        S.op("act", lambda e: e.activation(out=tmp_ap, in_=src_ap, func=AF.Exp, scale=-1.0), R=R_src, W=[b_tmp])
        S.op("act", lambda e: e.activation(out=tmp_ap, in_=tmp_ap, func=AF.Ln, bias=C_ONE), R=[b_tmp, b_cst],
             W=[b_tmp])
        S.op("act", lambda e: e.activation(out=tmp_ap, in_=tmp_ap, func=AF.Exp, scale=-1.0), R=[b_tmp], W=[b_tmp])
        S.op(mul_eng, lambda e: e.tensor_tensor(out=out_ap, in0=src_ap, in1=tmp_ap, op=ALU.mult),
             R=list(R_src) + [b_tmp], W=W_out)

    stmp, b_stmp = T_([128, 512], F32, "stmp")
    zraw, b_zraw = T_([128, 4, 128], F32, "zraw")

    hT2 = [hT, kb.sb([128, KD, 512], BF16, "hT_b")]
    b_hT2 = [b_hT, S.buf("hT_b")]
    Qblk = [kb.sb([128, 2, 512], BF16, "Qblk%d" % i) for i in range(2)]
    b_Qblk = S.bufs(2, "Qblk")
    for i_ in range(2):
        S.op("pool", lambda e, i_=i_: e.memset(Qblk[i_][:], 0.0), W=[b_Qblk[i_]])
    pT2 = [kb.sb([128, 512], BF16, "pT2_%d" % i) for i in range(2)]
    b_pT2 = S.bufs(2, "pT2")
    s2w, b_s2w = T_([128, 512], F32, "s2w")
    scan_l = []

    def pull_scan(n):
        for _ in range(min(n, len(scan_l))):
            scan_l.pop(0)()

    zs2 = [zs, kb.sb([128, 4, 128], BF16, "zs_b")]
    b_zs2 = [b_zs, S.buf("zs_b")]
    pend = []

    def pull(n):
        for _ in range(min(n, len(pend))):
            pend.pop(0)()

    def front(j):
        if dbg <= 0:
            return
        for tt in range(4):
            t = 4 * j + tt
            i = tt
            S.dma("sp" if i % 2 == 0 else "pool", xt[i][:],
                  (ext["x_tile"](t) if (ext and ext.get("x_tile") is not None) else x_v[t]), b_xt[i], W=[b_xt[i]],
                  R=(list(ext["b_x"](t)) if (ext and ext.get("b_x") is not None) else []))
        for tt in range(4):
            t = 4 * j + tt
            i = tt
            S.op("act", lambda e, i=i: e.activation(out=junk[:], in_=xt[i][:], func=AF.Square,
                                                    accum_out=stat[:, 0:1]), R=[b_xt[i]], W=[b_junk, b_stat])
            rstd_from_ss(0, D_MODEL, C_EPS6)
            S.op("act", lambda e, i=i: e.activation(out=xn[:], in_=xt[i][:], func=AF.Copy, scale=stat[:, 0:1]),
                 R=[b_xt[i], b_stat], W=[b_xn])

            def trx(e):
                ins = None
                for k in range(KD):
                    ins = e.transpose(out=pb0_bf[:, k * 128:(k + 1) * 128], in_=xn[:, k * 128:(k + 1) * 128],
                                      identity=idb[:])
                return ins
            S.op("pe", trx, R=[b_xn, b_idb], W=[b_pb[0]])
            S.op("dve", lambda e, tt=tt: e.tensor_copy(out=hT2[j % 2][:, :, tt * 128:(tt + 1) * 128],
                                                        in_=pb0_bf[:, 0:1024].rearrange("p (k t) -> p k t", k=KD)),
                 R=[b_pb[0]], W=[b_hT2[j % 2]])
        if dbg <= 1:
            return
        for g in range(5 if 'f' not in SKIP else 0):
            bk = 1 + (g % 2)

            def mmf(e, g=g, bk=bk):
                ins = None
                for k in range(KD):
                    ins = e.matmul(out=pb[bk][:], lhsT=Wb[:, k, g * 128:(g + 1) * 128], rhs=hT2[j % 2][:, k, :],
                                   start=(k == 0), stop=(k == KD - 1))
                return ins
            S.op("pe", mmf, R=[b_Wb, b_hT2[j % 2]], W=[b_pb[bk]])
            if g < 3:
                S.op("act", lambda e, g=g, bk=bk: e.copy(out=cstg[g][:, 3:515], in_=pb[bk][:]),
                     R=[b_pb[bk]], W=[b_cstg[g]])
            elif g == 3:
                S.op("act", lambda e, bk=bk: e.copy(out=Qblk[j % 2][0:64, :, 0:256],
                                                    in_=pb[bk][0:64, :].rearrange("p (q c) -> p q c", q=2)),
                     R=[b_pb[bk]], W=[b_Qblk[j % 2]])
                S.op("act", lambda e, bk=bk: e.copy(out=Qblk[j % 2][64:128, :, 256:512],
                                                    in_=pb[bk][64:128, :].rearrange("p (q c) -> p q c", q=2)),
                     R=[b_pb[bk]], W=[b_Qblk[j % 2]])
            else:
                S.op("act", lambda e, bk=bk, j=j: e.copy(out=KdT[:, j * 512:(j + 1) * 512], in_=pb[bk][:]),
                     R=[b_pb[bk]], W=[b_KdT])
        for tt in range(4 if 't' not in SKIP else 0):
            t = 4 * j + tt

            def mmt(e, tt=tt):
                ins = None
                for k in range(KD):
                    NN = 256 if 'n' in SKIP else 258
                    ins = e.matmul(out=pb[3][:, 0:NN], lhsT=hT2[j % 2][:, k, tt * 128:(tt + 1) * 128],
                                   rhs=Wb[:, k, 640:640 + NN], start=(k == 0), stop=(k == KD - 1))
                return ins
            S.op("pe", mmt, R=[b_Wb, b_hT2[j % 2]], W=[b_pb[3]])
            S.op("dve", lambda e, tt=tt: e.tensor_copy(out=zraw[:, tt, :], in_=pb[3][:, 0:128]),
                 R=[b_pb[3]], W=[b_zraw])
            S.op("dve", lambda e, t=t: e.tensor_copy(out=Vaug[:, t, 0:128], in_=pb[3][:, 128:256]),
                 R=[b_pb[3]], W=[b_Vaug])
            S.op("dve", lambda e, tt=tt: e.tensor_copy(out=ba[:, tt, :], in_=pb[3][:, 256:258]),
                 R=[b_pb[3]], W=[b_ba])
        silu_via_exp(zraw[:].rearrange("p a d -> p (a d)"), [b_zraw], stmp[:], b_stmp,
                     zs2[j % 2][:].rearrange("p a d -> p (a d)"), [b_zs2[j % 2]], mul_eng="pool")

    front(0)
    for j in range(NCH):
        if dbg <= 2:
            continue
        pend.clear()
        if j + 1 < NCH:
            S.defer = pend
            front(j + 1)
            S.defer = None
            pull(4)
        for g in range(3):
            ce = "dve"
            S.op(ce, lambda e, g=g: e.tensor_scalar(out=cacc[g][:], in0=cstg[g][:, 3:515],
                                                    scalar1=cw[:, g * 4 + 3:g * 4 + 4], scalar2=None, op0=ALU.mult),
                 R=[b_cstg[g], b_cw], W=[b_cacc[g]])
            for tap in (2, 1, 0):
                S.op(ce, lambda e, g=g, tap=tap: e.scalar_tensor_tensor(
                    out=cacc[g][:], in0=cstg[g][:, tap:tap + 512], scalar=cw[:, g * 4 + tap:g * 4 + tap + 1],
                    in1=cacc[g][:], op0=ALU.mult, op1=ALU.add),
                    R=[b_cstg[g], b_cw, b_cacc[g]], W=[b_cacc[g]])
            S.op(ce, lambda e, g=g: e.tensor_copy(out=cstg[g][:, 0:3], in_=cstg[g][:, 512:515]),
                 R=[b_cstg[g]], W=[b_cstg[g]])
            if g < 2:
                silu_via_exp(cacc[g][:], [b_cacc[g]], stmp[:], b_stmp, sil[g][:], [b_sil[g]])
            else:
                silu_via_exp(cacc[g][:], [b_cacc[g]], stmp[:], b_stmp, vsT[:], [b_vsT])
        if dbg <= 3:
            continue
        for g in range(2):
            S.op("pool", lambda e, g=g: e.tensor_tensor(out=sq[:], in0=sil[g][:], in1=sil[g][:], op=ALU.mult),
                 R=[b_sil[g]], W=[b_sq])
            bk = 1 + g
            S.op("pe", lambda e, bk=bk: e.matmul(out=pb[bk][:], lhsT=ONES, rhs=sq[:], start=True, stop=True),
                 R=[b_sq, b_cf], W=[b_pb[bk]])
            S.op("act", lambda e, bk=bk: e.activation(out=rs[:], in_=pb[bk][:], func=AF.Ln, bias=C_EPS6),
                 R=[b_pb[bk], b_cst], W=[b_rs])
            S.op("act", lambda e: e.activation(out=rs[:], in_=rs[:], func=AF.Exp, scale=-0.5), R=[b_rs], W=[b_rs])
            if g == 0:
                S.op("dve", lambda e: e.scalar_tensor_tensor(out=qnT[:], in0=sil[0][:], scalar=float(128 ** -0.5),
                                                             in1=rs[:], op0=ALU.mult, op1=ALU.mult),
                     R=[b_sil[0], b_rs], W=[b_qnT])
            else:
                S.op("dve", lambda e: e.tensor_tensor(out=knT[:], in0=sil[1][:], in1=rs[:], op=ALU.mult),
                     R=[b_sil[1], b_rs], W=[b_knT])
        if dbg <= 4:
            continue
        def trkv(e):
            ins = None
            for tt in range(4):
                ins = e.transpose(out=pb0_bf[:, (2 * tt) * 128:(2 * tt + 1) * 128],
                                  in_=knT[:, tt * 128:(tt + 1) * 128], identity=idb[:])
                ins = e.transpose(out=pb0_bf[:, (2 * tt + 1) * 128:(2 * tt + 2) * 128],
                                  in_=vsT[:, tt * 128:(tt + 1) * 128], identity=idb[:])
            return ins
        S.op("pe", trkv, R=[b_knT, b_vsT, b_idb], W=[b_pb[0]])
        S.op("dve", lambda e: e.tensor_copy(out=kvt[:].rearrange("p a d -> p (a d)"), in_=pb0_bf[:, 0:1024]),
             R=[b_pb[0]], W=[b_kvt])
        if dbg <= 5:
            continue
        P = lambda nm: pt[nm][0]
        B = lambda nm: pt[nm][1]
        S.op("act", lambda e: e.activation(out=P("eb")[:], in_=ba[:, :, 0], func=AF.Exp, scale=-1.0),
             R=[b_ba], W=[B("eb")])
        S.op("dve", lambda e: e.tensor_scalar(out=P("eb")[:], in0=P("eb")[:], scalar1=1.0, scalar2=None,
                                              op0=ALU.add), R=[B("eb")], W=[B("eb")])
        S.op("dve", lambda e: e.reciprocal(out=P("beta")[:], in_=P("eb")[:]), R=[B("eb")], W=[B("beta")])
        S.op("act", lambda e: e.activation(out=P("tmpa")[:], in_=ba[:, :, 1], func=AF.Exp, bias=sc[:, 1:2]),
             R=[b_ba, b_sc], W=[B("tmpa")])
        S.op("act", lambda e: e.activation(out=P("tmpa")[:], in_=P("tmpa")[:], func=AF.Ln, bias=C_ONE),
             R=[B("tmpa"), b_cst], W=[B("tmpa")])
        S.op("dve", lambda e: e.tensor_scalar(out=P("g")[:], in0=P("tmpa")[:], scalar1=C_NA, scalar2=None,
                                              op0=ALU.mult), R=[B("tmpa"), b_cst], W=[B("g")])
        S.op("pe", lambda e: e.matmul(out=pb[3][:, 0:4], lhsT=TRI, rhs=P("g")[:], start=True, stop=True),
             R=[B("g"), b_cf], W=[b_pb[3]])
        S.op("act", lambda e: e.copy(out=P("gc")[:], in_=pb[3][:, 0:4]), R=[b_pb[3]], W=[B("gc")])

        def mmgl(e):
            e.matmul(out=pb[3][:, 0:4], lhsT=SEL63, rhs=P("gc")[:], start=True, stop=True)
            return e.matmul(out=pb[3][:, 4:8], lhsT=SEL127, rhs=P("gc")[:], start=True, stop=True)
        S.op("pe", mmgl, R=[B("gc"), b_cf], W=[b_pb[3]])
        eglv = egl[:].rearrange("p (t h) -> p t h", h=2)
        S.op("act", lambda e: e.activation(out=eglv[:, :, 0], in_=pb[3][:, 0:4], func=AF.Exp),
             R=[b_pb[3]], W=[b_egl])
        S.op("act", lambda e: e.activation(out=eglv[:, :, 1], in_=pb[3][:, 4:8], func=AF.Exp),
             R=[b_pb[3], b_egl], W=[b_egl])
        S.op("dve", lambda e: e.tensor_copy(out=P("glt")[0:64, :], in_=pb[3][0:64, 0:4]),
             R=[b_pb[3]], W=[B("glt")])
        S.op("dve", lambda e: e.tensor_copy(out=P("glt")[64:128, :], in_=pb[3][64:128, 4:8]),
             R=[b_pb[3], B("glt")], W=[B("glt")])
        S.op("dve", lambda e: e.tensor_tensor(out=P("ekd")[:], in0=P("glt")[:], in1=P("gc")[:], op=ALU.subtract),
             R=[B("glt"), B("gc")], W=[B("ekd")])
        S.op("act", lambda e: e.activation(out=P("ekd")[:], in_=P("ekd")[:], func=AF.Exp),
             R=[B("ekd")], W=[B("ekd")])
        S.op("act", lambda e: e.activation(out=P("egc")[:], in_=P("gc")[:], func=AF.Exp),
             R=[B("gc")], W=[B("egc")])
        S.op("dve", lambda e: e.tensor_tensor(out=P("bg")[:], in0=P("beta")[:], in1=P("egc")[:], op=ALU.mult),
             R=[B("beta"), B("egc")], W=[B("bg")])

        if dbg <= 6:
            continue
        def tile_gen(tt):
            gs = gset[tt % NS]
            bk = 4 + tt
            G = lambda nm, gs=gs: gs[nm][0]
            GB = lambda nm, gs=gs: gs[nm][1]
            csl = slice(tt * 128, (tt + 1) * 128)
            S.op("dve", lambda e, G=G, tt=tt: e.tensor_scalar(out=G("dg")[:], in0=IDF, scalar1=P("gc")[:, tt:tt + 1],
                                                                scalar2=None, op0=ALU.mult),
                 R=[b_cf, B("gc")], W=[GB("dg")])

            def mm_abb(e, G=G, bk=bk, csl=csl):
                e.matmul(out=pb[bk][:, 0:128], lhsT=ONES, rhs=G("dg")[:], start=True, stop=True)
                e.matmul(out=pb[bk][:, 128:256], lhsT=knT[:, csl], rhs=knT[:, csl], start=True, stop=True)
                return e.matmul(out=pb[bk][:, 256:384], lhsT=knT[:, csl], rhs=qnT[:, csl], start=True, stop=True)
            yield
            S.op("pe", mm_abb, R=[b_cf, GB("dg"), b_knT, b_qnT], W=[b_pb[bk]])
            S.op("dve", lambda e, G=G, bk=bk, tt=tt: e.tensor_scalar(
                out=G("Eb")[:], in0=pb[bk][:, 0:128], scalar1=P("gc")[:, tt:tt + 1], scalar2=None,
                op0=ALU.subtract), R=[b_pb[bk], B("gc")], W=[GB("Eb")])
            S.op("act", lambda e, G=G: e.activation(out=G("Eb")[:], in_=G("Eb")[:], func=AF.Abs),
                 R=[GB("Eb")], W=[GB("Eb")])
            S.op("act", lambda e, G=G: e.activation(out=G("Ds")[:], in_=G("Eb")[:], func=AF.Exp, scale=-1.0),
                 R=[GB("Eb")], W=[GB("Ds")])
            S.op("act", lambda e, G=G, bk=bk: e.activation(out=G("EA")[:], in_=pb[bk][:, 0:128], func=AF.Exp),
                 R=[b_pb[bk]], W=[GB("EA")])
            S.op("dve", lambda e, G=G, bk=bk, tt=tt: e.scalar_tensor_tensor(
                out=G("t1")[:], in0=pb[bk][:, 128:256], scalar=P("beta")[:, tt:tt + 1], in1=G("Ds")[:],
                op0=ALU.mult, op1=ALU.mult), R=[b_pb[bk], B("beta"), GB("Ds")], W=[GB("t1")])
            S.op("dve", lambda e, G=G: e.tensor_tensor(out=G("MTa")[:], in0=G("t1")[:], in1=MASKL, op=ALU.mult),
                 R=[GB("t1"), b_cf], W=[GB("MTa")])
            S.op("dve", lambda e, G=G, bk=bk: e.tensor_tensor(out=G("t2")[:], in0=pb[bk][:, 256:384], in1=G("Ds")[:],
                                                              op=ALU.mult), R=[b_pb[bk], GB("Ds")], W=[GB("t2")])
            S.op("pool", lambda e, G=G, tt=tt: e.tensor_tensor(out=apT[:, tt, :], in0=G("t2")[:], in1=MASKU,
                                                               op=ALU.mult), R=[GB("t2"), b_cf], W=[b_apT])
            S.op("pool", lambda e, G=G, csl=csl: e.tensor_tensor(out=qgT[:, csl], in0=qnT[:, csl], in1=G("EA")[:],
                                                                 op=ALU.mult), R=[b_qnT, GB("EA")], W=[b_qgT])
            yield
            S.op("pe", lambda e, G=G, bk=bk: e.transpose(out=pb[bk][:, 384:512], in_=G("MTa")[:], identity=IDF),
                 R=[GB("MTa"), b_cf], W=[b_pb[bk]])
            S.op("act", lambda e, G=G, bk=bk: e.copy(out=G("MPa")[:, 0:128], in_=pb[bk][:, 384:512]),
                 R=[b_pb[bk]], W=[GB("MPa")])
            S.op("dve", lambda e, G=G, bk=bk: e.tensor_tensor(out=G("MPb")[:, 128:256], in0=pb[bk][:, 384:512],
                                                              in1=IDF, op=ALU.add),
                 R=[b_pb[bk], b_cf], W=[GB("MPb")])

            def st0(e, G=G, bk=bk):
                e.matmul(out=pb[bk][:, 0:128], lhsT=G("MTa")[:], rhs=G("MPa")[:, 0:128], start=True, stop=True)
                return e.matmul(out=pb[bk][:, 128:256], lhsT=G("MPa")[:, 0:128], rhs=G("MTa")[:], start=True,
                                stop=True)
            yield
            S.op("pe", st0, R=[GB("MTa"), GB("MPa")], W=[b_pb[bk]])
            S.op("act", lambda e, G=G, bk=bk: e.copy(out=G("MPb")[:, 0:128], in_=pb[bk][:, 0:128]),
                 R=[b_pb[bk]], W=[GB("MPb")])
            S.op("dve", lambda e, G=G, bk=bk: e.tensor_copy(out=G("MTb")[:], in_=pb[bk][:, 128:256]),
                 R=[b_pb[bk]], W=[GB("MTb")])
            cur, nxt = ("MPb", "MTb"), ("MPa", "MTa")
            for stp in range(1, 5):
                def stj(e, G=G, bk=bk, cur=cur):
                    e.matmul(out=pb[bk][:, 0:256], lhsT=G(cur[1])[:], rhs=G(cur[0])[:, 0:256], start=True, stop=True)
                    return e.matmul(out=pb[bk][:, 256:384], lhsT=G(cur[0])[:, 0:128], rhs=G(cur[1])[:], start=True,
                                    stop=True)
                yield
                S.op("pe", stj, R=[GB(cur[0]), GB(cur[1])], W=[b_pb[bk]])
                S.op("act", lambda e, G=G, bk=bk, nxt=nxt: e.copy(out=G(nxt[0])[:, 0:128], in_=pb[bk][:, 0:128]),
                     R=[b_pb[bk]], W=[GB(nxt[0])])
                S.op("dve", lambda e, G=G, bk=bk, cur=cur, nxt=nxt: e.tensor_tensor(
                    out=G(nxt[0])[:, 128:256], in0=pb[bk][:, 128:256], in1=G(cur[0])[:, 128:256], op=ALU.add),
                    R=[b_pb[bk], GB(cur[0])], W=[GB(nxt[0])])
                S.op("act", lambda e, G=G, bk=bk, nxt=nxt: e.copy(out=G(nxt[1])[:], in_=pb[bk][:, 256:384]),
                     R=[b_pb[bk]], W=[GB(nxt[1])])
                cur, nxt = nxt, cur
            yield
            S.op("pe", lambda e, G=G, bk=bk, cur=cur: e.matmul(out=pb[bk][:, 0:128], lhsT=G(cur[1])[:],
                                                               rhs=G(cur[0])[:, 128:256], start=True, stop=True),
                 R=[GB(cur[0]), GB(cur[1])], W=[b_pb[bk]])
            S.op("dve", lambda e, G=G, bk=bk, cur=cur: e.tensor_tensor(out=G("TT")[:], in0=pb[bk][:, 0:128],
                                                                       in1=G(cur[0])[:, 128:256], op=ALU.add),
                 R=[b_pb[bk], GB(cur[0])], W=[GB("TT")])
            S.op("pool", lambda e, G=G, tt=tt: e.tensor_scalar(out=G("vb")[:], in0=kvt[:, 2 * tt + 1, :],
                                                               scalar1=P("beta")[:, tt:tt + 1], scalar2=None,
                                                               op0=ALU.mult), R=[b_kvt, B("beta")], W=[GB("vb")])
            S.op("pool", lambda e, G=G, tt=tt: e.tensor_scalar(out=G("kbg")[:], in0=kvt[:, 2 * tt, :],
                                                               scalar1=P("bg")[:, tt:tt + 1], scalar2=None,
                                                               op0=ALU.mult), R=[b_kvt, B("bg")], W=[GB("kbg")])
            S.op("pool", lambda e, tt=tt: e.tensor_scalar(out=kd_sb[:, tt, :], in0=kvt[:, 2 * tt, :],
                                                          scalar1=P("ekd")[:, tt:tt + 1], scalar2=None,
                                                          op0=ALU.mult), R=[b_kvt, B("ekd")], W=[b_kd])

            def mm_uw(e, G=G, bk=bk):
                e.matmul(out=pb[bk][:, 0:128], lhsT=G("TT")[:], rhs=G("vb")[:], start=True, stop=True)
                return e.matmul(out=pb[bk][:, 128:256], lhsT=G("kbg")[:], rhs=G("TT")[:], start=True, stop=True)
            yield
            S.op("pe", mm_uw, R=[GB("TT"), GB("vb"), GB("kbg")], W=[b_pb[bk]])
            S.op("act", lambda e, bk=bk, tt=tt: e.copy(out=u_sb[:, tt, :], in_=pb[bk][:, 0:128]),
                 R=[b_pb[bk]], W=[b_u])
            S.op("act", lambda e, bk=bk, csl=csl: e.copy(out=wT_sb[:, csl], in_=pb[bk][:, 128:256]),
                 R=[b_pb[bk]], W=[b_wT])

        gens = [tile_gen(tt) for tt in range(4)]
        while gens:
            for g_ in gens[:]:
                try:
                    next(g_)
                except StopIteration:
                    gens.remove(g_)
            pull(NPULL)
        if dbg <= 7:
            continue
        S.defer = scan_l
        oi = j % 2
        for tt in range(4):
            csl = slice(tt * 128, (tt + 1) * 128)
            osb, b_o = o_sb[tt % 2], b_osb[tt % 2]
            for hh in range(2):
                r = slice(hh * 64, hh * 64 + 64)
                nl = 2 * tt + hh

                def mm1(e, csl=csl):
                    e.matmul(out=pb[6][:, 0:128], lhsT=wT_sb[:, csl], rhs=Sb[:], start=True, stop=True)
                    return e.matmul(out=pb[7][:, 0:128], lhsT=qgT[:, csl], rhs=Sb[:], start=True, stop=False)
                S.op("pe", mm1, R=[b_wT, b_qgT, b_Sb], W=[b_pb[6], b_pb[7]])
                S.op("dve", lambda e, r=r, tt=tt: e.tensor_tensor(out=vn[r, :], in0=u_sb[r, tt, :],
                                                                  in1=pb[6][r, 0:128], op=ALU.subtract),
                     R=[b_u, b_pb[6]], W=[b_vn])

                def mm2(e, r=r, tt=tt):
                    e.matmul(out=pb[7][:, 0:128], lhsT=apT[r, tt, :], rhs=vn[r, :], start=False, stop=True)
                    return e.matmul(out=pb[7][:, 128:256], lhsT=kd_sb[r, tt, :], rhs=vn[r, :], start=True, stop=True)
                S.op("pe", mm2, R=[b_apT, b_kd, b_vn], W=[b_pb[7]])
                S.op("act", lambda e, r=r, osb=osb: e.copy(out=osb[r, :], in_=pb[7][r, 0:128]),
                     R=[b_pb[7]], W=[b_o])
                S.op("dve", lambda e, nl=nl: e.scalar_tensor_tensor(out=S_f[:], in0=S_f[:], scalar=egl[:, nl:nl + 1],
                                                                    in1=pb[7][:, 128:256], op0=ALU.mult,
                                                                    op1=ALU.add),
                     R=[b_Sf, b_egl, b_pb[7]], W=[b_Sf])
                S.op("act", lambda e: e.copy(out=Sb[:], in_=S_f[:]), R=[b_Sf], W=[b_Sb])
            S.op("act", lambda e, osb=osb: e.activation(out=junkf[:], in_=osb[:], func=AF.Square,
                                                        accum_out=stat[:, 1:2]), R=[b_o], W=[b_junkf, b_stat])
            rstd_from_ss(1, 128, C_EPS6)
            S.op("dve", lambda e, osb=osb: e.scalar_tensor_tensor(out=on_[:], in0=osb[:], scalar=stat[:, 1:2],
                                                                  in1=lnw[:], op0=ALU.mult, op1=ALU.mult),
                 R=[b_o, b_stat, b_lnw], W=[b_on])
            S.op("dve", lambda e, tt=tt, oi=oi, zz=zs2[j % 2]: e.tensor_tensor(out=ola_st[oi][:, tt, :], in0=on_[:], in1=zz[:, tt, :],
                                                                op=ALU.mult), R=[b_on, b_zs2[j % 2]], W=[b_olast[oi]])
        S.dma("sp", ola_v[j], ola_st[oi][:], b_olast[oi], R=[b_olast[oi]])
        S.defer = None

        if dbg <= 8:
            pull_scan(len(scan_l))
            continue
        pull(len(pend))
        nscan = max(1, -(-len(scan_l) // (8 * j + 6)))
        for qq in range(2):
            qc = 2 * j + qq
            q0 = qc * 256
            qsl = slice(qq * 256, qq * 256 + 256)
            oi2 = qc % 2
            nkt = 2 * qc + 2
            def emit_qk(kt, qq=qq, qd=Qblk[j % 2], bq=b_Qblk[j % 2]):
                k0 = kt * 128
                bk = 4 + (kt % 2)
                S.op("pe", lambda e, bk=bk, k0=k0, qq=qq, qd=qd: e.matmul(
                    out=pb[bk][:, 0:512], lhsT=KdT[:, k0:k0 + 128], rhs=qd[:, qq, :], start=True, stop=True),
                    R=[b_KdT, bq], W=[b_pb[bk]])

            def emit_exp(kt, q0=q0):
                k0 = kt * 128
                d = q0 - k0
                par = kt % 2
                bk = 4 + par
                if d >= 256:
                    S.op("act", lambda e, bk=bk, par=par: e.activation(
                        out=pT2[par][:], in_=pb[bk][:, 0:512], func=AF.Exp, scale=0.125, bias=sc[:, 2:3]),
                        R=[b_pb[bk], b_sc], W=[b_pT2[par]])
                else:
                    for m in range(2):
                        S.op("dve", lambda e, bk=bk, m=m, d=d: e.scalar_tensor_tensor(
                            out=s2w[:, m * 256:(m + 1) * 256], in0=pb[bk][:, m * 256:(m + 1) * 256], scalar=0.125,
                            in1=bt[:, d + 128:d + 128 + 256], op0=ALU.mult, op1=ALU.add),
                            R=[b_pb[bk], b_bt], W=[b_s2w])
                    S.op("act", lambda e, par=par: e.activation(out=pT2[par][:], in_=s2w[:], func=AF.Exp),
                         R=[b_s2w], W=[b_pT2[par]])

            def emit_pv(kt, qc=qc):
                par = kt % 2
                for m in range(2):
                    for qb in range(2):
                        klast = 2 * qc + qb
                        if kt > klast:
                            continue
                        ab = qb * 2 + m
                        c0 = m * 256 + qb * 128
                        S.op("pe", lambda e, ab=ab, par=par, c0=c0, kt=kt, klast=klast: e.matmul(
                            out=pb[ab][:, 0:129], lhsT=pT2[par][:, c0:c0 + 128],
                            rhs=Vaug[:, kt, 0:129], start=(kt == 0), stop=(kt == klast)),
                            R=[b_pT2[par], b_Vaug], W=[b_pb[ab]])

            emit_qk(0)
            for kt in range(nkt):
                if kt + 1 < nkt:
                    emit_qk(kt + 1)
                emit_exp(kt)
                emit_pv(kt)
                pull_scan(nscan)
            for qb in range(2):
                a1, a2 = qb * 2, qb * 2 + 1
                S.op("dve", lambda e, a1=a1: e.reciprocal(out=rden[:, 0:1], in_=pb[a1][:, 128:129]),
                     R=[b_pb[a1]], W=[b_rden])
                S.op("dve", lambda e, a2=a2: e.reciprocal(out=rden[:, 1:2], in_=pb[a2][:, 128:129]),
                     R=[b_pb[a2], b_rden], W=[b_rden])
                S.op("dve", lambda e: e.tensor_scalar(out=rden[:, 2:3], in0=rden[:, 1:2], scalar1=C_NLAM,
                                                      scalar2=None, op0=ALU.mult), R=[b_rden, b_cst], W=[b_rden])
                S.op("act", lambda e, a1=a1: e.activation(out=O1[:], in_=pb[a1][:, 0:128], func=AF.Copy,
                                                          scale=rden[:, 0:1]), R=[b_pb[a1], b_rden], W=[b_O1])
                S.op("dve", lambda e, a2=a2: e.scalar_tensor_tensor(out=odf[:], in0=pb[a2][:, 0:128],
                                                                    scalar=rden[:, 2:3], in1=O1[:], op0=ALU.mult,
                                                                    op1=ALU.add),
                     R=[b_pb[a2], b_rden, b_O1], W=[b_odf])
                S.op("act", lambda e: e.activation(out=junkf[:], in_=odf[:], func=AF.Square,
                                                   accum_out=stat[:, 2:3]), R=[b_odf], W=[b_junkf, b_stat])
                rstd_from_ss(2, 128, C_EPS5)
                S.op("dve", lambda e, qb=qb, oi2=oi2: e.scalar_tensor_tensor(
                    out=od_st[oi2][:, qb, :], in0=odf[:], scalar=stat[:, 2:3], in1=dnw[:], op0=ALU.mult,
                    op1=ALU.mult), R=[b_odf, b_stat, b_dnw], W=[b_odst[oi2]])
            S.dma("sp", od_v[qc], od_st[oi2][:], b_odst[oi2], R=[b_odst[oi2]])
        pull_scan(len(scan_l))

    if ext:
        return None
    S.barrier_wait("sp", b_olast + b_odst)
    return kb.done()


def _t5_bucket_np(rel):
    n = np.maximum(rel, 0)
    nf = np.maximum(n, 1).astype(np.float32)
    large = 16 + (np.log(nf / np.float32(16)) / np.float32(math.log(128 / 16)) * np.float32(16)).astype(np.int32)
    large = np.minimum(large, 31)
    return np.where(n < 16, n, large)


def _mixer_consts():
    p = np.arange(128)
    same = (p[:, None] // 64) == (p[None, :] // 64)
    ident = np.eye(128, dtype=np.float32)
    ones = np.ones((128, 128), np.float32)
    maskl = np.where(same & (p[:, None] > p[None, :]), -1.0, 0.0).astype(np.float32)
    masku = np.where(same & (p[:, None] <= p[None, :]), 1.0, 0.0).astype(np.float32)
    tri = masku.copy()
    sel63 = np.zeros((128, 128), np.float32)
    sel63[63, :] = 1.0
    sel127 = np.zeros((128, 128), np.float32)
    sel127[127, :] = 1.0
    return np.ascontiguousarray(np.concatenate([ident, ones, maskl, masku, tri, sel63, sel127], axis=1))


def mixer_inputs(xb, l, h, P):
    w_in = P["w_in"][l]
    cols = np.concatenate([
        np.arange(h * 128, (h + 1) * 128),
        512 + np.arange(h * 128, (h + 1) * 128),
        1024 + np.arange(h * 128, (h + 1) * 128),
        2056 + np.arange(h * 128, (h + 1) * 128),
        2568 + np.arange(h * 128, (h + 1) * 128),
        1536 + np.arange(h * 128, (h + 1) * 128),
        3080 + np.arange(h * 128, (h + 1) * 128),
        np.array([2048 + h]),
        np.array([2052 + h]),
    ])
    wh = np.ascontiguousarray(w_in[:, cols])
    anw = np.ascontiguousarray(P["attn_norm_w"][l].reshape(8, 128).T)
    cwl = P["conv_w"][l]
    cw = np.concatenate([cwl[:, g * 512 + h * 128: g * 512 + (h + 1) * 128].T for g in range(3)], axis=1)
    sc = np.zeros((128, 4), np.float32)
    sc[:, 0] = P["a_log"][l, h]
    sc[:, 1] = P["dt_bias"][l, h]
    sc[:, 2] = P["rel_bias"][31, h]
    lamv = np.concatenate([P["lambda_q1"][l], P["lambda_k1"][l], P["lambda_q2"][l], P["lambda_k2"][l]])
    lamv = np.broadcast_to(lamv[None, :], (128, 256))
    lnw = np.broadcast_to(P["la_norm_w"][l][None, :], (128, 128))
    dnw = np.broadcast_to(P["diff_norm_w"][l][None, :], (128, 128))
    kl = np.arange(128)[:, None]
    jj = np.arange(512)[None, :]
    rel = jj - 128 - kl
    bt = np.where(rel >= 0, P["rel_bias"][_t5_bucket_np(rel), h], np.float32(-30000.0)).astype(np.float32)
    c = np.ascontiguousarray
    return {"x": c(xb), "wh": wh, "anw": anw, "cw": c(cw.astype(np.float32)), "sc": sc,
            "lamv": c(lamv.astype(np.float32)), "lnw": c(lnw.astype(np.float32)),
            "dnw": c(dnw.astype(np.float32)), "btoep": c(bt), "ident_bf": _ident_bf(), "cf": _mixer_consts()}


CC_GROUPS = [[0, 1, 2, 3], [4, 5, 6, 7]]
_MIX_KEYS = ("wh", "anw", "cw", "sc", "lamv", "lnw", "dnw")


def build_fused(T):
    kb = KB()
    nc, S = kb.nc, kb.S
    NT = T // NH
    NTL = NT // 128
    I32 = mybir.dt.int32
    shp = {"wh": [D_MODEL, NW], "anw": [128, 8], "cw": [128, 12], "sc": [128, 4], "lamv": [128, 256],
           "lnw": [128, 128], "dnw": [128, 128]}
    x_d = kb.din("x", [T, D_MODEL], F32)
    xs_d = kb.din("xs", [NT, D_MODEL], F32)
    idx_d = kb.din("idx", [128, NH * NTL], I32)
    bt_d = kb.din("btoep", [128, 512], F32)
    idb_d = kb.din("ident_bf", [128, 128], BF16)
    cf_d = kb.din("cf", [128, 7 * 128], F32)
    fin_d = kb.din("final_w_bc", [128, D_MODEL], F32)
    lay = []
    for l in range(DEPTH):
        d = {k: kb.din("%s%d" % (k, l), shp[k], F32) for k in _MIX_KEYS}
        d["w_out"] = kb.din("w_out%d" % l, [D_MODEL, D_MODEL], F32)
        d["w_gu"] = kb.din("w_gu%d" % l, [D_MODEL, 2 * D_FF], F32)
        d["w_down"] = kb.din("w_down%d" % l, [D_FF, D_MODEL], F32)
        d["ffn_norm_w"] = kb.din("fnw%d" % l, [128, 8], F32)
        lay.append(d)
    y_d = kb.dout("y", [NT, D_MODEL], F32)
    og_in = kb.dint("og_in", [T, 256], BF16)
    og_all = kb.dint("og_all", [NH * T, 256], BF16)
    xs_in = kb.dint("xs_in", [NT, D_MODEL], F32)
    x1_all = kb.dint("x1_all", [T, D_MODEL], F32)

    pb = [kb.ps([128, 512], F32, "pb%d" % i) for i in range(8)]
    b_pb = S.bufs(8, "pb", excl=True)
    ORC = min(T, 2048)
    NKO = T // ORC
    XRC = min(NT, 256)
    NKX = NT // XRC
    b_og_all = S.bufs(NKO, "og_all")
    b_x1all = S.bufs(NKX, "x1_all")
    kb.persist = b_pb + b_og_all + b_x1all

    def allgather(src, dst, b_dst, name):
        S.dma_fn("pool", lambda e: e.collective_compute("AllGather", ALU.bypass, replica_groups=CC_GROUPS,
                                                         ins=[src], outs=[dst]),
                 S.buf(name), W=[b_dst], inc=None)

    def x1_tile(t):
        tok = t * 128
        r, w = divmod(tok, NT)
        k, ww = divmod(w, XRC)
        base = k * (NH * XRC) + r * XRC + ww
        return x1_all[base:base + 128, :]

    for l in range(DEPTH):
        lam_init = 0.8 - 0.6 * math.exp(-0.3 * l)
        kb.begin_phase()
        if l > 0:
            for k in range(NKX):
                allgather(xs_in[k * XRC:(k + 1) * XRC, :], x1_all[k * NH * XRC:(k + 1) * NH * XRC, :], b_x1all[k],
                          "ccx%d_%d" % (l, k))
        ext = {"kb": kb, "pb": pb, "b_pb": b_pb, "x": x_d, "x_tile": (None if l == 0 else x1_tile),
               "b_x": (None if l == 0 else (lambda t: [b_x1all[((t * 128) % NT) // XRC]])), "og": og_in,
               "btoep": bt_d, "ident_bf": idb_d, "cf": cf_d}
        for k in _MIX_KEYS:
            ext[k] = lay[l][k]
        build_mixer(T, lam_init, ext=ext)
        kb.end_phase()
        kb.begin_phase()
        for k in range(NKO):
            allgather(og_in[k * ORC:(k + 1) * ORC, :], og_all[k * NH * ORC:(k + 1) * NH * ORC, :], b_og_all[k],
                      "cco%d_%d" % (l, k))
        last = (l == DEPTH - 1)
        ext = {"kb": kb, "pb": pb, "b_pb": b_pb, "x": (xs_d if l == 0 else xs_in), "y": (y_d if last else xs_in),
               "og_all": og_all, "b_og_all": b_og_all, "idx": idx_d, "T": T,
               "w_out": lay[l]["w_out"], "w_gu": lay[l]["w_gu"], "w_down": lay[l]["w_down"],
               "ffn_norm_w": lay[l]["ffn_norm_w"], "ident_bf": idb_d, "final_w_bc": fin_d}
        build_ffn(NT, last, ext=ext)
        kb.end_phase()
    kb.es.close()
    return nc


_FUSED_CACHE = {}


def _gather_idx(T, h):
    NT = T // NH
    NTL = NT // 128
    ORC = min(T, 2048)
    tok = h * NT + np.arange(NTL)[None, :] * 128 + np.arange(128)[:, None]
    k, w = tok // ORC, tok % ORC
    cols = [k * (NH * ORC) + r * ORC + w for r in range(NH)]
    return np.ascontiguousarray(np.concatenate(cols, axis=1).astype(np.int32))


def fused_inputs(P, T):
    NT = T // NH
    NTL = NT // 128
    perm = np.concatenate([np.concatenate([np.arange(r * 128, (r + 1) * 128),
                                           512 + np.arange(r * 128, (r + 1) * 128)]) for r in range(NH)])
    in_maps = []
    for c in range(NCORES):
        b, h = divmod(c, NH)
        xb = P["x"][b, :T]
        m = {"x": np.ascontiguousarray(xb), "xs": np.ascontiguousarray(xb[h * NT:(h + 1) * NT]),
             "idx": _gather_idx(T, h),
             "final_w_bc": np.ascontiguousarray(np.broadcast_to(P["final_norm_w"][None, :], (128, D_MODEL)))}
        for l in range(DEPTH):
            mi = mixer_inputs(xb, l, h, P)
            for k in _MIX_KEYS:
                m["%s%d" % (k, l)] = mi[k]
            if l == 0:
                m["btoep"], m["ident_bf"], m["cf"] = mi["btoep"], mi["ident_bf"], mi["cf"]
            m["w_out%d" % l] = np.ascontiguousarray(P["w_out"][l][perm])
            m["w_gu%d" % l] = P["w_gate_up"][l]
            m["w_down%d" % l] = P["w_down"][l]
            m["fnw%d" % l] = np.ascontiguousarray(P["ffn_norm_w"][l].reshape(8, 128).T)
        in_maps.append(m)
    return in_maps


def kernel_fused(P, T):
    if T not in _FUSED_CACHE:
        _FUSED_CACHE[T] = build_fused(T)
    nc = _FUSED_CACHE[T]
    NT = T // NH
    res = run_bass_kernel_spmd(nc, fused_inputs(P, T), core_ids=list(range(NCORES)))
    out = np.empty((BATCH, T, D_MODEL), np.float32)
    for c in range(NCORES):
        b, h = divmod(c, NH)
        out[b, h * NT:(h + 1) * NT] = np.asarray(res.results[c]["y"])
    return out


_MIX_CACHE = {}


def kernel(**inputs):
    P = {k: np.ascontiguousarray(np.asarray(v, dtype=np.float32)) for k, v in inputs.items()}
    return kernel_fused(P, P["x"].shape[1])


def kernel_unfused(**inputs):
    P = {k: np.ascontiguousarray(np.asarray(v, dtype=np.float32)) for k, v in inputs.items()}
    x = P["x"]
    B, T, D = x.shape
    NTOK = B * T
    per = NTOK // NCORES
    for l in range(DEPTH):
        lam_init = 0.8 - 0.6 * math.exp(-0.3 * l)
        key = (T, l)
        if key not in _MIX_CACHE:
            _MIX_CACHE[key] = build_mixer(T, lam_init)
        nc = _MIX_CACHE[key]
        in_maps = [mixer_inputs(x[c // NH], l, c % NH, P) for c in range(NCORES)]
        res = run_bass_kernel_spmd(nc, in_maps, core_ids=list(range(NCORES)))
        o = np.empty((B, T, D), dtype=ml_dtypes.bfloat16)
        for c in range(NCORES):
            b, h = divmod(c, NH)
            o[b, :, h * 128:(h + 1) * 128] = np.asarray(res.results[c]["o_la"])
            o[b, :, 512 + h * 128:512 + (h + 1) * 128] = np.asarray(res.results[c]["o_d"])
        xs = x.reshape(NTOK, D)
        os_ = o.reshape(NTOK, D)
        ys = run_ffn([xs[c * per:(c + 1) * per] for c in range(NCORES)],
                     [os_[c * per:(c + 1) * per] for c in range(NCORES)],
                     P["w_out"][l], P["ffn_norm_w"][l], P["w_gate_up"][l], P["w_down"][l],
                     P["final_norm_w"] if l == DEPTH - 1 else None)
        x = np.concatenate([np.asarray(y) for y in ys], axis=0).reshape(B, T, D)
    return np.ascontiguousarray(x.astype(np.float32))
```

```python
import math
from contextlib import ExitStack

import numpy as np
import ml_dtypes
import concourse.bass as bass
import concourse.mybir as mybir
from concourse.bass_utils import run_bass_kernel_spmd

F32 = mybir.dt.float32
BF16 = mybir.dt.bfloat16
AF = mybir.ActivationFunctionType
ALU = mybir.AluOpType
AX = mybir.AxisListType

D_MODEL = 1024
SEQ = 8192
BATCH = 2
DEPTH = 2
NH = 4
D_FF = 2816
IN_DIM = 3592
NORM_EPS = 1e-6
NCORES = 8

ENGS = ("pe", "act", "dve", "pool", "sp")


class Buf:
    __slots__ = ("name", "w", "r", "dsem", "dcnt", "excl")

    def __init__(self, name, excl=False):
        self.name = name
        self.excl = excl
        self.w = None
        self.r = []
        self.dsem = None
        self.dcnt = 0


class Op:
    __slots__ = ("eng", "fn", "deps", "dma", "sem", "val", "needed", "inc")

    def __init__(self, eng, fn, dma=False):
        self.eng = eng
        self.fn = fn
        self.deps = []
        self.dma = dma
        self.sem = None
        self.val = 0
        self.needed = False
        self.inc = 16


class Sched:
    def __init__(self, nc, es):
        self.nc = nc
        self.es = es
        self.ops = {e: [] for e in ENGS}
        self.esem = {e: es.enter_context(nc.semaphore("s_" + e)) for e in ENGS}
        self.nbuf = 0
        self.cnt = {e: 0 for e in ENGS}
        self.phase_dmas = []
        self.nsem = 0
        self.defer = None

    def buf(self, name=None, excl=False):
        self.nbuf += 1
        return Buf(name or ("b%d" % self.nbuf), excl)

    def bufs(self, n, name="b", excl=False):
        return [self.buf("%s%d" % (name, i), excl) for i in range(n)]

    def _link(self, o, R, W):
        deps = []
        for b in R:
            if b.w is not None:
                d = b.w
                if d.dma or o.dma or d.eng != o.eng or o.eng != "pe":
                    deps.append(d)
            if b.excl:
                for d in b.r:
                    if d.eng != o.eng:
                        deps.append(d)
        for b in W:
            if b.w is not None:
                d = b.w
                if d.dma or o.dma or d.eng != o.eng or o.eng != "pe":
                    deps.append(d)
            for d in b.r:
                if d.dma or o.dma or d.eng != o.eng or o.eng != "pe":
                    deps.append(d)
        o.deps = deps
        for b in R:
            if b in W:
                continue
            if b.excl:
                b.r = []
            elif not o.dma:
                b.r = [x for x in b.r if x.dma or x.eng != o.eng]
            b.r.append(o)
        for b in W:
            b.w = o
            b.r = []

    def op(self, eng, fn, R=(), W=()):
        if self.defer is not None:
            self.defer.append(lambda: self.op_now(eng, fn, R, W))
            return None
        return self.op_now(eng, fn, R, W)

    def op_now(self, eng, fn, R=(), W=()):
        o = Op(eng, fn)
        self._link(o, R, W)
        self.ops[eng].append(o)
        return o

    def dma(self, q, out, in_, sb, R=(), W=()):
        return self.dma_fn(q, lambda e, out=out, in_=in_: e.dma_start(out=out, in_=in_), sb, R, W)

    def dma_fn(self, q, fn, sb, R=(), W=(), inc=16):
        if self.defer is not None:
            self.defer.append(lambda: self.dma_fn_now(q, fn, sb, R, W, inc))
            return None
        return self.dma_fn_now(q, fn, sb, R, W, inc)

    def dma_fn_now(self, q, fn, sb, R=(), W=(), inc=16):
        if sb.dsem is None:
            self.nsem += 1
            sb.dsem = self.es.enter_context(self.nc.semaphore("d%d_%s" % (self.nsem, sb.name)))
        o = Op(q, fn, dma=True)
        o.inc = inc
        sb.dcnt += (inc if inc else 1)
        o.sem = sb.dsem
        o.val = sb.dcnt
        self._link(o, R, W)
        self.ops[q].append(o)
        self.phase_dmas.append(o)
        return o

    def phase_barrier(self):
        lasts = []
        for e in ENGS:
            real = [o for o in self.ops[e] if o.fn is not None and not o.dma]
            if real:
                lasts.append(real[-1])
        deps = lasts + list(self.phase_dmas)
        for e in ENGS:
            o = Op(e, None)
            o.deps = [d for d in deps if d.dma or d.eng != e]
            self.ops[e].append(o)
        self.phase_dmas = []

    def barrier_wait(self, eng, R):
        o = Op(eng, None)
        self._link(o, (), R)
        self.ops[eng].append(o)
        return o

    def finalize(self):
        for e in ENGS:
            for o in self.ops[e]:
                for d in o.deps:
                    d.needed = True
        for e in ENGS:
            c = self.cnt[e]
            for o in self.ops[e]:
                if not o.dma and o.needed and o.fn is not None:
                    c += 1
                    o.sem = self.esem[e]
                    o.val = c
            self.cnt[e] = c
        ops = self.ops
        self.ops = {e: [] for e in ENGS}

        def run(eng, lst):
            seen = {}
            for o in lst:
                waits = {}
                for d in o.deps:
                    k = id(d.sem)
                    if k not in waits or waits[k][1] < d.val:
                        waits[k] = (d.sem, d.val)
                for k, (sem, val) in waits.items():
                    if seen.get(k, 0) < val:
                        eng.wait_ge(sem, val)
                        seen[k] = val
                if o.fn is None:
                    continue
                ins = o.fn(eng)
                if o.dma:
                    if o.inc:
                        ins.then_inc(o.sem, o.inc)
                    else:
                        ins.then_inc(o.sem)
                elif o.needed:
                    ins.then_inc(o.sem, 1)

        with self.nc.Block() as block:
            @block.tensor
            def _(e):
                run(e, ops["pe"])

            @block.scalar
            def _(e):
                run(e, ops["act"])

            @block.vector
            def _(e):
                run(e, ops["dve"])

            @block.gpsimd
            def _(e):
                run(e, ops["pool"])

            @block.sync
            def _(e):
                run(e, ops["sp"])


class KB:
    def __init__(self):
        self.nc = bass.Bass("TRN2", target_bir_lowering=False)
        self.es = ExitStack()
        self.S = Sched(self.nc, self.es)
        self.n = 0
        self.pes = None
        self.phase = 0

    def begin_phase(self):
        self.phase += 1
        self.pes = ExitStack()

    def end_phase(self):
        self.S.phase_barrier()
        self.S.finalize()
        self.pes.close()
        self.pes = None
        for b in getattr(self, "persist", []):
            b.w = None
            b.r = []

    def sb(self, shape, dt, name=None):
        self.n += 1
        st = self.pes if self.pes is not None else self.es
        return st.enter_context(self.nc.sbuf_tensor("sb%d_" % self.phase + (name or ("t%d" % self.n)), list(shape), dt))

    def dint(self, name, shape, dt):
        return self.nc.dram_tensor(name, list(shape), dt).ap()

    def ps(self, shape, dt, name=None):
        self.n += 1
        return self.es.enter_context(self.nc.psum_tensor("ps_" + (name or ("p%d" % self.n)), list(shape), dt))

    def din(self, name, shape, dt):
        return self.nc.dram_tensor(name, list(shape), dt, kind="ExternalInput").ap()

    def dout(self, name, shape, dt):
        return self.nc.dram_tensor(name, list(shape), dt, kind="ExternalOutput").ap()

    def done(self):
        self.S.finalize()
        self.es.close()
        return self.nc


def build_ffn(NT, final_norm, ext=None):
    kb = ext["kb"] if ext else KB()
    nc, S = kb.nc, kb.S
    NTL = NT // 128
    KD = D_MODEL // 128
    JF = D_FF // 128

    if ext:
        x_d, wout_d, wgu_d, wdn_d, fnw_d, idb_d, y_d = (
            ext[k] for k in ("x", "w_out", "w_gu", "w_down", "ffn_norm_w", "ident_bf", "y"))
        if final_norm:
            fin_d = ext["final_w_bc"]
        o_d = None
    else:
        x_d = kb.din("x", [NT, D_MODEL], F32)
        o_d = kb.din("o", [NT, D_MODEL], BF16)
        wout_d = kb.din("w_out", [D_MODEL, D_MODEL], F32)
        wgu_d = kb.din("w_gu", [D_MODEL, 2 * D_FF], F32)
        wdn_d = kb.din("w_down", [D_FF, D_MODEL], F32)
        fnw_d = kb.din("ffn_norm_w", [128, KD], F32)
        idb_d = kb.din("ident_bf", [128, 128], BF16)
        if final_norm:
            fin_d = kb.din("final_w_bc", [128, D_MODEL], F32)
        y_d = kb.dout("y", [NT, D_MODEL], F32)

    wout = kb.sb([128, KD, D_MODEL], BF16, "wout")
    wgu = kb.sb([128, KD, 2 * D_FF], BF16, "wgu")
    wdn = kb.sb([128, JF, D_MODEL], BF16, "wdn")
    fnw = kb.sb([128, KD], F32, "fnw")
    idb = kb.sb([128, 128], BF16, "idb")
    b_wout, b_wgu, b_wdn, b_fnw, b_idb = S.bufs(5, "wres")
    if final_norm:
        finw = kb.sb([128, D_MODEL], F32, "finw")
        b_finw = S.buf("finw")
        S.dma("sp", finw[:], fin_d, b_finw, W=[b_finw])
    S.dma("sp", fnw[:], fnw_d, b_fnw, W=[b_fnw])
    S.dma("sp", idb[:], idb_d, b_idb, W=[b_idb])

    STG = 1408
    NSTG = 3
    stg = [kb.sb([128, STG], F32, "stg%d" % i) for i in range(NSTG)]
    b_stg = S.bufs(NSTG, "stg")
    cnt = [0]
    cast_engs = ("dve", "act")

    def load_cast(dst_ap, src_ap, n, wbuf, scale_ap=None):
        i = cnt[0] % NSTG
        q = "sp" if (cnt[0] % 2 == 0) else "pool"
        S.dma(q, stg[i][:, 0:n], src_ap, b_stg[i], W=[b_stg[i]])
        ce = cast_engs[cnt[0] % 2]
        if ce == "act":
            if scale_ap is None:
                S.op("act", lambda e, d=dst_ap, s=stg[i][:, 0:n]: e.copy(out=d, in_=s), R=[b_stg[i]], W=[wbuf])
            else:
                S.op("act", lambda e, d=dst_ap, s=stg[i][:, 0:n], sc=scale_ap:
                     e.activation(out=d, in_=s, func=AF.Copy, scale=sc), R=[b_stg[i], b_fnw], W=[wbuf])
        elif scale_ap is None:
            S.op(ce, lambda e, d=dst_ap, s=stg[i][:, 0:n]: e.tensor_copy(out=d, in_=s),
                 R=[b_stg[i]], W=[wbuf])
        else:
            S.op(ce, lambda e, d=dst_ap, s=stg[i][:, 0:n], sc=scale_ap:
                 e.tensor_scalar(out=d, in0=s, scalar1=sc, scalar2=None, op0=ALU.mult),
                 R=[b_stg[i], b_fnw], W=[wbuf])
        cnt[0] += 1

    wout_v = wout_d.rearrange("(ko p) n -> p ko n", p=128)
    for ko in range(KD):
        load_cast(wout[:, ko, :], wout_v[:, ko, :], D_MODEL, b_wout)
    wgu_v = wgu_d.rearrange("(ko p) n -> p ko n", p=128)
    for ko in range(KD):
        for c in range(4):
            load_cast(wgu[:, ko, c * STG:(c + 1) * STG], wgu_v[:, ko, c * STG:(c + 1) * STG], STG,
                      b_wgu, scale_ap=fnw[:, ko:ko + 1])
    wdn_v = wdn_d.rearrange("(j p) n -> p j n", p=128)
    for j in range(JF):
        load_cast(wdn[:, j, :], wdn_v[:, j, :], D_MODEL, b_wdn)

    xin = [kb.sb([128, D_MODEL], F32, "xin%d" % i) for i in range(2)]
    oin = [kb.sb([128, D_MODEL], BF16, "oin%d" % i) for i in range(2)]
    b_xin = S.bufs(2, "xin")
    b_oin = S.bufs(2, "oin")
    tbuf = kb.sb([128, KD, 128], BF16, "tbuf")
    b_tbuf = S.buf("tbuf")
    hn = kb.sb([128, D_MODEL], BF16, "hn")
    b_hn = S.buf("hn")
    junk = kb.sb([128, D_MODEL], BF16, "junk")
    b_junk = S.buf("junk")
    aT = kb.sb([128, JF, 128], BF16, "aT")
    b_aT = S.buf("aT")
    sg = [kb.sb([128, 128], F32, "sg%d" % i) for i in range(3)]
    b_sg = S.bufs(3, "sg")
    stat = kb.sb([128, 8], F32, "stat")
    b_stat = S.buf("stat")
    epsc = kb.sb([128, 1], F32, "epsc")
    b_epsc = S.buf("epsc")
    S.op("dve", lambda e: e.memset(epsc[:], NORM_EPS), W=[b_epsc])

    if ext:
        pbank, b_pb = ext["pb"], ext["b_pb"]
        ptb_t = pbank[0][:].bitcast(BF16)
        idx_sb = kb.sb([128, 4 * NTL], mybir.dt.int32, "idx")
        b_idx = S.buf("idx")
        S.dma("sp", idx_sb[:], ext["idx"], b_idx, W=[b_idx])
        TT_ = ext["T"]
    else:
        pbank = [None] + [kb.ps([128, 512], F32, "pb%d" % i) for i in range(1, 8)]
        b_pb = S.bufs(8, "pb", excl=True)
        ptb_t = kb.ps([128, D_MODEL], BF16, "ptb")

    x_v = x_d.rearrange("(t p) d -> t p d", p=128)
    o_v = o_d.rearrange("(t p) d -> t p d", p=128) if o_d is not None else None
    y_v = y_d.rearrange("(t p) d -> t p d", p=128)

    def rms_scale(src, b_src, col):
        S.op("act", lambda e: e.activation(out=junk[:], in_=src, func=AF.Square,
                                           accum_out=stat[:, col:col + 1]),
             R=[b_src], W=[b_junk, b_stat])
        S.op("act", lambda e: e.activation(out=stat[:, col:col + 1], in_=stat[:, col:col + 1], func=AF.Sqrt,
                                           scale=1.0 / D_MODEL, bias=epsc[:, 0:1]),
             R=[b_stat, b_epsc], W=[b_stat])
        S.op("dve", lambda e: e.reciprocal(out=stat[:, col:col + 1], in_=stat[:, col:col + 1]),
             R=[b_stat], W=[b_stat])

    hT2 = [kb.sb([128, KD, 128], BF16, "hT2_%d" % i) for i in range(2)]
    b_hT2 = S.bufs(2, "hT2")
    aT2 = [aT, kb.sb([128, JF, 128], BF16, "aT_1")]
    b_aT2 = [b_aT, S.buf("aT_1")]
    ptb = ptb_t

    def stageA(t):
        i = t % 2
        xt, ot = xin[i], oin[i]
        S.dma("sp", xt[:], x_v[t], b_xin[i], W=[b_xin[i]])
        if ext:
            for r_ in range(4):
                S.dma_fn("pool", lambda e, ot=ot, r_=r_, t=t: e.indirect_dma_start(
                    out=ot[:, r_ * 256:(r_ + 1) * 256], out_offset=None,
                    in_=ext["og_all"],
                    in_offset=bass.IndirectOffsetOnAxis(ap=idx_sb[:, r_ * NTL + t:r_ * NTL + t + 1], axis=0)),
                    b_oin[i], R=[b_idx] + list(ext["b_og_all"]), W=[b_oin[i]])
        else:
            S.dma("pool", ot[:], o_v[t], b_oin[i], W=[b_oin[i]])

        def tr_group(e, src=ot):
            ins = None
            for k in range(KD):
                ins = e.transpose(out=ptb[:, k * 128:(k + 1) * 128], in_=src[:, k * 128:(k + 1) * 128],
                                  identity=idb[:])
            return ins
        S.op("pe", tr_group, R=[b_oin[i], b_idb], W=[b_pb[0]])
        S.op("act", lambda e: e.copy(out=tbuf[:].rearrange("p k t -> p (k t)"), in_=ptb[:, 0:KD * 128]),
             R=[b_pb[0]], W=[b_tbuf])
        for nchunk in range(2):
            bk = 1 + nchunk

            def mm_out(e, bk=bk, nchunk=nchunk):
                ins = None
                for k in range(KD):
                    ins = e.matmul(out=pbank[bk][:], lhsT=tbuf[:, k, :],
                                   rhs=wout[:, k, nchunk * 512:(nchunk + 1) * 512],
                                   start=(k == 0), stop=(k == KD - 1))
                return ins
            S.op("pe", mm_out, R=[b_tbuf, b_wout], W=[b_pb[bk]])
            S.op("dve", lambda e, bk=bk, nchunk=nchunk, xt=xt:
                 e.tensor_tensor(out=xt[:, nchunk * 512:(nchunk + 1) * 512],
                                 in0=pbank[bk][:], in1=xt[:, nchunk * 512:(nchunk + 1) * 512], op=ALU.add),
                 R=[b_pb[bk], b_xin[i]], W=[b_xin[i]])
        rms_scale(xt[:], b_xin[i], 0)
        S.op("act", lambda e, xt=xt: e.activation(out=hn[:], in_=xt[:], func=AF.Copy, scale=stat[:, 0:1]),
             R=[b_xin[i], b_stat], W=[b_hn])

        def tr_group2(e):
            ins = None
            for k in range(KD):
                ins = e.transpose(out=ptb[:, k * 128:(k + 1) * 128], in_=hn[:, k * 128:(k + 1) * 128],
                                  identity=idb[:])
            return ins
        S.op("pe", tr_group2, R=[b_hn, b_idb], W=[b_pb[0]])
        S.op("act", lambda e, i=i: e.copy(out=hT2[i][:].rearrange("p k t -> p (k t)"), in_=ptb[:, 0:KD * 128]),
             R=[b_pb[0]], W=[b_hT2[i]])

    def stageB(t):
        i = t % 2
        for j in range(JF):
            bk = (3, 4, 7)[j % 3]

            def mm_gu(e, bk=bk, j=j, i=i):
                ins = None
                for half in range(2):
                    for k in range(KD):
                        c0 = half * D_FF + j * 128
                        ins = e.matmul(out=pbank[bk][:, half * 128:(half + 1) * 128],
                                       lhsT=wgu[:, k, c0:c0 + 128], rhs=hT2[i][:, k, :],
                                       start=(k == 0), stop=(k == KD - 1))
                return ins
            S.op("pe", mm_gu, R=[b_hT2[i], b_wgu], W=[b_pb[bk]])
            s_ = j % 3
            S.op("act", lambda e, bk=bk, s_=s_: e.activation(out=sg[s_][:], in_=pbank[bk][:, 0:128], func=AF.Silu),
                 R=[b_pb[bk]], W=[b_sg[s_]])
            S.op("dve", lambda e, bk=bk, s_=s_, j=j, i=i: e.tensor_tensor(out=aT2[i][:, j, :],
                                                                          in0=pbank[bk][:, 128:256],
                                                                          in1=sg[s_][:], op=ALU.mult),
                 R=[b_pb[bk], b_sg[s_]], W=[b_aT2[i]])

    def stageC(t):
        i = t % 2
        xt = xin[i]
        for nchunk in range(2):
            bk = 5 + nchunk

            def mm_dn(e, bk=bk, nchunk=nchunk, i=i):
                ins = None
                for j in range(JF):
                    ins = e.matmul(out=pbank[bk][:], lhsT=aT2[i][:, j, :],
                                   rhs=wdn[:, j, nchunk * 512:(nchunk + 1) * 512],
                                   start=(j == 0), stop=(j == JF - 1))
                return ins
            S.op("pe", mm_dn, R=[b_aT2[i], b_wdn], W=[b_pb[bk]])
            S.op("dve", lambda e, bk=bk, nchunk=nchunk, xt=xt:
                 e.tensor_tensor(out=xt[:, nchunk * 512:(nchunk + 1) * 512],
                                 in0=pbank[bk][:], in1=xt[:, nchunk * 512:(nchunk + 1) * 512], op=ALU.add),
                 R=[b_pb[bk], b_xin[i]], W=[b_xin[i]])
        if final_norm:
            rms_scale(xt[:], b_xin[i], 1)
            S.op("dve", lambda e, xt=xt: e.scalar_tensor_tensor(out=xt[:], in0=xt[:], scalar=stat[:, 1:2],
                                                                in1=finw[:], op0=ALU.mult, op1=ALU.mult),
                 R=[b_xin[i], b_stat, b_finw], W=[b_xin[i]])
        S.dma("sp", y_v[t], xt[:], b_xin[i], R=[b_xin[i]])

    stageA(0)
    for t in range(NTL):
        if t + 1 < NTL:
            stageA(t + 1)
        stageB(t)
        stageC(t)

    if ext:
        return None
    S.barrier_wait("sp", b_xin)
    return kb.done()


def _ident_bf():
    return np.eye(128, dtype=np.float32).astype(ml_dtypes.bfloat16)


def run_ffn(x_sl, o_sl, w_out, ffn_norm_w, w_gu, w_down, final_w, nc_cache={}):
    NT = x_sl[0].shape[0]
    key = (NT, final_w is not None)
    if key not in nc_cache:
        nc_cache[key] = build_ffn(NT, final_w is not None)
    nc = nc_cache[key]
    fnw = np.ascontiguousarray(ffn_norm_w.reshape(D_MODEL // 128, 128).T)
    in_maps = []
    for c in range(len(x_sl)):
        m = {"x": np.ascontiguousarray(x_sl[c]), "o": np.ascontiguousarray(o_sl[c]),
             "w_out": w_out, "w_gu": w_gu, "w_down": w_down, "ffn_norm_w": fnw,
             "ident_bf": _ident_bf()}
        if final_w is not None:
            m["final_w_bc"] = np.ascontiguousarray(np.broadcast_to(final_w[None, :], (128, D_MODEL)))
        in_maps.append(m)
    res = run_bass_kernel_spmd(nc, in_maps, core_ids=list(range(len(x_sl))))
    return [r["y"] for r in res.results]


NW = 898


NPULL = 6


def rr_merge(lists):
    out = []
    while any(lists):
        for l in lists:
            if l:
                out.append(l.pop(0))
    return out


def build_mixer(T, lam_init, dbg=99, ext=None):
    SKIP = ''
    kb = ext["kb"] if ext else KB()
    nc, S = kb.nc, kb.S
    NCH = T // 512
    NTL = T // 128
    KD = D_MODEL // 128

    if ext:
        x_d = ext["x"]
        wh_d, anw_d, cw_d, sc_d, lamv_d, lnw_d, dnw_d, bt_d, idb_d, cf_d = (
            ext[k] for k in ("wh", "anw", "cw", "sc", "lamv", "lnw", "dnw", "btoep", "ident_bf", "cf"))
        ola_d = ext["og"][:, 0:128]
        od_d = ext["og"][:, 128:256]
    else:
        x_d = kb.din("x", [T, D_MODEL], F32)
        wh_d = kb.din("wh", [D_MODEL, NW], F32)
        anw_d = kb.din("anw", [128, KD], F32)
        cw_d = kb.din("cw", [128, 12], F32)
        sc_d = kb.din("sc", [128, 4], F32)
        lamv_d = kb.din("lamv", [128, 4 * 64], F32)
        lnw_d = kb.din("lnw", [128, 128], F32)
        dnw_d = kb.din("dnw", [128, 128], F32)
        bt_d = kb.din("btoep", [128, 512], F32)
        idb_d = kb.din("ident_bf", [128, 128], BF16)
        cf_d = kb.din("cf", [128, 7 * 128], F32)
        ola_d = kb.dout("o_la", [T, 128], BF16)
        od_d = kb.dout("o_d", [T, 128], BF16)

    def T_(shape, dt, name):
        return kb.sb(shape, dt, name), S.buf(name)

    anw, b_anw = T_([128, KD], F32, "anw")
    cw, b_cw = T_([128, 12], F32, "cw")
    sc, b_sc = T_([128, 4], F32, "sc")
    lamv, b_lamv = T_([128, 256], F32, "lamv")
    lnw, b_lnw = T_([128, 128], F32, "lnw")
    dnw, b_dnw = T_([128, 128], F32, "dnw")
    bt, b_bt = T_([128, 512], F32, "bt")
    idb, b_idb = T_([128, 128], BF16, "idb")
    cf, b_cf = T_([128, 7 * 128], F32, "cf")
    for (t_, d_, b_) in ((anw, anw_d, b_anw), (cw, cw_d, b_cw), (sc, sc_d, b_sc), (lamv, lamv_d, b_lamv),
                         (lnw, lnw_d, b_lnw), (dnw, dnw_d, b_dnw), (bt, bt_d, b_bt), (idb, idb_d, b_idb),
                         (cf, cf_d, b_cf)):
        S.dma("sp", t_[:], d_, b_, W=[b_])
    IDF = cf[:, 0:128]
    ONES = cf[:, 128:256]
    MASKL = cf[:, 256:384]
    MASKU = cf[:, 384:512]
    TRI = cf[:, 512:640]
    SEL63 = cf[:, 640:768]
    SEL127 = cf[:, 768:896]

    cst_, b_cst = T_([128, 8], F32, "cst")
    S.op("dve", lambda e: e.memset(cst_[:, 0:1], 1.0), W=[b_cst])
    S.op("dve", lambda e: e.memset(cst_[:, 1:2], 1e-6), W=[b_cst])
    S.op("dve", lambda e: e.memset(cst_[:, 2:3], 1e-5), W=[b_cst])
    S.op("dve", lambda e: e.memset(cst_[:, 5:6], 0.0), W=[b_cst])
    C_ONE, C_EPS6, C_EPS5, C_NA, C_NLAM, C_ZERO = (cst_[:, i:i + 1] for i in range(6))
    S.op("act", lambda e: e.activation(out=cst_[:, 3:4], in_=sc[:, 0:1], func=AF.Exp), R=[b_sc], W=[b_cst])
    S.op("dve", lambda e: e.tensor_scalar(out=cst_[:, 3:4], in0=cst_[:, 3:4], scalar1=-1.0, scalar2=None,
                                          op0=ALU.mult), R=[b_cst], W=[b_cst])
    lt, b_lt = T_([128, 128], F32, "lamtmp")
    ls, b_ls = T_([128, 4], F32, "lamsum")
    S.op("dve", lambda e: e.tensor_tensor(out=lt[:, 0:64], in0=lamv[:, 0:64], in1=lamv[:, 64:128], op=ALU.mult),
         R=[b_lamv], W=[b_lt])
    S.op("dve", lambda e: e.tensor_tensor(out=lt[:, 64:128], in0=lamv[:, 128:192], in1=lamv[:, 192:256],
                                          op=ALU.mult), R=[b_lamv, b_lt], W=[b_lt])
    if 'r' not in SKIP:
        S.op("dve", lambda e: e.reduce_sum(out=ls[:, 0:1], in_=lt[:, 0:64], axis=AX.X), R=[b_lt], W=[b_ls])
        S.op("dve", lambda e: e.reduce_sum(out=ls[:, 1:2], in_=lt[:, 64:128], axis=AX.X), R=[b_lt, b_ls], W=[b_ls])
    S.op("act", lambda e: e.activation(out=ls[:, 2:4], in_=ls[:, 0:2], func=AF.Exp), R=[b_ls], W=[b_ls])
    S.op("dve", lambda e: e.scalar_tensor_tensor(out=cst_[:, 4:5], in0=ls[:, 3:4], scalar=float(-lam_init),
                                                 in1=ls[:, 2:3], op0=ALU.add, op1=ALU.subtract),
         R=[b_ls, b_cst], W=[b_cst])
    S.op("dve", lambda e: e.tensor_scalar(out=dnw[:], in0=dnw[:], scalar1=float(1.0 - lam_init), scalar2=None,
                                          op0=ALU.mult), R=[b_dnw], W=[b_dnw])

    Wb, b_Wb = T_([128, KD, 1024], BF16, "Wb")
    wst = [kb.sb([128, NW], F32, "wst%d" % i) for i in range(2)]
    b_wst = S.bufs(2, "wst")
    wh_v = wh_d.rearrange("(ko p) n -> p ko n", p=128)
    for ko in range(KD):
        i = ko % 2
        S.dma("sp", wst[i][:], wh_v[:, ko, :], b_wst[i], W=[b_wst[i]])
        if i == 0:
            S.op("dve", lambda e, i=i, ko=ko: e.tensor_scalar(out=Wb[:, ko, 0:NW], in0=wst[i][:],
                                                              scalar1=anw[:, ko:ko + 1], scalar2=None, op0=ALU.mult),
                 R=[b_wst[i], b_anw], W=[b_Wb])
        else:
            S.op("act", lambda e, i=i, ko=ko: e.activation(out=Wb[:, ko, 0:NW], in_=wst[i][:], func=AF.Copy,
                                                           scale=anw[:, ko:ko + 1]),
                 R=[b_wst[i], b_anw], W=[b_Wb])

    KdT, b_KdT = T_([128, T], BF16, "KdT")
    Vaug, b_Vaug = T_([128, NTL, 144], BF16, "Vaug")
    if 'v' not in SKIP:
        S.op("pool", lambda e: e.memset(Vaug[:, :, 128:129], 1.0), W=[b_Vaug])

    xt = [kb.sb([128, D_MODEL], F32, "xt%d" % i) for i in range(4)]
    b_xt = S.bufs(4, "xt")
    xn, b_xn = T_([128, D_MODEL], BF16, "xn")
    junk, b_junk = T_([128, D_MODEL], BF16, "junk")
    junkf, b_junkf = T_([128, 128], F32, "junkf")
    stat, b_stat = T_([128, 4], F32, "stat")
    hT, b_hT = T_([128, KD, 512], BF16, "hT")
    cstg = [kb.sb([128, 515], F32, "cstg%d" % g) for g in range(3)]
    b_cstg = S.bufs(3, "cstg")
    cacc = [kb.sb([128, 512], F32, "cacc%d" % g) for g in range(3)]
    b_cacc = S.bufs(3, "cacc")
    sil = [kb.sb([128, 512], F32, "sil%d" % g) for g in range(2)]
    b_sil = S.bufs(2, "sil")
    sq, b_sq = T_([128, 512], F32, "sq")
    rs, b_rs = T_([128, 512], F32, "rs")
    qnT, b_qnT = T_([128, 512], BF16, "qnT")
    knT, b_knT = T_([128, 512], BF16, "knT")
    vsT, b_vsT = T_([128, 512], BF16, "vsT")
    QdT, b_QdT = T_([128, 512], BF16, "QdT")
    qgT, b_qgT = T_([128, 512], BF16, "qgT")
    kvt, b_kvt = T_([128, 8, 128], BF16, "kvt")
    zs, b_zs = T_([128, 4, 128], BF16, "zs")
    ba, b_ba = T_([128, 4, 2], F32, "ba")
    for g in range(3):
        S.op("dve", lambda e, g=g: e.memset(cstg[g][:, 0:3], 0.0), W=[b_cstg[g]])
    pt = {}
    for nm in ("beta", "eb", "g", "gc", "egc", "bg", "glt", "ekd", "tmpa"):
        pt[nm] = T_([128, 4], F32, "pt_" + nm)
    egl, b_egl = T_([128, 8], F32, "egl")
    NS = 4
    gset = []
    for s_ in range(NS):
        d = {}
        for nm, shp, dt in (("dg", [128, 128], F32), ("Eb", [128, 128], F32), ("Ds", [128, 128], F32),
                            ("EA", [128, 128], F32), ("t1", [128, 128], F32), ("t2", [128, 128], F32),
                            ("MPa", [128, 256], F32), ("MPb", [128, 256], F32),
                            ("MTa", [128, 128], F32), ("MTb", [128, 128], F32),
                            ("TT", [128, 128], BF16), ("vb", [128, 128], BF16), ("kbg", [128, 128], BF16)):
            d[nm] = T_(shp, dt, "%s_%d" % (nm, s_))
        gset.append(d)
    u_sb, b_u = T_([128, 4, 128], F32, "u_sb")
    wT_sb, b_wT = T_([128, 512], BF16, "wT_sb")
    apT, b_apT = T_([128, 4, 128], BF16, "apT")
    kd_sb, b_kd = T_([128, 4, 128], BF16, "kd_sb")
    S_f, b_Sf = T_([128, 128], F32, "S_f")
    Sb, b_Sb = T_([128, 128], BF16, "Sb")
    vn, b_vn = T_([128, 128], BF16, "vn")
    o_sb = [kb.sb([128, 128], F32, "o_sb%d" % i) for i in range(2)]
    b_osb = S.bufs(2, "o_sb")
    on_, b_on = T_([128, 128], F32, "on")
    ola_st = [kb.sb([128, 4, 128], BF16, "ola_st%d" % i) for i in range(2)]
    b_olast = S.bufs(2, "ola_st")
    S.op("dve", lambda e: e.memset(S_f[:], 0.0), W=[b_Sf])
    S.op("dve", lambda e: e.memset(Sb[:], 0.0), W=[b_Sb])
    pT = [[kb.sb([128, 256], BF16, "pT%d%d" % (m, p)) for p in range(2)] for m in range(2)]
    b_pT = [[S.buf("pT%d%d" % (m, p)) for p in range(2)] for m in range(2)]
    s2 = [kb.sb([128, 256], F32, "s2_%d" % m) for m in range(2)]
    b_s2 = S.bufs(2, "s2")
    rden, b_rden = T_([128, 4], F32, "rden")
    O1, b_O1 = T_([128, 128], F32, "O1")
    odf, b_odf = T_([128, 128], F32, "odf")
    od_st = [kb.sb([128, 2, 128], BF16, "od_st%d" % i) for i in range(2)]
    b_odst = S.bufs(2, "od_st")

    if ext:
        pb, b_pb = ext["pb"], ext["b_pb"]
    else:
        pb = [kb.ps([128, 512], F32, "pb%d" % i) for i in range(8)]
        b_pb = S.bufs(8, "pb", excl=True)
    pb0_bf = pb[0][:].bitcast(BF16)

    x_v = x_d.rearrange("(t p) d -> t p d", p=128)
    ola_v = ola_d.rearrange("(c t p) e -> c p t e", p=128, t=4)
    od_v = od_d.rearrange("(c t p) e -> c p t e", p=128, t=2)

    def rstd_from_ss(col, n, epsc):
        S.op("act", lambda e: e.activation(out=stat[:, col:col + 1], in_=stat[:, col:col + 1], func=AF.Ln,
                                           scale=1.0 / n, bias=epsc), R=[b_stat, b_cst], W=[b_stat])
        S.op("act", lambda e: e.activation(out=stat[:, col:col + 1], in_=stat[:, col:col + 1], func=AF.Exp,
                                           scale=-0.5), R=[b_stat], W=[b_stat])

    def silu_via_exp(src_ap, R_src, tmp_ap, b_tmp, out_ap, W_out, mul_eng="dve"):
        S.op("act", lambda e: e.activation(out=tmp_ap, in_=src_ap, func=AF.Exp, scale=-1.0), R=R_src, W=[b_tmp])
        S.op("act", lambda e: e.activation(out=tmp_ap, in_=tmp_ap, func=AF.Ln, bias=C_ONE), R=[b_tmp, b_cst],
             W=[b_tmp])
        S.op("act", lambda e: e.activation(out=tmp_ap, in_=tmp_ap, func=AF.Exp, scale=-1.0), R=[b_tmp], W=[b_tmp])
        S.op(mul_eng, lambda e: e.tensor_tensor(out=out_ap, in0=src_ap, in1=tmp_ap, op=ALU.mult),
             R=list(R_src) + [b_tmp], W=W_out)

    stmp, b_stmp = T_([128, 512], F32, "stmp")
    stmp3 = [stmp, kb.sb([128, 512], F32, "stmp_1"), kb.sb([128, 512], F32, "stmp_2")]
    b_stmp3 = [b_stmp, S.buf("stmp_1"), S.buf("stmp_2")]
    sq2 = [sq, kb.sb([128, 512], F32, "sq_1")]
    b_sq2 = [b_sq, S.buf("sq_1")]
    rs2 = [rs, kb.sb([128, 512], F32, "rs_1")]
    b_rs2 = [b_rs, S.buf("rs_1")]
    zraw, b_zraw = T_([128, 4, 128], F32, "zraw")

    hT2 = [hT, kb.sb([128, KD, 512], BF16, "hT_b")]
    b_hT2 = [b_hT, S.buf("hT_b")]
    Qblk = [kb.sb([128, 2, 512], BF16, "Qblk%d" % i) for i in range(2)]
    b_Qblk = S.bufs(2, "Qblk")
    for i_ in range(2):
        S.op("pool", lambda e, i_=i_: e.memset(Qblk[i_][:], 0.0), W=[b_Qblk[i_]])
    pT2 = [kb.sb([128, 512], BF16, "pT2_%d" % i) for i in range(2)]
    b_pT2 = S.bufs(2, "pT2")
    s2w, b_s2w = T_([128, 512], F32, "s2w")
    scan_l = []

    def pull_scan(n):
        for _ in range(min(n, len(scan_l))):
            scan_l.pop(0)()

    zs2 = [zs, kb.sb([128, 4, 128], BF16, "zs_b")]
    b_zs2 = [b_zs, S.buf("zs_b")]
    pend = []

    def pull(n):
        for _ in range(min(n, len(pend))):
            pend.pop(0)()

    def front(j):
        if dbg <= 0:
            return
        for tt in range(4):
            t = 4 * j + tt
            i = tt
            S.dma("sp" if i % 2 == 0 else "pool", xt[i][:],
                  (ext["x_tile"](t) if (ext and ext.get("x_tile") is not None) else x_v[t]), b_xt[i], W=[b_xt[i]],
                  R=(list(ext["b_x"](t)) if (ext and ext.get("b_x") is not None) else []))
        for tt in range(4):
            t = 4 * j + tt
            i = tt
            S.op("act", lambda e, i=i: e.activation(out=junk[:], in_=xt[i][:], func=AF.Square,
                                                    accum_out=stat[:, 0:1]), R=[b_xt[i]], W=[b_junk, b_stat])
            rstd_from_ss(0, D_MODEL, C_EPS6)
            S.op("act", lambda e, i=i: e.activation(out=xn[:], in_=xt[i][:], func=AF.Copy, scale=stat[:, 0:1]),
                 R=[b_xt[i], b_stat], W=[b_xn])

            def trx(e):
                ins = None
                for k in range(KD):
                    ins = e.transpose(out=pb0_bf[:, k * 128:(k + 1) * 128], in_=xn[:, k * 128:(k + 1) * 128],
                                      identity=idb[:])
                return ins
            S.op("pe", trx, R=[b_xn, b_idb], W=[b_pb[0]])
            S.op("dve", lambda e, tt=tt: e.tensor_copy(out=hT2[j % 2][:, :, tt * 128:(tt + 1) * 128],
                                                        in_=pb0_bf[:, 0:1024].rearrange("p (k t) -> p k t", k=KD)),
                 R=[b_pb[0]], W=[b_hT2[j % 2]])
        if dbg <= 1:
            return
        for g in range(5 if 'f' not in SKIP else 0):
            bk = 1 + (g % 2)

            def mmf(e, g=g, bk=bk):
                ins = None
                for k in range(KD):
                    ins = e.matmul(out=pb[bk][:], lhsT=Wb[:, k, g * 128:(g + 1) * 128], rhs=hT2[j % 2][:, k, :],
                                   start=(k == 0), stop=(k == KD - 1))
                return ins
            S.op("pe", mmf, R=[b_Wb, b_hT2[j % 2]], W=[b_pb[bk]])
            if g < 3:
                S.op("act", lambda e, g=g, bk=bk: e.copy(out=cstg[g][:, 3:515], in_=pb[bk][:]),
                     R=[b_pb[bk]], W=[b_cstg[g]])
            elif g == 3:
                S.op("act", lambda e, bk=bk: e.copy(out=Qblk[j % 2][0:64, :, 0:256],
                                                    in_=pb[bk][0:64, :].rearrange("p (q c) -> p q c", q=2)),
                     R=[b_pb[bk]], W=[b_Qblk[j % 2]])
                S.op("act", lambda e, bk=bk: e.copy(out=Qblk[j % 2][64:128, :, 256:512],
                                                    in_=pb[bk][64:128, :].rearrange("p (q c) -> p q c", q=2)),
                     R=[b_pb[bk]], W=[b_Qblk[j % 2]])
            else:
                S.op("act", lambda e, bk=bk, j=j: e.copy(out=KdT[:, j * 512:(j + 1) * 512], in_=pb[bk][:]),
                     R=[b_pb[bk]], W=[b_KdT])
        for tt in range(4 if 't' not in SKIP else 0):
            t = 4 * j + tt

            def mmt(e, tt=tt):
                ins = None
                for k in range(KD):
                    NN = 256 if 'n' in SKIP else 258
                    ins = e.matmul(out=pb[3][:, 0:NN], lhsT=hT2[j % 2][:, k, tt * 128:(tt + 1) * 128],
                                   rhs=Wb[:, k, 640:640 + NN], start=(k == 0), stop=(k == KD - 1))
                return ins
            S.op("pe", mmt, R=[b_Wb, b_hT2[j % 2]], W=[b_pb[3]])
            S.op("dve", lambda e, tt=tt: e.tensor_copy(out=zraw[:, tt, :], in_=pb[3][:, 0:128]),
                 R=[b_pb[3]], W=[b_zraw])
            S.op("dve", lambda e, t=t: e.tensor_copy(out=Vaug[:, t, 0:128], in_=pb[3][:, 128:256]),
                 R=[b_pb[3]], W=[b_Vaug])
            S.op("dve", lambda e, tt=tt: e.tensor_copy(out=ba[:, tt, :], in_=pb[3][:, 256:258]),
                 R=[b_pb[3]], W=[b_ba])
        silu_via_exp(zraw[:].rearrange("p a d -> p (a d)"), [b_zraw], stmp[:], b_stmp,
                     zs2[j % 2][:].rearrange("p a d -> p (a d)"), [b_zs2[j % 2]], mul_eng="pool")

    front(0)
    for j in range(NCH):
        if dbg <= 2:
            continue
        pend.clear()
        if j + 1 < NCH:
            S.defer = pend
            front(j + 1)
            S.defer = None
            pull(4)
        S.defer = mA = []
        subs = []
        for g in range(3):
            S.defer = sub = []
            subs.append(sub)
            ce = "dve"
            S.op(ce, lambda e, g=g: e.tensor_scalar(out=cacc[g][:], in0=cstg[g][:, 3:515],
                                                    scalar1=cw[:, g * 4 + 3:g * 4 + 4], scalar2=None, op0=ALU.mult),
                 R=[b_cstg[g], b_cw], W=[b_cacc[g]])
            for tap in (2, 1, 0):
                S.op(ce, lambda e, g=g, tap=tap: e.scalar_tensor_tensor(
                    out=cacc[g][:], in0=cstg[g][:, tap:tap + 512], scalar=cw[:, g * 4 + tap:g * 4 + tap + 1],
                    in1=cacc[g][:], op0=ALU.mult, op1=ALU.add),
                    R=[b_cstg[g], b_cw, b_cacc[g]], W=[b_cacc[g]])
            S.op(ce, lambda e, g=g: e.tensor_copy(out=cstg[g][:, 0:3], in_=cstg[g][:, 512:515]),
                 R=[b_cstg[g]], W=[b_cstg[g]])
            if g < 2:
                silu_via_exp(cacc[g][:], [b_cacc[g]], stmp3[g][:], b_stmp3[g], sil[g][:], [b_sil[g]])
            else:
                silu_via_exp(cacc[g][:], [b_cacc[g]], stmp3[g][:], b_stmp3[g], vsT[:], [b_vsT])
        S.defer = mA
        mA.extend(rr_merge(subs))
        subs = []
        for g in range(2):
            S.defer = sub = []
            subs.append(sub)
            S.op("pool", lambda e, g=g: e.tensor_tensor(out=sq2[g][:], in0=sil[g][:], in1=sil[g][:], op=ALU.mult),
                 R=[b_sil[g]], W=[b_sq2[g]])
            bk = 1 + g
            S.op("pe", lambda e, bk=bk, g=g: e.matmul(out=pb[bk][:], lhsT=ONES, rhs=sq2[g][:], start=True, stop=True),
                 R=[b_sq2[g], b_cf], W=[b_pb[bk]])
            S.op("act", lambda e, bk=bk, g=g: e.activation(out=rs2[g][:], in_=pb[bk][:], func=AF.Ln, bias=C_EPS6),
                 R=[b_pb[bk], b_cst], W=[b_rs2[g]])
            S.op("act", lambda e, g=g: e.activation(out=rs2[g][:], in_=rs2[g][:], func=AF.Exp, scale=-0.5), R=[b_rs2[g]], W=[b_rs2[g]])
            if g == 0:
                S.op("dve", lambda e, g=g: e.scalar_tensor_tensor(out=qnT[:], in0=sil[0][:], scalar=float(128 ** -0.5),
                                                             in1=rs2[g][:], op0=ALU.mult, op1=ALU.mult),
                     R=[b_sil[0], b_rs2[g]], W=[b_qnT])
            else:
                S.op("dve", lambda e, g=g: e.tensor_tensor(out=knT[:], in0=sil[1][:], in1=rs2[g][:], op=ALU.mult),
                     R=[b_sil[1], b_rs2[g]], W=[b_knT])
        S.defer = mA
        mA.extend(rr_merge(subs))
        def trkv(e):
            ins = None
            for tt in range(4):
                ins = e.transpose(out=pb0_bf[:, (2 * tt) * 128:(2 * tt + 1) * 128],
                                  in_=knT[:, tt * 128:(tt + 1) * 128], identity=idb[:])
                ins = e.transpose(out=pb0_bf[:, (2 * tt + 1) * 128:(2 * tt + 2) * 128],
                                  in_=vsT[:, tt * 128:(tt + 1) * 128], identity=idb[:])
            return ins
        S.op("pe", trkv, R=[b_knT, b_vsT, b_idb], W=[b_pb[0]])
        S.op("dve", lambda e: e.tensor_copy(out=kvt[:].rearrange("p a d -> p (a d)"), in_=pb0_bf[:, 0:1024]),
             R=[b_pb[0]], W=[b_kvt])
        if dbg <= 5:
            continue
        S.defer = mB = []
        P = lambda nm: pt[nm][0]
        B = lambda nm: pt[nm][1]
        S.op("act", lambda e: e.activation(out=P("eb")[:], in_=ba[:, :, 0], func=AF.Exp, scale=-1.0),
             R=[b_ba], W=[B("eb")])
        S.op("dve", lambda e: e.tensor_scalar(out=P("eb")[:], in0=P("eb")[:], scalar1=1.0, scalar2=None,
                                              op0=ALU.add), R=[B("eb")], W=[B("eb")])
        S.op("dve", lambda e: e.reciprocal(out=P("beta")[:], in_=P("eb")[:]), R=[B("eb")], W=[B("beta")])
        S.op("act", lambda e: e.activation(out=P("tmpa")[:], in_=ba[:, :, 1], func=AF.Exp, bias=sc[:, 1:2]),
             R=[b_ba, b_sc], W=[B("tmpa")])
        S.op("act", lambda e: e.activation(out=P("tmpa")[:], in_=P("tmpa")[:], func=AF.Ln, bias=C_ONE),
             R=[B("tmpa"), b_cst], W=[B("tmpa")])
        S.op("dve", lambda e: e.tensor_scalar(out=P("g")[:], in0=P("tmpa")[:], scalar1=C_NA, scalar2=None,
                                              op0=ALU.mult), R=[B("tmpa"), b_cst], W=[B("g")])
        S.op("pe", lambda e: e.matmul(out=pb[3][:, 0:4], lhsT=TRI, rhs=P("g")[:], start=True, stop=True),
             R=[B("g"), b_cf], W=[b_pb[3]])
        S.op("act", lambda e: e.copy(out=P("gc")[:], in_=pb[3][:, 0:4]), R=[b_pb[3]], W=[B("gc")])

        def mmgl(e):
            e.matmul(out=pb[3][:, 0:4], lhsT=SEL63, rhs=P("gc")[:], start=True, stop=True)
            return e.matmul(out=pb[3][:, 4:8], lhsT=SEL127, rhs=P("gc")[:], start=True, stop=True)
        S.op("pe", mmgl, R=[B("gc"), b_cf], W=[b_pb[3]])
        eglv = egl[:].rearrange("p (t h) -> p t h", h=2)
        S.op("act", lambda e: e.activation(out=eglv[:, :, 0], in_=pb[3][:, 0:4], func=AF.Exp),
             R=[b_pb[3]], W=[b_egl])
        S.op("act", lambda e: e.activation(out=eglv[:, :, 1], in_=pb[3][:, 4:8], func=AF.Exp),
             R=[b_pb[3], b_egl], W=[b_egl])
        S.op("dve", lambda e: e.tensor_copy(out=P("glt")[0:64, :], in_=pb[3][0:64, 0:4]),
             R=[b_pb[3]], W=[B("glt")])
        S.op("dve", lambda e: e.tensor_copy(out=P("glt")[64:128, :], in_=pb[3][64:128, 4:8]),
             R=[b_pb[3], B("glt")], W=[B("glt")])
        S.op("dve", lambda e: e.tensor_tensor(out=P("ekd")[:], in0=P("glt")[:], in1=P("gc")[:], op=ALU.subtract),
             R=[B("glt"), B("gc")], W=[B("ekd")])
        S.op("act", lambda e: e.activation(out=P("ekd")[:], in_=P("ekd")[:], func=AF.Exp),
             R=[B("ekd")], W=[B("ekd")])
        S.op("act", lambda e: e.activation(out=P("egc")[:], in_=P("gc")[:], func=AF.Exp),
             R=[B("gc")], W=[B("egc")])
        S.op("dve", lambda e: e.tensor_tensor(out=P("bg")[:], in0=P("beta")[:], in1=P("egc")[:], op=ALU.mult),
             R=[B("beta"), B("egc")], W=[B("bg")])

        if dbg <= 6:
            continue
        S.defer = None
        na, nb = len(mA), len(mB)
        ia = ib = 0
        while ia < na or ib < nb:
            if ib >= nb or (ia < na and ia * nb <= ib * na):
                mA[ia]()
                ia += 1
            else:
                mB[ib]()
                ib += 1
        def tile_gen(tt):
            gs = gset[tt % NS]
            bk = 4 + tt
            G = lambda nm, gs=gs: gs[nm][0]
            GB = lambda nm, gs=gs: gs[nm][1]
            csl = slice(tt * 128, (tt + 1) * 128)
            S.op("dve", lambda e, G=G, tt=tt: e.tensor_scalar(out=G("dg")[:], in0=IDF, scalar1=P("gc")[:, tt:tt + 1],
                                                                scalar2=None, op0=ALU.mult),
                 R=[b_cf, B("gc")], W=[GB("dg")])

            def mm_abb(e, G=G, bk=bk, csl=csl):
                e.matmul(out=pb[bk][:, 0:128], lhsT=ONES, rhs=G("dg")[:], start=True, stop=True)
                e.matmul(out=pb[bk][:, 128:256], lhsT=knT[:, csl], rhs=knT[:, csl], start=True, stop=True)
                return e.matmul(out=pb[bk][:, 256:384], lhsT=knT[:, csl], rhs=qnT[:, csl], start=True, stop=True)
            yield
            S.op("pe", mm_abb, R=[b_cf, GB("dg"), b_knT, b_qnT], W=[b_pb[bk]])
            S.op("dve", lambda e, G=G, bk=bk, tt=tt: e.tensor_scalar(
                out=G("Eb")[:], in0=pb[bk][:, 0:128], scalar1=P("gc")[:, tt:tt + 1], scalar2=None,
                op0=ALU.subtract), R=[b_pb[bk], B("gc")], W=[GB("Eb")])
            S.op("act", lambda e, G=G: e.activation(out=G("Eb")[:], in_=G("Eb")[:], func=AF.Abs),
                 R=[GB("Eb")], W=[GB("Eb")])
            S.op("act", lambda e, G=G: e.activation(out=G("Ds")[:], in_=G("Eb")[:], func=AF.Exp, scale=-1.0),
                 R=[GB("Eb")], W=[GB("Ds")])
            S.op("act", lambda e, G=G, bk=bk: e.activation(out=G("EA")[:], in_=pb[bk][:, 0:128], func=AF.Exp),
                 R=[b_pb[bk]], W=[GB("EA")])
            S.op("dve", lambda e, G=G, bk=bk, tt=tt: e.scalar_tensor_tensor(
                out=G("t1")[:], in0=pb[bk][:, 128:256], scalar=P("beta")[:, tt:tt + 1], in1=G("Ds")[:],
                op0=ALU.mult, op1=ALU.mult), R=[b_pb[bk], B("beta"), GB("Ds")], W=[GB("t1")])
            S.op("dve", lambda e, G=G: e.tensor_tensor(out=G("MTa")[:], in0=G("t1")[:], in1=MASKL, op=ALU.mult),
                 R=[GB("t1"), b_cf], W=[GB("MTa")])
            S.op("dve", lambda e, G=G, bk=bk: e.tensor_tensor(out=G("t2")[:], in0=pb[bk][:, 256:384], in1=G("Ds")[:],
                                                              op=ALU.mult), R=[b_pb[bk], GB("Ds")], W=[GB("t2")])
            S.op("pool", lambda e, G=G, tt=tt: e.tensor_tensor(out=apT[:, tt, :], in0=G("t2")[:], in1=MASKU,
                                                               op=ALU.mult), R=[GB("t2"), b_cf], W=[b_apT])
            S.op("pool", lambda e, G=G, csl=csl: e.tensor_tensor(out=qgT[:, csl], in0=qnT[:, csl], in1=G("EA")[:],
                                                                 op=ALU.mult), R=[b_qnT, GB("EA")], W=[b_qgT])
            yield
            S.op("pe", lambda e, G=G, bk=bk: e.transpose(out=pb[bk][:, 384:512], in_=G("MTa")[:], identity=IDF),
                 R=[GB("MTa"), b_cf], W=[b_pb[bk]])
            S.op("act", lambda e, G=G, bk=bk: e.copy(out=G("MPa")[:, 0:128], in_=pb[bk][:, 384:512]),
                 R=[b_pb[bk]], W=[GB("MPa")])
            S.op("dve", lambda e, G=G, bk=bk: e.tensor_tensor(out=G("MPb")[:, 128:256], in0=pb[bk][:, 384:512],
                                                              in1=IDF, op=ALU.add),
                 R=[b_pb[bk], b_cf], W=[GB("MPb")])

            def st0(e, G=G, bk=bk):
                e.matmul(out=pb[bk][:, 0:128], lhsT=G("MTa")[:], rhs=G("MPa")[:, 0:128], start=True, stop=True)
                return e.matmul(out=pb[bk][:, 128:256], lhsT=G("MPa")[:, 0:128], rhs=G("MTa")[:], start=True,
                                stop=True)
            yield
            S.op("pe", st0, R=[GB("MTa"), GB("MPa")], W=[b_pb[bk]])
            S.op("act", lambda e, G=G, bk=bk: e.copy(out=G("MPb")[:, 0:128], in_=pb[bk][:, 0:128]),
                 R=[b_pb[bk]], W=[GB("MPb")])
            S.op("dve", lambda e, G=G, bk=bk: e.tensor_copy(out=G("MTb")[:], in_=pb[bk][:, 128:256]),
                 R=[b_pb[bk]], W=[GB("MTb")])
            cur, nxt = ("MPb", "MTb"), ("MPa", "MTa")
            for stp in range(1, 5):
                def stj(e, G=G, bk=bk, cur=cur):
                    e.matmul(out=pb[bk][:, 0:256], lhsT=G(cur[1])[:], rhs=G(cur[0])[:, 0:256], start=True, stop=True)
                    return e.matmul(out=pb[bk][:, 256:384], lhsT=G(cur[0])[:, 0:128], rhs=G(cur[1])[:], start=True,
                                    stop=True)
                yield
                S.op("pe", stj, R=[GB(cur[0]), GB(cur[1])], W=[b_pb[bk]])
                S.op("act", lambda e, G=G, bk=bk, nxt=nxt: e.copy(out=G(nxt[0])[:, 0:128], in_=pb[bk][:, 0:128]),
                     R=[b_pb[bk]], W=[GB(nxt[0])])
                S.op("dve", lambda e, G=G, bk=bk, cur=cur, nxt=nxt: e.tensor_tensor(
                    out=G(nxt[0])[:, 128:256], in0=pb[bk][:, 128:256], in1=G(cur[0])[:, 128:256], op=ALU.add),
                    R=[b_pb[bk], GB(cur[0])], W=[GB(nxt[0])])
                S.op("act", lambda e, G=G, bk=bk, nxt=nxt: e.copy(out=G(nxt[1])[:], in_=pb[bk][:, 256:384]),
                     R=[b_pb[bk]], W=[GB(nxt[1])])
                cur, nxt = nxt, cur
            yield
            S.op("pe", lambda e, G=G, bk=bk, cur=cur: e.matmul(out=pb[bk][:, 0:128], lhsT=G(cur[1])[:],
                                                               rhs=G(cur[0])[:, 128:256], start=True, stop=True),
                 R=[GB(cur[0]), GB(cur[1])], W=[b_pb[bk]])
            S.op("dve", lambda e, G=G, bk=bk, cur=cur: e.tensor_tensor(out=G("TT")[:], in0=pb[bk][:, 0:128],
                                                                       in1=G(cur[0])[:, 128:256], op=ALU.add),
                 R=[b_pb[bk], GB(cur[0])], W=[GB("TT")])
            S.op("pool", lambda e, G=G, tt=tt: e.tensor_scalar(out=G("vb")[:], in0=kvt[:, 2 * tt + 1, :],
                                                               scalar1=P("beta")[:, tt:tt + 1], scalar2=None,
                                                               op0=ALU.mult), R=[b_kvt, B("beta")], W=[GB("vb")])
            S.op("pool", lambda e, G=G, tt=tt: e.tensor_scalar(out=G("kbg")[:], in0=kvt[:, 2 * tt, :],
                                                               scalar1=P("bg")[:, tt:tt + 1], scalar2=None,
                                                               op0=ALU.mult), R=[b_kvt, B("bg")], W=[GB("kbg")])
            S.op("pool", lambda e, tt=tt: e.tensor_scalar(out=kd_sb[:, tt, :], in0=kvt[:, 2 * tt, :],
                                                          scalar1=P("ekd")[:, tt:tt + 1], scalar2=None,
                                                          op0=ALU.mult), R=[b_kvt, B("ekd")], W=[b_kd])

            def mm_uw(e, G=G, bk=bk):
                e.matmul(out=pb[bk][:, 0:128], lhsT=G("TT")[:], rhs=G("vb")[:], start=True, stop=True)
                return e.matmul(out=pb[bk][:, 128:256], lhsT=G("kbg")[:], rhs=G("TT")[:], start=True, stop=True)
            yield
            S.op("pe", mm_uw, R=[GB("TT"), GB("vb"), GB("kbg")], W=[b_pb[bk]])
            S.op("act", lambda e, bk=bk, tt=tt: e.copy(out=u_sb[:, tt, :], in_=pb[bk][:, 0:128]),
                 R=[b_pb[bk]], W=[b_u])
            S.op("act", lambda e, bk=bk, csl=csl: e.copy(out=wT_sb[:, csl], in_=pb[bk][:, 128:256]),
                 R=[b_pb[bk]], W=[b_wT])

        gens = [tile_gen(tt) for tt in range(4)]
        while gens:
            for g_ in gens[:]:
                try:
                    next(g_)
                except StopIteration:
                    gens.remove(g_)
            pull(NPULL)
        if dbg <= 7:
            continue
        S.defer = scan_l
        oi = j % 2
        for tt in range(4):
            csl = slice(tt * 128, (tt + 1) * 128)
            osb, b_o = o_sb[tt % 2], b_osb[tt % 2]
            for hh in range(2):
                r = slice(hh * 64, hh * 64 + 64)
                nl = 2 * tt + hh

                def mm1(e, csl=csl):
                    e.matmul(out=pb[6][:, 0:128], lhsT=wT_sb[:, csl], rhs=Sb[:], start=True, stop=True)
                    return e.matmul(out=pb[7][:, 0:128], lhsT=qgT[:, csl], rhs=Sb[:], start=True, stop=False)
                S.op("pe", mm1, R=[b_wT, b_qgT, b_Sb], W=[b_pb[6], b_pb[7]])
                S.op("dve", lambda e, r=r, tt=tt: e.tensor_tensor(out=vn[r, :], in0=u_sb[r, tt, :],
                                                                  in1=pb[6][r, 0:128], op=ALU.subtract),
                     R=[b_u, b_pb[6]], W=[b_vn])

                def mm2(e, r=r, tt=tt):
                    e.matmul(out=pb[7][:, 0:128], lhsT=apT[r, tt, :], rhs=vn[r, :], start=False, stop=True)
                    return e.matmul(out=pb[7][:, 128:256], lhsT=kd_sb[r, tt, :], rhs=vn[r, :], start=True, stop=True)
                S.op("pe", mm2, R=[b_apT, b_kd, b_vn], W=[b_pb[7]])
                S.op("act", lambda e, r=r, osb=osb: e.copy(out=osb[r, :], in_=pb[7][r, 0:128]),
                     R=[b_pb[7]], W=[b_o])
                S.op("dve", lambda e, nl=nl: e.scalar_tensor_tensor(out=S_f[:], in0=S_f[:], scalar=egl[:, nl:nl + 1],
                                                                    in1=pb[7][:, 128:256], op0=ALU.mult,
                                                                    op1=ALU.add),
                     R=[b_Sf, b_egl, b_pb[7]], W=[b_Sf])
                S.op("act", lambda e: e.copy(out=Sb[:], in_=S_f[:]), R=[b_Sf], W=[b_Sb])
            S.op("act", lambda e, osb=osb: e.activation(out=junkf[:], in_=osb[:], func=AF.Square,
                                                        accum_out=stat[:, 1:2]), R=[b_o], W=[b_junkf, b_stat])
            rstd_from_ss(1, 128, C_EPS6)
            S.op("dve", lambda e, osb=osb: e.scalar_tensor_tensor(out=on_[:], in0=osb[:], scalar=stat[:, 1:2],
                                                                  in1=lnw[:], op0=ALU.mult, op1=ALU.mult),
                 R=[b_o, b_stat, b_lnw], W=[b_on])
            S.op("dve", lambda e, tt=tt, oi=oi, zz=zs2[j % 2]: e.tensor_tensor(out=ola_st[oi][:, tt, :], in0=on_[:], in1=zz[:, tt, :],
                                                                op=ALU.mult), R=[b_on, b_zs2[j % 2]], W=[b_olast[oi]])
        S.dma("sp", ola_v[j], ola_st[oi][:], b_olast[oi], R=[b_olast[oi]])
        S.defer = None

        if dbg <= 8:
            pull_scan(len(scan_l))
            continue
        pull(len(pend))
        nscan = max(1, -(-len(scan_l) // (8 * j + 6)))
        for qq in range(2):
            qc = 2 * j + qq
            q0 = qc * 256
            qsl = slice(qq * 256, qq * 256 + 256)
            oi2 = qc % 2
            nkt = 2 * qc + 2
            def emit_qk(kt, qq=qq, qd=Qblk[j % 2], bq=b_Qblk[j % 2]):
                k0 = kt * 128
                bk = 4 + (kt % 2)
                S.op("pe", lambda e, bk=bk, k0=k0, qq=qq, qd=qd: e.matmul(
                    out=pb[bk][:, 0:512], lhsT=KdT[:, k0:k0 + 128], rhs=qd[:, qq, :], start=True, stop=True),
                    R=[b_KdT, bq], W=[b_pb[bk]])

            def emit_exp(kt, q0=q0):
                k0 = kt * 128
                d = q0 - k0
                par = kt % 2
                bk = 4 + par
                if d >= 256:
                    S.op("act", lambda e, bk=bk, par=par: e.activation(
                        out=pT2[par][:], in_=pb[bk][:, 0:512], func=AF.Exp, scale=0.125, bias=sc[:, 2:3]),
                        R=[b_pb[bk], b_sc], W=[b_pT2[par]])
                else:
                    for m in range(2):
                        S.op("dve", lambda e, bk=bk, m=m, d=d: e.scalar_tensor_tensor(
                            out=s2w[:, m * 256:(m + 1) * 256], in0=pb[bk][:, m * 256:(m + 1) * 256], scalar=0.125,
                            in1=bt[:, d + 128:d + 128 + 256], op0=ALU.mult, op1=ALU.add),
                            R=[b_pb[bk], b_bt], W=[b_s2w])
                    S.op("act", lambda e, par=par: e.activation(out=pT2[par][:], in_=s2w[:], func=AF.Exp),
                         R=[b_s2w], W=[b_pT2[par]])

            def emit_pv(kt, qc=qc):
                par = kt % 2
                for m in range(2):
                    for qb in range(2):
                        klast = 2 * qc + qb
                        if kt > klast:
                            continue
                        ab = qb * 2 + m
                        c0 = m * 256 + qb * 128
                        S.op("pe", lambda e, ab=ab, par=par, c0=c0, kt=kt, klast=klast: e.matmul(
                            out=pb[ab][:, 0:129], lhsT=pT2[par][:, c0:c0 + 128],
                            rhs=Vaug[:, kt, 0:129], start=(kt == 0), stop=(kt == klast)),
                            R=[b_pT2[par], b_Vaug], W=[b_pb[ab]])

            emit_qk(0)
            for kt in range(nkt):
                if kt + 1 < nkt:
                    emit_qk(kt + 1)
                emit_exp(kt)
                emit_pv(kt)
                pull_scan(nscan)
            for qb in range(2):
                a1, a2 = qb * 2, qb * 2 + 1
                S.op("dve", lambda e, a1=a1: e.reciprocal(out=rden[:, 0:1], in_=pb[a1][:, 128:129]),
                     R=[b_pb[a1]], W=[b_rden])
                S.op("dve", lambda e, a2=a2: e.reciprocal(out=rden[:, 1:2], in_=pb[a2][:, 128:129]),
                     R=[b_pb[a2], b_rden], W=[b_rden])
                S.op("dve", lambda e: e.tensor_scalar(out=rden[:, 2:3], in0=rden[:, 1:2], scalar1=C_NLAM,
                                                      scalar2=None, op0=ALU.mult), R=[b_rden, b_cst], W=[b_rden])
                S.op("act", lambda e, a1=a1: e.activation(out=O1[:], in_=pb[a1][:, 0:128], func=AF.Copy,
                                                          scale=rden[:, 0:1]), R=[b_pb[a1], b_rden], W=[b_O1])
                S.op("dve", lambda e, a2=a2: e.scalar_tensor_tensor(out=odf[:], in0=pb[a2][:, 0:128],
                                                                    scalar=rden[:, 2:3], in1=O1[:], op0=ALU.mult,
                                                                    op1=ALU.add),
                     R=[b_pb[a2], b_rden, b_O1], W=[b_odf])
                S.op("act", lambda e: e.activation(out=junkf[:], in_=odf[:], func=AF.Square,
                                                   accum_out=stat[:, 2:3]), R=[b_odf], W=[b_junkf, b_stat])
                rstd_from_ss(2, 128, C_EPS5)
                S.op("dve", lambda e, qb=qb, oi2=oi2: e.scalar_tensor_tensor(
                    out=od_st[oi2][:, qb, :], in0=odf[:], scalar=stat[:, 2:3], in1=dnw[:], op0=ALU.mult,
                    op1=ALU.mult), R=[b_odf, b_stat, b_dnw], W=[b_odst[oi2]])
            S.dma("sp", od_v[qc], od_st[oi2][:], b_odst[oi2], R=[b_odst[oi2]])
        pull_scan(len(scan_l))

    if ext:
        return None
    S.barrier_wait("sp", b_olast + b_odst)
    return kb.done()


def _t5_bucket_np(rel):
    n = np.maximum(rel, 0)
    nf = np.maximum(n, 1).astype(np.float32)
    large = 16 + (np.log(nf / np.float32(16)) / np.float32(math.log(128 / 16)) * np.float32(16)).astype(np.int32)
    large = np.minimum(large, 31)
    return np.where(n < 16, n, large)


def _mixer_consts():
    p = np.arange(128)
    same = (p[:, None] // 64) == (p[None, :] // 64)
    ident = np.eye(128, dtype=np.float32)
    ones = np.ones((128, 128), np.float32)
    maskl = np.where(same & (p[:, None] > p[None, :]), -1.0, 0.0).astype(np.float32)
    masku = np.where(same & (p[:, None] <= p[None, :]), 1.0, 0.0).astype(np.float32)
    tri = masku.copy()
    sel63 = np.zeros((128, 128), np.float32)
    sel63[63, :] = 1.0
    sel127 = np.zeros((128, 128), np.float32)
    sel127[127, :] = 1.0
    return np.ascontiguousarray(np.concatenate([ident, ones, maskl, masku, tri, sel63, sel127], axis=1))


def mixer_inputs(xb, l, h, P):
    w_in = P["w_in"][l]
    cols = np.concatenate([
        np.arange(h * 128, (h + 1) * 128),
        512 + np.arange(h * 128, (h + 1) * 128),
        1024 + np.arange(h * 128, (h + 1) * 128),
        2056 + np.arange(h * 128, (h + 1) * 128),
        2568 + np.arange(h * 128, (h + 1) * 128),
        1536 + np.arange(h * 128, (h + 1) * 128),
        3080 + np.arange(h * 128, (h + 1) * 128),
        np.array([2048 + h]),
        np.array([2052 + h]),
    ])
    wh = np.ascontiguousarray(w_in[:, cols])
    anw = np.ascontiguousarray(P["attn_norm_w"][l].reshape(8, 128).T)
    cwl = P["conv_w"][l]
    cw = np.concatenate([cwl[:, g * 512 + h * 128: g * 512 + (h + 1) * 128].T for g in range(3)], axis=1)
    sc = np.zeros((128, 4), np.float32)
    sc[:, 0] = P["a_log"][l, h]
    sc[:, 1] = P["dt_bias"][l, h]
    sc[:, 2] = P["rel_bias"][31, h]
    lamv = np.concatenate([P["lambda_q1"][l], P["lambda_k1"][l], P["lambda_q2"][l], P["lambda_k2"][l]])
    lamv = np.broadcast_to(lamv[None, :], (128, 256))
    lnw = np.broadcast_to(P["la_norm_w"][l][None, :], (128, 128))
    dnw = np.broadcast_to(P["diff_norm_w"][l][None, :], (128, 128))
    kl = np.arange(128)[:, None]
    jj = np.arange(512)[None, :]
    rel = jj - 128 - kl
    bt = np.where(rel >= 0, P["rel_bias"][_t5_bucket_np(rel), h], np.float32(-30000.0)).astype(np.float32)
    c = np.ascontiguousarray
    return {"x": c(xb), "wh": wh, "anw": anw, "cw": c(cw.astype(np.float32)), "sc": sc,
            "lamv": c(lamv.astype(np.float32)), "lnw": c(lnw.astype(np.float32)),
            "dnw": c(dnw.astype(np.float32)), "btoep": c(bt), "ident_bf": _ident_bf(), "cf": _mixer_consts()}


CC_GROUPS = [[0, 1, 2, 3], [4, 5, 6, 7]]
_MIX_KEYS = ("wh", "anw", "cw", "sc", "lamv", "lnw", "dnw")


def build_fused(T):
    kb = KB()
    nc, S = kb.nc, kb.S
    NT = T // NH
    NTL = NT // 128
    I32 = mybir.dt.int32
    shp = {"wh": [D_MODEL, NW], "anw": [128, 8], "cw": [128, 12], "sc": [128, 4], "lamv": [128, 256],
           "lnw": [128, 128], "dnw": [128, 128]}
    x_d = kb.din("x", [T, D_MODEL], F32)
    xs_d = kb.din("xs", [NT, D_MODEL], F32)
    idx_d = kb.din("idx", [128, NH * NTL], I32)
    bt_d = kb.din("btoep", [128, 512], F32)
    idb_d = kb.din("ident_bf", [128, 128], BF16)
    cf_d = kb.din("cf", [128, 7 * 128], F32)
    fin_d = kb.din("final_w_bc", [128, D_MODEL], F32)
    lay = []
    for l in range(DEPTH):
        d = {k: kb.din("%s%d" % (k, l), shp[k], F32) for k in _MIX_KEYS}
        d["w_out"] = kb.din("w_out%d" % l, [D_MODEL, D_MODEL], F32)
        d["w_gu"] = kb.din("w_gu%d" % l, [D_MODEL, 2 * D_FF], F32)
        d["w_down"] = kb.din("w_down%d" % l, [D_FF, D_MODEL], F32)
        d["ffn_norm_w"] = kb.din("fnw%d" % l, [128, 8], F32)
        lay.append(d)
    y_d = kb.dout("y", [NT, D_MODEL], F32)
    og_in = kb.dint("og_in", [T, 256], BF16)
    og_all = kb.dint("og_all", [NH * T, 256], BF16)
    xs_in = kb.dint("xs_in", [NT, D_MODEL], F32)
    x1_all = kb.dint("x1_all", [T, D_MODEL], F32)

    pb = [kb.ps([128, 512], F32, "pb%d" % i) for i in range(8)]
    b_pb = S.bufs(8, "pb", excl=True)
    ORC = min(T, 2048)
    NKO = T // ORC
    XRC = min(NT, 256)
    NKX = NT // XRC
    b_og_all = S.bufs(NKO, "og_all")
    b_x1all = S.bufs(NKX, "x1_all")
    kb.persist = b_pb + b_og_all + b_x1all

    def allgather(src, dst, b_dst, name):
        S.dma_fn("pool", lambda e: e.collective_compute("AllGather", ALU.bypass, replica_groups=CC_GROUPS,
                                                         ins=[src], outs=[dst]),
                 S.buf(name), W=[b_dst], inc=None)

    def x1_tile(t):
        tok = t * 128
        r, w = divmod(tok, NT)
        k, ww = divmod(w, XRC)
        base = k * (NH * XRC) + r * XRC + ww
        return x1_all[base:base + 128, :]

    for l in range(DEPTH):
        lam_init = 0.8 - 0.6 * math.exp(-0.3 * l)
        kb.begin_phase()
        if l > 0:
            for k in range(NKX):
                allgather(xs_in[k * XRC:(k + 1) * XRC, :], x1_all[k * NH * XRC:(k + 1) * NH * XRC, :], b_x1all[k],
                          "ccx%d_%d" % (l, k))
        ext = {"kb": kb, "pb": pb, "b_pb": b_pb, "x": x_d, "x_tile": (None if l == 0 else x1_tile),
               "b_x": (None if l == 0 else (lambda t: [b_x1all[((t * 128) % NT) // XRC]])), "og": og_in,
               "btoep": bt_d, "ident_bf": idb_d, "cf": cf_d}
        for k in _MIX_KEYS:
            ext[k] = lay[l][k]
        build_mixer(T, lam_init, ext=ext)
        kb.end_phase()
        kb.begin_phase()
        for k in range(NKO):
            allgather(og_in[k * ORC:(k + 1) * ORC, :], og_all[k * NH * ORC:(k + 1) * NH * ORC, :], b_og_all[k],
                      "cco%d_%d" % (l, k))
        last = (l == DEPTH - 1)
        ext = {"kb": kb, "pb": pb, "b_pb": b_pb, "x": (xs_d if l == 0 else xs_in), "y": (y_d if last else xs_in),
               "og_all": og_all, "b_og_all": b_og_all, "idx": idx_d, "T": T,
               "w_out": lay[l]["w_out"], "w_gu": lay[l]["w_gu"], "w_down": lay[l]["w_down"],
               "ffn_norm_w": lay[l]["ffn_norm_w"], "ident_bf": idb_d, "final_w_bc": fin_d}
        build_ffn(NT, last, ext=ext)
        kb.end_phase()
    kb.es.close()
    return nc


_FUSED_CACHE = {}


def _gather_idx(T, h):
    NT = T // NH
    NTL = NT // 128
    ORC = min(T, 2048)
    tok = h * NT + np.arange(NTL)[None, :] * 128 + np.arange(128)[:, None]
    k, w = tok // ORC, tok % ORC
    cols = [k * (NH * ORC) + r * ORC + w for r in range(NH)]
    return np.ascontiguousarray(np.concatenate(cols, axis=1).astype(np.int32))


def fused_inputs(P, T):
    NT = T // NH
    NTL = NT // 128
    perm = np.concatenate([np.concatenate([np.arange(r * 128, (r + 1) * 128),
                                           512 + np.arange(r * 128, (r + 1) * 128)]) for r in range(NH)])
    in_maps = []
    for c in range(NCORES):
        b, h = divmod(c, NH)
        xb = P["x"][b, :T]
        m = {"x": np.ascontiguousarray(xb), "xs": np.ascontiguousarray(xb[h * NT:(h + 1) * NT]),
             "idx": _gather_idx(T, h),
             "final_w_bc": np.ascontiguousarray(np.broadcast_to(P["final_norm_w"][None, :], (128, D_MODEL)))}
        for l in range(DEPTH):
            mi = mixer_inputs(xb, l, h, P)
            for k in _MIX_KEYS:
                m["%s%d" % (k, l)] = mi[k]
            if l == 0:
                m["btoep"], m["ident_bf"], m["cf"] = mi["btoep"], mi["ident_bf"], mi["cf"]
            m["w_out%d" % l] = np.ascontiguousarray(P["w_out"][l][perm])
            m["w_gu%d" % l] = P["w_gate_up"][l]
            m["w_down%d" % l] = P["w_down"][l]
            m["fnw%d" % l] = np.ascontiguousarray(P["ffn_norm_w"][l].reshape(8, 128).T)
        in_maps.append(m)
    return in_maps


def kernel_fused(P, T):
    if T not in _FUSED_CACHE:
        _FUSED_CACHE[T] = build_fused(T)
    nc = _FUSED_CACHE[T]
    NT = T // NH
    res = run_bass_kernel_spmd(nc, fused_inputs(P, T), core_ids=list(range(NCORES)))
    out = np.empty((BATCH, T, D_MODEL), np.float32)
    for c in range(NCORES):
        b, h = divmod(c, NH)
        out[b, h * NT:(h + 1) * NT] = np.asarray(res.results[c]["y"])
    return out


_MIX_CACHE = {}


def kernel(**inputs):
    P = {k: np.ascontiguousarray(np.asarray(v, dtype=np.float32)) for k, v in inputs.items()}
    return kernel_fused(P, P["x"].shape[1])


def kernel_unfused(**inputs):
    P = {k: np.ascontiguousarray(np.asarray(v, dtype=np.float32)) for k, v in inputs.items()}
    x = P["x"]
    B, T, D = x.shape
    NTOK = B * T
    per = NTOK // NCORES
    for l in range(DEPTH):
        lam_init = 0.8 - 0.6 * math.exp(-0.3 * l)
        key = (T, l)
        if key not in _MIX_CACHE:
            _MIX_CACHE[key] = build_mixer(T, lam_init)
        nc = _MIX_CACHE[key]
        in_maps = [mixer_inputs(x[c // NH], l, c % NH, P) for c in range(NCORES)]
        res = run_bass_kernel_spmd(nc, in_maps, core_ids=list(range(NCORES)))
        o = np.empty((B, T, D), dtype=ml_dtypes.bfloat16)
        for c in range(NCORES):
            b, h = divmod(c, NH)
            o[b, :, h * 128:(h + 1) * 128] = np.asarray(res.results[c]["o_la"])
            o[b, :, 512 + h * 128:512 + (h + 1) * 128] = np.asarray(res.results[c]["o_d"])
        xs = x.reshape(NTOK, D)
        os_ = o.reshape(NTOK, D)
        ys = run_ffn([xs[c * per:(c + 1) * per] for c in range(NCORES)],
                     [os_[c * per:(c + 1) * per] for c in range(NCORES)],
                     P["w_out"][l], P["ffn_norm_w"][l], P["w_gate_up"][l], P["w_down"][l],
                     P["final_norm_w"] if l == DEPTH - 1 else None)
        x = np.concatenate([np.asarray(y) for y in ys], axis=0).reshape(B, T, D)
    return np.ascontiguousarray(x.astype(np.float32))
```

```python
import math
from contextlib import ExitStack

import numpy as np
import ml_dtypes
import concourse.bass as bass
import concourse.mybir as mybir
from concourse.bass_utils import run_bass_kernel_spmd

F32 = mybir.dt.float32
BF16 = mybir.dt.bfloat16
AF = mybir.ActivationFunctionType
ALU = mybir.AluOpType
AX = mybir.AxisListType

D_MODEL = 1024
SEQ = 8192
BATCH = 2
DEPTH = 2
NH = 4
D_FF = 2816
IN_DIM = 3592
NORM_EPS = 1e-6
NCORES = 8

ENGS = ("pe", "act", "dve", "pool", "sp")


class Buf:
    __slots__ = ("name", "w", "r", "dsem", "dcnt", "excl")

    def __init__(self, name, excl=False):
        self.name = name
        self.excl = excl
        self.w = None
        self.r = []
        self.dsem = None
        self.dcnt = 0


class Op:
    __slots__ = ("eng", "fn", "deps", "dma", "sem", "val", "needed", "inc")

    def __init__(self, eng, fn, dma=False):
        self.eng = eng
        self.fn = fn
        self.deps = []
        self.dma = dma
        self.sem = None
        self.val = 0
        self.needed = False
        self.inc = 16


class Sched:
    def __init__(self, nc, es):
        self.nc = nc
        self.es = es
        self.ops = {e: [] for e in ENGS}
        self.esem = {e: es.enter_context(nc.semaphore("s_" + e)) for e in ENGS}
        self.nbuf = 0
        self.cnt = {e: 0 for e in ENGS}
        self.phase_dmas = []
        self.nsem = 0
        self.defer = None

    def buf(self, name=None, excl=False):
        self.nbuf += 1
        return Buf(name or ("b%d" % self.nbuf), excl)

    def bufs(self, n, name="b", excl=False):
        return [self.buf("%s%d" % (name, i), excl) for i in range(n)]

    def _link(self, o, R, W):
        deps = []
        for b in R:
            if b.w is not None:
                d = b.w
                if d.dma or o.dma or d.eng != o.eng or o.eng != "pe":
                    deps.append(d)
            if b.excl:
                for d in b.r:
                    if d.eng != o.eng:
                        deps.append(d)
        for b in W:
            if b.w is not None:
                d = b.w
                if d.dma or o.dma or d.eng != o.eng or o.eng != "pe":
                    deps.append(d)
            for d in b.r:
                if d.dma or o.dma or d.eng != o.eng or o.eng != "pe":
                    deps.append(d)
        o.deps = deps
        for b in R:
            if b in W:
                continue
            if b.excl:
                b.r = []
            elif not o.dma:
                b.r = [x for x in b.r if x.dma or x.eng != o.eng]
            b.r.append(o)
        for b in W:
            b.w = o
            b.r = []

    def op(self, eng, fn, R=(), W=()):
        if self.defer is not None:
            self.defer.append(lambda: self.op_now(eng, fn, R, W))
            return None
        return self.op_now(eng, fn, R, W)

    def op_now(self, eng, fn, R=(), W=()):
        o = Op(eng, fn)
        self._link(o, R, W)
        self.ops[eng].append(o)
        return o

    def dma(self, q, out, in_, sb, R=(), W=()):
        return self.dma_fn(q, lambda e, out=out, in_=in_: e.dma_start(out=out, in_=in_), sb, R, W)

    def dma_fn(self, q, fn, sb, R=(), W=(), inc=16):
        if self.defer is not None:
            self.defer.append(lambda: self.dma_fn_now(q, fn, sb, R, W, inc))
            return None
        return self.dma_fn_now(q, fn, sb, R, W, inc)

    def dma_fn_now(self, q, fn, sb, R=(), W=(), inc=16):
        if sb.dsem is None:
            self.nsem += 1
            sb.dsem = self.es.enter_context(self.nc.semaphore("d%d_%s" % (self.nsem, sb.name)))
        o = Op(q, fn, dma=True)
        o.inc = inc
        sb.dcnt += (inc if inc else 1)
        o.sem = sb.dsem
        o.val = sb.dcnt
        self._link(o, R, W)
        self.ops[q].append(o)
        self.phase_dmas.append(o)
        return o

    def phase_barrier(self):
        lasts = []
        for e in ENGS:
            real = [o for o in self.ops[e] if o.fn is not None and not o.dma]
            if real:
                lasts.append(real[-1])
        deps = lasts + list(self.phase_dmas)
        for e in ENGS:
            o = Op(e, None)
            o.deps = [d for d in deps if d.dma or d.eng != e]
            self.ops[e].append(o)
        self.phase_dmas = []

    def barrier_wait(self, eng, R):
        o = Op(eng, None)
        self._link(o, (), R)
        self.ops[eng].append(o)
        return o

    def finalize(self):
        for e in ENGS:
            for o in self.ops[e]:
                for d in o.deps:
                    d.needed = True
        for e in ENGS:
            c = self.cnt[e]
            for o in self.ops[e]:
                if not o.dma and o.needed and o.fn is not None:
                    c += 1
                    o.sem = self.esem[e]
                    o.val = c
            self.cnt[e] = c
        ops = self.ops
        self.ops = {e: [] for e in ENGS}

        def run(eng, lst):
            seen = {}
            for o in lst:
                waits = {}
                for d in o.deps:
                    k = id(d.sem)
                    if k not in waits or waits[k][1] < d.val:
                        waits[k] = (d.sem, d.val)
                for k, (sem, val) in waits.items():
                    if seen.get(k, 0) < val:
                        eng.wait_ge(sem, val)
                        seen[k] = val
                if o.fn is None:
                    continue
                ins = o.fn(eng)
                if o.dma:
                    if o.inc:
                        ins.then_inc(o.sem, o.inc)
                    else:
                        ins.then_inc(o.sem)
                elif o.needed:
                    ins.then_inc(o.sem, 1)

        with self.nc.Block() as block:
            @block.tensor
            def _(e):
                run(e, ops["pe"])

            @block.scalar
            def _(e):
                run(e, ops["act"])

            @block.vector
            def _(e):
                run(e, ops["dve"])

            @block.gpsimd
            def _(e):
                run(e, ops["pool"])

            @block.sync
            def _(e):
                run(e, ops["sp"])


class KB:
    def __init__(self):
        self.nc = bass.Bass("TRN2", target_bir_lowering=False)
        self.es = ExitStack()
        self.S = Sched(self.nc, self.es)
        self.n = 0
        self.pes = None
        self.phase = 0

    def begin_phase(self):
        self.phase += 1
        self.pes = ExitStack()

    def end_phase(self):
        self.S.phase_barrier()
        self.S.finalize()
        self.pes.close()
        self.pes = None
        for b in getattr(self, "persist", []):
            b.w = None
            b.r = []

    def sb(self, shape, dt, name=None):
        self.n += 1
        st = self.pes if self.pes is not None else self.es
        return st.enter_context(self.nc.sbuf_tensor("sb%d_" % self.phase + (name or ("t%d" % self.n)), list(shape), dt))

    def dint(self, name, shape, dt):
        return self.nc.dram_tensor(name, list(shape), dt).ap()

    def ps(self, shape, dt, name=None):
        self.n += 1
        return self.es.enter_context(self.nc.psum_tensor("ps_" + (name or ("p%d" % self.n)), list(shape), dt))

    def din(self, name, shape, dt):
        return self.nc.dram_tensor(name, list(shape), dt, kind="ExternalInput").ap()

    def dout(self, name, shape, dt):
        return self.nc.dram_tensor(name, list(shape), dt, kind="ExternalOutput").ap()

    def done(self):
        self.S.finalize()
        self.es.close()
        return self.nc


def build_ffn(NT, final_norm, ext=None):
    kb = ext["kb"] if ext else KB()
    nc, S = kb.nc, kb.S
    NTL = NT // 128
    KD = D_MODEL // 128
    JF = D_FF // 128

    if ext:
        x_d, wout_d, wgu_d, wdn_d, fnw_d, idb_d, y_d = (
            ext[k] for k in ("x", "w_out", "w_gu", "w_down", "ffn_norm_w", "ident_bf", "y"))
        if final_norm:
            fin_d = ext["final_w_bc"]
        o_d = None
    else:
        x_d = kb.din("x", [NT, D_MODEL], F32)
        o_d = kb.din("o", [NT, D_MODEL], BF16)
        wout_d = kb.din("w_out", [D_MODEL, D_MODEL], F32)
        wgu_d = kb.din("w_gu", [D_MODEL, 2 * D_FF], F32)
        wdn_d = kb.din("w_down", [D_FF, D_MODEL], F32)
        fnw_d = kb.din("ffn_norm_w", [128, KD], F32)
        idb_d = kb.din("ident_bf", [128, 128], BF16)
        if final_norm:
            fin_d = kb.din("final_w_bc", [128, D_MODEL], F32)
        y_d = kb.dout("y", [NT, D_MODEL], F32)

    wout = kb.sb([128, KD, D_MODEL], BF16, "wout")
    wgu = kb.sb([128, KD, 2 * D_FF], BF16, "wgu")
    wdn = kb.sb([128, JF, D_MODEL], BF16, "wdn")
    fnw = kb.sb([128, KD], F32, "fnw")
    idb = kb.sb([128, 128], BF16, "idb")
    b_wout, b_wgu, b_wdn, b_fnw, b_idb = S.bufs(5, "wres")
    if final_norm:
        finw = kb.sb([128, D_MODEL], F32, "finw")
        b_finw = S.buf("finw")
        S.dma("sp", finw[:], fin_d, b_finw, W=[b_finw])
    S.dma("sp", fnw[:], fnw_d, b_fnw, W=[b_fnw])
    S.dma("sp", idb[:], idb_d, b_idb, W=[b_idb])

    STG = 1408
    NSTG = 3
    stg = [kb.sb([128, STG], F32, "stg%d" % i) for i in range(NSTG)]
    b_stg = S.bufs(NSTG, "stg")
    cnt = [0]
    cast_engs = ("dve", "act")

    def load_cast(dst_ap, src_ap, n, wbuf, scale_ap=None):
        i = cnt[0] % NSTG
        q = "sp" if (cnt[0] % 2 == 0) else "pool"
        S.dma(q, stg[i][:, 0:n], src_ap, b_stg[i], W=[b_stg[i]])
        ce = cast_engs[cnt[0] % 2]
        if ce == "act":
            if scale_ap is None:
                S.op("act", lambda e, d=dst_ap, s=stg[i][:, 0:n]: e.copy(out=d, in_=s), R=[b_stg[i]], W=[wbuf])
            else:
                S.op("act", lambda e, d=dst_ap, s=stg[i][:, 0:n], sc=scale_ap:
                     e.activation(out=d, in_=s, func=AF.Copy, scale=sc), R=[b_stg[i], b_fnw], W=[wbuf])
        elif scale_ap is None:
            S.op(ce, lambda e, d=dst_ap, s=stg[i][:, 0:n]: e.tensor_copy(out=d, in_=s),
                 R=[b_stg[i]], W=[wbuf])
        else:
            S.op(ce, lambda e, d=dst_ap, s=stg[i][:, 0:n], sc=scale_ap:
                 e.tensor_scalar(out=d, in0=s, scalar1=sc, scalar2=None, op0=ALU.mult),
                 R=[b_stg[i], b_fnw], W=[wbuf])
        cnt[0] += 1

    wout_v = wout_d.rearrange("(ko p) n -> p ko n", p=128)
    for ko in range(KD):
        load_cast(wout[:, ko, :], wout_v[:, ko, :], D_MODEL, b_wout)
    wgu_v = wgu_d.rearrange("(ko p) n -> p ko n", p=128)
    for ko in range(KD):
        for c in range(4):
            load_cast(wgu[:, ko, c * STG:(c + 1) * STG], wgu_v[:, ko, c * STG:(c + 1) * STG], STG,
                      b_wgu, scale_ap=fnw[:, ko:ko + 1])
    wdn_v = wdn_d.rearrange("(j p) n -> p j n", p=128)
    for j in range(JF):
        load_cast(wdn[:, j, :], wdn_v[:, j, :], D_MODEL, b_wdn)

    xin = [kb.sb([128, D_MODEL], F32, "xin%d" % i) for i in range(2)]
    oin = [kb.sb([128, D_MODEL], BF16, "oin%d" % i) for i in range(2)]
    b_xin = S.bufs(2, "xin")
    b_oin = S.bufs(2, "oin")
    tbuf = kb.sb([128, KD, 128], BF16, "tbuf")
    b_tbuf = S.buf("tbuf")
    hn = kb.sb([128, D_MODEL], BF16, "hn")
    b_hn = S.buf("hn")
    junk = kb.sb([128, D_MODEL], BF16, "junk")
    b_junk = S.buf("junk")
    aT = kb.sb([128, JF, 128], BF16, "aT")
    b_aT = S.buf("aT")
    sg = [kb.sb([128, 128], F32, "sg%d" % i) for i in range(3)]
    b_sg = S.bufs(3, "sg")
    stat = kb.sb([128, 8], F32, "stat")
    b_stat = S.buf("stat")
    epsc = kb.sb([128, 1], F32, "epsc")
    b_epsc = S.buf("epsc")
    S.op("dve", lambda e: e.memset(epsc[:], NORM_EPS), W=[b_epsc])

    if ext:
        pbank, b_pb = ext["pb"], ext["b_pb"]
        ptb_t = pbank[0][:].bitcast(BF16)
        idx_sb = kb.sb([128, 4 * NTL], mybir.dt.int32, "idx")
        b_idx = S.buf("idx")
        S.dma("sp", idx_sb[:], ext["idx"], b_idx, W=[b_idx])
        TT_ = ext["T"]
    else:
        pbank = [None] + [kb.ps([128, 512], F32, "pb%d" % i) for i in range(1, 8)]
        b_pb = S.bufs(8, "pb", excl=True)
        ptb_t = kb.ps([128, D_MODEL], BF16, "ptb")

    x_v = x_d.rearrange("(t p) d -> t p d", p=128)
    o_v = o_d.rearrange("(t p) d -> t p d", p=128) if o_d is not None else None
    y_v = y_d.rearrange("(t p) d -> t p d", p=128)

    def rms_scale(src, b_src, col):
        S.op("act", lambda e: e.activation(out=junk[:], in_=src, func=AF.Square,
                                           accum_out=stat[:, col:col + 1]),
             R=[b_src], W=[b_junk, b_stat])
        S.op("act", lambda e: e.activation(out=stat[:, col:col + 1], in_=stat[:, col:col + 1], func=AF.Sqrt,
                                           scale=1.0 / D_MODEL, bias=epsc[:, 0:1]),
             R=[b_stat, b_epsc], W=[b_stat])
        S.op("dve", lambda e: e.reciprocal(out=stat[:, col:col + 1], in_=stat[:, col:col + 1]),
             R=[b_stat], W=[b_stat])

    hT2 = [kb.sb([128, KD, 128], BF16, "hT2_%d" % i) for i in range(2)]
    b_hT2 = S.bufs(2, "hT2")
    aT2 = [aT, kb.sb([128, JF, 128], BF16, "aT_1")]
    b_aT2 = [b_aT, S.buf("aT_1")]
    ptb = ptb_t

    def stageA(t):
        i = t % 2
        xt, ot = xin[i], oin[i]
        S.dma("sp", xt[:], x_v[t], b_xin[i], W=[b_xin[i]])
        if ext:
            for r_ in range(4):
                S.dma_fn("pool", lambda e, ot=ot, r_=r_, t=t: e.indirect_dma_start(
                    out=ot[:, r_ * 256:(r_ + 1) * 256], out_offset=None,
                    in_=ext["og_all"],
                    in_offset=bass.IndirectOffsetOnAxis(ap=idx_sb[:, r_ * NTL + t:r_ * NTL + t + 1], axis=0)),
                    b_oin[i], R=[b_idx] + list(ext["b_og_all"]), W=[b_oin[i]])
        else:
            S.dma("pool", ot[:], o_v[t], b_oin[i], W=[b_oin[i]])

        def tr_group(e, src=ot):
            ins = None
            for k in range(KD):
                ins = e.transpose(out=ptb[:, k * 128:(k + 1) * 128], in_=src[:, k * 128:(k + 1) * 128],
                                  identity=idb[:])
            return ins
        S.op("pe", tr_group, R=[b_oin[i], b_idb], W=[b_pb[0]])
        S.op("act", lambda e: e.copy(out=tbuf[:].rearrange("p k t -> p (k t)"), in_=ptb[:, 0:KD * 128]),
             R=[b_pb[0]], W=[b_tbuf])
        for nchunk in range(2):
            bk = 1 + nchunk

            def mm_out(e, bk=bk, nchunk=nchunk):
                ins = None
                for k in range(KD):
                    ins = e.matmul(out=pbank[bk][:], lhsT=tbuf[:, k, :],
                                   rhs=wout[:, k, nchunk * 512:(nchunk + 1) * 512],
                                   start=(k == 0), stop=(k == KD - 1))
                return ins
            S.op("pe", mm_out, R=[b_tbuf, b_wout], W=[b_pb[bk]])
            S.op("dve", lambda e, bk=bk, nchunk=nchunk, xt=xt:
                 e.tensor_tensor(out=xt[:, nchunk * 512:(nchunk + 1) * 512],
                                 in0=pbank[bk][:], in1=xt[:, nchunk * 512:(nchunk + 1) * 512], op=ALU.add),
                 R=[b_pb[bk], b_xin[i]], W=[b_xin[i]])
        rms_scale(xt[:], b_xin[i], 0)
        S.op("act", lambda e, xt=xt: e.activation(out=hn[:], in_=xt[:], func=AF.Copy, scale=stat[:, 0:1]),
             R=[b_xin[i], b_stat], W=[b_hn])

        def tr_group2(e):
            ins = None
            for k in range(KD):
                ins = e.transpose(out=ptb[:, k * 128:(k + 1) * 128], in_=hn[:, k * 128:(k + 1) * 128],
                                  identity=idb[:])
            return ins
        S.op("pe", tr_group2, R=[b_hn, b_idb], W=[b_pb[0]])
        S.op("act", lambda e, i=i: e.copy(out=hT2[i][:].rearrange("p k t -> p (k t)"), in_=ptb[:, 0:KD * 128]),
             R=[b_pb[0]], W=[b_hT2[i]])

    la = []

    def stageB(t):
        i = t % 2
        for j in range(JF):
            if la:
                la.pop(0)()
            bk = (3, 4, 7)[j % 3]

            def mm_gu(e, bk=bk, j=j, i=i):
                ins = None
                for half in range(2):
                    for k in range(KD):
                        c0 = half * D_FF + j * 128
                        ins = e.matmul(out=pbank[bk][:, half * 128:(half + 1) * 128],
                                       lhsT=wgu[:, k, c0:c0 + 128], rhs=hT2[i][:, k, :],
                                       start=(k == 0), stop=(k == KD - 1))
                return ins
            S.op("pe", mm_gu, R=[b_hT2[i], b_wgu], W=[b_pb[bk]])
            s_ = j % 3
            S.op("act", lambda e, bk=bk, s_=s_: e.activation(out=sg[s_][:], in_=pbank[bk][:, 0:128], func=AF.Silu),
                 R=[b_pb[bk]], W=[b_sg[s_]])
            S.op("dve", lambda e, bk=bk, s_=s_, j=j, i=i: e.tensor_tensor(out=aT2[i][:, j, :],
                                                                          in0=pbank[bk][:, 128:256],
                                                                          in1=sg[s_][:], op=ALU.mult),
                 R=[b_pb[bk], b_sg[s_]], W=[b_aT2[i]])

    def stageC(t):
        i = t % 2
        xt = xin[i]
        for nchunk in range(2):
            bk = 5 + nchunk

            def mm_dn(e, bk=bk, nchunk=nchunk, i=i):
                ins = None
                for j in range(JF):
                    ins = e.matmul(out=pbank[bk][:], lhsT=aT2[i][:, j, :],
                                   rhs=wdn[:, j, nchunk * 512:(nchunk + 1) * 512],
                                   start=(j == 0), stop=(j == JF - 1))
                return ins
            S.op("pe", mm_dn, R=[b_aT2[i], b_wdn], W=[b_pb[bk]])
            S.op("dve", lambda e, bk=bk, nchunk=nchunk, xt=xt:
                 e.tensor_tensor(out=xt[:, nchunk * 512:(nchunk + 1) * 512],
                                 in0=pbank[bk][:], in1=xt[:, nchunk * 512:(nchunk + 1) * 512], op=ALU.add),
                 R=[b_pb[bk], b_xin[i]], W=[b_xin[i]])
        if final_norm:
            rms_scale(xt[:], b_xin[i], 1)
            S.op("dve", lambda e, xt=xt: e.scalar_tensor_tensor(out=xt[:], in0=xt[:], scalar=stat[:, 1:2],
                                                                in1=finw[:], op0=ALU.mult, op1=ALU.mult),
                 R=[b_xin[i], b_stat, b_finw], W=[b_xin[i]])
        S.dma("sp", y_v[t], xt[:], b_xin[i], R=[b_xin[i]])

    stageA(0)
    for t in range(NTL):
        if t + 1 < NTL:
            S.defer = la
            stageA(t + 1)
            S.defer = None
            for _ in range(5 if ext else 2):
                la.pop(0)()
        stageB(t)
        while la:
            la.pop(0)()
        stageC(t)

    if ext:
        return None
    S.barrier_wait("sp", b_xin)
    return kb.done()


def _ident_bf():
    return np.eye(128, dtype=np.float32).astype(ml_dtypes.bfloat16)


def run_ffn(x_sl, o_sl, w_out, ffn_norm_w, w_gu, w_down, final_w, nc_cache={}):
    NT = x_sl[0].shape[0]
    key = (NT, final_w is not None)
    if key not in nc_cache:
        nc_cache[key] = build_ffn(NT, final_w is not None)
    nc = nc_cache[key]
    fnw = np.ascontiguousarray(ffn_norm_w.reshape(D_MODEL // 128, 128).T)
    in_maps = []
    for c in range(len(x_sl)):
        m = {"x": np.ascontiguousarray(x_sl[c]), "o": np.ascontiguousarray(o_sl[c]),
             "w_out": w_out, "w_gu": w_gu, "w_down": w_down, "ffn_norm_w": fnw,
             "ident_bf": _ident_bf()}
        if final_w is not None:
            m["final_w_bc"] = np.ascontiguousarray(np.broadcast_to(final_w[None, :], (128, D_MODEL)))
        in_maps.append(m)
    res = run_bass_kernel_spmd(nc, in_maps, core_ids=list(range(len(x_sl))))
    return [r["y"] for r in res.results]


NW = 898


NPULL = 6


def rr_merge(lists):
    out = []
    while any(lists):
        for l in lists:
            if l:
                out.append(l.pop(0))
    return out


def build_mixer(T, lam_init, dbg=99, ext=None):
    SKIP = ''
    kb = ext["kb"] if ext else KB()
    nc, S = kb.nc, kb.S
    NCH = T // 512
    NTL = T // 128
    KD = D_MODEL // 128

    if ext:
        x_d = ext["x"]
        wh_d, anw_d, cw_d, sc_d, lamv_d, lnw_d, dnw_d, bt_d, idb_d, cf_d = (
            ext[k] for k in ("wh", "anw", "cw", "sc", "lamv", "lnw", "dnw", "btoep", "ident_bf", "cf"))
        ola_d = ext["og"][:, 0:128]
        od_d = ext["og"][:, 128:256]
    else:
        x_d = kb.din("x", [T, D_MODEL], F32)
        wh_d = kb.din("wh", [D_MODEL, NW], F32)
        anw_d = kb.din("anw", [128, KD], F32)
        cw_d = kb.din("cw", [128, 12], F32)
        sc_d = kb.din("sc", [128, 4], F32)
        lamv_d = kb.din("lamv", [128, 4 * 64], F32)
        lnw_d = kb.din("lnw", [128, 128], F32)
        dnw_d = kb.din("dnw", [128, 128], F32)
        bt_d = kb.din("btoep", [128, 512], F32)
        idb_d = kb.din("ident_bf", [128, 128], BF16)
        cf_d = kb.din("cf", [128, 7 * 128], F32)
        ola_d = kb.dout("o_la", [T, 128], BF16)
        od_d = kb.dout("o_d", [T, 128], BF16)

    def T_(shape, dt, name):
        return kb.sb(shape, dt, name), S.buf(name)

    anw, b_anw = T_([128, KD], F32, "anw")
    cw, b_cw = T_([128, 12], F32, "cw")
    sc, b_sc = T_([128, 4], F32, "sc")
    lamv, b_lamv = T_([128, 256], F32, "lamv")
    lnw, b_lnw = T_([128, 128], F32, "lnw")
    dnw, b_dnw = T_([128, 128], F32, "dnw")
    bt, b_bt = T_([128, 512], F32, "bt")
    idb, b_idb = T_([128, 128], BF16, "idb")
    cf, b_cf = T_([128, 7 * 128], F32, "cf")
    for (t_, d_, b_) in ((anw, anw_d, b_anw), (cw, cw_d, b_cw), (sc, sc_d, b_sc), (lamv, lamv_d, b_lamv),
                         (lnw, lnw_d, b_lnw), (dnw, dnw_d, b_dnw), (bt, bt_d, b_bt), (idb, idb_d, b_idb),
                         (cf, cf_d, b_cf)):
        S.dma("sp", t_[:], d_, b_, W=[b_])
    IDF = cf[:, 0:128]
    ONES = cf[:, 128:256]
    MASKL = cf[:, 256:384]
    MASKU = cf[:, 384:512]
    TRI = cf[:, 512:640]
    SEL63 = cf[:, 640:768]
    SEL127 = cf[:, 768:896]

    cst_, b_cst = T_([128, 8], F32, "cst")
    S.op("dve", lambda e: e.memset(cst_[:, 0:1], 1.0), W=[b_cst])
    S.op("dve", lambda e: e.memset(cst_[:, 1:2], 1e-6), W=[b_cst])
    S.op("dve", lambda e: e.memset(cst_[:, 2:3], 1e-5), W=[b_cst])
    S.op("dve", lambda e: e.memset(cst_[:, 5:6], 0.0), W=[b_cst])
    C_ONE, C_EPS6, C_EPS5, C_NA, C_NLAM, C_ZERO = (cst_[:, i:i + 1] for i in range(6))
    S.op("act", lambda e: e.activation(out=cst_[:, 3:4], in_=sc[:, 0:1], func=AF.Exp), R=[b_sc], W=[b_cst])
    S.op("dve", lambda e: e.tensor_scalar(out=cst_[:, 3:4], in0=cst_[:, 3:4], scalar1=-1.0, scalar2=None,
                                          op0=ALU.mult), R=[b_cst], W=[b_cst])
    lt, b_lt = T_([128, 128], F32, "lamtmp")
    ls, b_ls = T_([128, 4], F32, "lamsum")
    S.op("dve", lambda e: e.tensor_tensor(out=lt[:, 0:64], in0=lamv[:, 0:64], in1=lamv[:, 64:128], op=ALU.mult),
         R=[b_lamv], W=[b_lt])
    S.op("dve", lambda e: e.tensor_tensor(out=lt[:, 64:128], in0=lamv[:, 128:192], in1=lamv[:, 192:256],
                                          op=ALU.mult), R=[b_lamv, b_lt], W=[b_lt])
    if 'r' not in SKIP:
        S.op("dve", lambda e: e.reduce_sum(out=ls[:, 0:1], in_=lt[:, 0:64], axis=AX.X), R=[b_lt], W=[b_ls])
        S.op("dve", lambda e: e.reduce_sum(out=ls[:, 1:2], in_=lt[:, 64:128], axis=AX.X), R=[b_lt, b_ls], W=[b_ls])
    S.op("act", lambda e: e.activation(out=ls[:, 2:4], in_=ls[:, 0:2], func=AF.Exp), R=[b_ls], W=[b_ls])
    S.op("dve", lambda e: e.scalar_tensor_tensor(out=cst_[:, 4:5], in0=ls[:, 3:4], scalar=float(-lam_init),
                                                 in1=ls[:, 2:3], op0=ALU.add, op1=ALU.subtract),
         R=[b_ls, b_cst], W=[b_cst])
    S.op("dve", lambda e: e.tensor_scalar(out=dnw[:], in0=dnw[:], scalar1=float(1.0 - lam_init), scalar2=None,
                                          op0=ALU.mult), R=[b_dnw], W=[b_dnw])

    Wb, b_Wb = T_([128, KD, 1024], BF16, "Wb")
    wst = [kb.sb([128, NW], F32, "wst%d" % i) for i in range(2)]
    b_wst = S.bufs(2, "wst")
    wh_v = wh_d.rearrange("(ko p) n -> p ko n", p=128)
    for ko in range(KD):
        i = ko % 2
        S.dma("sp", wst[i][:], wh_v[:, ko, :], b_wst[i], W=[b_wst[i]])
        if i == 0:
            S.op("dve", lambda e, i=i, ko=ko: e.tensor_scalar(out=Wb[:, ko, 0:NW], in0=wst[i][:],
                                                              scalar1=anw[:, ko:ko + 1], scalar2=None, op0=ALU.mult),
                 R=[b_wst[i], b_anw], W=[b_Wb])
        else:
            S.op("act", lambda e, i=i, ko=ko: e.activation(out=Wb[:, ko, 0:NW], in_=wst[i][:], func=AF.Copy,
                                                           scale=anw[:, ko:ko + 1]),
                 R=[b_wst[i], b_anw], W=[b_Wb])

    KdT, b_KdT = T_([128, T], BF16, "KdT")
    Vaug, b_Vaug = T_([128, NTL, 144], BF16, "Vaug")
    if 'v' not in SKIP:
        S.op("pool", lambda e: e.memset(Vaug[:, :, 128:129], 1.0), W=[b_Vaug])

    xt = [kb.sb([128, D_MODEL], F32, "xt%d" % i) for i in range(4)]
    b_xt = S.bufs(4, "xt")
    xn, b_xn = T_([128, D_MODEL], BF16, "xn")
    junk, b_junk = T_([128, D_MODEL], BF16, "junk")
    junkf, b_junkf = T_([128, 128], F32, "junkf")
    stat, b_stat = T_([128, 4], F32, "stat")
    hT, b_hT = T_([128, KD, 512], BF16, "hT")
    cstg = [kb.sb([128, 515], F32, "cstg%d" % g) for g in range(3)]
    b_cstg = S.bufs(3, "cstg")
    cacc = [kb.sb([128, 512], F32, "cacc%d" % g) for g in range(3)]
    b_cacc = S.bufs(3, "cacc")
    sil = [kb.sb([128, 512], F32, "sil%d" % g) for g in range(2)]
    b_sil = S.bufs(2, "sil")
    sq, b_sq = T_([128, 512], F32, "sq")
    rs, b_rs = T_([128, 512], F32, "rs")
    qnT, b_qnT = T_([128, 512], BF16, "qnT")
    knT, b_knT = T_([128, 512], BF16, "knT")
    vsT, b_vsT = T_([128, 512], BF16, "vsT")
    QdT, b_QdT = T_([128, 512], BF16, "QdT")
    qgT, b_qgT = T_([128, 512], BF16, "qgT")
    kvt, b_kvt = T_([128, 8, 128], BF16, "kvt")
    zs, b_zs = T_([128, 4, 128], BF16, "zs")
    ba, b_ba = T_([128, 4, 2], F32, "ba")
    for g in range(3):
        S.op("dve", lambda e, g=g: e.memset(cstg[g][:, 0:3], 0.0), W=[b_cstg[g]])
    pt = {}
    for nm in ("beta", "eb", "g", "gc", "egc", "bg", "glt", "ekd", "tmpa"):
        pt[nm] = T_([128, 4], F32, "pt_" + nm)
    egl, b_egl = T_([128, 8], F32, "egl")
    NS = 4
    gset = []
    for s_ in range(NS):
        d = {}
        for nm, shp, dt in (("dg", [128, 128], F32), ("Eb", [128, 128], F32), ("Ds", [128, 128], F32),
                            ("EA", [128, 128], F32), ("t1", [128, 128], F32), ("t2", [128, 128], F32),
                            ("MPa", [128, 256], F32), ("MPb", [128, 256], F32),
                            ("MTa", [128, 128], F32), ("MTb", [128, 128], F32),
                            ("TT", [128, 128], BF16), ("vb", [128, 128], BF16), ("kbg", [128, 128], BF16)):
            d[nm] = T_(shp, dt, "%s_%d" % (nm, s_))
        gset.append(d)
    u_sb, b_u = T_([128, 4, 128], F32, "u_sb")
    wT_sb, b_wT = T_([128, 512], BF16, "wT_sb")
    apT, b_apT = T_([128, 4, 128], BF16, "apT")
    kd_sb, b_kd = T_([128, 4, 128], BF16, "kd_sb")
    S_f, b_Sf = T_([128, 128], F32, "S_f")
    Sb, b_Sb = T_([128, 128], BF16, "Sb")
    vn, b_vn = T_([128, 128], BF16, "vn")
    o_sb = [kb.sb([128, 128], F32, "o_sb%d" % i) for i in range(2)]
    b_osb = S.bufs(2, "o_sb")
    on_, b_on = T_([128, 128], F32, "on")
    ola_st = [kb.sb([128, 4, 128], BF16, "ola_st%d" % i) for i in range(2)]
    b_olast = S.bufs(2, "ola_st")
    S.op("dve", lambda e: e.memset(S_f[:], 0.0), W=[b_Sf])
    S.op("dve", lambda e: e.memset(Sb[:], 0.0), W=[b_Sb])
    pT = [[kb.sb([128, 256], BF16, "pT%d%d" % (m, p)) for p in range(2)] for m in range(2)]
    b_pT = [[S.buf("pT%d%d" % (m, p)) for p in range(2)] for m in range(2)]
    s2 = [kb.sb([128, 256], F32, "s2_%d" % m) for m in range(2)]
    b_s2 = S.bufs(2, "s2")
    rden, b_rden = T_([128, 4], F32, "rden")
    O1, b_O1 = T_([128, 128], F32, "O1")
    odf, b_odf = T_([128, 128], F32, "odf")
    od_st = [kb.sb([128, 2, 128], BF16, "od_st%d" % i) for i in range(2)]
    b_odst = S.bufs(2, "od_st")

    if ext:
        pb, b_pb = ext["pb"], ext["b_pb"]
    else:
        pb = [kb.ps([128, 512], F32, "pb%d" % i) for i in range(8)]
        b_pb = S.bufs(8, "pb", excl=True)
    pb0_bf = pb[0][:].bitcast(BF16)

    x_v = x_d.rearrange("(t p) d -> t p d", p=128)
    ola_v = ola_d.rearrange("(c t p) e -> c p t e", p=128, t=4)
    od_v = od_d.rearrange("(c t p) e -> c p t e", p=128, t=2)

    def rstd_from_ss(col, n, epsc):
        S.op("act", lambda e: e.activation(out=stat[:, col:col + 1], in_=stat[:, col:col + 1], func=AF.Ln,
                                           scale=1.0 / n, bias=epsc), R=[b_stat, b_cst], W=[b_stat])
        S.op("act", lambda e: e.activation(out=stat[:, col:col + 1], in_=stat[:, col:col + 1], func=AF.Exp,
                                           scale=-0.5), R=[b_stat], W=[b_stat])

    def silu_via_exp(src_ap, R_src, tmp_ap, b_tmp, out_ap, W_out, mul_eng="dve"):
        S.op("act", lambda e: e.activation(out=tmp_ap, in_=src_ap, func=AF.Exp, scale=-1.0), R=R_src, W=[b_tmp])
        S.op("act", lambda e: e.activation(out=tmp_ap, in_=tmp_ap, func=AF.Ln, bias=C_ONE), R=[b_tmp, b_cst],
             W=[b_tmp])
        S.op("act", lambda e: e.activation(out=tmp_ap, in_=tmp_ap, func=AF.Exp, scale=-1.0), R=[b_tmp], W=[b_tmp])
        S.op(mul_eng, lambda e: e.tensor_tensor(out=out_ap, in0=src_ap, in1=tmp_ap, op=ALU.mult),
             R=list(R_src) + [b_tmp], W=W_out)

    stmp, b_stmp = T_([128, 512], F32, "stmp")
    stmp3 = [stmp, kb.sb([128, 512], F32, "stmp_1"), kb.sb([128, 512], F32, "stmp_2")]
    b_stmp3 = [b_stmp, S.buf("stmp_1"), S.buf("stmp_2")]
    sq2 = [sq, kb.sb([128, 512], F32, "sq_1")]
    b_sq2 = [b_sq, S.buf("sq_1")]
    rs2 = [rs, kb.sb([128, 512], F32, "rs_1")]
    b_rs2 = [b_rs, S.buf("rs_1")]
    zraw, b_zraw = T_([128, 4, 128], F32, "zraw")

    hT2 = [hT, kb.sb([128, KD, 512], BF16, "hT_b")]
    b_hT2 = [b_hT, S.buf("hT_b")]
    Qblk = [kb.sb([128, 2, 512], BF16, "Qblk%d" % i) for i in range(2)]
    b_Qblk = S.bufs(2, "Qblk")
    for i_ in range(2):
        S.op("pool", lambda e, i_=i_: e.memset(Qblk[i_][:], 0.0), W=[b_Qblk[i_]])
    pT2 = [kb.sb([128, 512], BF16, "pT2_%d" % i) for i in range(2)]
    b_pT2 = S.bufs(2, "pT2")
    s2w, b_s2w = T_([128, 512], F32, "s2w")
    scan_l = []

    def pull_scan(n):
        for _ in range(min(n, len(scan_l))):
            scan_l.pop(0)()

    zs2 = [zs, kb.sb([128, 4, 128], BF16, "zs_b")]
    b_zs2 = [b_zs, S.buf("zs_b")]
    pend = []

    def pull(n):
        for _ in range(min(n, len(pend))):
            pend.pop(0)()

    def front(j):
        if dbg <= 0:
            return
        for tt in range(4):
            t = 4 * j + tt
            i = tt
            S.dma("sp" if i % 2 == 0 else "pool", xt[i][:],
                  (ext["x_tile"](t) if (ext and ext.get("x_tile") is not None) else x_v[t]), b_xt[i], W=[b_xt[i]],
                  R=(list(ext["b_x"](t)) if (ext and ext.get("b_x") is not None) else []))
        for tt in range(4):
            t = 4 * j + tt
            i = tt
            S.op("act", lambda e, i=i: e.activation(out=junk[:], in_=xt[i][:], func=AF.Square,
                                                    accum_out=stat[:, 0:1]), R=[b_xt[i]], W=[b_junk, b_stat])
            rstd_from_ss(0, D_MODEL, C_EPS6)
            S.op("act", lambda e, i=i: e.activation(out=xn[:], in_=xt[i][:], func=AF.Copy, scale=stat[:, 0:1]),
                 R=[b_xt[i], b_stat], W=[b_xn])

            def trx(e):
                ins = None
                for k in range(KD):
                    ins = e.transpose(out=pb0_bf[:, k * 128:(k + 1) * 128], in_=xn[:, k * 128:(k + 1) * 128],
                                      identity=idb[:])
                return ins
            S.op("pe", trx, R=[b_xn, b_idb], W=[b_pb[0]])
            S.op("dve", lambda e, tt=tt: e.tensor_copy(out=hT2[j % 2][:, :, tt * 128:(tt + 1) * 128],
                                                        in_=pb0_bf[:, 0:1024].rearrange("p (k t) -> p k t", k=KD)),
                 R=[b_pb[0]], W=[b_hT2[j % 2]])
        if dbg <= 1:
            return
        for g in range(5 if 'f' not in SKIP else 0):
            bk = 1 + (g % 2)

            def mmf(e, g=g, bk=bk):
                ins = None
                for k in range(KD):
                    ins = e.matmul(out=pb[bk][:], lhsT=Wb[:, k, g * 128:(g + 1) * 128], rhs=hT2[j % 2][:, k, :],
                                   start=(k == 0), stop=(k == KD - 1))
                return ins
            S.op("pe", mmf, R=[b_Wb, b_hT2[j % 2]], W=[b_pb[bk]])
            if g < 3:
                S.op("act", lambda e, g=g, bk=bk: e.copy(out=cstg[g][:, 3:515], in_=pb[bk][:]),
                     R=[b_pb[bk]], W=[b_cstg[g]])
            elif g == 3:
                S.op("act", lambda e, bk=bk: e.copy(out=Qblk[j % 2][0:64, :, 0:256],
                                                    in_=pb[bk][0:64, :].rearrange("p (q c) -> p q c", q=2)),
                     R=[b_pb[bk]], W=[b_Qblk[j % 2]])
                S.op("act", lambda e, bk=bk: e.copy(out=Qblk[j % 2][64:128, :, 256:512],
                                                    in_=pb[bk][64:128, :].rearrange("p (q c) -> p q c", q=2)),
                     R=[b_pb[bk]], W=[b_Qblk[j % 2]])
            else:
                S.op("act", lambda e, bk=bk, j=j: e.copy(out=KdT[:, j * 512:(j + 1) * 512], in_=pb[bk][:]),
                     R=[b_pb[bk]], W=[b_KdT])
        for tt in range(4 if 't' not in SKIP else 0):
            t = 4 * j + tt

            def mmt(e, tt=tt):
                ins = None
                for k in range(KD):
                    NN = 256 if 'n' in SKIP else 258
                    ins = e.matmul(out=pb[3][:, 0:NN], lhsT=hT2[j % 2][:, k, tt * 128:(tt + 1) * 128],
                                   rhs=Wb[:, k, 640:640 + NN], start=(k == 0), stop=(k == KD - 1))
                return ins
            S.op("pe", mmt, R=[b_Wb, b_hT2[j % 2]], W=[b_pb[3]])
            S.op("dve", lambda e, tt=tt: e.tensor_copy(out=zraw[:, tt, :], in_=pb[3][:, 0:128]),
                 R=[b_pb[3]], W=[b_zraw])
            S.op("dve", lambda e, t=t: e.tensor_copy(out=Vaug[:, t, 0:128], in_=pb[3][:, 128:256]),
                 R=[b_pb[3]], W=[b_Vaug])
            S.op("dve", lambda e, tt=tt: e.tensor_copy(out=ba[:, tt, :], in_=pb[3][:, 256:258]),
                 R=[b_pb[3]], W=[b_ba])
        silu_via_exp(zraw[:].rearrange("p a d -> p (a d)"), [b_zraw], stmp[:], b_stmp,
                     zs2[j % 2][:].rearrange("p a d -> p (a d)"), [b_zs2[j % 2]], mul_eng="pool")

    front(0)
    for j in range(NCH):
        if dbg <= 2:
            continue
        pend.clear()
        if j + 1 < NCH:
            S.defer = pend
            front(j + 1)
            S.defer = None
            pull(4)
        S.defer = mA = []
        subs = []
        for g in range(3):
            S.defer = sub = []
            subs.append(sub)
            ce = "dve"
            S.op(ce, lambda e, g=g: e.tensor_scalar(out=cacc[g][:], in0=cstg[g][:, 3:515],
                                                    scalar1=cw[:, g * 4 + 3:g * 4 + 4], scalar2=None, op0=ALU.mult),
                 R=[b_cstg[g], b_cw], W=[b_cacc[g]])
            for tap in (2, 1, 0):
                S.op(ce, lambda e, g=g, tap=tap: e.scalar_tensor_tensor(
                    out=cacc[g][:], in0=cstg[g][:, tap:tap + 512], scalar=cw[:, g * 4 + tap:g * 4 + tap + 1],
                    in1=cacc[g][:], op0=ALU.mult, op1=ALU.add),
                    R=[b_cstg[g], b_cw, b_cacc[g]], W=[b_cacc[g]])
            S.op(ce, lambda e, g=g: e.tensor_copy(out=cstg[g][:, 0:3], in_=cstg[g][:, 512:515]),
                 R=[b_cstg[g]], W=[b_cstg[g]])
            if g < 2:
                silu_via_exp(cacc[g][:], [b_cacc[g]], stmp3[g][:], b_stmp3[g], sil[g][:], [b_sil[g]])
            else:
                silu_via_exp(cacc[g][:], [b_cacc[g]], stmp3[g][:], b_stmp3[g], vsT[:], [b_vsT])
        S.defer = mA
        mA.extend(rr_merge(subs))
        subs = []
        for g in range(2):
            S.defer = sub = []
            subs.append(sub)
            S.op("pool", lambda e, g=g: e.tensor_tensor(out=sq2[g][:], in0=sil[g][:], in1=sil[g][:], op=ALU.mult),
                 R=[b_sil[g]], W=[b_sq2[g]])
            bk = 1 + g
            S.op("pe", lambda e, bk=bk, g=g: e.matmul(out=pb[bk][:], lhsT=ONES, rhs=sq2[g][:], start=True, stop=True),
                 R=[b_sq2[g], b_cf], W=[b_pb[bk]])
            S.op("act", lambda e, bk=bk, g=g: e.activation(out=rs2[g][:], in_=pb[bk][:], func=AF.Ln, bias=C_EPS6),
                 R=[b_pb[bk], b_cst], W=[b_rs2[g]])
            S.op("act", lambda e, g=g: e.activation(out=rs2[g][:], in_=rs2[g][:], func=AF.Exp, scale=-0.5), R=[b_rs2[g]], W=[b_rs2[g]])
            if g == 0:
                S.op("dve", lambda e, g=g: e.scalar_tensor_tensor(out=qnT[:], in0=sil[0][:], scalar=float(128 ** -0.5),
                                                             in1=rs2[g][:], op0=ALU.mult, op1=ALU.mult),
                     R=[b_sil[0], b_rs2[g]], W=[b_qnT])
            else:
                S.op("dve", lambda e, g=g: e.tensor_tensor(out=knT[:], in0=sil[1][:], in1=rs2[g][:], op=ALU.mult),
                     R=[b_sil[1], b_rs2[g]], W=[b_knT])
        S.defer = mA
        mA.extend(rr_merge(subs))
        def trkv(e):
            ins = None
            for tt in range(4):
                ins = e.transpose(out=pb0_bf[:, (2 * tt) * 128:(2 * tt + 1) * 128],
                                  in_=knT[:, tt * 128:(tt + 1) * 128], identity=idb[:])
                ins = e.transpose(out=pb0_bf[:, (2 * tt + 1) * 128:(2 * tt + 2) * 128],
                                  in_=vsT[:, tt * 128:(tt + 1) * 128], identity=idb[:])
            return ins
        S.op("pe", trkv, R=[b_knT, b_vsT, b_idb], W=[b_pb[0]])
        S.op("dve", lambda e: e.tensor_copy(out=kvt[:].rearrange("p a d -> p (a d)"), in_=pb0_bf[:, 0:1024]),
             R=[b_pb[0]], W=[b_kvt])
        if dbg <= 5:
            continue
        S.defer = mB = []
        P = lambda nm: pt[nm][0]
        B = lambda nm: pt[nm][1]
        S.op("act", lambda e: e.activation(out=P("eb")[:], in_=ba[:, :, 0], func=AF.Exp, scale=-1.0),
             R=[b_ba], W=[B("eb")])
        S.op("dve", lambda e: e.tensor_scalar(out=P("eb")[:], in0=P("eb")[:], scalar1=1.0, scalar2=None,
                                              op0=ALU.add), R=[B("eb")], W=[B("eb")])
        S.op("dve", lambda e: e.reciprocal(out=P("beta")[:], in_=P("eb")[:]), R=[B("eb")], W=[B("beta")])
        S.op("act", lambda e: e.activation(out=P("tmpa")[:], in_=ba[:, :, 1], func=AF.Exp, bias=sc[:, 1:2]),
             R=[b_ba, b_sc], W=[B("tmpa")])
        S.op("act", lambda e: e.activation(out=P("tmpa")[:], in_=P("tmpa")[:], func=AF.Ln, bias=C_ONE),
             R=[B("tmpa"), b_cst], W=[B("tmpa")])
        S.op("dve", lambda e: e.tensor_scalar(out=P("g")[:], in0=P("tmpa")[:], scalar1=C_NA, scalar2=None,
                                              op0=ALU.mult), R=[B("tmpa"), b_cst], W=[B("g")])
        S.op("pe", lambda e: e.matmul(out=pb[3][:, 0:4], lhsT=TRI, rhs=P("g")[:], start=True, stop=True),
             R=[B("g"), b_cf], W=[b_pb[3]])
        S.op("act", lambda e: e.copy(out=P("gc")[:], in_=pb[3][:, 0:4]), R=[b_pb[3]], W=[B("gc")])

        def mmgl(e):
            e.matmul(out=pb[3][:, 0:4], lhsT=SEL63, rhs=P("gc")[:], start=True, stop=True)
            return e.matmul(out=pb[3][:, 4:8], lhsT=SEL127, rhs=P("gc")[:], start=True, stop=True)
        S.op("pe", mmgl, R=[B("gc"), b_cf], W=[b_pb[3]])
        eglv = egl[:].rearrange("p (t h) -> p t h", h=2)
        S.op("act", lambda e: e.activation(out=eglv[:, :, 0], in_=pb[3][:, 0:4], func=AF.Exp),
             R=[b_pb[3]], W=[b_egl])
        S.op("act", lambda e: e.activation(out=eglv[:, :, 1], in_=pb[3][:, 4:8], func=AF.Exp),
             R=[b_pb[3], b_egl], W=[b_egl])
        S.op("dve", lambda e: e.tensor_copy(out=P("glt")[0:64, :], in_=pb[3][0:64, 0:4]),
             R=[b_pb[3]], W=[B("glt")])
        S.op("dve", lambda e: e.tensor_copy(out=P("glt")[64:128, :], in_=pb[3][64:128, 4:8]),
             R=[b_pb[3], B("glt")], W=[B("glt")])
        S.op("dve", lambda e: e.tensor_tensor(out=P("ekd")[:], in0=P("glt")[:], in1=P("gc")[:], op=ALU.subtract),
             R=[B("glt"), B("gc")], W=[B("ekd")])
        S.op("act", lambda e: e.activation(out=P("ekd")[:], in_=P("ekd")[:], func=AF.Exp),
             R=[B("ekd")], W=[B("ekd")])
        S.op("act", lambda e: e.activation(out=P("egc")[:], in_=P("gc")[:], func=AF.Exp),
             R=[B("gc")], W=[B("egc")])
        S.op("dve", lambda e: e.tensor_tensor(out=P("bg")[:], in0=P("beta")[:], in1=P("egc")[:], op=ALU.mult),
             R=[B("beta"), B("egc")], W=[B("bg")])

        if dbg <= 6:
            continue
        S.defer = None
        na, nb = len(mA), len(mB)
        ia = ib = 0
        while ia < na or ib < nb:
            if ib >= nb or (ia < na and ia * nb <= ib * na):
                mA[ia]()
                ia += 1
            else:
                mB[ib]()
                ib += 1
        def tile_gen(tt):
            gs = gset[tt % NS]
            bk = 4 + tt
            G = lambda nm, gs=gs: gs[nm][0]
            GB = lambda nm, gs=gs: gs[nm][1]
            csl = slice(tt * 128, (tt + 1) * 128)
            S.op("dve", lambda e, G=G, tt=tt: e.tensor_scalar(out=G("dg")[:], in0=IDF, scalar1=P("gc")[:, tt:tt + 1],
                                                                scalar2=None, op0=ALU.mult),
                 R=[b_cf, B("gc")], W=[GB("dg")])

            def mm_abb(e, G=G, bk=bk, csl=csl):
                e.matmul(out=pb[bk][:, 0:128], lhsT=ONES, rhs=G("dg")[:], start=True, stop=True)
                e.matmul(out=pb[bk][:, 128:256], lhsT=knT[:, csl], rhs=knT[:, csl], start=True, stop=True)
                return e.matmul(out=pb[bk][:, 256:384], lhsT=knT[:, csl], rhs=qnT[:, csl], start=True, stop=True)
            yield
            S.op("pe", mm_abb, R=[b_cf, GB("dg"), b_knT, b_qnT], W=[b_pb[bk]])
            S.op("dve", lambda e, G=G, bk=bk, tt=tt: e.tensor_scalar(
                out=G("Eb")[:], in0=pb[bk][:, 0:128], scalar1=P("gc")[:, tt:tt + 1], scalar2=None,
                op0=ALU.subtract), R=[b_pb[bk], B("gc")], W=[GB("Eb")])
            S.op("act", lambda e, G=G: e.activation(out=G("Eb")[:], in_=G("Eb")[:], func=AF.Abs),
                 R=[GB("Eb")], W=[GB("Eb")])
            S.op("act", lambda e, G=G: e.activation(out=G("Ds")[:], in_=G("Eb")[:], func=AF.Exp, scale=-1.0),
                 R=[GB("Eb")], W=[GB("Ds")])
            S.op("act", lambda e, G=G, bk=bk: e.activation(out=G("EA")[:], in_=pb[bk][:, 0:128], func=AF.Exp),
                 R=[b_pb[bk]], W=[GB("EA")])
            S.op("dve", lambda e, G=G, bk=bk, tt=tt: e.scalar_tensor_tensor(
                out=G("t1")[:], in0=pb[bk][:, 128:256], scalar=P("beta")[:, tt:tt + 1], in1=G("Ds")[:],
                op0=ALU.mult, op1=ALU.mult), R=[b_pb[bk], B("beta"), GB("Ds")], W=[GB("t1")])
            S.op("dve", lambda e, G=G: e.tensor_tensor(out=G("MTa")[:], in0=G("t1")[:], in1=MASKL, op=ALU.mult),
                 R=[GB("t1"), b_cf], W=[GB("MTa")])
            S.op("dve", lambda e, G=G, bk=bk: e.tensor_tensor(out=G("t2")[:], in0=pb[bk][:, 256:384], in1=G("Ds")[:],
                                                              op=ALU.mult), R=[b_pb[bk], GB("Ds")], W=[GB("t2")])
            S.op("pool", lambda e, G=G, tt=tt: e.tensor_tensor(out=apT[:, tt, :], in0=G("t2")[:], in1=MASKU,
                                                               op=ALU.mult), R=[GB("t2"), b_cf], W=[b_apT])
            S.op("pool", lambda e, G=G, csl=csl: e.tensor_tensor(out=qgT[:, csl], in0=qnT[:, csl], in1=G("EA")[:],
                                                                 op=ALU.mult), R=[b_qnT, GB("EA")], W=[b_qgT])
            yield
            S.op("pe", lambda e, G=G, bk=bk: e.transpose(out=pb[bk][:, 384:512], in_=G("MTa")[:], identity=IDF),
                 R=[GB("MTa"), b_cf], W=[b_pb[bk]])
            S.op("act", lambda e, G=G, bk=bk: e.copy(out=G("MPa")[:, 0:128], in_=pb[bk][:, 384:512]),
                 R=[b_pb[bk]], W=[GB("MPa")])
            S.op("dve", lambda e, G=G, bk=bk: e.tensor_tensor(out=G("MPb")[:, 128:256], in0=pb[bk][:, 384:512],
                                                              in1=IDF, op=ALU.add),
                 R=[b_pb[bk], b_cf], W=[GB("MPb")])

            def st0(e, G=G, bk=bk):
                e.matmul(out=pb[bk][:, 0:128], lhsT=G("MTa")[:], rhs=G("MPa")[:, 0:128], start=True, stop=True)
                return e.matmul(out=pb[bk][:, 128:256], lhsT=G("MPa")[:, 0:128], rhs=G("MTa")[:], start=True,
                                stop=True)
            yield
            S.op("pe", st0, R=[GB("MTa"), GB("MPa")], W=[b_pb[bk]])
            S.op("act", lambda e, G=G, bk=bk: e.copy(out=G("MPb")[:, 0:128], in_=pb[bk][:, 0:128]),
                 R=[b_pb[bk]], W=[GB("MPb")])
            S.op("dve", lambda e, G=G, bk=bk: e.tensor_copy(out=G("MTb")[:], in_=pb[bk][:, 128:256]),
                 R=[b_pb[bk]], W=[GB("MTb")])
            cur, nxt = ("MPb", "MTb"), ("MPa", "MTa")
            for stp in range(1, 5):
                def stj(e, G=G, bk=bk, cur=cur):
                    e.matmul(out=pb[bk][:, 0:256], lhsT=G(cur[1])[:], rhs=G(cur[0])[:, 0:256], start=True, stop=True)
                    return e.matmul(out=pb[bk][:, 256:384], lhsT=G(cur[0])[:, 0:128], rhs=G(cur[1])[:], start=True,
                                    stop=True)
                yield
                S.op("pe", stj, R=[GB(cur[0]), GB(cur[1])], W=[b_pb[bk]])
                S.op("act", lambda e, G=G, bk=bk, nxt=nxt: e.copy(out=G(nxt[0])[:, 0:128], in_=pb[bk][:, 0:128]),
                     R=[b_pb[bk]], W=[GB(nxt[0])])
                S.op("dve", lambda e, G=G, bk=bk, cur=cur, nxt=nxt: e.tensor_tensor(
                    out=G(nxt[0])[:, 128:256], in0=pb[bk][:, 128:256], in1=G(cur[0])[:, 128:256], op=ALU.add),
                    R=[b_pb[bk], GB(cur[0])], W=[GB(nxt[0])])
                S.op("act", lambda e, G=G, bk=bk, nxt=nxt: e.copy(out=G(nxt[1])[:], in_=pb[bk][:, 256:384]),
                     R=[b_pb[bk]], W=[GB(nxt[1])])
                cur, nxt = nxt, cur
            yield
            S.op("pe", lambda e, G=G, bk=bk, cur=cur: e.matmul(out=pb[bk][:, 0:128], lhsT=G(cur[1])[:],
                                                               rhs=G(cur[0])[:, 128:256], start=True, stop=True),
                 R=[GB(cur[0]), GB(cur[1])], W=[b_pb[bk]])
            S.op("dve", lambda e, G=G, bk=bk, cur=cur: e.tensor_tensor(out=G("TT")[:], in0=pb[bk][:, 0:128],
                                                                       in1=G(cur[0])[:, 128:256], op=ALU.add),
                 R=[b_pb[bk], GB(cur[0])], W=[GB("TT")])
            S.op("pool", lambda e, G=G, tt=tt: e.tensor_scalar(out=G("vb")[:], in0=kvt[:, 2 * tt + 1, :],
                                                               scalar1=P("beta")[:, tt:tt + 1], scalar2=None,
                                                               op0=ALU.mult), R=[b_kvt, B("beta")], W=[GB("vb")])
            S.op("pool", lambda e, G=G, tt=tt: e.tensor_scalar(out=G("kbg")[:], in0=kvt[:, 2 * tt, :],
                                                               scalar1=P("bg")[:, tt:tt + 1], scalar2=None,
                                                               op0=ALU.mult), R=[b_kvt, B("bg")], W=[GB("kbg")])
            S.op("pool", lambda e, tt=tt: e.tensor_scalar(out=kd_sb[:, tt, :], in0=kvt[:, 2 * tt, :],
                                                          scalar1=P("ekd")[:, tt:tt + 1], scalar2=None,
                                                          op0=ALU.mult), R=[b_kvt, B("ekd")], W=[b_kd])

            def mm_uw(e, G=G, bk=bk):
                e.matmul(out=pb[bk][:, 0:128], lhsT=G("TT")[:], rhs=G("vb")[:], start=True, stop=True)
                return e.matmul(out=pb[bk][:, 128:256], lhsT=G("kbg")[:], rhs=G("TT")[:], start=True, stop=True)
            yield
            S.op("pe", mm_uw, R=[GB("TT"), GB("vb"), GB("kbg")], W=[b_pb[bk]])
            S.op("act", lambda e, bk=bk, tt=tt: e.copy(out=u_sb[:, tt, :], in_=pb[bk][:, 0:128]),
                 R=[b_pb[bk]], W=[b_u])
            S.op("act", lambda e, bk=bk, csl=csl: e.copy(out=wT_sb[:, csl], in_=pb[bk][:, 128:256]),
                 R=[b_pb[bk]], W=[b_wT])

        gens = [tile_gen(tt) for tt in range(4)]
        while gens:
            for g_ in gens[:]:
                try:
                    next(g_)
                except StopIteration:
                    gens.remove(g_)
            pull(NPULL)
        if dbg <= 7:
            continue
        S.defer = scan_l
        oi = j % 2
        for tt in range(4):
            csl = slice(tt * 128, (tt + 1) * 128)
            osb, b_o = o_sb[tt % 2], b_osb[tt % 2]
            for hh in range(2):
                r = slice(hh * 64, hh * 64 + 64)
                nl = 2 * tt + hh

                def mm1(e, csl=csl):
                    e.matmul(out=pb[6][:, 0:128], lhsT=wT_sb[:, csl], rhs=Sb[:], start=True, stop=True)
                    return e.matmul(out=pb[7][:, 0:128], lhsT=qgT[:, csl], rhs=Sb[:], start=True, stop=False)
                S.op("pe", mm1, R=[b_wT, b_qgT, b_Sb], W=[b_pb[6], b_pb[7]])
                S.op("dve", lambda e, r=r, tt=tt: e.tensor_tensor(out=vn[r, :], in0=u_sb[r, tt, :],
                                                                  in1=pb[6][r, 0:128], op=ALU.subtract),
                     R=[b_u, b_pb[6]], W=[b_vn])

                def mm2(e, r=r, tt=tt):
                    e.matmul(out=pb[7][:, 0:128], lhsT=apT[r, tt, :], rhs=vn[r, :], start=False, stop=True)
                    return e.matmul(out=pb[7][:, 128:256], lhsT=kd_sb[r, tt, :], rhs=vn[r, :], start=True, stop=True)
                S.op("pe", mm2, R=[b_apT, b_kd, b_vn], W=[b_pb[7]])
                S.op("act", lambda e, r=r, osb=osb: e.copy(out=osb[r, :], in_=pb[7][r, 0:128]),
                     R=[b_pb[7]], W=[b_o])
                S.op("dve", lambda e, nl=nl: e.scalar_tensor_tensor(out=S_f[:], in0=S_f[:], scalar=egl[:, nl:nl + 1],
                                                                    in1=pb[7][:, 128:256], op0=ALU.mult,
                                                                    op1=ALU.add),
                     R=[b_Sf, b_egl, b_pb[7]], W=[b_Sf])
                S.op("act", lambda e: e.copy(out=Sb[:], in_=S_f[:]), R=[b_Sf], W=[b_Sb])
            S.op("act", lambda e, osb=osb: e.activation(out=junkf[:], in_=osb[:], func=AF.Square,
                                                        accum_out=stat[:, 1:2]), R=[b_o], W=[b_junkf, b_stat])
            rstd_from_ss(1, 128, C_EPS6)
            S.op("dve", lambda e, osb=osb: e.scalar_tensor_tensor(out=on_[:], in0=osb[:], scalar=stat[:, 1:2],
                                                                  in1=lnw[:], op0=ALU.mult, op1=ALU.mult),
                 R=[b_o, b_stat, b_lnw], W=[b_on])
            S.op("dve", lambda e, tt=tt, oi=oi, zz=zs2[j % 2]: e.tensor_tensor(out=ola_st[oi][:, tt, :], in0=on_[:], in1=zz[:, tt, :],
                                                                op=ALU.mult), R=[b_on, b_zs2[j % 2]], W=[b_olast[oi]])
        S.dma("sp", ola_v[j], ola_st[oi][:], b_olast[oi], R=[b_olast[oi]])
        S.defer = None

        if dbg <= 8:
            pull_scan(len(scan_l))
            continue
        pull(len(pend))
        nscan = max(1, -(-len(scan_l) // (8 * j + 6)))
        for qq in range(2):
            qc = 2 * j + qq
            q0 = qc * 256
            qsl = slice(qq * 256, qq * 256 + 256)
            oi2 = qc % 2
            nkt = 2 * qc + 2
            def emit_qk(kt, qq=qq, qd=Qblk[j % 2], bq=b_Qblk[j % 2]):
                k0 = kt * 128
                bk = 4 + (kt % 2)
                S.op("pe", lambda e, bk=bk, k0=k0, qq=qq, qd=qd: e.matmul(
                    out=pb[bk][:, 0:512], lhsT=KdT[:, k0:k0 + 128], rhs=qd[:, qq, :], start=True, stop=True),
                    R=[b_KdT, bq], W=[b_pb[bk]])

            def emit_exp(kt, q0=q0):
                k0 = kt * 128
                d = q0 - k0
                par = kt % 2
                bk = 4 + par
                if d >= 256:
                    S.op("act", lambda e, bk=bk, par=par: e.activation(
                        out=pT2[par][:], in_=pb[bk][:, 0:512], func=AF.Exp, scale=0.125, bias=sc[:, 2:3]),
                        R=[b_pb[bk], b_sc], W=[b_pT2[par]])
                else:
                    for m in range(2):
                        S.op("dve", lambda e, bk=bk, m=m, d=d: e.scalar_tensor_tensor(
                            out=s2w[:, m * 256:(m + 1) * 256], in0=pb[bk][:, m * 256:(m + 1) * 256], scalar=0.125,
                            in1=bt[:, d + 128:d + 128 + 256], op0=ALU.mult, op1=ALU.add),
                            R=[b_pb[bk], b_bt], W=[b_s2w])
                    S.op("act", lambda e, par=par: e.activation(out=pT2[par][:], in_=s2w[:], func=AF.Exp),
                         R=[b_s2w], W=[b_pT2[par]])

            def emit_pv(kt, qc=qc):
                par = kt % 2
                for m in range(2):
                    for qb in range(2):
                        klast = 2 * qc + qb
                        if kt > klast:
                            continue
                        ab = qb * 2 + m
                        c0 = m * 256 + qb * 128
                        S.op("pe", lambda e, ab=ab, par=par, c0=c0, kt=kt, klast=klast: e.matmul(
                            out=pb[ab][:, 0:129], lhsT=pT2[par][:, c0:c0 + 128],
                            rhs=Vaug[:, kt, 0:129], start=(kt == 0), stop=(kt == klast)),
                            R=[b_pT2[par], b_Vaug], W=[b_pb[ab]])

            emit_qk(0)
            for kt in range(nkt):
                if kt + 1 < nkt:
                    emit_qk(kt + 1)
                emit_exp(kt)
                emit_pv(kt)
                pull_scan(nscan)
            for qb in range(2):
                a1, a2 = qb * 2, qb * 2 + 1
                S.op("dve", lambda e, a1=a1: e.reciprocal(out=rden[:, 0:1], in_=pb[a1][:, 128:129]),
                     R=[b_pb[a1]], W=[b_rden])
                S.op("dve", lambda e, a2=a2: e.reciprocal(out=rden[:, 1:2], in_=pb[a2][:, 128:129]),
                     R=[b_pb[a2], b_rden], W=[b_rden])
                S.op("dve", lambda e: e.tensor_scalar(out=rden[:, 2:3], in0=rden[:, 1:2], scalar1=C_NLAM,
                                                      scalar2=None, op0=ALU.mult), R=[b_rden, b_cst], W=[b_rden])
                S.op("act", lambda e, a1=a1: e.activation(out=O1[:], in_=pb[a1][:, 0:128], func=AF.Copy,
                                                          scale=rden[:, 0:1]), R=[b_pb[a1], b_rden], W=[b_O1])
                S.op("dve", lambda e, a2=a2: e.scalar_tensor_tensor(out=odf[:], in0=pb[a2][:, 0:128],
                                                                    scalar=rden[:, 2:3], in1=O1[:], op0=ALU.mult,
                                                                    op1=ALU.add),
                     R=[b_pb[a2], b_rden, b_O1], W=[b_odf])
                S.op("act", lambda e: e.activation(out=junkf[:], in_=odf[:], func=AF.Square,
                                                   accum_out=stat[:, 2:3]), R=[b_odf], W=[b_junkf, b_stat])
                rstd_from_ss(2, 128, C_EPS5)
                S.op("dve", lambda e, qb=qb, oi2=oi2: e.scalar_tensor_tensor(
                    out=od_st[oi2][:, qb, :], in0=odf[:], scalar=stat[:, 2:3], in1=dnw[:], op0=ALU.mult,
                    op1=ALU.mult), R=[b_odf, b_stat, b_dnw], W=[b_odst[oi2]])
            S.dma("sp", od_v[qc], od_st[oi2][:], b_odst[oi2], R=[b_odst[oi2]])
        pull_scan(len(scan_l))

    if ext:
        return None
    S.barrier_wait("sp", b_olast + b_odst)
    return kb.done()


def _t5_bucket_np(rel):
    n = np.maximum(rel, 0)
    nf = np.maximum(n, 1).astype(np.float32)
    large = 16 + (np.log(nf / np.float32(16)) / np.float32(math.log(128 / 16)) * np.float32(16)).astype(np.int32)
    large = np.minimum(large, 31)
    return np.where(n < 16, n, large)


def _mixer_consts():
    p = np.arange(128)
    same = (p[:, None] // 64) == (p[None, :] // 64)
    ident = np.eye(128, dtype=np.float32)
    ones = np.ones((128, 128), np.float32)
    maskl = np.where(same & (p[:, None] > p[None, :]), -1.0, 0.0).astype(np.float32)
    masku = np.where(same & (p[:, None] <= p[None, :]), 1.0, 0.0).astype(np.float32)
    tri = masku.copy()
    sel63 = np.zeros((128, 128), np.float32)
    sel63[63, :] = 1.0
    sel127 = np.zeros((128, 128), np.float32)
    sel127[127, :] = 1.0
    return np.ascontiguousarray(np.concatenate([ident, ones, maskl, masku, tri, sel63, sel127], axis=1))


def mixer_inputs(xb, l, h, P):
    w_in = P["w_in"][l]
    cols = np.concatenate([
        np.arange(h * 128, (h + 1) * 128),
        512 + np.arange(h * 128, (h + 1) * 128),
        1024 + np.arange(h * 128, (h + 1) * 128),
        2056 + np.arange(h * 128, (h + 1) * 128),
        2568 + np.arange(h * 128, (h + 1) * 128),
        1536 + np.arange(h * 128, (h + 1) * 128),
        3080 + np.arange(h * 128, (h + 1) * 128),
        np.array([2048 + h]),
        np.array([2052 + h]),
    ])
    wh = np.ascontiguousarray(w_in[:, cols])
    anw = np.ascontiguousarray(P["attn_norm_w"][l].reshape(8, 128).T)
    cwl = P["conv_w"][l]
    cw = np.concatenate([cwl[:, g * 512 + h * 128: g * 512 + (h + 1) * 128].T for g in range(3)], axis=1)
    sc = np.zeros((128, 4), np.float32)
    sc[:, 0] = P["a_log"][l, h]
    sc[:, 1] = P["dt_bias"][l, h]
    sc[:, 2] = P["rel_bias"][31, h]
    lamv = np.concatenate([P["lambda_q1"][l], P["lambda_k1"][l], P["lambda_q2"][l], P["lambda_k2"][l]])
    lamv = np.broadcast_to(lamv[None, :], (128, 256))
    lnw = np.broadcast_to(P["la_norm_w"][l][None, :], (128, 128))
    dnw = np.broadcast_to(P["diff_norm_w"][l][None, :], (128, 128))
    kl = np.arange(128)[:, None]
    jj = np.arange(512)[None, :]
    rel = jj - 128 - kl
    bt = np.where(rel >= 0, P["rel_bias"][_t5_bucket_np(rel), h], np.float32(-30000.0)).astype(np.float32)
    c = np.ascontiguousarray
    return {"x": c(xb), "wh": wh, "anw": anw, "cw": c(cw.astype(np.float32)), "sc": sc,
            "lamv": c(lamv.astype(np.float32)), "lnw": c(lnw.astype(np.float32)),
            "dnw": c(dnw.astype(np.float32)), "btoep": c(bt), "ident_bf": _ident_bf(), "cf": _mixer_consts()}


CC_GROUPS = [[0, 1, 2, 3], [4, 5, 6, 7]]
_MIX_KEYS = ("wh", "anw", "cw", "sc", "lamv", "lnw", "dnw")


def build_fused(T):
    kb = KB()
    nc, S = kb.nc, kb.S
    NT = T // NH
    NTL = NT // 128
    I32 = mybir.dt.int32
    shp = {"wh": [D_MODEL, NW], "anw": [128, 8], "cw": [128, 12], "sc": [128, 4], "lamv": [128, 256],
           "lnw": [128, 128], "dnw": [128, 128]}
    x_d = kb.din("x", [T, D_MODEL], F32)
    xs_d = kb.din("xs", [NT, D_MODEL], F32)
    idx_d = kb.din("idx", [128, NH * NTL], I32)
    bt_d = kb.din("btoep", [128, 512], F32)
    idb_d = kb.din("ident_bf", [128, 128], BF16)
    cf_d = kb.din("cf", [128, 7 * 128], F32)
    fin_d = kb.din("final_w_bc", [128, D_MODEL], F32)
    lay = []
    for l in range(DEPTH):
        d = {k: kb.din("%s%d" % (k, l), shp[k], F32) for k in _MIX_KEYS}
        d["w_out"] = kb.din("w_out%d" % l, [D_MODEL, D_MODEL], F32)
        d["w_gu"] = kb.din("w_gu%d" % l, [D_MODEL, 2 * D_FF], F32)
        d["w_down"] = kb.din("w_down%d" % l, [D_FF, D_MODEL], F32)
        d["ffn_norm_w"] = kb.din("fnw%d" % l, [128, 8], F32)
        lay.append(d)
    y_d = kb.dout("y", [NT, D_MODEL], F32)
    og_in = kb.dint("og_in", [T, 256], BF16)
    og_all = kb.dint("og_all", [NH * T, 256], BF16)
    xs_in = kb.dint("xs_in", [NT, D_MODEL], F32)
    x1_all = kb.dint("x1_all", [T, D_MODEL], F32)

    pb = [kb.ps([128, 512], F32, "pb%d" % i) for i in range(8)]
    b_pb = S.bufs(8, "pb", excl=True)
    ORC = min(T, 2048)
    NKO = T // ORC
    XRC = min(NT, 256)
    NKX = NT // XRC
    b_og_all = S.bufs(NKO, "og_all")
    b_x1all = S.bufs(NKX, "x1_all")
    kb.persist = b_pb + b_og_all + b_x1all

    def allgather(src, dst, b_dst, name):
        S.dma_fn("pool", lambda e: e.collective_compute("AllGather", ALU.bypass, replica_groups=CC_GROUPS,
                                                         ins=[src], outs=[dst]),
                 S.buf(name), W=[b_dst], inc=None)

    def x1_tile(t):
        tok = t * 128
        r, w = divmod(tok, NT)
        k, ww = divmod(w, XRC)
        base = k * (NH * XRC) + r * XRC + ww
        return x1_all[base:base + 128, :]

    for l in range(DEPTH):
        lam_init = 0.8 - 0.6 * math.exp(-0.3 * l)
        kb.begin_phase()
        if l > 0:
            for k in range(NKX):
                allgather(xs_in[k * XRC:(k + 1) * XRC, :], x1_all[k * NH * XRC:(k + 1) * NH * XRC, :], b_x1all[k],
                          "ccx%d_%d" % (l, k))
        ext = {"kb": kb, "pb": pb, "b_pb": b_pb, "x": x_d, "x_tile": (None if l == 0 else x1_tile),
               "b_x": (None if l == 0 else (lambda t: [b_x1all[((t * 128) % NT) // XRC]])), "og": og_in,
               "btoep": bt_d, "ident_bf": idb_d, "cf": cf_d}
        for k in _MIX_KEYS:
            ext[k] = lay[l][k]
        build_mixer(T, lam_init, ext=ext)
        kb.end_phase()
        kb.begin_phase()
        for k in range(NKO):
            allgather(og_in[k * ORC:(k + 1) * ORC, :], og_all[k * NH * ORC:(k + 1) * NH * ORC, :], b_og_all[k],
                      "cco%d_%d" % (l, k))
        last = (l == DEPTH - 1)
        ext = {"kb": kb, "pb": pb, "b_pb": b_pb, "x": (xs_d if l == 0 else xs_in), "y": (y_d if last else xs_in),
               "og_all": og_all, "b_og_all": b_og_all, "idx": idx_d, "T": T,
               "w_out": lay[l]["w_out"], "w_gu": lay[l]["w_gu"], "w_down": lay[l]["w_down"],
               "ffn_norm_w": lay[l]["ffn_norm_w"], "ident_bf": idb_d, "final_w_bc": fin_d}
        build_ffn(NT, last, ext=ext)
        kb.end_phase()
    kb.es.close()
    return nc


_FUSED_CACHE = {}


def _gather_idx(T, h):
    NT = T // NH
    NTL = NT // 128
    ORC = min(T, 2048)
    tok = h * NT + np.arange(NTL)[None, :] * 128 + np.arange(128)[:, None]
    k, w = tok // ORC, tok % ORC
    cols = [k * (NH * ORC) + r * ORC + w for r in range(NH)]
    return np.ascontiguousarray(np.concatenate(cols, axis=1).astype(np.int32))


def fused_inputs(P, T):
    NT = T // NH
    NTL = NT // 128
    perm = np.concatenate([np.concatenate([np.arange(r * 128, (r + 1) * 128),
                                           512 + np.arange(r * 128, (r + 1) * 128)]) for r in range(NH)])
    in_maps = []
    for c in range(NCORES):
        b, h = divmod(c, NH)
        xb = P["x"][b, :T]
        m = {"x": np.ascontiguousarray(xb), "xs": np.ascontiguousarray(xb[h * NT:(h + 1) * NT]),
             "idx": _gather_idx(T, h),
             "final_w_bc": np.ascontiguousarray(np.broadcast_to(P["final_norm_w"][None, :], (128, D_MODEL)))}
        for l in range(DEPTH):
            mi = mixer_inputs(xb, l, h, P)
            for k in _MIX_KEYS:
                m["%s%d" % (k, l)] = mi[k]
            if l == 0:
                m["btoep"], m["ident_bf"], m["cf"] = mi["btoep"], mi["ident_bf"], mi["cf"]
            m["w_out%d" % l] = np.ascontiguousarray(P["w_out"][l][perm])
            m["w_gu%d" % l] = P["w_gate_up"][l]
            m["w_down%d" % l] = P["w_down"][l]
            m["fnw%d" % l] = np.ascontiguousarray(P["ffn_norm_w"][l].reshape(8, 128).T)
        in_maps.append(m)
    return in_maps


def kernel_fused(P, T):
    if T not in _FUSED_CACHE:
        _FUSED_CACHE[T] = build_fused(T)
    nc = _FUSED_CACHE[T]
    NT = T // NH
    res = run_bass_kernel_spmd(nc, fused_inputs(P, T), core_ids=list(range(NCORES)))
    out = np.empty((BATCH, T, D_MODEL), np.float32)
    for c in range(NCORES):
        b, h = divmod(c, NH)
        out[b, h * NT:(h + 1) * NT] = np.asarray(res.results[c]["y"])
    return out


_MIX_CACHE = {}


def kernel(**inputs):
    P = {k: np.ascontiguousarray(np.asarray(v, dtype=np.float32)) for k, v in inputs.items()}
    return kernel_fused(P, P["x"].shape[1])


def kernel_unfused(**inputs):
    P = {k: np.ascontiguousarray(np.asarray(v, dtype=np.float32)) for k, v in inputs.items()}
    x = P["x"]
    B, T, D = x.shape
    NTOK = B * T
    per = NTOK // NCORES
    for l in range(DEPTH):
        lam_init = 0.8 - 0.6 * math.exp(-0.3 * l)
        key = (T, l)
        if key not in _MIX_CACHE:
            _MIX_CACHE[key] = build_mixer(T, lam_init)
        nc = _MIX_CACHE[key]
        in_maps = [mixer_inputs(x[c // NH], l, c % NH, P) for c in range(NCORES)]
        res = run_bass_kernel_spmd(nc, in_maps, core_ids=list(range(NCORES)))
        o = np.empty((B, T, D), dtype=ml_dtypes.bfloat16)
        for c in range(NCORES):
            b, h = divmod(c, NH)
            o[b, :, h * 128:(h + 1) * 128] = np.asarray(res.results[c]["o_la"])
            o[b, :, 512 + h * 128:512 + (h + 1) * 128] = np.asarray(res.results[c]["o_d"])
        xs = x.reshape(NTOK, D)
        os_ = o.reshape(NTOK, D)
        ys = run_ffn([xs[c * per:(c + 1) * per] for c in range(NCORES)],
                     [os_[c * per:(c + 1) * per] for c in range(NCORES)],
                     P["w_out"][l], P["ffn_norm_w"][l], P["w_gate_up"][l], P["w_down"][l],
                     P["final_norm_w"] if l == DEPTH - 1 else None)
        x = np.concatenate([np.asarray(y) for y in ys], axis=0).reshape(B, T, D)
    return np.ascontiguousarray(x.astype(np.float32))
```

```python
import math
from contextlib import ExitStack

import numpy as np
import ml_dtypes
import concourse.bass as bass
import concourse.mybir as mybir
from concourse.bass_utils import run_bass_kernel_spmd

F32 = mybir.dt.float32
BF16 = mybir.dt.bfloat16
AF = mybir.ActivationFunctionType
ALU = mybir.AluOpType
AX = mybir.AxisListType

D_MODEL = 1024
SEQ = 8192
BATCH = 2
DEPTH = 2
NH = 4
D_FF = 2816
IN_DIM = 3592
NORM_EPS = 1e-6
NCORES = 8

ENGS = ("pe", "act", "dve", "pool", "sp")


class Buf:
    __slots__ = ("name", "w", "r", "dsem", "dcnt", "excl")

    def __init__(self, name, excl=False):
        self.name = name
        self.excl = excl
        self.w = None
        self.r = []
        self.dsem = None
        self.dcnt = 0


class Op:
    __slots__ = ("eng", "fn", "deps", "dma", "sem", "val", "needed", "inc")

    def __init__(self, eng, fn, dma=False):
        self.eng = eng
        self.fn = fn
        self.deps = []
        self.dma = dma
        self.sem = None
        self.val = 0
        self.needed = False
        self.inc = 16


class Sched:
    def __init__(self, nc, es):
        self.nc = nc
        self.es = es
        self.ops = {e: [] for e in ENGS}
        self.esem = {e: es.enter_context(nc.semaphore("s_" + e)) for e in ENGS}
        self.nbuf = 0
        self.cnt = {e: 0 for e in ENGS}
        self.phase_dmas = []
        self.nsem = 0
        self.defer = None

    def buf(self, name=None, excl=False):
        self.nbuf += 1
        return Buf(name or ("b%d" % self.nbuf), excl)

    def bufs(self, n, name="b", excl=False):
        return [self.buf("%s%d" % (name, i), excl) for i in range(n)]

    def _link(self, o, R, W):
        deps = []
        for b in R:
            if b.w is not None:
                d = b.w
                if d.dma or o.dma or d.eng != o.eng or o.eng != "pe":
                    deps.append(d)
            if b.excl:
                for d in b.r:
                    if d.eng != o.eng:
                        deps.append(d)
        for b in W:
            if b.w is not None:
                d = b.w
                if d.dma or o.dma or d.eng != o.eng or o.eng != "pe":
                    deps.append(d)
            for d in b.r:
                if d.dma or o.dma or d.eng != o.eng or o.eng != "pe":
                    deps.append(d)
        o.deps = deps
        for b in R:
            if b in W:
                continue
            if b.excl:
                b.r = []
            elif not o.dma:
                b.r = [x for x in b.r if x.dma or x.eng != o.eng]
            b.r.append(o)
        for b in W:
            b.w = o
            b.r = []

    def op(self, eng, fn, R=(), W=()):
        if self.defer is not None:
            self.defer.append(lambda: self.op_now(eng, fn, R, W))
            return None
        return self.op_now(eng, fn, R, W)

    def op_now(self, eng, fn, R=(), W=()):
        o = Op(eng, fn)
        self._link(o, R, W)
        self.ops[eng].append(o)
        return o

    def dma(self, q, out, in_, sb, R=(), W=()):
        return self.dma_fn(q, lambda e, out=out, in_=in_: e.dma_start(out=out, in_=in_), sb, R, W)

    def dma_fn(self, q, fn, sb, R=(), W=(), inc=16):
        if self.defer is not None:
            self.defer.append(lambda: self.dma_fn_now(q, fn, sb, R, W, inc))
            return None
        return self.dma_fn_now(q, fn, sb, R, W, inc)

    def dma_fn_now(self, q, fn, sb, R=(), W=(), inc=16):
        if sb.dsem is None:
            self.nsem += 1
            sb.dsem = self.es.enter_context(self.nc.semaphore("d%d_%s" % (self.nsem, sb.name)))
        o = Op(q, fn, dma=True)
        o.inc = inc
        sb.dcnt += (inc if inc else 1)
        o.sem = sb.dsem
        o.val = sb.dcnt
        self._link(o, R, W)
        self.ops[q].append(o)
        self.phase_dmas.append(o)
        return o

    def phase_barrier(self):
        lasts = []
        for e in ENGS:
            real = [o for o in self.ops[e] if o.fn is not None and not o.dma]
            if real:
                lasts.append(real[-1])
        deps = lasts + list(self.phase_dmas)
        for e in ENGS:
            o = Op(e, None)
            o.deps = [d for d in deps if d.dma or d.eng != e]
            self.ops[e].append(o)
        self.phase_dmas = []

    def barrier_wait(self, eng, R):
        o = Op(eng, None)
        self._link(o, (), R)
        self.ops[eng].append(o)
        return o

    def finalize(self):
        for e in ENGS:
            for o in self.ops[e]:
                for d in o.deps:
                    d.needed = True
        for e in ENGS:
            c = self.cnt[e]
            for o in self.ops[e]:
                if not o.dma and o.needed and o.fn is not None:
                    c += 1
                    o.sem = self.esem[e]
                    o.val = c
            self.cnt[e] = c
        ops = self.ops
        self.ops = {e: [] for e in ENGS}

        def run(eng, lst):
            seen = {}
            for o in lst:
                waits = {}
                for d in o.deps:
                    k = id(d.sem)
                    if k not in waits or waits[k][1] < d.val:
                        waits[k] = (d.sem, d.val)
                for k, (sem, val) in waits.items():
                    if seen.get(k, 0) < val:
                        eng.wait_ge(sem, val)
                        seen[k] = val
                if o.fn is None:
                    continue
                ins = o.fn(eng)
                if o.dma:
                    if o.inc:
                        ins.then_inc(o.sem, o.inc)
                    else:
                        ins.then_inc(o.sem)
                elif o.needed:
                    ins.then_inc(o.sem, 1)

        with self.nc.Block() as block:
            @block.tensor
            def _(e):
                run(e, ops["pe"])

            @block.scalar
            def _(e):
                run(e, ops["act"])

            @block.vector
            def _(e):
                run(e, ops["dve"])

            @block.gpsimd
            def _(e):
                run(e, ops["pool"])

            @block.sync
            def _(e):
                run(e, ops["sp"])


class KB:
    def __init__(self):
        self.nc = bass.Bass("TRN2", target_bir_lowering=False)
        self.es = ExitStack()
        self.S = Sched(self.nc, self.es)
        self.n = 0
        self.pes = None
        self.phase = 0

    def begin_phase(self):
        self.phase += 1
        self.pes = ExitStack()

    def end_phase(self):
        self.S.phase_barrier()
        self.S.finalize()
        self.pes.close()
        self.pes = None
        for b in getattr(self, "persist", []):
            b.w = None
            b.r = []

    def sb(self, shape, dt, name=None):
        self.n += 1
        st = self.pes if self.pes is not None else self.es
        return st.enter_context(self.nc.sbuf_tensor("sb%d_" % self.phase + (name or ("t%d" % self.n)), list(shape), dt))

    def dint(self, name, shape, dt):
        return self.nc.dram_tensor(name, list(shape), dt).ap()

    def ps(self, shape, dt, name=None):
        self.n += 1
        return self.es.enter_context(self.nc.psum_tensor("ps_" + (name or ("p%d" % self.n)), list(shape), dt))

    def din(self, name, shape, dt):
        return self.nc.dram_tensor(name, list(shape), dt, kind="ExternalInput").ap()

    def dout(self, name, shape, dt):
        return self.nc.dram_tensor(name, list(shape), dt, kind="ExternalOutput").ap()

    def done(self):
        self.S.finalize()
        self.es.close()
        return self.nc


def build_ffn(NT, final_norm, ext=None):
    kb = ext["kb"] if ext else KB()
    nc, S = kb.nc, kb.S
    NTL = NT // 128
    KD = D_MODEL // 128
    JF = D_FF // 128

    if ext:
        x_d, wout_d, wgu_d, wdn_d, fnw_d, idb_d, y_d = (
            ext[k] for k in ("x", "w_out", "w_gu", "w_down", "ffn_norm_w", "ident_bf", "y"))
        if final_norm:
            fin_d = ext["final_w_bc"]
        o_d = None
    else:
        x_d = kb.din("x", [NT, D_MODEL], F32)
        o_d = kb.din("o", [NT, D_MODEL], BF16)
        wout_d = kb.din("w_out", [D_MODEL, D_MODEL], F32)
        wgu_d = kb.din("w_gu", [D_MODEL, 2 * D_FF], F32)
        wdn_d = kb.din("w_down", [D_FF, D_MODEL], F32)
        fnw_d = kb.din("ffn_norm_w", [128, KD], F32)
        idb_d = kb.din("ident_bf", [128, 128], BF16)
        if final_norm:
            fin_d = kb.din("final_w_bc", [128, D_MODEL], F32)
        y_d = kb.dout("y", [NT, D_MODEL], F32)

    wout = kb.sb([128, KD, D_MODEL], BF16, "wout")
    wgu = kb.sb([128, KD, 2 * D_FF], BF16, "wgu")
    wdn = kb.sb([128, JF, D_MODEL], BF16, "wdn")
    fnw = kb.sb([128, KD], F32, "fnw")
    idb = kb.sb([128, 128], BF16, "idb")
    b_wout, b_wgu, b_wdn, b_fnw, b_idb = S.bufs(5, "wres")
    if final_norm:
        finw = kb.sb([128, D_MODEL], F32, "finw")
        b_finw = S.buf("finw")
        S.dma("sp", finw[:], fin_d, b_finw, W=[b_finw])
    S.dma("sp", fnw[:], fnw_d, b_fnw, W=[b_fnw])
    S.dma("sp", idb[:], idb_d, b_idb, W=[b_idb])

    STG = 1408
    NSTG = 3
    stg = [kb.sb([128, STG], F32, "stg%d" % i) for i in range(NSTG)]
    b_stg = S.bufs(NSTG, "stg")
    cnt = [0]
    cast_engs = ("dve", "act")

    def load_cast(dst_ap, src_ap, n, wbuf, scale_ap=None):
        i = cnt[0] % NSTG
        q = "sp" if (i % 2 == 0) else "pool"
        S.dma(q, stg[i][:, 0:n], src_ap, b_stg[i], W=[b_stg[i]])
        ce = cast_engs[cnt[0] % 2]
        if ce == "act":
            if scale_ap is None:
                S.op("act", lambda e, d=dst_ap, s=stg[i][:, 0:n]: e.copy(out=d, in_=s), R=[b_stg[i]], W=[wbuf])
            else:
                S.op("act", lambda e, d=dst_ap, s=stg[i][:, 0:n], sc=scale_ap:
                     e.activation(out=d, in_=s, func=AF.Copy, scale=sc), R=[b_stg[i], b_fnw], W=[wbuf])
        elif scale_ap is None:
            S.op(ce, lambda e, d=dst_ap, s=stg[i][:, 0:n]: e.tensor_copy(out=d, in_=s),
                 R=[b_stg[i]], W=[wbuf])
        else:
            S.op(ce, lambda e, d=dst_ap, s=stg[i][:, 0:n], sc=scale_ap:
                 e.tensor_scalar(out=d, in0=s, scalar1=sc, scalar2=None, op0=ALU.mult),
                 R=[b_stg[i], b_fnw], W=[wbuf])
        cnt[0] += 1

    wout_v = wout_d.rearrange("(ko p) n -> p ko n", p=128)
    for ko in range(KD):
        load_cast(wout[:, ko, :], wout_v[:, ko, :], D_MODEL, b_wout)
    wgu_v = wgu_d.rearrange("(ko p) n -> p ko n", p=128)
    for ko in range(KD):
        for c in range(4):
            load_cast(wgu[:, ko, c * STG:(c + 1) * STG], wgu_v[:, ko, c * STG:(c + 1) * STG], STG,
                      b_wgu, scale_ap=fnw[:, ko:ko + 1])
    wdn_v = wdn_d.rearrange("(j p) n -> p j n", p=128)
    for j in range(JF):
        load_cast(wdn[:, j, :], wdn_v[:, j, :], D_MODEL, b_wdn)

    xin = [kb.sb([128, D_MODEL], F32, "xin%d" % i) for i in range(2)]
    oin = [kb.sb([128, D_MODEL], BF16, "oin%d" % i) for i in range(2)]
    b_xin = S.bufs(2, "xin")
    b_oin = S.bufs(2, "oin")
    tbuf = kb.sb([128, KD, 128], BF16, "tbuf")
    b_tbuf = S.buf("tbuf")
    hn = kb.sb([128, D_MODEL], BF16, "hn")
    b_hn = S.buf("hn")
    junk = kb.sb([128, D_MODEL], BF16, "junk")
    b_junk = S.buf("junk")
    aT = kb.sb([128, JF, 128], BF16, "aT")
    b_aT = S.buf("aT")
    sg = [kb.sb([128, 128], F32, "sg%d" % i) for i in range(3)]
    b_sg = S.bufs(3, "sg")
    stat = kb.sb([128, 8], F32, "stat")
    b_stat = S.buf("stat")
    epsc = kb.sb([128, 1], F32, "epsc")
    b_epsc = S.buf("epsc")
    S.op("dve", lambda e: e.memset(epsc[:], NORM_EPS), W=[b_epsc])

    if ext:
        pbank, b_pb = ext["pb"], ext["b_pb"]
        ptb_t = pbank[0][:].bitcast(BF16)
        idx_sb = kb.sb([128, 4 * NTL], mybir.dt.int32, "idx")
        b_idx = S.buf("idx")
        S.dma("sp", idx_sb[:], ext["idx"], b_idx, W=[b_idx])
        TT_ = ext["T"]
    else:
        pbank = [None] + [kb.ps([128, 512], F32, "pb%d" % i) for i in range(1, 8)]
        b_pb = S.bufs(8, "pb", excl=True)
        ptb_t = kb.ps([128, D_MODEL], BF16, "ptb")

    x_v = x_d.rearrange("(t p) d -> t p d", p=128)
    o_v = o_d.rearrange("(t p) d -> t p d", p=128) if o_d is not None else None
    y_v = y_d.rearrange("(t p) d -> t p d", p=128)

    def rms_scale(src, b_src, col):
        S.op("act", lambda e: e.activation(out=junk[:], in_=src, func=AF.Square,
                                           accum_out=stat[:, col:col + 1]),
             R=[b_src], W=[b_junk, b_stat])
        S.op("act", lambda e: e.activation(out=stat[:, col:col + 1], in_=stat[:, col:col + 1], func=AF.Sqrt,
                                           scale=1.0 / D_MODEL, bias=epsc[:, 0:1]),
             R=[b_stat, b_epsc], W=[b_stat])
        S.op("dve", lambda e: e.reciprocal(out=stat[:, col:col + 1], in_=stat[:, col:col + 1]),
             R=[b_stat], W=[b_stat])

    hT2 = [kb.sb([128, KD, 128], BF16, "hT2_%d" % i) for i in range(2)]
    b_hT2 = S.bufs(2, "hT2")
    aT2 = [aT, kb.sb([128, JF, 128], BF16, "aT_1")]
    b_aT2 = [b_aT, S.buf("aT_1")]
    ptb = ptb_t

    def stageA(t):
        i = t % 2
        xt, ot = xin[i], oin[i]
        S.dma("sp", xt[:], x_v[t], b_xin[i], W=[b_xin[i]])
        if ext:
            for r_ in range(4):
                S.dma_fn("pool", lambda e, ot=ot, r_=r_, t=t: e.indirect_dma_start(
                    out=ot[:, r_ * 256:(r_ + 1) * 256], out_offset=None,
                    in_=ext["og_all"],
                    in_offset=bass.IndirectOffsetOnAxis(ap=idx_sb[:, r_ * NTL + t:r_ * NTL + t + 1], axis=0)),
                    b_oin[i], R=[b_idx] + list(ext["b_og_all"]), W=[b_oin[i]])
        else:
            S.dma("pool", ot[:], o_v[t], b_oin[i], W=[b_oin[i]])

        def tr_group(e, src=ot):
            ins = None
            for k in range(KD):
                ins = e.transpose(out=ptb[:, k * 128:(k + 1) * 128], in_=src[:, k * 128:(k + 1) * 128],
                                  identity=idb[:])
            return ins
        S.op("pe", tr_group, R=[b_oin[i], b_idb], W=[b_pb[0]])
        S.op("act", lambda e: e.copy(out=tbuf[:].rearrange("p k t -> p (k t)"), in_=ptb[:, 0:KD * 128]),
             R=[b_pb[0]], W=[b_tbuf])
        for nchunk in range(2):
            bk = 1 + nchunk

            def mm_out(e, bk=bk, nchunk=nchunk):
                ins = None
                for k in range(KD):
                    ins = e.matmul(out=pbank[bk][:], lhsT=tbuf[:, k, :],
                                   rhs=wout[:, k, nchunk * 512:(nchunk + 1) * 512],
                                   start=(k == 0), stop=(k == KD - 1))
                return ins
            S.op("pe", mm_out, R=[b_tbuf, b_wout], W=[b_pb[bk]])
            S.op("dve", lambda e, bk=bk, nchunk=nchunk, xt=xt:
                 e.tensor_tensor(out=xt[:, nchunk * 512:(nchunk + 1) * 512],
                                 in0=pbank[bk][:], in1=xt[:, nchunk * 512:(nchunk + 1) * 512], op=ALU.add),
                 R=[b_pb[bk], b_xin[i]], W=[b_xin[i]])
        rms_scale(xt[:], b_xin[i], 0)
        S.op("act", lambda e, xt=xt: e.activation(out=hn[:], in_=xt[:], func=AF.Copy, scale=stat[:, 0:1]),
             R=[b_xin[i], b_stat], W=[b_hn])

        def tr_group2(e):
            ins = None
            for k in range(KD):
                ins = e.transpose(out=ptb[:, k * 128:(k + 1) * 128], in_=hn[:, k * 128:(k + 1) * 128],
                                  identity=idb[:])
            return ins
        S.op("pe", tr_group2, R=[b_hn, b_idb], W=[b_pb[0]])
        S.op("act", lambda e, i=i: e.copy(out=hT2[i][:].rearrange("p k t -> p (k t)"), in_=ptb[:, 0:KD * 128]),
             R=[b_pb[0]], W=[b_hT2[i]])

    la = []

    def stageB(t):
        i = t % 2
        for j in range(JF):
            if la:
                la.pop(0)()
            bk = (3, 4, 7)[j % 3]

            def mm_gu(e, bk=bk, j=j, i=i):
                ins = None
                for half in range(2):
                    for k in range(KD):
                        c0 = half * D_FF + j * 128
                        ins = e.matmul(out=pbank[bk][:, half * 128:(half + 1) * 128],
                                       lhsT=wgu[:, k, c0:c0 + 128], rhs=hT2[i][:, k, :],
                                       start=(k == 0), stop=(k == KD - 1))
                return ins
            S.op("pe", mm_gu, R=[b_hT2[i], b_wgu], W=[b_pb[bk]])
            s_ = j % 3
            S.op("act", lambda e, bk=bk, s_=s_: e.activation(out=sg[s_][:], in_=pbank[bk][:, 0:128], func=AF.Silu),
                 R=[b_pb[bk]], W=[b_sg[s_]])
            S.op("dve", lambda e, bk=bk, s_=s_, j=j, i=i: e.tensor_tensor(out=aT2[i][:, j, :],
                                                                          in0=pbank[bk][:, 128:256],
                                                                          in1=sg[s_][:], op=ALU.mult),
                 R=[b_pb[bk], b_sg[s_]], W=[b_aT2[i]])

    def stageC(t):
        i = t % 2
        xt = xin[i]
        for nchunk in range(2):
            bk = 5 + nchunk

            def mm_dn(e, bk=bk, nchunk=nchunk, i=i):
                ins = None
                for j in range(JF):
                    ins = e.matmul(out=pbank[bk][:], lhsT=aT2[i][:, j, :],
                                   rhs=wdn[:, j, nchunk * 512:(nchunk + 1) * 512],
                                   start=(j == 0), stop=(j == JF - 1))
                return ins
            S.op("pe", mm_dn, R=[b_aT2[i], b_wdn], W=[b_pb[bk]])
            S.op("dve", lambda e, bk=bk, nchunk=nchunk, xt=xt:
                 e.tensor_tensor(out=xt[:, nchunk * 512:(nchunk + 1) * 512],
                                 in0=pbank[bk][:], in1=xt[:, nchunk * 512:(nchunk + 1) * 512], op=ALU.add),
                 R=[b_pb[bk], b_xin[i]], W=[b_xin[i]])
        if final_norm:
            rms_scale(xt[:], b_xin[i], 1)
            S.op("dve", lambda e, xt=xt: e.scalar_tensor_tensor(out=xt[:], in0=xt[:], scalar=stat[:, 1:2],
                                                                in1=finw[:], op0=ALU.mult, op1=ALU.mult),
                 R=[b_xin[i], b_stat, b_finw], W=[b_xin[i]])
        S.dma("sp", y_v[t], xt[:], b_xin[i], R=[b_xin[i]])

    stageA(0)
    for t in range(NTL):
        if t + 1 < NTL:
            S.defer = la
            stageA(t + 1)
            S.defer = None
            for _ in range(5 if ext else 2):
                la.pop(0)()
        stageB(t)
        while la:
            la.pop(0)()
        stageC(t)

    if ext:
        return None
    S.barrier_wait("sp", b_xin)
    return kb.done()


def _ident_bf():
    return np.eye(128, dtype=np.float32).astype(ml_dtypes.bfloat16)


def run_ffn(x_sl, o_sl, w_out, ffn_norm_w, w_gu, w_down, final_w, nc_cache={}):
    NT = x_sl[0].shape[0]
    key = (NT, final_w is not None)
    if key not in nc_cache:
        nc_cache[key] = build_ffn(NT, final_w is not None)
    nc = nc_cache[key]
    fnw = np.ascontiguousarray(ffn_norm_w.reshape(D_MODEL // 128, 128).T)
    in_maps = []
    for c in range(len(x_sl)):
        m = {"x": np.ascontiguousarray(x_sl[c]), "o": np.ascontiguousarray(o_sl[c]),
             "w_out": w_out, "w_gu": w_gu, "w_down": w_down, "ffn_norm_w": fnw,
             "ident_bf": _ident_bf()}
        if final_w is not None:
            m["final_w_bc"] = np.ascontiguousarray(np.broadcast_to(final_w[None, :], (128, D_MODEL)))
        in_maps.append(m)
    res = run_bass_kernel_spmd(nc, in_maps, core_ids=list(range(len(x_sl))))
    return [r["y"] for r in res.results]


NW = 898


NPULL = 6


def rr_merge(lists):
    out = []
    while any(lists):
        for l in lists:
            if l:
                out.append(l.pop(0))
    return out


def build_mixer(T, lam_init, dbg=99, ext=None):
    SKIP = ''
    kb = ext["kb"] if ext else KB()
    nc, S = kb.nc, kb.S
    NCH = T // 512
    NTL = T // 128
    KD = D_MODEL // 128

    if ext:
        x_d = ext["x"]
        wh_d, anw_d, cw_d, sc_d, lamv_d, lnw_d, dnw_d, bt_d, idb_d, cf_d = (
            ext[k] for k in ("wh", "anw", "cw", "sc", "lamv", "lnw", "dnw", "btoep", "ident_bf", "cf"))
        ola_d = ext["og"][:, 0:128]
        od_d = ext["og"][:, 128:256]
    else:
        x_d = kb.din("x", [T, D_MODEL], F32)
        wh_d = kb.din("wh", [D_MODEL, NW], F32)
        anw_d = kb.din("anw", [128, KD], F32)
        cw_d = kb.din("cw", [128, 12], F32)
        sc_d = kb.din("sc", [128, 4], F32)
        lamv_d = kb.din("lamv", [128, 4 * 64], F32)
        lnw_d = kb.din("lnw", [128, 128], F32)
        dnw_d = kb.din("dnw", [128, 128], F32)
        bt_d = kb.din("btoep", [128, 512], F32)
        idb_d = kb.din("ident_bf", [128, 128], BF16)
        cf_d = kb.din("cf", [128, 7 * 128], F32)
        ola_d = kb.dout("o_la", [T, 128], BF16)
        od_d = kb.dout("o_d", [T, 128], BF16)

    def T_(shape, dt, name):
        return kb.sb(shape, dt, name), S.buf(name)

    anw, b_anw = T_([128, KD], F32, "anw")
    cw, b_cw = T_([128, 12], F32, "cw")
    sc, b_sc = T_([128, 4], F32, "sc")
    lamv, b_lamv = T_([128, 256], F32, "lamv")
    lnw, b_lnw = T_([128, 128], F32, "lnw")
    dnw, b_dnw = T_([128, 128], F32, "dnw")
    bt, b_bt = T_([128, 512], F32, "bt")
    idb, b_idb = T_([128, 128], BF16, "idb")
    cf, b_cf = T_([128, 7 * 128], F32, "cf")
    for (t_, d_, b_) in ((anw, anw_d, b_anw), (cw, cw_d, b_cw), (sc, sc_d, b_sc), (lamv, lamv_d, b_lamv),
                         (lnw, lnw_d, b_lnw), (dnw, dnw_d, b_dnw), (bt, bt_d, b_bt), (idb, idb_d, b_idb),
                         (cf, cf_d, b_cf)):
        S.dma("sp", t_[:], d_, b_, W=[b_])
    IDF = cf[:, 0:128]
    ONES = cf[:, 128:256]
    MASKL = cf[:, 256:384]
    MASKU = cf[:, 384:512]
    TRI = cf[:, 512:640]
    SEL63 = cf[:, 640:768]
    SEL127 = cf[:, 768:896]

    cst_, b_cst = T_([128, 8], F32, "cst")
    S.op("dve", lambda e: e.memset(cst_[:, 0:1], 1.0), W=[b_cst])
    S.op("dve", lambda e: e.memset(cst_[:, 1:2], 1e-6), W=[b_cst])
    S.op("dve", lambda e: e.memset(cst_[:, 2:3], 1e-5), W=[b_cst])
    S.op("dve", lambda e: e.memset(cst_[:, 5:6], 0.0), W=[b_cst])
    C_ONE, C_EPS6, C_EPS5, C_NA, C_NLAM, C_ZERO = (cst_[:, i:i + 1] for i in range(6))
    S.op("act", lambda e: e.activation(out=cst_[:, 3:4], in_=sc[:, 0:1], func=AF.Exp), R=[b_sc], W=[b_cst])
    S.op("dve", lambda e: e.tensor_scalar(out=cst_[:, 3:4], in0=cst_[:, 3:4], scalar1=-1.0, scalar2=None,
                                          op0=ALU.mult), R=[b_cst], W=[b_cst])
    lt, b_lt = T_([128, 128], F32, "lamtmp")
    ls, b_ls = T_([128, 4], F32, "lamsum")
    S.op("dve", lambda e: e.tensor_tensor(out=lt[:, 0:64], in0=lamv[:, 0:64], in1=lamv[:, 64:128], op=ALU.mult),
         R=[b_lamv], W=[b_lt])
    S.op("dve", lambda e: e.tensor_tensor(out=lt[:, 64:128], in0=lamv[:, 128:192], in1=lamv[:, 192:256],
                                          op=ALU.mult), R=[b_lamv, b_lt], W=[b_lt])
    if 'r' not in SKIP:
        S.op("dve", lambda e: e.reduce_sum(out=ls[:, 0:1], in_=lt[:, 0:64], axis=AX.X), R=[b_lt], W=[b_ls])
        S.op("dve", lambda e: e.reduce_sum(out=ls[:, 1:2], in_=lt[:, 64:128], axis=AX.X), R=[b_lt, b_ls], W=[b_ls])
    S.op("act", lambda e: e.activation(out=ls[:, 2:4], in_=ls[:, 0:2], func=AF.Exp), R=[b_ls], W=[b_ls])
    S.op("dve", lambda e: e.scalar_tensor_tensor(out=cst_[:, 4:5], in0=ls[:, 3:4], scalar=float(-lam_init),
                                                 in1=ls[:, 2:3], op0=ALU.add, op1=ALU.subtract),
         R=[b_ls, b_cst], W=[b_cst])
    S.op("dve", lambda e: e.tensor_scalar(out=dnw[:], in0=dnw[:], scalar1=float(1.0 - lam_init), scalar2=None,
                                          op0=ALU.mult), R=[b_dnw], W=[b_dnw])

    Wb, b_Wb = T_([128, KD, 1024], BF16, "Wb")
    wst = [kb.sb([128, NW], F32, "wst%d" % i) for i in range(2)]
    b_wst = S.bufs(2, "wst")
    wh_v = wh_d.rearrange("(ko p) n -> p ko n", p=128)
    for ko in range(KD):
        i = ko % 2
        S.dma("sp", wst[i][:], wh_v[:, ko, :], b_wst[i], W=[b_wst[i]])
        if i == 0:
            S.op("dve", lambda e, i=i, ko=ko: e.tensor_scalar(out=Wb[:, ko, 0:NW], in0=wst[i][:],
                                                              scalar1=anw[:, ko:ko + 1], scalar2=None, op0=ALU.mult),
                 R=[b_wst[i], b_anw], W=[b_Wb])
        else:
            S.op("act", lambda e, i=i, ko=ko: e.activation(out=Wb[:, ko, 0:NW], in_=wst[i][:], func=AF.Copy,
                                                           scale=anw[:, ko:ko + 1]),
                 R=[b_wst[i], b_anw], W=[b_Wb])

    KdT, b_KdT = T_([128, T], BF16, "KdT")
    Vaug, b_Vaug = T_([128, NTL, 144], BF16, "Vaug")
    if 'v' not in SKIP:
        S.op("pool", lambda e: e.memset(Vaug[:, :, 128:129], 1.0), W=[b_Vaug])

    xt = [kb.sb([128, D_MODEL], F32, "xt%d" % i) for i in range(4)]
    b_xt = S.bufs(4, "xt")
    xn, b_xn = T_([128, D_MODEL], BF16, "xn")
    junk, b_junk = T_([128, D_MODEL], BF16, "junk")
    junkf, b_junkf = T_([128, 128], F32, "junkf")
    stat, b_stat = T_([128, 4], F32, "stat")
    hT, b_hT = T_([128, KD, 512], BF16, "hT")
    cstg = [kb.sb([128, 515], F32, "cstg%d" % g) for g in range(3)]
    b_cstg = S.bufs(3, "cstg")
    cacc = [kb.sb([128, 512], F32, "cacc%d" % g) for g in range(3)]
    b_cacc = S.bufs(3, "cacc")
    sil = [kb.sb([128, 512], F32, "sil%d" % g) for g in range(2)]
    b_sil = S.bufs(2, "sil")
    sq, b_sq = T_([128, 512], F32, "sq")
    rs, b_rs = T_([128, 512], F32, "rs")
    qnT, b_qnT = T_([128, 512], BF16, "qnT")
    knT, b_knT = T_([128, 512], BF16, "knT")
    vsT, b_vsT = T_([128, 512], BF16, "vsT")
    QdT, b_QdT = T_([128, 512], BF16, "QdT")
    qgT, b_qgT = T_([128, 512], BF16, "qgT")
    kvt, b_kvt = T_([128, 8, 128], BF16, "kvt")
    zs, b_zs = T_([128, 4, 128], BF16, "zs")
    ba, b_ba = T_([128, 4, 2], F32, "ba")
    for g in range(3):
        S.op("dve", lambda e, g=g: e.memset(cstg[g][:, 0:3], 0.0), W=[b_cstg[g]])
    pt = {}
    for nm in ("beta", "eb", "g", "gc", "egc", "bg", "glt", "ekd", "tmpa"):
        pt[nm] = T_([128, 4], F32, "pt_" + nm)
    egl, b_egl = T_([128, 8], F32, "egl")
    NS = 4
    gset = []
    for s_ in range(NS):
        d = {}
        for nm, shp, dt in (("dg", [128, 128], F32), ("Eb", [128, 128], F32), ("Ds", [128, 128], F32),
                            ("EA", [128, 128], F32), ("t1", [128, 128], F32), ("t2", [128, 128], F32),
                            ("MPa", [128, 256], F32), ("MPb", [128, 256], F32),
                            ("MTa", [128, 128], F32), ("MTb", [128, 128], F32),
                            ("TT", [128, 128], BF16), ("vb", [128, 128], BF16), ("kbg", [128, 128], BF16)):
            d[nm] = T_(shp, dt, "%s_%d" % (nm, s_))
        gset.append(d)
    u_sb, b_u = T_([128, 4, 128], F32, "u_sb")
    wT_sb, b_wT = T_([128, 512], BF16, "wT_sb")
    apT, b_apT = T_([128, 4, 128], BF16, "apT")
    kd_sb, b_kd = T_([128, 4, 128], BF16, "kd_sb")
    S_f, b_Sf = T_([128, 128], F32, "S_f")
    Sb, b_Sb = T_([128, 128], BF16, "Sb")
    vn, b_vn = T_([128, 128], BF16, "vn")
    o_sb = [kb.sb([128, 128], F32, "o_sb%d" % i) for i in range(2)]
    b_osb = S.bufs(2, "o_sb")
    on_, b_on = T_([128, 128], F32, "on")
    ola_st = [kb.sb([128, 4, 128], BF16, "ola_st%d" % i) for i in range(2)]
    b_olast = S.bufs(2, "ola_st")
    S.op("dve", lambda e: e.memset(S_f[:], 0.0), W=[b_Sf])
    S.op("dve", lambda e: e.memset(Sb[:], 0.0), W=[b_Sb])
    pT = [[kb.sb([128, 256], BF16, "pT%d%d" % (m, p)) for p in range(2)] for m in range(2)]
    b_pT = [[S.buf("pT%d%d" % (m, p)) for p in range(2)] for m in range(2)]
    s2 = [kb.sb([128, 256], F32, "s2_%d" % m) for m in range(2)]
    b_s2 = S.bufs(2, "s2")
    rden, b_rden = T_([128, 4], F32, "rden")
    O1, b_O1 = T_([128, 128], F32, "O1")
    odf, b_odf = T_([128, 128], F32, "odf")
    od_st = [kb.sb([128, 2, 128], BF16, "od_st%d" % i) for i in range(2)]
    b_odst = S.bufs(2, "od_st")

    if ext:
        pb, b_pb = ext["pb"], ext["b_pb"]
    else:
        pb = [kb.ps([128, 512], F32, "pb%d" % i) for i in range(8)]
        b_pb = S.bufs(8, "pb", excl=True)
    pb0_bf = pb[0][:].bitcast(BF16)

    x_v = x_d.rearrange("(t p) d -> t p d", p=128)
    ola_v = ola_d.rearrange("(c t p) e -> c p t e", p=128, t=4)
    od_v = od_d.rearrange("(c t p) e -> c p t e", p=128, t=2)

    def rstd_from_ss(col, n, epsc):
        S.op("act", lambda e: e.activation(out=stat[:, col:col + 1], in_=stat[:, col:col + 1], func=AF.Ln,
                                           scale=1.0 / n, bias=epsc), R=[b_stat, b_cst], W=[b_stat])
        S.op("act", lambda e: e.activation(out=stat[:, col:col + 1], in_=stat[:, col:col + 1], func=AF.Exp,
                                           scale=-0.5), R=[b_stat], W=[b_stat])

    def silu_via_exp(src_ap, R_src, tmp_ap, b_tmp, out_ap, W_out, mul_eng="dve"):
        S.op("act", lambda e: e.activation(out=tmp_ap, in_=src_ap, func=AF.Exp, scale=-1.0), R=R_src, W=[b_tmp])
        S.op("act", lambda e: e.activation(out=tmp_ap, in_=tmp_ap, func=AF.Ln, bias=C_ONE), R=[b_tmp, b_cst],
             W=[b_tmp])
        S.op("act", lambda e: e.activation(out=tmp_ap, in_=tmp_ap, func=AF.Exp, scale=-1.0), R=[b_tmp], W=[b_tmp])
        S.op(mul_eng, lambda e: e.tensor_tensor(out=out_ap, in0=src_ap, in1=tmp_ap, op=ALU.mult),
             R=list(R_src) + [b_tmp], W=W_out)

    stmp, b_stmp = T_([128, 512], F32, "stmp")
    stmp3 = [stmp, kb.sb([128, 512], F32, "stmp_1"), kb.sb([128, 512], F32, "stmp_2")]
    b_stmp3 = [b_stmp, S.buf("stmp_1"), S.buf("stmp_2")]
    sq2 = [sq, kb.sb([128, 512], F32, "sq_1")]
    b_sq2 = [b_sq, S.buf("sq_1")]
    rs2 = [rs, kb.sb([128, 512], F32, "rs_1")]
    b_rs2 = [b_rs, S.buf("rs_1")]
    zraw, b_zraw = T_([128, 4, 128], F32, "zraw")

    hT2 = [hT, kb.sb([128, KD, 512], BF16, "hT_b")]
    b_hT2 = [b_hT, S.buf("hT_b")]
    Qblk = [kb.sb([128, 2, 512], BF16, "Qblk%d" % i) for i in range(2)]
    b_Qblk = S.bufs(2, "Qblk")
    for i_ in range(2):
        S.op("pool", lambda e, i_=i_: e.memset(Qblk[i_][:], 0.0), W=[b_Qblk[i_]])
    pT2 = [kb.sb([128, 512], BF16, "pT2_%d" % i) for i in range(2)]
    b_pT2 = S.bufs(2, "pT2")
    s2w, b_s2w = T_([128, 512], F32, "s2w")
    scan_l = []

    def pull_scan(n):
        for _ in range(min(n, len(scan_l))):
            scan_l.pop(0)()

    zs2 = [zs, kb.sb([128, 4, 128], BF16, "zs_b")]
    b_zs2 = [b_zs, S.buf("zs_b")]
    pend = []

    def pull(n):
        for _ in range(min(n, len(pend))):
            pend.pop(0)()

    def front(j):
        if dbg <= 0:
            return
        for tt in range(4):
            t = 4 * j + tt
            i = tt
            S.dma("sp" if i % 2 == 0 else "pool", xt[i][:],
                  (ext["x_tile"](t) if (ext and ext.get("x_tile") is not None) else x_v[t]), b_xt[i], W=[b_xt[i]],
                  R=(list(ext["b_x"](t)) if (ext and ext.get("b_x") is not None) else []))
        for tt in range(4):
            t = 4 * j + tt
            i = tt
            S.op("act", lambda e, i=i: e.activation(out=junk[:], in_=xt[i][:], func=AF.Square,
                                                    accum_out=stat[:, 0:1]), R=[b_xt[i]], W=[b_junk, b_stat])
            rstd_from_ss(0, D_MODEL, C_EPS6)
            S.op("act", lambda e, i=i: e.activation(out=xn[:], in_=xt[i][:], func=AF.Copy, scale=stat[:, 0:1]),
                 R=[b_xt[i], b_stat], W=[b_xn])

            def trx(e):
                ins = None
                for k in range(KD):
                    ins = e.transpose(out=pb0_bf[:, k * 128:(k + 1) * 128], in_=xn[:, k * 128:(k + 1) * 128],
                                      identity=idb[:])
                return ins
            S.op("pe", trx, R=[b_xn, b_idb], W=[b_pb[0]])
            S.op("dve", lambda e, tt=tt: e.tensor_copy(out=hT2[j % 2][:, :, tt * 128:(tt + 1) * 128],
                                                        in_=pb0_bf[:, 0:1024].rearrange("p (k t) -> p k t", k=KD)),
                 R=[b_pb[0]], W=[b_hT2[j % 2]])
        if dbg <= 1:
            return
        for g in range(5 if 'f' not in SKIP else 0):
            bk = 1 + (g % 2)

            def mmf(e, g=g, bk=bk):
                ins = None
                for k in range(KD):
                    ins = e.matmul(out=pb[bk][:], lhsT=Wb[:, k, g * 128:(g + 1) * 128], rhs=hT2[j % 2][:, k, :],
                                   start=(k == 0), stop=(k == KD - 1))
                return ins
            S.op("pe", mmf, R=[b_Wb, b_hT2[j % 2]], W=[b_pb[bk]])
            if g < 3:
                S.op("act", lambda e, g=g, bk=bk: e.copy(out=cstg[g][:, 3:515], in_=pb[bk][:]),
                     R=[b_pb[bk]], W=[b_cstg[g]])
            elif g == 3:
                S.op("act", lambda e, bk=bk: e.copy(out=Qblk[j % 2][0:64, :, 0:256],
                                                    in_=pb[bk][0:64, :].rearrange("p (q c) -> p q c", q=2)),
                     R=[b_pb[bk]], W=[b_Qblk[j % 2]])
                S.op("act", lambda e, bk=bk: e.copy(out=Qblk[j % 2][64:128, :, 256:512],
                                                    in_=pb[bk][64:128, :].rearrange("p (q c) -> p q c", q=2)),
                     R=[b_pb[bk]], W=[b_Qblk[j % 2]])
            else:
                S.op("act", lambda e, bk=bk, j=j: e.copy(out=KdT[:, j * 512:(j + 1) * 512], in_=pb[bk][:]),
                     R=[b_pb[bk]], W=[b_KdT])
        for tt in range(4 if 't' not in SKIP else 0):
            t = 4 * j + tt

            def mmt(e, tt=tt):
                ins = None
                for k in range(KD):
                    NN = 256 if 'n' in SKIP else 258
                    ins = e.matmul(out=pb[3][:, 0:NN], lhsT=hT2[j % 2][:, k, tt * 128:(tt + 1) * 128],
                                   rhs=Wb[:, k, 640:640 + NN], start=(k == 0), stop=(k == KD - 1))
                return ins
            S.op("pe", mmt, R=[b_Wb, b_hT2[j % 2]], W=[b_pb[3]])
            S.op("dve", lambda e, tt=tt: e.tensor_copy(out=zraw[:, tt, :], in_=pb[3][:, 0:128]),
                 R=[b_pb[3]], W=[b_zraw])
            S.op("dve", lambda e, t=t: e.tensor_copy(out=Vaug[:, t, 0:128], in_=pb[3][:, 128:256]),
                 R=[b_pb[3]], W=[b_Vaug])
            S.op("dve", lambda e, tt=tt: e.tensor_copy(out=ba[:, tt, :], in_=pb[3][:, 256:258]),
                 R=[b_pb[3]], W=[b_ba])
        silu_via_exp(zraw[:].rearrange("p a d -> p (a d)"), [b_zraw], stmp[:], b_stmp,
                     zs2[j % 2][:].rearrange("p a d -> p (a d)"), [b_zs2[j % 2]], mul_eng="pool")

    front(0)
    for j in range(NCH):
        if dbg <= 2:
            continue
        pend.clear()
        if j + 1 < NCH:
            S.defer = pend
            front(j + 1)
            S.defer = None
            pull(4)
        S.defer = mA = []
        subs = []
        for g in range(3):
            S.defer = sub = []
            subs.append(sub)
            ce = "dve"
            S.op(ce, lambda e, g=g: e.tensor_scalar(out=cacc[g][:], in0=cstg[g][:, 3:515],
                                                    scalar1=cw[:, g * 4 + 3:g * 4 + 4], scalar2=None, op0=ALU.mult),
                 R=[b_cstg[g], b_cw], W=[b_cacc[g]])
            for tap in (2, 1, 0):
                S.op(ce, lambda e, g=g, tap=tap: e.scalar_tensor_tensor(
                    out=cacc[g][:], in0=cstg[g][:, tap:tap + 512], scalar=cw[:, g * 4 + tap:g * 4 + tap + 1],
                    in1=cacc[g][:], op0=ALU.mult, op1=ALU.add),
                    R=[b_cstg[g], b_cw, b_cacc[g]], W=[b_cacc[g]])
            S.op(ce, lambda e, g=g: e.tensor_copy(out=cstg[g][:, 0:3], in_=cstg[g][:, 512:515]),
                 R=[b_cstg[g]], W=[b_cstg[g]])
            if g < 2:
                silu_via_exp(cacc[g][:], [b_cacc[g]], stmp3[g][:], b_stmp3[g], sil[g][:], [b_sil[g]])
            else:
                silu_via_exp(cacc[g][:], [b_cacc[g]], stmp3[g][:], b_stmp3[g], vsT[:], [b_vsT])
        S.defer = mA
        mA.extend(rr_merge(subs))
        subs = []
        for g in range(2):
            S.defer = sub = []
            subs.append(sub)
            S.op("pool", lambda e, g=g: e.tensor_tensor(out=sq2[g][:], in0=sil[g][:], in1=sil[g][:], op=ALU.mult),
                 R=[b_sil[g]], W=[b_sq2[g]])
            bk = 1 + g
            S.op("pe", lambda e, bk=bk, g=g: e.matmul(out=pb[bk][:], lhsT=ONES, rhs=sq2[g][:], start=True, stop=True),
                 R=[b_sq2[g], b_cf], W=[b_pb[bk]])
            S.op("act", lambda e, bk=bk, g=g: e.activation(out=rs2[g][:], in_=pb[bk][:], func=AF.Ln, bias=C_EPS6),
                 R=[b_pb[bk], b_cst], W=[b_rs2[g]])
            S.op("act", lambda e, g=g: e.activation(out=rs2[g][:], in_=rs2[g][:], func=AF.Exp, scale=-0.5), R=[b_rs2[g]], W=[b_rs2[g]])
            if g == 0:
                S.op("dve", lambda e, g=g: e.scalar_tensor_tensor(out=qnT[:], in0=sil[0][:], scalar=float(128 ** -0.5),
                                                             in1=rs2[g][:], op0=ALU.mult, op1=ALU.mult),
                     R=[b_sil[0], b_rs2[g]], W=[b_qnT])
            else:
                S.op("dve", lambda e, g=g: e.tensor_tensor(out=knT[:], in0=sil[1][:], in1=rs2[g][:], op=ALU.mult),
                     R=[b_sil[1], b_rs2[g]], W=[b_knT])
        S.defer = mA
        mA.extend(rr_merge(subs))
        def trkv(e):
            ins = None
            for tt in range(4):
                ins = e.transpose(out=pb0_bf[:, (2 * tt) * 128:(2 * tt + 1) * 128],
                                  in_=knT[:, tt * 128:(tt + 1) * 128], identity=idb[:])
                ins = e.transpose(out=pb0_bf[:, (2 * tt + 1) * 128:(2 * tt + 2) * 128],
                                  in_=vsT[:, tt * 128:(tt + 1) * 128], identity=idb[:])
            return ins
        S.op("pe", trkv, R=[b_knT, b_vsT, b_idb], W=[b_pb[0]])
        S.op("dve", lambda e: e.tensor_copy(out=kvt[:].rearrange("p a d -> p (a d)"), in_=pb0_bf[:, 0:1024]),
             R=[b_pb[0]], W=[b_kvt])
        if dbg <= 5:
            continue
        S.defer = mB = []
        P = lambda nm: pt[nm][0]
        B = lambda nm: pt[nm][1]
        S.op("act", lambda e: e.activation(out=P("eb")[:], in_=ba[:, :, 0], func=AF.Exp, scale=-1.0),
             R=[b_ba], W=[B("eb")])
        S.op("dve", lambda e: e.tensor_scalar(out=P("eb")[:], in0=P("eb")[:], scalar1=1.0, scalar2=None,
                                              op0=ALU.add), R=[B("eb")], W=[B("eb")])
        S.op("dve", lambda e: e.reciprocal(out=P("beta")[:], in_=P("eb")[:]), R=[B("eb")], W=[B("beta")])
        S.op("act", lambda e: e.activation(out=P("tmpa")[:], in_=ba[:, :, 1], func=AF.Exp, bias=sc[:, 1:2]),
             R=[b_ba, b_sc], W=[B("tmpa")])
        S.op("act", lambda e: e.activation(out=P("tmpa")[:], in_=P("tmpa")[:], func=AF.Ln, bias=C_ONE),
             R=[B("tmpa"), b_cst], W=[B("tmpa")])
        S.op("dve", lambda e: e.tensor_scalar(out=P("g")[:], in0=P("tmpa")[:], scalar1=C_NA, scalar2=None,
                                              op0=ALU.mult), R=[B("tmpa"), b_cst], W=[B("g")])
        S.op("pe", lambda e: e.matmul(out=pb[3][:, 0:4], lhsT=TRI, rhs=P("g")[:], start=True, stop=True),
             R=[B("g"), b_cf], W=[b_pb[3]])
        S.op("act", lambda e: e.copy(out=P("gc")[:], in_=pb[3][:, 0:4]), R=[b_pb[3]], W=[B("gc")])

        def mmgl(e):
            e.matmul(out=pb[3][:, 0:4], lhsT=SEL63, rhs=P("gc")[:], start=True, stop=True)
            return e.matmul(out=pb[3][:, 4:8], lhsT=SEL127, rhs=P("gc")[:], start=True, stop=True)
        S.op("pe", mmgl, R=[B("gc"), b_cf], W=[b_pb[3]])
        eglv = egl[:].rearrange("p (t h) -> p t h", h=2)
        S.op("act", lambda e: e.activation(out=eglv[:, :, 0], in_=pb[3][:, 0:4], func=AF.Exp),
             R=[b_pb[3]], W=[b_egl])
        S.op("act", lambda e: e.activation(out=eglv[:, :, 1], in_=pb[3][:, 4:8], func=AF.Exp),
             R=[b_pb[3], b_egl], W=[b_egl])
        S.op("dve", lambda e: e.tensor_copy(out=P("glt")[0:64, :], in_=pb[3][0:64, 0:4]),
             R=[b_pb[3]], W=[B("glt")])
        S.op("dve", lambda e: e.tensor_copy(out=P("glt")[64:128, :], in_=pb[3][64:128, 4:8]),
             R=[b_pb[3], B("glt")], W=[B("glt")])
        S.op("dve", lambda e: e.tensor_tensor(out=P("ekd")[:], in0=P("glt")[:], in1=P("gc")[:], op=ALU.subtract),
             R=[B("glt"), B("gc")], W=[B("ekd")])
        S.op("act", lambda e: e.activation(out=P("ekd")[:], in_=P("ekd")[:], func=AF.Exp),
             R=[B("ekd")], W=[B("ekd")])
        S.op("act", lambda e: e.activation(out=P("egc")[:], in_=P("gc")[:], func=AF.Exp),
             R=[B("gc")], W=[B("egc")])
        S.op("dve", lambda e: e.tensor_tensor(out=P("bg")[:], in0=P("beta")[:], in1=P("egc")[:], op=ALU.mult),
             R=[B("beta"), B("egc")], W=[B("bg")])

        if dbg <= 6:
            continue
        S.defer = None
        na, nb = len(mA), len(mB)
        ia = ib = 0
        while ia < na or ib < nb:
            if ib >= nb or (ia < na and ia * nb <= ib * na):
                mA[ia]()
                ia += 1
            else:
                mB[ib]()
                ib += 1
        def tile_gen(tt):
            gs = gset[tt % NS]
            bk = 4 + tt
            G = lambda nm, gs=gs: gs[nm][0]
            GB = lambda nm, gs=gs: gs[nm][1]
            csl = slice(tt * 128, (tt + 1) * 128)
            S.op("dve", lambda e, G=G, tt=tt: e.tensor_scalar(out=G("dg")[:], in0=IDF, scalar1=P("gc")[:, tt:tt + 1],
                                                                scalar2=None, op0=ALU.mult),
                 R=[b_cf, B("gc")], W=[GB("dg")])

            def mm_abb(e, G=G, bk=bk, csl=csl):
                e.matmul(out=pb[bk][:, 0:128], lhsT=ONES, rhs=G("dg")[:], start=True, stop=True)
                e.matmul(out=pb[bk][:, 128:256], lhsT=knT[:, csl], rhs=knT[:, csl], start=True, stop=True)
                return e.matmul(out=pb[bk][:, 256:384], lhsT=knT[:, csl], rhs=qnT[:, csl], start=True, stop=True)
            yield
            S.op("pe", mm_abb, R=[b_cf, GB("dg"), b_knT, b_qnT], W=[b_pb[bk]])
            S.op("dve", lambda e, G=G, bk=bk, tt=tt: e.tensor_scalar(
                out=G("Eb")[:], in0=pb[bk][:, 0:128], scalar1=P("gc")[:, tt:tt + 1], scalar2=None,
                op0=ALU.subtract), R=[b_pb[bk], B("gc")], W=[GB("Eb")])
            S.op("act", lambda e, G=G: e.activation(out=G("Eb")[:], in_=G("Eb")[:], func=AF.Abs),
                 R=[GB("Eb")], W=[GB("Eb")])
            S.op("act", lambda e, G=G: e.activation(out=G("Ds")[:], in_=G("Eb")[:], func=AF.Exp, scale=-1.0),
                 R=[GB("Eb")], W=[GB("Ds")])
            S.op("act", lambda e, G=G, bk=bk: e.activation(out=G("EA")[:], in_=pb[bk][:, 0:128], func=AF.Exp),
                 R=[b_pb[bk]], W=[GB("EA")])
            S.op("dve", lambda e, G=G, bk=bk, tt=tt: e.scalar_tensor_tensor(
                out=G("t1")[:], in0=pb[bk][:, 128:256], scalar=P("beta")[:, tt:tt + 1], in1=G("Ds")[:],
                op0=ALU.mult, op1=ALU.mult), R=[b_pb[bk], B("beta"), GB("Ds")], W=[GB("t1")])
            S.op("dve", lambda e, G=G: e.tensor_tensor(out=G("MTa")[:], in0=G("t1")[:], in1=MASKL, op=ALU.mult),
                 R=[GB("t1"), b_cf], W=[GB("MTa")])
            S.op("dve", lambda e, G=G, bk=bk: e.tensor_tensor(out=G("t2")[:], in0=pb[bk][:, 256:384], in1=G("Ds")[:],
                                                              op=ALU.mult), R=[b_pb[bk], GB("Ds")], W=[GB("t2")])
            S.op("pool", lambda e, G=G, tt=tt: e.tensor_tensor(out=apT[:, tt, :], in0=G("t2")[:], in1=MASKU,
                                                               op=ALU.mult), R=[GB("t2"), b_cf], W=[b_apT])
            S.op("pool", lambda e, G=G, csl=csl: e.tensor_tensor(out=qgT[:, csl], in0=qnT[:, csl], in1=G("EA")[:],
                                                                 op=ALU.mult), R=[b_qnT, GB("EA")], W=[b_qgT])
            yield
            S.op("pe", lambda e, G=G, bk=bk: e.transpose(out=pb[bk][:, 384:512], in_=G("MTa")[:], identity=IDF),
                 R=[GB("MTa"), b_cf], W=[b_pb[bk]])
            S.op("act", lambda e, G=G, bk=bk: e.copy(out=G("MPa")[:, 0:128], in_=pb[bk][:, 384:512]),
                 R=[b_pb[bk]], W=[GB("MPa")])
            S.op("dve", lambda e, G=G, bk=bk: e.tensor_tensor(out=G("MPb")[:, 128:256], in0=pb[bk][:, 384:512],
                                                              in1=IDF, op=ALU.add),
                 R=[b_pb[bk], b_cf], W=[GB("MPb")])

            def st0(e, G=G, bk=bk):
                e.matmul(out=pb[bk][:, 0:128], lhsT=G("MTa")[:], rhs=G("MPa")[:, 0:128], start=True, stop=True)
                return e.matmul(out=pb[bk][:, 128:256], lhsT=G("MPa")[:, 0:128], rhs=G("MTa")[:], start=True,
                                stop=True)
            yield
            S.op("pe", st0, R=[GB("MTa"), GB("MPa")], W=[b_pb[bk]])
            S.op("act", lambda e, G=G, bk=bk: e.copy(out=G("MPb")[:, 0:128], in_=pb[bk][:, 0:128]),
                 R=[b_pb[bk]], W=[GB("MPb")])
            S.op("dve", lambda e, G=G, bk=bk: e.tensor_copy(out=G("MTb")[:], in_=pb[bk][:, 128:256]),
                 R=[b_pb[bk]], W=[GB("MTb")])
            cur, nxt = ("MPb", "MTb"), ("MPa", "MTa")
            for stp in range(1, 5):
                def stj(e, G=G, bk=bk, cur=cur):
                    e.matmul(out=pb[bk][:, 0:256], lhsT=G(cur[1])[:], rhs=G(cur[0])[:, 0:256], start=True, stop=True)
                    return e.matmul(out=pb[bk][:, 256:384], lhsT=G(cur[0])[:, 0:128], rhs=G(cur[1])[:], start=True,
                                    stop=True)
                yield
                S.op("pe", stj, R=[GB(cur[0]), GB(cur[1])], W=[b_pb[bk]])
                S.op("act", lambda e, G=G, bk=bk, nxt=nxt: e.copy(out=G(nxt[0])[:, 0:128], in_=pb[bk][:, 0:128]),
                     R=[b_pb[bk]], W=[GB(nxt[0])])
                S.op("dve", lambda e, G=G, bk=bk, cur=cur, nxt=nxt: e.tensor_tensor(
                    out=G(nxt[0])[:, 128:256], in0=pb[bk][:, 128:256], in1=G(cur[0])[:, 128:256], op=ALU.add),
                    R=[b_pb[bk], GB(cur[0])], W=[GB(nxt[0])])
                S.op("act", lambda e, G=G, bk=bk, nxt=nxt: e.copy(out=G(nxt[1])[:], in_=pb[bk][:, 256:384]),
                     R=[b_pb[bk]], W=[GB(nxt[1])])
                cur, nxt = nxt, cur
            yield
            S.op("pe", lambda e, G=G, bk=bk, cur=cur: e.matmul(out=pb[bk][:, 0:128], lhsT=G(cur[1])[:],
                                                               rhs=G(cur[0])[:, 128:256], start=True, stop=True),
                 R=[GB(cur[0]), GB(cur[1])], W=[b_pb[bk]])
            S.op("dve", lambda e, G=G, bk=bk, cur=cur: e.tensor_tensor(out=G("TT")[:], in0=pb[bk][:, 0:128],
                                                                       in1=G(cur[0])[:, 128:256], op=ALU.add),
                 R=[b_pb[bk], GB(cur[0])], W=[GB("TT")])
            S.op("pool", lambda e, G=G, tt=tt: e.tensor_scalar(out=G("vb")[:], in0=kvt[:, 2 * tt + 1, :],
                                                               scalar1=P("beta")[:, tt:tt + 1], scalar2=None,
                                                               op0=ALU.mult), R=[b_kvt, B("beta")], W=[GB("vb")])
            S.op("pool", lambda e, G=G, tt=tt: e.tensor_scalar(out=G("kbg")[:], in0=kvt[:, 2 * tt, :],
                                                               scalar1=P("bg")[:, tt:tt + 1], scalar2=None,
                                                               op0=ALU.mult), R=[b_kvt, B("bg")], W=[GB("kbg")])
            S.op("pool", lambda e, tt=tt: e.tensor_scalar(out=kd_sb[:, tt, :], in0=kvt[:, 2 * tt, :],
                                                          scalar1=P("ekd")[:, tt:tt + 1], scalar2=None,
                                                          op0=ALU.mult), R=[b_kvt, B("ekd")], W=[b_kd])

            def mm_uw(e, G=G, bk=bk):
                e.matmul(out=pb[bk][:, 0:128], lhsT=G("TT")[:], rhs=G("vb")[:], start=True, stop=True)
                return e.matmul(out=pb[bk][:, 128:256], lhsT=G("kbg")[:], rhs=G("TT")[:], start=True, stop=True)
            yield
            S.op("pe", mm_uw, R=[GB("TT"), GB("vb"), GB("kbg")], W=[b_pb[bk]])
            S.op("act", lambda e, bk=bk, tt=tt: e.copy(out=u_sb[:, tt, :], in_=pb[bk][:, 0:128]),
                 R=[b_pb[bk]], W=[b_u])
            S.op("act", lambda e, bk=bk, csl=csl: e.copy(out=wT_sb[:, csl], in_=pb[bk][:, 128:256]),
                 R=[b_pb[bk]], W=[b_wT])

        gens = [tile_gen(tt) for tt in range(4)]
        while gens:
            for g_ in gens[:]:
                try:
                    next(g_)
                except StopIteration:
                    gens.remove(g_)
            pull(NPULL)
        if dbg <= 7:
            continue
        S.defer = scan_l
        oi = j % 2
        for tt in range(4):
            csl = slice(tt * 128, (tt + 1) * 128)
            osb, b_o = o_sb[tt % 2], b_osb[tt % 2]
            for hh in range(2):
                r = slice(hh * 64, hh * 64 + 64)
                nl = 2 * tt + hh

                def mm1(e, csl=csl):
                    e.matmul(out=pb[6][:, 0:128], lhsT=wT_sb[:, csl], rhs=Sb[:], start=True, stop=True)
                    return e.matmul(out=pb[7][:, 0:128], lhsT=qgT[:, csl], rhs=Sb[:], start=True, stop=False)
                S.op("pe", mm1, R=[b_wT, b_qgT, b_Sb], W=[b_pb[6], b_pb[7]])
                S.op("dve", lambda e, r=r, tt=tt: e.tensor_tensor(out=vn[r, :], in0=u_sb[r, tt, :],
                                                                  in1=pb[6][r, 0:128], op=ALU.subtract),
                     R=[b_u, b_pb[6]], W=[b_vn])

                def mm2(e, r=r, tt=tt):
                    e.matmul(out=pb[7][:, 0:128], lhsT=apT[r, tt, :], rhs=vn[r, :], start=False, stop=True)
                    return e.matmul(out=pb[7][:, 128:256], lhsT=kd_sb[r, tt, :], rhs=vn[r, :], start=True, stop=True)
                S.op("pe", mm2, R=[b_apT, b_kd, b_vn], W=[b_pb[7]])
                S.op("act", lambda e, r=r, osb=osb: e.copy(out=osb[r, :], in_=pb[7][r, 0:128]),
                     R=[b_pb[7]], W=[b_o])
                S.op("dve", lambda e, nl=nl: e.scalar_tensor_tensor(out=S_f[:], in0=S_f[:], scalar=egl[:, nl:nl + 1],
                                                                    in1=pb[7][:, 128:256], op0=ALU.mult,
                                                                    op1=ALU.add),
                     R=[b_Sf, b_egl, b_pb[7]], W=[b_Sf])
                S.op("act", lambda e: e.copy(out=Sb[:], in_=S_f[:]), R=[b_Sf], W=[b_Sb])
            S.op("act", lambda e, osb=osb: e.activation(out=junkf[:], in_=osb[:], func=AF.Square,
                                                        accum_out=stat[:, 1:2]), R=[b_o], W=[b_junkf, b_stat])
            rstd_from_ss(1, 128, C_EPS6)
            S.op("dve", lambda e, osb=osb: e.scalar_tensor_tensor(out=on_[:], in0=osb[:], scalar=stat[:, 1:2],
                                                                  in1=lnw[:], op0=ALU.mult, op1=ALU.mult),
                 R=[b_o, b_stat, b_lnw], W=[b_on])
            S.op("dve", lambda e, tt=tt, oi=oi, zz=zs2[j % 2]: e.tensor_tensor(out=ola_st[oi][:, tt, :], in0=on_[:], in1=zz[:, tt, :],
                                                                op=ALU.mult), R=[b_on, b_zs2[j % 2]], W=[b_olast[oi]])
        S.dma("sp", ola_v[j], ola_st[oi][:], b_olast[oi], R=[b_olast[oi]])
        S.defer = None

        if dbg <= 8:
            pull_scan(len(scan_l))
            continue
        pull(len(pend))
        nscan = max(1, -(-len(scan_l) // (8 * j + 6)))
        for qq in range(2):
            qc = 2 * j + qq
            q0 = qc * 256
            qsl = slice(qq * 256, qq * 256 + 256)
            oi2 = qc % 2
            nkt = 2 * qc + 2
            def emit_qk(kt, qq=qq, qd=Qblk[j % 2], bq=b_Qblk[j % 2]):
                k0 = kt * 128
                bk = 4 + (kt % 2)
                S.op("pe", lambda e, bk=bk, k0=k0, qq=qq, qd=qd: e.matmul(
                    out=pb[bk][:, 0:512], lhsT=KdT[:, k0:k0 + 128], rhs=qd[:, qq, :], start=True, stop=True),
                    R=[b_KdT, bq], W=[b_pb[bk]])

            def emit_exp(kt, q0=q0):
                k0 = kt * 128
                d = q0 - k0
                par = kt % 2
                bk = 4 + par
                if d >= 256:
                    S.op("act", lambda e, bk=bk, par=par: e.activation(
                        out=pT2[par][:], in_=pb[bk][:, 0:512], func=AF.Exp, scale=0.125, bias=sc[:, 2:3]),
                        R=[b_pb[bk], b_sc], W=[b_pT2[par]])
                else:
                    for m in range(2):
                        S.op("dve", lambda e, bk=bk, m=m, d=d: e.scalar_tensor_tensor(
                            out=s2w[:, m * 256:(m + 1) * 256], in0=pb[bk][:, m * 256:(m + 1) * 256], scalar=0.125,
                            in1=bt[:, d + 128:d + 128 + 256], op0=ALU.mult, op1=ALU.add),
                            R=[b_pb[bk], b_bt], W=[b_s2w])
                    S.op("act", lambda e, par=par: e.activation(out=pT2[par][:], in_=s2w[:], func=AF.Exp),
                         R=[b_s2w], W=[b_pT2[par]])

            def emit_pv(kt, qc=qc):
                par = kt % 2
                for m in range(2):
                    for qb in range(2):
                        klast = 2 * qc + qb
                        if kt > klast:
                            continue
                        ab = qb * 2 + m
                        c0 = m * 256 + qb * 128
                        S.op("pe", lambda e, ab=ab, par=par, c0=c0, kt=kt, klast=klast: e.matmul(
                            out=pb[ab][:, 0:129], lhsT=pT2[par][:, c0:c0 + 128],
                            rhs=Vaug[:, kt, 0:129], start=(kt == 0), stop=(kt == klast)),
                            R=[b_pT2[par], b_Vaug], W=[b_pb[ab]])

            emit_qk(0)
            for kt in range(nkt):
                if kt + 1 < nkt:
                    emit_qk(kt + 1)
                emit_exp(kt)
                emit_pv(kt)
                pull_scan(nscan)
            for qb in range(2):
                a1, a2 = qb * 2, qb * 2 + 1
                S.op("dve", lambda e, a1=a1: e.reciprocal(out=rden[:, 0:1], in_=pb[a1][:, 128:129]),
                     R=[b_pb[a1]], W=[b_rden])
                S.op("dve", lambda e, a2=a2: e.reciprocal(out=rden[:, 1:2], in_=pb[a2][:, 128:129]),
                     R=[b_pb[a2], b_rden], W=[b_rden])
                S.op("dve", lambda e: e.tensor_scalar(out=rden[:, 2:3], in0=rden[:, 1:2], scalar1=C_NLAM,
                                                      scalar2=None, op0=ALU.mult), R=[b_rden, b_cst], W=[b_rden])
                S.op("act", lambda e, a1=a1: e.activation(out=O1[:], in_=pb[a1][:, 0:128], func=AF.Copy,
                                                          scale=rden[:, 0:1]), R=[b_pb[a1], b_rden], W=[b_O1])
                S.op("dve", lambda e, a2=a2: e.scalar_tensor_tensor(out=odf[:], in0=pb[a2][:, 0:128],
                                                                    scalar=rden[:, 2:3], in1=O1[:], op0=ALU.mult,
                                                                    op1=ALU.add),
                     R=[b_pb[a2], b_rden, b_O1], W=[b_odf])
                S.op("act", lambda e: e.activation(out=junkf[:], in_=odf[:], func=AF.Square,
                                                   accum_out=stat[:, 2:3]), R=[b_odf], W=[b_junkf, b_stat])
                rstd_from_ss(2, 128, C_EPS5)
                S.op("dve", lambda e, qb=qb, oi2=oi2: e.scalar_tensor_tensor(
                    out=od_st[oi2][:, qb, :], in0=odf[:], scalar=stat[:, 2:3], in1=dnw[:], op0=ALU.mult,
                    op1=ALU.mult), R=[b_odf, b_stat, b_dnw], W=[b_odst[oi2]])
            S.dma("sp", od_v[qc], od_st[oi2][:], b_odst[oi2], R=[b_odst[oi2]])
        pull_scan(len(scan_l))

    if ext:
        return None
    S.barrier_wait("sp", b_olast + b_odst)
    return kb.done()


def _t5_bucket_np(rel):
    n = np.maximum(rel, 0)
    nf = np.maximum(n, 1).astype(np.float32)
    large = 16 + (np.log(nf / np.float32(16)) / np.float32(math.log(128 / 16)) * np.float32(16)).astype(np.int32)
    large = np.minimum(large, 31)
    return np.where(n < 16, n, large)


def _mixer_consts():
    p = np.arange(128)
    same = (p[:, None] // 64) == (p[None, :] // 64)
    ident = np.eye(128, dtype=np.float32)
    ones = np.ones((128, 128), np.float32)
    maskl = np.where(same & (p[:, None] > p[None, :]), -1.0, 0.0).astype(np.float32)
    masku = np.where(same & (p[:, None] <= p[None, :]), 1.0, 0.0).astype(np.float32)
    tri = masku.copy()
    sel63 = np.zeros((128, 128), np.float32)
    sel63[63, :] = 1.0
    sel127 = np.zeros((128, 128), np.float32)
    sel127[127, :] = 1.0
    return np.ascontiguousarray(np.concatenate([ident, ones, maskl, masku, tri, sel63, sel127], axis=1))


def mixer_inputs(xb, l, h, P):
    w_in = P["w_in"][l]
    cols = np.concatenate([
        np.arange(h * 128, (h + 1) * 128),
        512 + np.arange(h * 128, (h + 1) * 128),
        1024 + np.arange(h * 128, (h + 1) * 128),
        2056 + np.arange(h * 128, (h + 1) * 128),
        2568 + np.arange(h * 128, (h + 1) * 128),
        1536 + np.arange(h * 128, (h + 1) * 128),
        3080 + np.arange(h * 128, (h + 1) * 128),
        np.array([2048 + h]),
        np.array([2052 + h]),
    ])
    wh = np.ascontiguousarray(w_in[:, cols])
    anw = np.ascontiguousarray(P["attn_norm_w"][l].reshape(8, 128).T)
    cwl = P["conv_w"][l]
    cw = np.concatenate([cwl[:, g * 512 + h * 128: g * 512 + (h + 1) * 128].T for g in range(3)], axis=1)
    sc = np.zeros((128, 4), np.float32)
    sc[:, 0] = P["a_log"][l, h]
    sc[:, 1] = P["dt_bias"][l, h]
    sc[:, 2] = P["rel_bias"][31, h]
    lamv = np.concatenate([P["lambda_q1"][l], P["lambda_k1"][l], P["lambda_q2"][l], P["lambda_k2"][l]])
    lamv = np.broadcast_to(lamv[None, :], (128, 256))
    lnw = np.broadcast_to(P["la_norm_w"][l][None, :], (128, 128))
    dnw = np.broadcast_to(P["diff_norm_w"][l][None, :], (128, 128))
    kl = np.arange(128)[:, None]
    jj = np.arange(512)[None, :]
    rel = jj - 128 - kl
    bt = np.where(rel >= 0, P["rel_bias"][_t5_bucket_np(rel), h], np.float32(-30000.0)).astype(np.float32)
    c = np.ascontiguousarray
    return {"x": c(xb), "wh": wh, "anw": anw, "cw": c(cw.astype(np.float32)), "sc": sc,
            "lamv": c(lamv.astype(np.float32)), "lnw": c(lnw.astype(np.float32)),
            "dnw": c(dnw.astype(np.float32)), "btoep": c(bt), "ident_bf": _ident_bf(), "cf": _mixer_consts()}


CC_GROUPS = [[0, 1, 2, 3], [4, 5, 6, 7]]
_MIX_KEYS = ("wh", "anw", "cw", "sc", "lamv", "lnw", "dnw")


def build_fused(T):
    kb = KB()
    nc, S = kb.nc, kb.S
    NT = T // NH
    NTL = NT // 128
    I32 = mybir.dt.int32
    shp = {"wh": [D_MODEL, NW], "anw": [128, 8], "cw": [128, 12], "sc": [128, 4], "lamv": [128, 256],
           "lnw": [128, 128], "dnw": [128, 128]}
    x_d = kb.din("x", [T, D_MODEL], F32)
    xs_d = kb.din("xs", [NT, D_MODEL], F32)
    idx_d = kb.din("idx", [128, NH * NTL], I32)
    bt_d = kb.din("btoep", [128, 512], F32)
    idb_d = kb.din("ident_bf", [128, 128], BF16)
    cf_d = kb.din("cf", [128, 7 * 128], F32)
    fin_d = kb.din("final_w_bc", [128, D_MODEL], F32)
    lay = []
    for l in range(DEPTH):
        d = {k: kb.din("%s%d" % (k, l), shp[k], F32) for k in _MIX_KEYS}
        d["w_out"] = kb.din("w_out%d" % l, [D_MODEL, D_MODEL], F32)
        d["w_gu"] = kb.din("w_gu%d" % l, [D_MODEL, 2 * D_FF], F32)
        d["w_down"] = kb.din("w_down%d" % l, [D_FF, D_MODEL], F32)
        d["ffn_norm_w"] = kb.din("fnw%d" % l, [128, 8], F32)
        lay.append(d)
    y_d = kb.dout("y", [NT, D_MODEL], F32)
    og_in = kb.dint("og_in", [T, 256], BF16)
    og_all = kb.dint("og_all", [NH * T, 256], BF16)
    xs_in = kb.dint("xs_in", [NT, D_MODEL], F32)
    x1_all = kb.dint("x1_all", [T, D_MODEL], F32)

    pb = [kb.ps([128, 512], F32, "pb%d" % i) for i in range(8)]
    b_pb = S.bufs(8, "pb", excl=True)
    ORC = min(T, 2048)
    NKO = T // ORC
    XRC = min(NT, 256)
    NKX = NT // XRC
    b_og_all = S.bufs(NKO, "og_all")
    b_x1all = S.bufs(NKX, "x1_all")
    kb.persist = b_pb + b_og_all + b_x1all

    def allgather(src, dst, b_dst, name):
        S.dma_fn("pool", lambda e: e.collective_compute("AllGather", ALU.bypass, replica_groups=CC_GROUPS,
                                                         ins=[src], outs=[dst]),
                 S.buf(name), W=[b_dst], inc=None)

    def x1_tile(t):
        tok = t * 128
        r, w = divmod(tok, NT)
        k, ww = divmod(w, XRC)
        base = k * (NH * XRC) + r * XRC + ww
        return x1_all[base:base + 128, :]

    for l in range(DEPTH):
        lam_init = 0.8 - 0.6 * math.exp(-0.3 * l)
        kb.begin_phase()
        if l > 0:
            for k in range(NKX):
                allgather(xs_in[k * XRC:(k + 1) * XRC, :], x1_all[k * NH * XRC:(k + 1) * NH * XRC, :], b_x1all[k],
                          "ccx%d_%d" % (l, k))
        ext = {"kb": kb, "pb": pb, "b_pb": b_pb, "x": x_d, "x_tile": (None if l == 0 else x1_tile),
               "b_x": (None if l == 0 else (lambda t: [b_x1all[((t * 128) % NT) // XRC]])), "og": og_in,
               "btoep": bt_d, "ident_bf": idb_d, "cf": cf_d}
        for k in _MIX_KEYS:
            ext[k] = lay[l][k]
        build_mixer(T, lam_init, ext=ext)
        kb.end_phase()
        kb.begin_phase()
        for k in range(NKO):
            allgather(og_in[k * ORC:(k + 1) * ORC, :], og_all[k * NH * ORC:(k + 1) * NH * ORC, :], b_og_all[k],
                      "cco%d_%d" % (l, k))
        last = (l == DEPTH - 1)
        ext = {"kb": kb, "pb": pb, "b_pb": b_pb, "x": (xs_d if l == 0 else xs_in), "y": (y_d if last else xs_in),
               "og_all": og_all, "b_og_all": b_og_all, "idx": idx_d, "T": T,
               "w_out": lay[l]["w_out"], "w_gu": lay[l]["w_gu"], "w_down": lay[l]["w_down"],
               "ffn_norm_w": lay[l]["ffn_norm_w"], "ident_bf": idb_d, "final_w_bc": fin_d}
        build_ffn(NT, last, ext=ext)
        kb.end_phase()
    kb.es.close()
    return nc


_FUSED_CACHE = {}


def _gather_idx(T, h):
    NT = T // NH
    NTL = NT // 128
    ORC = min(T, 2048)
    tok = h * NT + np.arange(NTL)[None, :] * 128 + np.arange(128)[:, None]
    k, w = tok // ORC, tok % ORC
    cols = [k * (NH * ORC) + r * ORC + w for r in range(NH)]
    return np.ascontiguousarray(np.concatenate(cols, axis=1).astype(np.int32))


def fused_inputs(P, T):
    NT = T // NH
    NTL = NT // 128
    perm = np.concatenate([np.concatenate([np.arange(r * 128, (r + 1) * 128),
                                           512 + np.arange(r * 128, (r + 1) * 128)]) for r in range(NH)])
    in_maps = []
    for c in range(NCORES):
        b, h = divmod(c, NH)
        xb = P["x"][b, :T]
        m = {"x": np.ascontiguousarray(xb), "xs": np.ascontiguousarray(xb[h * NT:(h + 1) * NT]),
             "idx": _gather_idx(T, h),
             "final_w_bc": np.ascontiguousarray(np.broadcast_to(P["final_norm_w"][None, :], (128, D_MODEL)))}
        for l in range(DEPTH):
            mi = mixer_inputs(xb, l, h, P)
            for k in _MIX_KEYS:
                m["%s%d" % (k, l)] = mi[k]
            if l == 0:
                m["btoep"], m["ident_bf"], m["cf"] = mi["btoep"], mi["ident_bf"], mi["cf"]
            m["w_out%d" % l] = np.ascontiguousarray(P["w_out"][l][perm])
            m["w_gu%d" % l] = P["w_gate_up"][l]
            m["w_down%d" % l] = P["w_down"][l]
            m["fnw%d" % l] = np.ascontiguousarray(P["ffn_norm_w"][l].reshape(8, 128).T)
        in_maps.append(m)
    return in_maps


def kernel_fused(P, T):
    if T not in _FUSED_CACHE:
        _FUSED_CACHE[T] = build_fused(T)
    nc = _FUSED_CACHE[T]
    NT = T // NH
    res = run_bass_kernel_spmd(nc, fused_inputs(P, T), core_ids=list(range(NCORES)))
    out = np.empty((BATCH, T, D_MODEL), np.float32)
    for c in range(NCORES):
        b, h = divmod(c, NH)
        out[b, h * NT:(h + 1) * NT] = np.asarray(res.results[c]["y"])
    return out


_MIX_CACHE = {}


def kernel(**inputs):
    P = {k: np.ascontiguousarray(np.asarray(v, dtype=np.float32)) for k, v in inputs.items()}
    return kernel_fused(P, P["x"].shape[1])


def kernel_unfused(**inputs):
    P = {k: np.ascontiguousarray(np.asarray(v, dtype=np.float32)) for k, v in inputs.items()}
    x = P["x"]
    B, T, D = x.shape
    NTOK = B * T
    per = NTOK // NCORES
    for l in range(DEPTH):
        lam_init = 0.8 - 0.6 * math.exp(-0.3 * l)
        key = (T, l)
        if key not in _MIX_CACHE:
            _MIX_CACHE[key] = build_mixer(T, lam_init)
        nc = _MIX_CACHE[key]
        in_maps = [mixer_inputs(x[c // NH], l, c % NH, P) for c in range(NCORES)]
        res = run_bass_kernel_spmd(nc, in_maps, core_ids=list(range(NCORES)))
        o = np.empty((B, T, D), dtype=ml_dtypes.bfloat16)
        for c in range(NCORES):
            b, h = divmod(c, NH)
            o[b, :, h * 128:(h + 1) * 128] = np.asarray(res.results[c]["o_la"])
            o[b, :, 512 + h * 128:512 + (h + 1) * 128] = np.asarray(res.results[c]["o_d"])
        xs = x.reshape(NTOK, D)
        os_ = o.reshape(NTOK, D)
        ys = run_ffn([xs[c * per:(c + 1) * per] for c in range(NCORES)],
                     [os_[c * per:(c + 1) * per] for c in range(NCORES)],
                     P["w_out"][l], P["ffn_norm_w"][l], P["w_gate_up"][l], P["w_down"][l],
                     P["final_norm_w"] if l == DEPTH - 1 else None)
        x = np.concatenate([np.asarray(y) for y in ys], axis=0).reshape(B, T, D)
    return np.ascontiguousarray(x.astype(np.float32))
```
